# Optimizing a Trainium2 kernel written in Bass

```python
import math
import jax, jax.numpy as jnp
from jax import lax
import numpy as np

D_MODEL = 1024
BATCH = 32
SEQ = 256
DEPTH = 2
DEC_BATCH = 8
DEC_SEQ = 2048
PAST_LEN = 256

GRID_W = 64
Q_BLOCK = 128
EPS = 1e-6
ROPE_BASE = 10000.0
A_HEADS = 8
A_KV = 2
A_GROUP = A_HEADS // A_KV
HEAD_DIM = 64
M_HEADS = 4
M_Q_RANK = 256
M_KV_RANK = 128
M_NOPE = 64
M_ROPE = 32
M_V = 64
C_GROUPS = 4
C_DIM = 64
C_CHUNK = 128
A_Q_W = A_HEADS * HEAD_DIM
A_KV_W = A_KV * HEAD_DIM
C_W = C_GROUPS * C_DIM
IN_SIZES = [A_Q_W, A_KV_W, A_KV_W, M_Q_RANK, M_KV_RANK, M_ROPE, C_W, C_W]
IN_SPLITS = [int(v) for v in np.cumsum(IN_SIZES)[:-1]]
IN_W = int(sum(IN_SIZES))
MIX_W = A_HEADS * HEAD_DIM + M_HEADS * M_V + C_W
PEER_HEADS = 8
N_KEYS = 128
N_EXPERTS = N_KEYS * N_KEYS
PEER_DK = 256
PEER_HALF = PEER_DK // 2
PEER_TOPK = 16
ALPHA = (2.0 * DEPTH) ** 0.25
BETA = (8.0 * DEPTH) ** -0.25

kernel_name = "hybrid_diffusion_gqa_mla_gmlp_peer_step"


def _ln(x, g=None, b=None):
    xf = x.astype(jnp.float32)
    mu = jnp.mean(xf, axis=-1, keepdims=True)
    var = jnp.mean(jnp.square(xf - mu), axis=-1, keepdims=True)
    y = (xf - mu) * lax.rsqrt(var + EPS)
    if g is not None:
        y = y * g.astype(jnp.float32) + b.astype(jnp.float32)
    return y.astype(x.dtype)


def _rms(x, g):
    xf = x.astype(jnp.float32)
    y = xf * lax.rsqrt(jnp.mean(jnp.square(xf), axis=-1, keepdims=True) + EPS) * g.astype(jnp.float32)
    return y.astype(x.dtype)


def _rot(x, ang):
    m = ang.shape[-1]
    c = jnp.cos(ang)[None, :, None, :]
    s = jnp.sin(ang)[None, :, None, :]
    x1, x2 = x[..., :m], x[..., m:]
    return jnp.concatenate([x1 * c - x2 * s, x1 * s + x2 * c], axis=-1)


def axial_rope(x):
    S, d = x.shape[1], x.shape[-1]
    rows = S // GRID_W
    m = d // 4
    row_idx = jnp.repeat(jnp.arange(rows), GRID_W).astype(jnp.float32)
    col_idx = jnp.tile(jnp.arange(GRID_W), rows).astype(jnp.float32)
    freqs = ROPE_BASE ** (-jnp.arange(m, dtype=jnp.float32) / m)
    xf = x.astype(jnp.float32)
    out = jnp.concatenate([_rot(xf[..., :2 * m], row_idx[:, None] * freqs),
                           _rot(xf[..., 2 * m:], col_idx[:, None] * freqs)], axis=-1)
    return out.astype(x.dtype)


def block_attention(q, k, v):
    B, Sq, Hk, G, dk = q.shape
    dv = v.shape[-1]
    nb = Sq // Q_BLOCK
    qb = jnp.moveaxis(q.reshape(B, nb, Q_BLOCK, Hk, G, dk), 1, 0)
    scale = 1.0 / math.sqrt(dk)

    def one(qblk):
        s = jnp.einsum('bqhgd,bkhd->bhgqk', qblk, k).astype(jnp.float32) * scale
        p = jax.nn.softmax(s, axis=-1).astype(v.dtype)
        return jnp.einsum('bhgqk,bkhe->bqhge', p, v)

    o = lax.map(one, qb)
    return jnp.moveaxis(o, 0, 1).reshape(B, Sq, Hk * G * dv)


def chunk_gmlp(u, v, w_s, b_s):
    B, S, _ = u.shape
    vg = _ln(v.reshape(B, S // C_CHUNK, C_CHUNK, C_GROUPS, C_DIM))
    mixed = jnp.einsum('gpq,bnqgd->bnpgd', w_s, vg) + b_s.T[None, None, :, :, None]
    return u * mixed.reshape(B, S, C_GROUPS * C_DIM)


def peer(h, w_q, k1, k2, u_tab, v_tab):
    B, S, D = h.shape
    xt = h.reshape(-1, Q_BLOCK, D)

    def one(xb):
        q = (xb @ w_q).reshape(Q_BLOCK, PEER_HEADS, 2, PEER_HALF)
        s1 = jnp.einsum('thd,nd->thn', q[:, :, 0], k1)
        s2 = jnp.einsum('thd,nd->thn', q[:, :, 1], k2)
        v1, i1 = lax.top_k(s1, PEER_TOPK)
        v2, i2 = lax.top_k(s2, PEER_TOPK)
        cand = (v1[..., :, None] + v2[..., None, :]).reshape(Q_BLOCK, PEER_HEADS, PEER_TOPK * PEER_TOPK)
        cidx = (i1[..., :, None] * N_KEYS + i2[..., None, :]).reshape(Q_BLOCK, PEER_HEADS, PEER_TOPK * PEER_TOPK)
        top, pos = lax.top_k(cand, PEER_TOPK)
        idx = jnp.take_along_axis(cidx, pos, axis=-1)
        g = jax.nn.softmax(top.astype(jnp.float32), axis=-1).astype(xb.dtype)
        a = jax.nn.gelu(jnp.einsum('thkd,td->thk', u_tab[idx], xb))
        return jnp.einsum('thk,thkd->td', g * a, v_tab[idx])

    return lax.map(one, xt).reshape(B, S, D)


def _layer(x, mod, P, cache):
    B, S, _ = x.shape
    sh1, sc1, g1, sh2, sc2, g2 = jnp.split(mod[:, None, :], 6, axis=-1)
    h = _ln(x) * (1 + sc1) + sh1
    q_a, k_a, v_a, cq, ckv, krope, u_c, v_c = jnp.split(h @ P['w_in'], IN_SPLITS, axis=-1)
    q_a = _rms(q_a.reshape(B, S, A_HEADS, HEAD_DIM), P['aqn'])
    k_a = _rms(k_a.reshape(B, S, A_KV, HEAD_DIM), P['akn'])
    v_a = v_a.reshape(B, S, A_KV, HEAD_DIM)
    cq = _rms(cq, P['mqn'])
    ckv = _rms(ckv, P['mkvn'])
    qm = (cq @ P['w_uq']).reshape(B, S, M_HEADS, M_NOPE + M_ROPE)
    qm_nope, qm_rope = qm[..., :M_NOPE], qm[..., M_NOPE:]
    if cache is None:
        new = (k_a, v_a, ckv, krope)
    else:
        new = None
        c_k, c_v, c_ckv, c_krope = cache
        q_a = axial_rope(q_a)
        qm_rope = axial_rope(qm_rope)
        k_a = jnp.concatenate([c_k, axial_rope(k_a)], axis=1)
        v_a = jnp.concatenate([c_v, v_a], axis=1)
        ckv = jnp.concatenate([c_ckv, ckv], axis=1)
        krope = jnp.concatenate([c_krope, axial_rope(krope[:, :, None, :])[:, :, 0]], axis=1)
    L = ckv.shape[1]
    kv = (ckv @ P['w_ukv']).reshape(B, L, M_HEADS, M_NOPE + M_V)
    k_m = jnp.concatenate([kv[..., :M_NOPE],
                           jnp.broadcast_to(krope[:, :, None, :], (B, L, M_HEADS, M_ROPE))], axis=-1)
    v_m = kv[..., M_NOPE:]
    q_m = jnp.concatenate([qm_nope, qm_rope], axis=-1)
    o_a = block_attention(q_a.reshape(B, S, A_KV, A_GROUP, HEAD_DIM), k_a, v_a)
    o_m = block_attention(q_m[:, :, :, None, :], k_m, v_m)
    o_c = chunk_gmlp(u_c, v_c, P['ws'], P['bs'])
    mix = jnp.concatenate([o_a, o_m, o_c], axis=-1) @ P['w_o']
    x = _ln(ALPHA * x + g1 * mix, P['ln1_g'], P['ln1_b'])
    h2 = _ln(x) * (1 + sc2) + sh2
    ff = peer(h2, P['pwq'], P['pk1'], P['pk2'], P['pu'], P['pv'])
    x = _ln(ALPHA * x + g2 * ff, P['ln2_g'], P['ln2_b'])
    return x, new


def setup_inputs(seed: int = 0) -> dict:
    key = jax.random.key(seed)
    ks = jax.random.split(key, 32)
    f32 = jnp.float32

    def nrm(k, shape, s):
        return jax.random.normal(k, shape, f32) * s

    D = D_MODEL
    return {
        "x_prompt": nrm(ks[0], (BATCH, SEQ, D), 1.0),
        "x_sample": nrm(ks[1], (DEC_BATCH, DEC_SEQ, D), 1.0),
        "cache_attn_k": nrm(ks[2], (DEC_BATCH, DEPTH, PAST_LEN, A_KV, HEAD_DIM), 1.0),
        "cache_attn_v": nrm(ks[3], (DEC_BATCH, DEPTH, PAST_LEN, A_KV, HEAD_DIM), 1.0),
        "cache_mla_ckv": nrm(ks[4], (DEC_BATCH, DEPTH, PAST_LEN, M_KV_RANK), 1.0),
        "cache_mla_krope": nrm(ks[5], (DEC_BATCH, DEPTH, PAST_LEN, M_ROPE), 1.0),
        "c": nrm(ks[6], (DEC_BATCH, D), 1.0),
        "c_ctx": nrm(ks[7], (D,), 1.0),
        "w_mod": nrm(ks[8], (DEPTH, D, 6 * D), 0.5 * D ** -0.5),
        "b_mod": nrm(ks[9], (DEPTH, 6 * D), 0.02),
        "w_in": nrm(ks[10], (DEPTH, D, IN_W), D ** -0.5),
        "attn_q_norm": 1.0 + nrm(ks[11], (DEPTH, HEAD_DIM), 0.02),
        "attn_k_norm": 1.0 + nrm(ks[12], (DEPTH, HEAD_DIM), 0.02),
        "mla_q_norm": 1.0 + nrm(ks[13], (DEPTH, M_Q_RANK), 0.02),
        "mla_kv_norm": 1.0 + nrm(ks[14], (DEPTH, M_KV_RANK), 0.02),
        "w_uq": nrm(ks[15], (DEPTH, M_Q_RANK, M_HEADS * (M_NOPE + M_ROPE)), M_Q_RANK ** -0.5),
        "w_ukv": nrm(ks[16], (DEPTH, M_KV_RANK, M_HEADS * (M_NOPE + M_V)), M_KV_RANK ** -0.5),
        "gmlp_ws": nrm(ks[17], (DEPTH, C_GROUPS, C_CHUNK, C_CHUNK), C_CHUNK ** -0.5),
        "gmlp_b": 1.0 + nrm(ks[18], (DEPTH, C_GROUPS, C_CHUNK), 0.02),
        "w_o": nrm(ks[19], (DEPTH, MIX_W, D), BETA * MIX_W ** -0.5),
        "ln1_g": 1.0 + nrm(ks[20], (DEPTH, D), 0.02),
        "ln1_b": nrm(ks[21], (DEPTH, D), 0.02),
        "ln2_g": 1.0 + nrm(ks[22], (DEPTH, D), 0.02),
        "ln2_b": nrm(ks[23], (DEPTH, D), 0.02),
        "peer_wq": nrm(ks[24], (DEPTH, D, PEER_HEADS * PEER_DK), D ** -0.5),
        "peer_k1": nrm(ks[25], (DEPTH, N_KEYS, PEER_HALF), PEER_HALF ** -0.5),
        "peer_k2": nrm(ks[26], (DEPTH, N_KEYS, PEER_HALF), PEER_HALF ** -0.5),
        "peer_u": nrm(ks[27], (DEPTH, N_EXPERTS, D), D ** -0.5),
        "peer_v": nrm(ks[28], (DEPTH, N_EXPERTS, D), BETA * PEER_HEADS ** -0.5),
    }


def reference(x_prompt, x_sample, cache_attn_k, cache_attn_v, cache_mla_ckv, cache_mla_krope, c, c_ctx,
              w_mod, b_mod, w_in, attn_q_norm, attn_k_norm, mla_q_norm, mla_kv_norm, w_uq, w_ukv,
              gmlp_ws, gmlp_b, w_o, ln1_g, ln1_b, ln2_g, ln2_b, peer_wq, peer_k1, peer_k2, peer_u, peer_v):
    xp, xs = x_prompt, x_sample
    st_k, st_v, st_ckv, st_kr = [], [], [], []
    for l in range(DEPTH):
        P = {'w_in': w_in[l], 'aqn': attn_q_norm[l], 'akn': attn_k_norm[l], 'mqn': mla_q_norm[l],
             'mkvn': mla_kv_norm[l], 'w_uq': w_uq[l], 'w_ukv': w_ukv[l], 'ws': gmlp_ws[l], 'bs': gmlp_b[l],
             'w_o': w_o[l], 'ln1_g': ln1_g[l], 'ln1_b': ln1_b[l], 'ln2_g': ln2_g[l], 'ln2_b': ln2_b[l],
             'pwq': peer_wq[l], 'pk1': peer_k1[l], 'pk2': peer_k2[l], 'pu': peer_u[l], 'pv': peer_v[l]}
        mod_ctx = (jax.nn.silu(c_ctx) @ w_mod[l] + b_mod[l])[None]
        mod_lat = jax.nn.silu(c) @ w_mod[l] + b_mod[l]
        xp, (k_l, v_l, ckv_l, kr_l) = _layer(xp, mod_ctx, P, None)
        st_k.append(k_l)
        st_v.append(v_l)
        st_ckv.append(ckv_l)
        st_kr.append(kr_l)
        cache_l = (cache_attn_k[:, l], cache_attn_v[:, l], cache_mla_ckv[:, l], cache_mla_krope[:, l])
        xs, _ = _layer(xs, mod_lat, P, cache_l)
    new_attn_k = jnp.stack(st_k, axis=1)
    new_attn_v = jnp.stack(st_v, axis=1)
    new_mla_ckv = jnp.stack(st_ckv, axis=1)
    new_mla_krope = jnp.stack(st_kr, axis=1)
    return (xp, xs, new_attn_k, new_attn_v, new_mla_ckv, new_mla_krope)
```

```python
import contextlib
import math
import os

import numpy as np
import concourse.bass as bass
import concourse.mybir as mybir
from concourse.bass_utils import run_bass_kernel_spmd

F32 = mybir.dt.float32
BF16 = mybir.dt.bfloat16
I32 = mybir.dt.int32
U32 = mybir.dt.uint32
AF = mybir.ActivationFunctionType
ALU = mybir.AluOpType
AX = mybir.AxisListType

D = 1024
NL = 2
IN_W = 1696
EPS = 1e-6
ALPHA = (2.0 * NL) ** 0.25
NCORES = 8
PB = 4
PS = 256
SS = 2048
PAST = 256
NTILES = (PB * PS + SS) // 128


class _Stop(Exception):
    pass


class Tile:
    def __init__(self, t, k):
        self.t = t
        self.k = k


def _keys(lst):
    out = []
    for x in lst:
        out.append(x.k if isinstance(x, Tile) else x)
    return out


class FW:
    def __init__(self, nc, es):
        self.nc = nc
        self.es = es
        self.eng = {'pe': nc.tensor, 'act': nc.scalar, 'dve': nc.vector, 'pool': nc.gpsimd, 'sp': nc.sync}
        self.sem = {}
        self.cnt = {}
        self.semobj = {}
        for e in self.eng:
            self.sem[e] = es.enter_context(nc.semaphore("sem_" + e))
            self.cnt[e] = 0
            self.semobj["sem_" + e] = self.sem[e]
        self.dsem = {}
        self.dall = []
        self.dfree = {'hw': [], 'sw': []}
        self.waited = {e: {} for e in self.eng}
        self.lastw = {}
        self.readers = {}
        self.ninst = 0
        self.nops = 0
        self.limit = int(os.environ["MK_LIMIT"]) if os.environ.get("MK_LIMIT") else None
        self.marks = []

    def mark(self, name):
        self.marks.append((name, self.nops))

    def _wait(self, e, ev):
        if ev is None:
            return
        name, val = ev
        if e == 'pe' and name == 'sem_pe':
            return
        w = self.waited[e]
        if w.get(name, 0) >= val:
            return
        w[name] = val
        self.eng[e].wait_ge(self.semobj[name], val)
        self.ninst += 1

    def _deps(self, e, reads, writes):
        for k in reads:
            self._wait(e, self.lastw.get(k))
        for k in writes:
            self._wait(e, self.lastw.get(k))
            for ev in self.readers.get(k, {}).values():
                self._wait(e, ev)

    def _record(self, ev, reads, writes):
        for k in reads:
            self.readers.setdefault(k, {})[ev[0]] = ev
        for k in writes:
            self.lastw[k] = ev
            self.readers[k] = {}

    def op(self, e, fn, r=(), w=(), inc=True):
        self.nops += 1
        if self.limit is not None and self.nops > self.limit:
            return None
        reads = _keys(r)
        writes = _keys(w)
        if e != 'pe':
            ps = [k for k in reads if k.startswith("ps")]
            if ps:
                reads = [k for k in reads if not k.startswith("ps")]
                writes = list(writes) + ps
        self._deps(e, reads, writes)
        inst = fn(self.eng[e])
        self.ninst += 1
        name = "sem_" + e
        if inc:
            self.cnt[e] += 1
            inst.then_inc(self.sem[e], 1)
            ev = (name, self.cnt[e])
        else:
            ev = (name, self.cnt[e] + 1)
        self._record(ev, reads, writes)
        return inst

    def dma(self, q, key, fn, r=(), w=()):
        self.nops += 1
        if self.limit is not None and self.nops > self.limit:
            return None
        reads = _keys(r)
        writes = _keys(w)
        self._deps(q, reads, writes)
        cls = 'sw' if q == 'pool' else 'hw'
        key = cls + ":" + key
        if key not in self.dsem:
            if self.dfree[cls]:
                d = self.dfree[cls].pop()
            else:
                nm = "dsem_%d" % len(self.dall)
                sm = self.es.enter_context(self.nc.semaphore(nm))
                self.semobj[nm] = sm
                d = [sm, 0, nm, cls]
                self.dall.append(d)
            self.dsem[key] = d
        d = self.dsem[key]
        inst = fn(self.eng[q])
        self.ninst += 1
        d[1] += 16
        inst.then_inc(d[0], 16)
        ev = (d[2], d[1])
        self._record(ev, reads, writes)
        return inst

    def _all_events(self):
        evs = [("sem_" + e, self.cnt[e]) for e in self.eng if self.cnt[e] > 0]
        evs += [(d[2], d[1]) for d in self.dall if d[1] > 0]
        return evs

    def barrier(self):
        evs = self._all_events()
        for e in self.eng:
            for ev in evs:
                self._wait(e, ev)
        self.lastw = {}
        self.readers = {}
        for d in self.dsem.values():
            self.dfree[d[3]].append(d)
        self.dsem = {}

    def finish(self):
        for ev in self._all_events():
            self._wait('sp', ev)


def mkap(ap, dims, off=0):
    return bass.AP(ap.tensor, ap.offset + off, [list(x) for x in dims])


def fdims(ap, dims, off=0):
    base = list(ap.ap)
    return bass.AP(ap.tensor, ap.offset + off, [list(base[0])] + [list(x) for x in dims])


def build_nc(dbg=False):
    nc = bass.Bass("TRN2", target_bir_lowering=False)

    def din(name, shape, dt=F32):
        return nc.dram_tensor(name, shape, dt, kind="ExternalInput").ap()

    def dout(name, shape, dt=F32):
        return nc.dram_tensor(name, shape, dt, kind="ExternalOutput").ap()

    xp = din("xp", [PB * PS, D])
    xs = din("xs", [SS, D])
    ck = din("ck", [NL, PAST, 128])
    cv = din("cv", [NL, PAST, 128])
    cckv = din("cckv", [NL, PAST, 128])
    ckr = din("ckr", [NL, PAST, 32])
    cT = din("cT", [128, 8, 2])
    w_mod = din("w_mod", [NL, D, 6 * D])
    b_modT = din("b_modT", [NL, 128, 48])
    b_mod = din("b_mod", [NL, 6 * D])
    w_in = din("w_in", [NL, D, IN_W])
    gqk = din("gqk", [NL, 640])
    mqn = din("mqn", [NL, 256])
    mkvn = din("mkvn", [NL, 128])
    w_uq = din("w_uq", [NL, 256, 384])
    w_ukv = din("w_ukv", [NL, 128, 512])
    wsT = din("wsT", [NL, 128, 4, 128])
    bsT = din("bsT", [NL, 128, 4])
    w_o = din("w_o", [NL, D, D])
    lnp = din("lnp", [NL, 4, D])
    peer_wq = din("peer_wq", [NL, D, 2048])
    k12T = din("k12T", [NL, 128, 2, 128])
    pu = [din("pu%d" % l, [16384, D]) for l in range(NL)]
    pv = [din("pv%d" % l, [16384, D]) for l in range(NL)]
    ropeC = din("ropeC", [SS, 640])
    ropeS = din("ropeS", [SS, 640])
    ropeMC = din("ropeMC", [SS, 160])
    ropeMS = din("ropeMS", [SS, 160])

    yp = dout("yp", [PB * PS, D])
    ys = dout("ys", [SS, D])
    nk = dout("nk", [PB, NL, PS, 128])
    nv = dout("nv", [PB, NL, PS, 128])
    nckv = dout("nckv", [PB, NL, PS, 128])
    nkr = dout("nkr", [PB, NL, PS, 32])
    if dbg:
        xsc = dout("xsc", [NL, 2, NTILES * 128, D])
    else:
        xsc = nc.dram_tensor("xsc", [NL, 2, NTILES * 128, D], F32, kind="Internal").ap()
    gsc = nc.dram_tensor("gsc", [NL, 2, 4, 128, D], F32, kind="Internal").ap()

    uid = [0]

    with contextlib.ExitStack() as es:
        fw = FW(nc, es)

        def sb(stack, name, shape, dt=F32):
            uid[0] += 1
            nm = "%s_%d" % (name, uid[0])
            return Tile(stack.enter_context(nc.sbuf_tensor(nm, shape, dt)), nm)

        def dve(fn, r=(), w=(), inc=True):
            return fw.op('dve', fn, r, w, inc)

        def act(fn, r=(), w=(), inc=True):
            return fw.op('act', fn, r, w, inc)

        def pe(fn, r=(), w=(), inc=True):
            return fw.op('pe', fn, r, w, inc)

        def pool(fn, r=(), w=(), inc=True):
            return fw.op('pool', fn, r, w, inc)

        def ld(key, out_ap, in_ap, r=(), w=(), q='sp'):
            return fw.dma(q, key, lambda e: e.dma_start(out=out_ap, in_=in_ap), r, w)

        psF = es.enter_context(nc.psum_tensor("psF", [128, 6, 512], F32))
        psB = es.enter_context(nc.psum_tensor("psB", [128, 2, 1024], BF16))
        PF = ["psF%d" % i for i in range(6)]
        PBK = ["psB0", "psB1"]

        ident = sb(es, "ident", [128, 128], BF16)
        pool(lambda p: p.memset(ident.t[:], 0.0), w=[ident])
        pool(lambda p: p.affine_select(out=ident.t[:], in_=ident.t[:], pattern=[[-1, 128]],
                                       compare_op=ALU.not_equal, fill=1.0, base=0, channel_multiplier=1),
             r=[ident], w=[ident])
        epsb = sb(es, "epsb", [128, 1])
        dve(lambda v: v.memset(epsb.t[:], EPS), w=[epsb])
        iota16 = sb(es, "iota16", [128, 16])
        iota16i = sb(es, "iota16i", [128, 16], I32)
        pool(lambda p: p.iota(iota16i.t[:], pattern=[[1, 16]], base=0, channel_multiplier=0), w=[iota16i])
        dve(lambda v: v.tensor_copy(out=iota16.t[:], in_=iota16i.t[:]), r=[iota16i], w=[iota16])
        modcol = sb(es, "modcol", [128, NL, 48, 2])

        with contextlib.ExitStack() as ps0:
            cTs = sb(ps0, "cTs", [128, 8, 2])
            scT = sb(ps0, "scT", [128, 8, 2], BF16)
            scR = sb(ps0, "scR", [128, 8, 2, 128], BF16)
            ld("cTs", cTs.t[:], cT, w=[cTs])
            act(lambda a: a.activation(out=scT.t[:], in_=cTs.t[:], func=AF.Silu), r=[cTs], w=[scT])
            dve(lambda v: v.tensor_copy(out=scR.t[:].rearrange("p a g m -> p (a g) m"),
                                        in_=fdims(scT.t[:], [[1, 16], [0, 128]])), r=[scT], w=[scR])
            bmT = sb(ps0, "bmT", [128, NL, 48])
            for l in range(NL):
                ld("bmT", bmT.t[:, l, :], b_modT[l], w=[bmT])
            wm = [sb(ps0, "wm%d" % i, [128, 8, 512], BF16) for i in range(2)]
            bmb = [sb(ps0, "bmb%d" % i, [128, 512]) for i in range(2)]
            gst = [sb(ps0, "gst%d" % i, [128, 512]) for i in range(2)]
            it = 0
            for l in range(NL):
                for ci in range(12):
                    role = ci // 2
                    W = wm[it % 2]
                    ld(W.k, W.t[:], w_mod[l, :, ci * 512:(ci + 1) * 512].rearrange("(c p) n -> p c n", p=128),
                       w=[W], q='pool')
                    if role in (0, 1, 3, 4):
                        for b4 in range(4):
                            blk = ci * 4 + b4
                            for kc in range(8):
                                pe(lambda t: t.matmul(psF[:, 4, b4 * 2:b4 * 2 + 2], lhsT=W.t[:, kc, b4 * 128:(b4 + 1) * 128],
                                                      rhs=scT.t[:, kc, :], start=(kc == 0), stop=(kc == 7)),
                                   r=[W, scT], w=[PF[4]], inc=(kc == 7))
                            addc = 1.0 if role in (1, 4) else 0.0
                            dve(lambda v: v.scalar_tensor_tensor(out=modcol.t[:, l, blk, :], in0=psF[:, 4, b4 * 2:b4 * 2 + 2],
                                                                 scalar=addc, in1=fdims(bmT.t[:, l, blk:blk + 1], [[0, 2]]),
                                                                 op0=ALU.add, op1=ALU.add),
                                r=[PF[4], bmT], w=[modcol])
                    if role in (2, 3, 4, 5):
                        slot = {2: 0, 4: 1, 3: 2, 5: 3}[role]
                        Bb = bmb[it % 2]
                        ld(Bb.k, Bb.t[:], mkap(b_mod[l, ci * 512:(ci + 1) * 512], [[0, 128], [1, 512]]), w=[Bb])
                        for g in range(2):
                            pb = PF[2 + g]
                            for kc in range(8):
                                pe(lambda t: t.matmul(psF[:, 2 + g, :], lhsT=scR.t[:, kc, g, :], rhs=W.t[:, kc, :],
                                                      start=(kc == 0), stop=(kc == 7)),
                                   r=[W, scR], w=[pb], inc=(kc == 7))
                            G = gst[g]
                            addc = 1.0 if role == 4 else 0.0
                            dve(lambda v: v.scalar_tensor_tensor(out=G.t[:], in0=psF[:, 2 + g, :], scalar=addc, in1=Bb.t[:],
                                                                 op0=ALU.add, op1=ALU.add),
                                r=[pb, Bb], w=[G])
                            half = ci % 2
                            ld("gst%d" % g, gsc[l, g, slot, :, half * 512:(half + 1) * 512], G.t[:], r=[G], w=["gsc"])
                    it += 1
        fw.barrier()

        def chk(tag):
            if dbg and os.environ.get("MK_STOP") == tag:
                raise _Stop()

        seqs = [(2 * b, 2, 0, False, b) for b in range(PB)] + [(8, 16, 1, True, -1)]
        if dbg and os.environ.get("MK_SEQS"):
            seqs = [seqs[int(i)] for i in os.environ["MK_SEQS"].split(",")]
        nlayers = int(os.environ.get("MK_LAYERS", NL)) if dbg else NL
        skip_peer = bool(dbg and os.environ.get("MK_SKIP_PEER"))
        stopA = bool(dbg and os.environ.get('MK_STOP') == 'A')
        stopB = bool(dbg and os.environ.get('MK_STOP') == 'B')

        def x_src(l, ph, gt):
            if l < 0:
                if gt < 8:
                    return xp[gt * 128:(gt + 1) * 128, :]
                return xs[(gt - 8) * 128:(gt - 7) * 128, :]
            return xsc[l, ph, gt * 128:(gt + 1) * 128, :]

        try:
            chk('pro')
            for (tile0, nt, grp, is_s, pbi) in seqs:
                nkc = nt + (2 if is_s else 0)
                koff = 2 if is_s else 0
                for l in range(nlayers):
                    with contextlib.ExitStack() as sa:
                        NTOK = nt * 128
                        NKEY = nkc * 128
                        QT = sb(sa, "QT", [128, 4, NTOK], BF16)
                        KT = sb(sa, "KT", [128, NKEY], BF16)
                        VA = sb(sa, "VA", [128, nkc, 2, 65], BF16)
                        QMT = sb(sa, "QMT", [128, 4, NTOK], BF16)
                        KMT = sb(sa, "KMT", [128, 4, NKEY], BF16)
                        VM = sb(sa, "VM", [128, nkc, 4, 65], BF16)
                        OC = sb(sa, "OC", [128, nt, 256], BF16)
                        pool(lambda p: p.memset(VA.t[:], 1.0), w=[VA])
                        pool(lambda p: p.memset(VM.t[:], 1.0), w=[VM])

                        sA = contextlib.ExitStack()
                        cur = [sA]
                        gq = sb(sA, "gq", [128, 640])
                        ld(gq.k, gq.t[:], mkap(gqk[l], [[0, 128], [1, 640]]), w=[gq])
                        gmq = sb(sA, "gmq", [128, 256])
                        ld(gmq.k, gmq.t[:], mkap(mqn[l], [[0, 128], [1, 256]]), w=[gmq])
                        gmk = sb(sA, "gmk", [128, 128])
                        ld(gmk.k, gmk.t[:], mkap(mkvn[l], [[0, 128], [1, 128]]), w=[gmk])
                        bst = sb(sA, "bst", [128, 4])
                        ld(bst.k, bst.t[:], bsT[l], w=[bst])
                        win = sb(sA, "win", [128, 8, IN_W], BF16)
                        for kc in range(8):
                            ld(win.k, win.t[:, kc, :], w_in[l, kc * 128:(kc + 1) * 128, :], w=[win], q='pool')
                        wuq = sb(sA, "wuq", [128, 2, 384], BF16)
                        ld(wuq.k, wuq.t[:], w_uq[l].rearrange("(c p) n -> p c n", p=128), w=[wuq], q='pool')
                        wukv = sb(sA, "wukv", [128, 512], BF16)
                        ld(wukv.k, wukv.t[:], w_ukv[l], w=[wukv], q='pool')
                        wst = sb(sA, "wst", [128, 4, 128], BF16)
                        ld(wst.k, wst.t[:], wsT[l], w=[wst], q='pool')

                        def rot(name, shape, dt=F32, n=2):
                            return [sb(cur[0], name + str(i), shape, dt) for i in range(n)]
                        xts = rot("xt", [128, D])
                        xnb = rot("xnb", [128, D], BF16, n=1)
                        hT = rot("hT", [128, 8, 128], BF16, n=1)
                        pj = rot("pj", [128, IN_W], n=1)
                        sq = rot("sq", [128, 1152], n=1)
                        small = rot("small", [128, 64], n=3)
                        qkn = rot("qkn", [128, 640], n=1)
                        rc_t = rot("ropeC", [128, 640], n=1)
                        rs_t = rot("ropeS", [128, 640], n=1)
                        rmc_t = rot("ropeMC", [128, 160], n=1)
                        rms_t = rot("ropeMS", [128, 160], n=1)
                        tmp640 = rot("tmp640", [128, 640], n=1)
                        tmp640b = rot("tmp640b", [128, 640], n=1)
                        qkb = rot("qkb", [128, 640], BF16)
                        cqb = rot("cqb", [128, 256], BF16)
                        cqT = rot("cqT", [128, 2, 128], BF16)
                        ckvn = rot("ckvn", [128, 128])
                        ckb = rot("ckb", [128, 128], BF16)
                        ckT = rot("ckT", [128, 128], BF16)
                        qm = rot("qm", [128, 4, 96])
                        mr = rot("mr", [128, 160], n=1)
                        mr3 = rot("mr3", [128, 160], n=1)
                        qmb = rot("qmb", [128, 4, 96], BF16)
                        kmb = rot("kmb", [128, 4, 96], BF16)
                        vg = rot("vg", [128, 256], BF16)
                        vtmp = rot("vtmp", [128, 256])
                        cst = rot("cst", [128, 128])
                        cstb = rot("cstb", [128, 128], BF16)
                        krs = rot("krs", [128, 32])
                        rr = {}

                        def nxt(lst):
                            i = rr.get(id(lst), 0)
                            rr[id(lst)] = i + 1
                            return lst[i % len(lst)]

                        def ln_stats(xin, xkeys):
                            S = nxt(small)
                            dve(lambda v: v.bn_stats(out=S.t[:, 8:14], in_=xin[:, 0:512]), r=xkeys, w=[S])
                            dve(lambda v: v.bn_stats(out=S.t[:, 14:20], in_=xin[:, 512:1024]), r=xkeys, w=[S])
                            dve(lambda v: v.bn_aggr(out=S.t[:, 0:2], in_=S.t[:, 8:20]), r=[S], w=[S])
                            act(lambda a: a.activation(out=S.t[:, 2:3], in_=S.t[:, 1:2], func=AF.Sqrt, bias=epsb.t[:], scale=1.0),
                                r=[S, epsb], w=[S])
                            dve(lambda v: v.reciprocal(out=S.t[:, 4:5], in_=S.t[:, 2:3]), r=[S], w=[S])
                            dve(lambda v: v.scalar_tensor_tensor(out=S.t[:, 5:6], in0=S.t[:, 0:1], scalar=-1.0, in1=S.t[:, 4:5],
                                                                 op0=ALU.mult, op1=ALU.mult), r=[S], w=[S])
                            return S

                        tcount = [0]

                        def transposes(srcs, dst_fn, rkeys, wkeys, evac='act'):
                            b = tcount[0] % 2
                            tcount[0] += 1
                            for i, (src, n) in enumerate(srcs):
                                pe(lambda t: t.transpose(out=psB[0:n, b, i * 128:(i + 1) * 128], in_=src, identity=ident.t[:]),
                                   r=list(rkeys) + [ident], w=[PBK[b]], inc=(i == len(srcs) - 1))
                            dst_fn(b)

                        def rope(src, nh, hd, ctab, stab, coff, dsts, rkeys, wkeys):
                            n = nh * hd
                            q4 = hd // 4
                            T1 = nxt(tmp640)
                            T2 = nxt(tmp640b)
                            dve(lambda v: v.tensor_tensor(out=T1.t[:, 0:n], in0=src, in1=ctab.t[:, coff:coff + n], op=ALU.mult),
                                r=list(rkeys) + [ctab], w=[T1])
                            nb = n // (2 * q4)
                            def hv(ap, off, half):
                                return fdims(ap, [[2 * q4, nb], [1, q4]], off + half * q4)
                            dve(lambda v: v.tensor_tensor(out=hv(T2.t[:, 0:n], 0, 0), in0=hv(src, 0, 1), in1=hv(stab.t[:, 0:n], coff, 0),
                                                          op=ALU.mult), r=list(rkeys) + [stab], w=[T2])
                            dve(lambda v: v.tensor_tensor(out=hv(T2.t[:, 0:n], 0, 1), in0=hv(src, 0, 0), in1=hv(stab.t[:, 0:n], coff, 1),
                                                          op=ALU.mult), r=list(rkeys) + [stab], w=[T2])
                            for (oap, i0, i1, j0, j1) in dsts:
                                dve(lambda v: v.tensor_tensor(out=oap, in0=i0(T1.t), in1=i0(T2.t), op=ALU.add), r=[T1, T2], w=wkeys)

                        def mla_keys(ckb_ap, ckb_keys, kr_ap, kr_keys, kc):
                            CT = nxt(ckT)
                            def ev(b):
                                act(lambda a: a.copy(out=CT.t[:], in_=psB[:, b, 0:128]), r=[PBK[b]], w=[CT])
                            transposes([(ckb_ap, 128)], ev, ckb_keys, None)
                            pe(lambda t: t.matmul(psF[:, 5, :], lhsT=CT.t[:], rhs=wukv.t[:], start=True, stop=True),
                               r=[CT, wukv], w=[PF[5]])
                            KB = nxt(kmb)
                            kvv = psF[:, 5, :].rearrange("p (h c) -> p h c", h=4)
                            act(lambda a: a.copy(out=KB.t[:, :, 0:64], in_=kvv[:, :, 0:64]), r=[PF[5]], w=[KB])
                            dve(lambda v: v.tensor_copy(out=VM.t[:, kc, :, 0:64], in_=kvv[:, :, 64:128]), r=[PF[5], VM], w=["VM.%d" % kc])
                            dve(lambda v: v.tensor_copy(out=KB.t[:, :, 64:96], in_=fdims(kr_ap, [[0, 4], [1, 32]])),
                                r=list(kr_keys), w=[KB])
                            def ev2(b):
                                act(lambda a: a.copy(out=KMT.t[0:96, :, kc * 128:(kc + 1) * 128],
                                                     in_=psB[0:96, b, 0:512].rearrange("p (h c) -> p h c", h=4)),
                                    r=[PBK[b]], w=["KMT.%d" % kc])
                            transposes([(KB.t[:, h, :], 96) for h in range(4)], ev2, [KB], None)

                        fw.mark('A-start')
                        if is_s:
                            for j in range(2):
                                C1 = nxt(cst)
                                ld(C1.k, C1.t[:], ck[l, j * 128:(j + 1) * 128, :], w=[C1])
                                CB = nxt(cstb)
                                dve(lambda v: v.tensor_copy(out=CB.t[:], in_=C1.t[:]), r=[C1], w=[CB])
                                def evk(b):
                                    act(lambda a: a.copy(out=KT.t[:, j * 128:(j + 1) * 128], in_=psB[:, b, 0:128]),
                                        r=[PBK[b]], w=["KT.%d" % j])
                                transposes([(CB.t[:], 128)], evk, [CB], None)
                                C2 = nxt(cst)
                                ld(C2.k, C2.t[:], cv[l, j * 128:(j + 1) * 128, :], w=[C2])
                                dve(lambda v: v.tensor_copy(out=VA.t[:, j, :, 0:64], in_=C2.t[:].rearrange("p (h c) -> p h c", h=2)),
                                    r=[C2, VA], w=["VA.%d" % j])
                                C3 = nxt(cst)
                                ld(C3.k, C3.t[:], cckv[l, j * 128:(j + 1) * 128, :], w=[C3])
                                CB3 = nxt(cstb)
                                dve(lambda v: v.tensor_copy(out=CB3.t[:], in_=C3.t[:]), r=[C3], w=[CB3])
                                K4 = nxt(krs)
                                ld(K4.k, K4.t[:], ckr[l, j * 128:(j + 1) * 128, :], w=[K4])
                                mla_keys(CB3.t[:], [CB3], K4.t[:], [K4], j)

                        for ti in range(nt):
                            gt = tile0 + ti
                            kc_new = koff + ti
                            X = nxt(xts)
                            ld(X.k, X.t[:], x_src(l - 1, 1, gt), w=[X])
                            fw.mark('tile%d-ln' % ti)
                            S = ln_stats(X.t[:], [X])
                            XN = nxt(xnb)
                            act(lambda a: a.activation(out=XN.t[:], in_=X.t[:], func=AF.Identity, scale=S.t[:, 4:5], bias=S.t[:, 5:6]),
                                r=[X, S], w=[XN])
                            H = nxt(hT)
                            def evh(b):
                                for c in range(8):
                                    act(lambda a: a.activation(out=H.t[:, c, :], in_=psB[:, b, c * 128:(c + 1) * 128], func=AF.Identity,
                                                               scale=modcol.t[:, l, 8 + c, grp:grp + 1], bias=modcol.t[:, l, c, grp:grp + 1]),
                                        r=[PBK[b], modcol], w=[H])
                            transposes([(XN.t[:, c * 128:(c + 1) * 128], 128) for c in range(8)], evh, [XN], None)
                            fw.mark('proj')
                            segs = [(0, 512), (512, 512), (1024, 512), (1536, 160)]
                            for bi, (c0, cn) in enumerate(segs):
                                for kc in range(8):
                                    pe(lambda t: t.matmul(psF[:, bi, 0:cn], lhsT=H.t[:, kc, :], rhs=win.t[:, kc, c0:c0 + cn],
                                                          start=(kc == 0), stop=(kc == 7)), r=[H, win], w=[PF[bi]], inc=(kc == 7))
                            P = nxt(pj)
                            for bi, (c0, cn) in enumerate(segs):
                                act(lambda a: a.copy(out=P.t[:, c0:c0 + cn], in_=psF[:, bi, 0:cn]), r=[PF[bi]], w=[P])
                            fw.mark('rms')
                            SQ = nxt(sq)
                            dve(lambda v: v.tensor_tensor(out=SQ.t[:], in0=P.t[:, 0:1152], in1=P.t[:, 0:1152], op=ALU.mult), r=[P], w=[SQ])
                            S2 = nxt(small)
                            dve(lambda v: v.tensor_reduce(out=S2.t[:, 0:10], in_=SQ.t[:, 0:640].rearrange("p (h c) -> p h c", c=64),
                                                          axis=AX.X, op=ALU.add), r=[SQ], w=[S2])
                            dve(lambda v: v.tensor_reduce(out=S2.t[:, 10:11], in_=SQ.t[:, 768:1024], axis=AX.X, op=ALU.add), r=[SQ], w=[S2])
                            dve(lambda v: v.tensor_reduce(out=S2.t[:, 11:12], in_=SQ.t[:, 1024:1152], axis=AX.X, op=ALU.add), r=[SQ], w=[S2])
                            dve(lambda v: v.tensor_scalar(out=S2.t[:, 16:26], in0=S2.t[:, 0:10], scalar1=1.0 / 64, scalar2=EPS,
                                                          op0=ALU.mult, op1=ALU.add), r=[S2], w=[S2])
                            dve(lambda v: v.tensor_scalar(out=S2.t[:, 26:27], in0=S2.t[:, 10:11], scalar1=1.0 / 256, scalar2=EPS,
                                                          op0=ALU.mult, op1=ALU.add), r=[S2], w=[S2])
                            dve(lambda v: v.tensor_scalar(out=S2.t[:, 27:28], in0=S2.t[:, 11:12], scalar1=1.0 / 128, scalar2=EPS,
                                                          op0=ALU.mult, op1=ALU.add), r=[S2], w=[S2])
                            act(lambda a: a.activation(out=S2.t[:, 32:44], in_=S2.t[:, 16:28], func=AF.Sqrt), r=[S2], w=[S2])
                            dve(lambda v: v.reciprocal(out=S2.t[:, 48:60], in_=S2.t[:, 32:44]), r=[S2], w=[S2])
                            fw.mark('qk')
                            QK = nxt(qkn)
                            dve(lambda v: v.tensor_tensor(out=QK.t[:].rearrange("p (h c) -> p h c", c=64),
                                                          in0=P.t[:, 0:640].rearrange("p (h c) -> p h c", c=64),
                                                          in1=fdims(S2.t[:, 48:58], [[1, 10], [0, 64]]), op=ALU.mult), r=[P, S2], w=[QK])
                            dve(lambda v: v.tensor_tensor(out=QK.t[:], in0=QK.t[:], in1=gq.t[:], op=ALU.mult), r=[QK, gq], w=[QK])
                            QB = nxt(qkb)
                            def qdst(T, j):
                                return T[:, j * 256:(j + 1) * 256].rearrange("p (s c) -> p s c", c=64)
                            def qout(j):
                                return fdims(QB.t[:], [[128, 4], [1, 64]], j * 64)
                            if is_s:
                                RC = rc_t[0]
                                RS = rs_t[0]
                                ld(RC.k, RC.t[:], ropeC[ti * 128:(ti + 1) * 128, :], w=[RC])
                                ld(RS.k, RS.t[:], ropeS[ti * 128:(ti + 1) * 128, :], w=[RS])
                                dsts = [(qout(0), lambda T: qdst(T, 0), None, 0, 0), (qout(1), lambda T: qdst(T, 1), None, 0, 0),
                                        (QB.t[:, 512:640], lambda T: T[:, 512:640], None, 0, 0)]
                                rope(QK.t[:], 10, 64, RC, RS, 0, dsts, [QK], [QB])
                            else:
                                for j in range(2):
                                    dve(lambda v: v.tensor_copy(out=qout(j), in_=qdst(QK.t, j)), r=[QK], w=[QB])
                                dve(lambda v: v.tensor_copy(out=QB.t[:, 512:640], in_=QK.t[:, 512:640]), r=[QK], w=[QB])
                            def evq(b):
                                act(lambda a: a.copy(out=QT.t[:, :, ti * 128:(ti + 1) * 128],
                                                     in_=psB[:, b, 0:512].rearrange("p (s c) -> p s c", s=4)),
                                    r=[PBK[b]], w=["QT.%d" % ti])
                                act(lambda a: a.copy(out=KT.t[:, kc_new * 128:(kc_new + 1) * 128], in_=psB[:, b, 512:640]),
                                    r=[PBK[b]], w=["KT.%d" % kc_new])
                            transposes([(QB.t[:, i * 128:(i + 1) * 128], 128) for i in range(5)], evq, [QB], None)
                            fw.mark('va')
                            dve(lambda v: v.tensor_copy(out=VA.t[:, kc_new, :, 0:64], in_=P.t[:, 640:768].rearrange("p (h c) -> p h c", h=2)),
                                r=[P, VA], w=["VA.%d" % kc_new])
                            fw.mark('mlaq')
                            CQ = nxt(cqb)
                            dve(lambda v: v.scalar_tensor_tensor(out=CQ.t[:], in0=P.t[:, 768:1024], scalar=S2.t[:, 58:59], in1=gmq.t[:],
                                                                 op0=ALU.mult, op1=ALU.mult), r=[P, S2, gmq], w=[CQ])
                            CQT = nxt(cqT)
                            def evc(b):
                                act(lambda a: a.copy(out=CQT.t[:], in_=psB[:, b, 0:256].rearrange("p (s c) -> p s c", s=2)),
                                    r=[PBK[b]], w=[CQT])
                            transposes([(CQ.t[:, i * 128:(i + 1) * 128], 128) for i in range(2)], evc, [CQ], None)
                            for kc in range(2):
                                pe(lambda t: t.matmul(psF[:, 4, 0:384], lhsT=CQT.t[:, kc, :], rhs=wuq.t[:, kc, :],
                                                      start=(kc == 0), stop=(kc == 1)), r=[CQT, wuq], w=[PF[4]], inc=(kc == 1))
                            QM = nxt(qm)
                            act(lambda a: a.copy(out=QM.t[:].rearrange("p h c -> p (h c)"), in_=psF[:, 4, 0:384]), r=[PF[4]], w=[QM])
                            QMB = nxt(qmb)
                            dve(lambda v: v.tensor_copy(out=QMB.t[:, :, 0:64], in_=QM.t[:, :, 0:64]), r=[QM], w=[QMB])
                            MR = nxt(mr)
                            dve(lambda v: v.tensor_copy(out=MR.t[:, 0:128].rearrange("p (h c) -> p h c", h=4), in_=QM.t[:, :, 64:96]),
                                r=[QM], w=[MR])
                            dve(lambda v: v.tensor_copy(out=MR.t[:, 128:160], in_=P.t[:, 1152:1184]), r=[P], w=[MR])
                            MR3 = nxt(mr3)
                            if is_s:
                                RMC = rmc_t[0]
                                RMS = rms_t[0]
                                ld(RMC.k, RMC.t[:], ropeMC[ti * 128:(ti + 1) * 128, :], w=[RMC])
                                ld(RMS.k, RMS.t[:], ropeMS[ti * 128:(ti + 1) * 128, :], w=[RMS])
                                rope(MR.t[:], 5, 32, RMC, RMS, 0, [(MR3.t[:], lambda T: T[:, 0:160], None, 0, 0)], [MR], [MR3])
                            else:
                                dve(lambda v: v.tensor_copy(out=MR3.t[:], in_=MR.t[:]), r=[MR], w=[MR3])
                            dve(lambda v: v.tensor_copy(out=QMB.t[:, :, 64:96], in_=MR3.t[:, 0:128].rearrange("p (h c) -> p h c", h=4)),
                                r=[MR3], w=[QMB])
                            def evqm(b):
                                act(lambda a: a.copy(out=QMT.t[0:96, :, ti * 128:(ti + 1) * 128],
                                                     in_=psB[0:96, b, 0:512].rearrange("p (h c) -> p h c", h=4)),
                                    r=[PBK[b]], w=["QMT.%d" % ti])
                            transposes([(QMB.t[:, h, :], 96) for h in range(4)], evqm, [QMB], None)
                            fw.mark('mlakv')
                            CK = nxt(ckvn)
                            dve(lambda v: v.scalar_tensor_tensor(out=CK.t[:], in0=P.t[:, 1024:1152], scalar=S2.t[:, 59:60], in1=gmk.t[:],
                                                                 op0=ALU.mult, op1=ALU.mult), r=[P, S2, gmk], w=[CK])
                            CKB = nxt(ckb)
                            dve(lambda v: v.tensor_copy(out=CKB.t[:], in_=CK.t[:]), r=[CK], w=[CKB])
                            mla_keys(CKB.t[:], [CKB], MR3.t[:, 128:160], [MR3], kc_new)
                            fw.mark('gmlp')
                            VT = nxt(vtmp)
                            S3 = nxt(small)
                            vcv = P.t[:, 1440:1696].rearrange("p (g c) -> p g c", g=4)
                            dve(lambda v: v.tensor_reduce(out=S3.t[:, 0:4], in_=vcv, axis=AX.X, op=ALU.add), r=[P], w=[S3])
                            dve(lambda v: v.tensor_tensor(out=VT.t[:], in0=P.t[:, 1440:1696], in1=P.t[:, 1440:1696], op=ALU.mult), r=[P], w=[VT])
                            dve(lambda v: v.tensor_reduce(out=S3.t[:, 4:8], in_=VT.t[:].rearrange("p (g c) -> p g c", g=4), axis=AX.X, op=ALU.add),
                                r=[VT], w=[S3])
                            dve(lambda v: v.tensor_scalar(out=S3.t[:, 8:12], in0=S3.t[:, 0:4], scalar1=1.0 / 64, scalar2=None, op0=ALU.mult),
                                r=[S3], w=[S3])
                            dve(lambda v: v.tensor_tensor(out=S3.t[:, 12:16], in0=S3.t[:, 8:12], in1=S3.t[:, 8:12], op=ALU.mult), r=[S3], w=[S3])
                            dve(lambda v: v.scalar_tensor_tensor(out=S3.t[:, 16:20], in0=S3.t[:, 4:8], scalar=1.0 / 64, in1=S3.t[:, 12:16],
                                                                 op0=ALU.mult, op1=ALU.subtract), r=[S3], w=[S3])
                            dve(lambda v: v.tensor_scalar(out=S3.t[:, 20:24], in0=S3.t[:, 16:20], scalar1=EPS, scalar2=None, op0=ALU.add),
                                r=[S3], w=[S3])
                            act(lambda a: a.activation(out=S3.t[:, 24:28], in_=S3.t[:, 20:24], func=AF.Sqrt), r=[S3], w=[S3])
                            dve(lambda v: v.reciprocal(out=S3.t[:, 28:32], in_=S3.t[:, 24:28]), r=[S3], w=[S3])
                            dve(lambda v: v.tensor_tensor(out=VT.t[:].rearrange("p (g c) -> p g c", g=4), in0=vcv,
                                                          in1=fdims(S3.t[:, 8:12], [[1, 4], [0, 64]]), op=ALU.subtract), r=[P, S3], w=[VT])
                            VG = nxt(vg)
                            dve(lambda v: v.tensor_tensor(out=VG.t[:].rearrange("p (g c) -> p g c", g=4),
                                                          in0=VT.t[:].rearrange("p (g c) -> p g c", g=4),
                                                          in1=fdims(S3.t[:, 28:32], [[1, 4], [0, 64]]), op=ALU.mult), r=[VT, S3], w=[VG])
                            for g in range(4):
                                pe(lambda t: t.matmul(psF[:, 4, g * 64:(g + 1) * 64], lhsT=wst.t[:, g, :], rhs=VG.t[:, g * 64:(g + 1) * 64],
                                                      start=True, stop=True), r=[VG, wst], w=[PF[4]], inc=(g == 3))
                            dve(lambda v: v.tensor_tensor(out=VT.t[:].rearrange("p (g c) -> p g c", g=4),
                                                          in0=psF[:, 4, 0:256].rearrange("p (g c) -> p g c", g=4),
                                                          in1=fdims(bst.t[:], [[1, 4], [0, 64]]), op=ALU.add), r=[PF[4], bst], w=[VT])
                            dve(lambda v: v.tensor_tensor(out=OC.t[:, ti, :], in0=VT.t[:], in1=P.t[:, 1184:1440], op=ALU.mult),
                                r=[VT, P], w=["OC.%d" % ti])
                            fw.mark('outs')
                            if not is_s:
                                r0 = ti * 128
                                ld("o_nk%d" % (ti % 2), nk[pbi, l, r0:r0 + 128, :], QK.t[:, 512:640], r=[QK])
                                ld("o_nv%d" % (ti % 2), nv[pbi, l, r0:r0 + 128, :], P.t[:, 640:768], r=[P])
                                ld("o_nc%d" % (ti % 2), nckv[pbi, l, r0:r0 + 128, :], CK.t[:], r=[CK])
                                ld("o_nr%d" % (ti % 2), nkr[pbi, l, r0:r0 + 128, :], P.t[:, 1152:1184], r=[P])

                        sA.close()
                        fw.barrier()
                        sB = contextlib.ExitStack()
                        cur[0] = sB
                        G1 = sb(sB, "G1", [128, D])
                        ld(G1.k, G1.t[:], gsc[l, grp, 0], w=[G1])
                        lnb = sb(sB, "lnb", [128, 2, D])
                        for j in range(2):
                            ld(lnb.k, lnb.t[:, j, :], mkap(lnp[l, j], [[0, 128], [1, D]]), w=[lnb])
                        wo = sb(sB, "wo", [128, 8, D], BF16)
                        for kc in range(8):
                            ld(wo.k, wo.t[:, kc, :], w_o[l, kc * 128:(kc + 1) * 128, :], w=[wo], q='pool')
                        xts = rot("xtB", [128, D])
                        small = rot("smallB", [128, 64], n=3)
                        ntq = 4 if is_s else 2
                        NQ = ntq * 128
                        PT = rot("PT", [128, NQ], BF16, n=3)
                        mix = sb(sB, "mix", [128, ntq, D], BF16)
                        mixT = rot("mixT", [128, 8, 128], BF16)
                        ybuf = rot("ybuf", [128, D], n=1)
                        x1b = rot("x1b", [128, D])
                        rec = rot("rec", [128, 8])
                        sc_ = [0]
                        oc_ = [0]
                        for qg in range(0 if stopA else nt // ntq):
                            q0 = qg * NQ
                            allkeys_q = ["QT.%d" % (qg * ntq + i) for i in range(ntq)]
                            allkeys_qm = ["QMT.%d" % (qg * ntq + i) for i in range(ntq)]
                            for hh in range(12):
                                isA = hh < 8
                                h = hh if isA else hh - 8
                                ob = 3 + (oc_[0] % 2)
                                oc_[0] += 1
                                oview = psF[:, ob, 0:ntq * 65].rearrange("p (q c) -> p q c", c=65)
                                for kc in range(nkc):
                                    sbk = sc_[0] % 3
                                    sc_[0] += 1
                                    if isA:
                                        base = (h // 4) * 64
                                        pe(lambda t: t.matmul(psF[:, sbk, 0:NQ], lhsT=KT.t[base:base + 64, kc * 128:(kc + 1) * 128],
                                                              rhs=QT.t[base:base + 64, h % 4, q0:q0 + NQ], start=True, stop=True),
                                           r=["KT.%d" % kc] + allkeys_q, w=[PF[sbk]])
                                        scl = 0.125
                                    else:
                                        pe(lambda t: t.matmul(psF[:, sbk, 0:NQ], lhsT=KMT.t[0:96, h, kc * 128:(kc + 1) * 128],
                                                              rhs=QMT.t[0:96, h, q0:q0 + NQ], start=True, stop=True),
                                           r=["KMT.%d" % kc] + allkeys_qm, w=[PF[sbk]])
                                        scl = 1.0 / math.sqrt(96.0)
                                    Pt = nxt(PT)
                                    act(lambda a: a.activation(out=Pt.t[:], in_=psF[:, sbk, 0:NQ], func=AF.Exp, scale=scl), r=[PF[sbk]], w=[Pt])
                                    for qt in range(ntq):
                                        if isA:
                                            rhs = VA.t[:, kc, h // 4, :]
                                            vk = "VA.%d" % kc
                                        else:
                                            rhs = VM.t[:, kc, h, :]
                                            vk = "VM.%d" % kc
                                        pe(lambda t: t.matmul(oview[:, qt, :], lhsT=Pt.t[:, qt * 128:(qt + 1) * 128], rhs=rhs,
                                                              start=(kc == 0 and qt == 0), stop=(kc == nkc - 1 and qt == ntq - 1)),
                                           r=[Pt, vk], w=[PF[ob]], inc=(qt == ntq - 1))
                                R = nxt(rec)
                                dve(lambda v: v.reciprocal(out=R.t[:, 0:ntq], in_=oview[:, :, 64]), r=[PF[ob]], w=[R])
                                col = h * 64 if isA else 512 + h * 64
                                dve(lambda v: v.tensor_tensor(out=mix.t[:, :, col:col + 64], in0=oview[:, :, 0:64],
                                                              in1=fdims(R.t[:, 0:ntq], [[1, ntq], [0, 64]]), op=ALU.mult),
                                    r=[PF[ob], R], w=[mix])
                            dve(lambda v: v.tensor_copy(out=mix.t[:, :, 768:1024], in_=OC.t[:, qg * ntq:(qg + 1) * ntq, :]),
                                r=["OC.%d" % (qg * ntq + i) for i in range(ntq)], w=[mix])
                            for qt in range(ntq):
                                ti = qg * ntq + qt
                                gt = tile0 + ti
                                MT = nxt(mixT)
                                def evm(b):
                                    act(lambda a: a.copy(out=MT.t[:].rearrange("p a c -> p (a c)"), in_=psB[:, b, :]), r=[PBK[b]], w=[MT])
                                transposes([(mix.t[:, qt, c * 128:(c + 1) * 128], 128) for c in range(8)], evm, [mix], None)
                                for hf in range(2):
                                    for kc in range(8):
                                        pe(lambda t: t.matmul(psF[:, hf, :], lhsT=MT.t[:, kc, :], rhs=wo.t[:, kc, hf * 512:(hf + 1) * 512],
                                                              start=(kc == 0), stop=(kc == 7)), r=[MT, wo], w=[PF[hf]], inc=(kc == 7))
                                X = nxt(xts)
                                ld(X.k, X.t[:], x_src(l - 1, 1, gt), w=[X])
                                Y = nxt(ybuf)
                                dve(lambda v: v.tensor_tensor(out=Y.t[:].rearrange("p (a c) -> p a c", a=2), in0=psF[:, 0:2, :],
                                                              in1=G1.t[:].rearrange("p (a c) -> p a c", a=2), op=ALU.mult),
                                    r=[PF[0], PF[1], G1], w=[Y])
                                dve(lambda v: v.scalar_tensor_tensor(out=Y.t[:], in0=X.t[:], scalar=ALPHA, in1=Y.t[:],
                                                                     op0=ALU.mult, op1=ALU.add), r=[X, Y], w=[Y])
                                S = ln_stats(Y.t[:], [Y])
                                X1 = nxt(x1b)
                                act(lambda a: a.activation(out=X1.t[:], in_=Y.t[:], func=AF.Identity, scale=S.t[:, 4:5], bias=S.t[:, 5:6]),
                                    r=[Y, S], w=[X1])
                                dve(lambda v: v.tensor_tensor(out=X1.t[:], in0=X1.t[:], in1=lnb.t[:, 0, :], op=ALU.mult), r=[X1, lnb], w=[X1])
                                dve(lambda v: v.tensor_tensor(out=X1.t[:], in0=X1.t[:], in1=lnb.t[:, 1, :], op=ALU.add), r=[X1, lnb], w=[X1])
                                ld(X1.k + "st", xsc[l, 0, gt * 128:(gt + 1) * 128, :], X1.t[:], r=[X1], w=["xsc.%d" % gt])
                        sB.close()
                    fw.barrier()

                    with contextlib.ExitStack() as sc:
                        G2 = sb(sc, "G2", [128, D])
                        ld(G2.k, G2.t[:], gsc[l, grp, 3], w=[G2])
                        A2 = sb(sc, "A2", [128, D])
                        ld(A2.k, A2.t[:], gsc[l, grp, 1], w=[A2])
                        B2 = sb(sc, "B2", [128, D])
                        ld(B2.k, B2.t[:], gsc[l, grp, 2], w=[B2])
                        lnb = sb(sc, "lnb2", [128, 2, D])
                        for j in range(2):
                            ld(lnb.k, lnb.t[:, j, :], mkap(lnp[l, 2 + j], [[0, 128], [1, D]]), w=[lnb])

                        def rot(name, shape, dt=F32, n=2):
                            return [sb(sc, name + str(i), shape, dt) for i in range(n)]
                        rr = {}

                        def nxt(lst):
                            i = rr.get(id(lst), 0)
                            rr[id(lst)] = i + 1
                            return lst[i % len(lst)]
                        xts = rot("cxt", [128, D])
                        small = rot("csmall", [128, 64], n=3)
                        h2 = rot("h2", [128, D], n=1)
                        h2b = rot("h2b", [128, D], BF16, n=1)
                        h2T = rot("h2T", [128, 8, 128], BF16, n=1)
                        ybuf = rot("cy", [128, D], n=1)
                        x2b = rot("x2b", [128, D])
                        acc = rot("acc", [128, D], n=1)
                        if not skip_peer:
                            wq = sb(sc, "wq", [128, 8, 2048], BF16)
                            for kc in range(8):
                                ld(wq.k, wq.t[:, kc, :], peer_wq[l, kc * 128:(kc + 1) * 128, :], w=[wq], q='pool')
                            kT = sb(sc, "kT", [128, 2, 128], BF16)
                            ld(kT.k, kT.t[:], k12T[l], w=[kT], q='pool')
                            qTs = rot("qTs", [128, 16, 128], BF16, n=1)
                            sc_s = rot("scs", [128, 16, 128], n=1)
                            sc_r = rot("scr", [128, 16, 128], n=1)
                            v12 = rot("v12", [128, 16, 16], n=1)
                            i12 = rot("i12", [128, 16, 16], U32, n=1)
                            i12f = rot("i12f", [128, 16, 16], n=1)
                            cand = rot("cand", [128, 8, 256], n=1)
                            top = rot("top", [128, 8, 16], n=1)
                            pos = rot("pos", [128, 8, 16], U32, n=1)
                            posr = rot("posr", [128, 128], U32, n=1)
                            posc = rot("posc", [128, 128], U32, n=1)
                            posrf = rot("posrf", [128, 128], n=1)
                            poscf = rot("poscf", [128, 128], n=1)
                            ik = rot("ik", [128, 128], n=1)
                            jk = rot("jk", [128, 128], n=1)
                            ef = rot("ef", [128, 128], n=1)
                            ei = rot("ei", [128, 128], I32, n=2)
                            gw = rot("gw", [128, 128], n=1)
                            gsm = rot("gsm", [128, 16], n=1)
                            araw = rot("araw", [128, 128], n=1)
                            wgt = rot("wgt", [128, 128], n=2)
                            junk = rot("junk", [128, D], n=1)
                            xnf = junk
                            NG = 8
                            UG = rot("UG", [128, D], n=NG)
                            VG_ = UG

                        def ln_stats(xin, xkeys):
                            S = nxt(small)
                            dve(lambda v: v.bn_stats(out=S.t[:, 8:14], in_=xin[:, 0:512]), r=xkeys, w=[S])
                            dve(lambda v: v.bn_stats(out=S.t[:, 14:20], in_=xin[:, 512:1024]), r=xkeys, w=[S])
                            dve(lambda v: v.bn_aggr(out=S.t[:, 0:2], in_=S.t[:, 8:20]), r=[S], w=[S])
                            act(lambda a: a.activation(out=S.t[:, 2:3], in_=S.t[:, 1:2], func=AF.Sqrt, bias=epsb.t[:], scale=1.0),
                                r=[S, epsb], w=[S])
                            dve(lambda v: v.reciprocal(out=S.t[:, 4:5], in_=S.t[:, 2:3]), r=[S], w=[S])
                            dve(lambda v: v.scalar_tensor_tensor(out=S.t[:, 5:6], in0=S.t[:, 0:1], scalar=-1.0, in1=S.t[:, 4:5],
                                                                 op0=ALU.mult, op1=ALU.mult), r=[S], w=[S])
                            return S

                        tcount = [0]
                        for ti in range(0 if (stopA or stopB) else nt):
                            gt = tile0 + ti
                            X = nxt(xts)
                            ld(X.k, X.t[:], xsc[l, 0, gt * 128:(gt + 1) * 128, :], r=["xsc.%d" % gt], w=[X])
                            AC = nxt(acc)
                            if skip_peer:
                                dve(lambda v: v.memset(AC.t[:], 0.0), w=[AC])
                            else:
                                S = ln_stats(X.t[:], [X])
                                XN = nxt(xnf)
                                act(lambda a: a.activation(out=XN.t[:], in_=X.t[:], func=AF.Identity, scale=S.t[:, 4:5], bias=S.t[:, 5:6]),
                                    r=[X, S], w=[XN])
                                H2 = nxt(h2)
                                dve(lambda v: v.tensor_tensor(out=H2.t[:], in0=XN.t[:], in1=A2.t[:], op=ALU.mult), r=[XN, A2], w=[H2])
                                dve(lambda v: v.tensor_tensor(out=H2.t[:], in0=H2.t[:], in1=B2.t[:], op=ALU.add), r=[H2, B2], w=[H2])
                                H2B = nxt(h2b)
                                act(lambda a: a.copy(out=H2B.t[:], in_=H2.t[:]), r=[H2], w=[H2B])
                                HT = nxt(h2T)
                                b = tcount[0] % 2
                                tcount[0] += 1
                                for c in range(8):
                                    pe(lambda t: t.transpose(out=psB[:, b, c * 128:(c + 1) * 128], in_=H2B.t[:, c * 128:(c + 1) * 128],
                                                             identity=ident.t[:]), r=[H2B, ident], w=[PBK[b]], inc=(c == 7))
                                act(lambda a: a.copy(out=HT.t[:].rearrange("p a c -> p (a c)"), in_=psB[:, b, :]), r=[PBK[b]], w=[HT])
                                QS = qTs[0]
                                for c in range(16):
                                    bk = c // 4
                                    for kc in range(8):
                                        pe(lambda t: t.matmul(psF[:, bk, (c % 4) * 128:(c % 4 + 1) * 128], lhsT=wq.t[:, kc, c * 128:(c + 1) * 128],
                                                              rhs=HT.t[:, kc, :], start=(kc == 0), stop=(kc == 7)),
                                           r=[HT, wq], w=[PF[bk]], inc=(kc == 7))
                                for bk in range(4):
                                    act(lambda a: a.copy(out=QS.t[:, bk * 4:(bk + 1) * 4, :].rearrange("p a c -> p (a c)"), in_=psF[:, bk, :]),
                                        r=[PF[bk]], w=[QS])
                                SS_ = sc_s[0]
                                for c in range(16):
                                    bk = c // 4
                                    pe(lambda t: t.matmul(psF[:, bk, (c % 4) * 128:(c % 4 + 1) * 128], lhsT=QS.t[:, c, :], rhs=kT.t[:, c // 8, :],
                                                          start=True, stop=True), r=[QS, kT], w=[PF[bk]], inc=(c % 4 == 3))
                                for bk in range(4):
                                    act(lambda a: a.copy(out=SS_.t[:, bk * 4:(bk + 1) * 4, :].rearrange("p a c -> p (a c)"), in_=psF[:, bk, :]),
                                        r=[PF[bk]], w=[SS_])
                                SR = sc_r[0]
                                V12 = v12[0]
                                I12 = i12[0]
                                for c in range(16):
                                    dve(lambda v: v.max(out=V12.t[:, c, 0:8], in_=SS_.t[:, c, :]), r=[SS_], w=[V12])
                                    dve(lambda v: v.max_index(out=I12.t[:, c, 0:8], in_max=V12.t[:, c, 0:8], in_values=SS_.t[:, c, :]),
                                        r=[SS_, V12], w=[I12])
                                    dve(lambda v: v.match_replace(out=SR.t[:, c, :], in_to_replace=V12.t[:, c, 0:8], in_values=SS_.t[:, c, :],
                                                                  imm_value=-1e30), r=[SS_, V12], w=[SR])
                                    dve(lambda v: v.max(out=V12.t[:, c, 8:16], in_=SR.t[:, c, :]), r=[SR], w=[V12])
                                    dve(lambda v: v.max_index(out=I12.t[:, c, 8:16], in_max=V12.t[:, c, 8:16], in_values=SR.t[:, c, :]),
                                        r=[SR, V12], w=[I12])
                                I12F = i12f[0]
                                dve(lambda v: v.tensor_copy(out=I12F.t[:], in_=I12.t[:]), r=[I12], w=[I12F])
                                CD = cand[0]
                                for hd in range(8):
                                    dve(lambda v: v.tensor_tensor(out=CD.t[:, hd, :].rearrange("p (r c) -> p r c", c=16),
                                                                  in0=fdims(V12.t[:, hd, :], [[1, 16], [0, 16]]),
                                                                  in1=fdims(V12.t[:, 8 + hd, :], [[0, 16], [1, 16]]), op=ALU.add),
                                        r=[V12], w=[CD])
                                CD2 = Tile(SR.t[:].rearrange("p a c -> p (a c)").rearrange("p (h c) -> p h c", h=8), SR.k)
                                TP = top[0]
                                PS_ = pos[0]
                                for hd in range(8):
                                    dve(lambda v: v.max(out=TP.t[:, hd, 0:8], in_=CD.t[:, hd, :]), r=[CD], w=[TP])
                                    dve(lambda v: v.max_index(out=PS_.t[:, hd, 0:8], in_max=TP.t[:, hd, 0:8], in_values=CD.t[:, hd, :]),
                                        r=[CD, TP], w=[PS_])
                                    dve(lambda v: v.match_replace(out=CD2.t[:, hd, :], in_to_replace=TP.t[:, hd, 0:8], in_values=CD.t[:, hd, :],
                                                                  imm_value=-1e30), r=[CD, TP], w=[CD2])
                                    dve(lambda v: v.max(out=TP.t[:, hd, 8:16], in_=CD2.t[:, hd, :]), r=[CD2], w=[TP])
                                    dve(lambda v: v.max_index(out=PS_.t[:, hd, 8:16], in_max=TP.t[:, hd, 8:16], in_values=CD2.t[:, hd, :]),
                                        r=[CD2, TP], w=[PS_])
                                PR = posr[0]
                                PC = posc[0]
                                pflat = PS_.t[:].rearrange("p h k -> p (h k)")
                                dve(lambda v: v.tensor_single_scalar(out=PR.t[:], in_=pflat, scalar=4, op=ALU.logical_shift_right), r=[PS_], w=[PR])
                                dve(lambda v: v.tensor_single_scalar(out=PC.t[:], in_=pflat, scalar=15, op=ALU.bitwise_and), r=[PS_], w=[PC])
                                PRF = posrf[0]
                                PCF = poscf[0]
                                dve(lambda v: v.tensor_copy(out=PRF.t[:], in_=PR.t[:]), r=[PR], w=[PRF])
                                dve(lambda v: v.tensor_copy(out=PCF.t[:], in_=PC.t[:]), r=[PC], w=[PCF])
                                OH = Tile(SS_.t[:].rearrange("p a c -> p (a c)").rearrange("p (k r) -> p k r", r=16), SS_.k)
                                IK = ik[0]
                                JK = jk[0]
                                for (PF_, dst, half) in ((PRF, IK, 0), (PCF, JK, 1)):
                                    dve(lambda v: v.tensor_tensor(out=OH.t[:], in0=fdims(PF_.t[:], [[1, 128], [0, 16]]),
                                                                  in1=fdims(iota16.t[:], [[0, 128], [1, 16]]), op=ALU.is_equal),
                                        r=[PF_, iota16], w=[OH])
                                    for hd in range(8):
                                        dve(lambda v: v.tensor_tensor(out=OH.t[:, hd * 16:(hd + 1) * 16, :], in0=OH.t[:, hd * 16:(hd + 1) * 16, :],
                                                                      in1=fdims(I12F.t[:, half * 8 + hd, :], [[0, 16], [1, 16]]), op=ALU.mult),
                                            r=[OH, I12F], w=[OH])
                                    dve(lambda v: v.tensor_reduce(out=dst.t[:], in_=OH.t[:], axis=AX.X, op=ALU.add), r=[OH], w=[dst])
                                EF = ef[0]
                                dve(lambda v: v.scalar_tensor_tensor(out=EF.t[:], in0=IK.t[:], scalar=128.0, in1=JK.t[:], op0=ALU.mult, op1=ALU.add),
                                    r=[IK, JK], w=[EF])
                                EI = nxt(ei)
                                dve(lambda v: v.tensor_copy(out=EI.t[:], in_=EF.t[:]), r=[EF], w=[EI])
                                GW = gw[0]
                                GS = gsm[0]
                                dve(lambda v: v.tensor_tensor(out=GW.t[:].rearrange("p (h k) -> p h k", k=16), in0=TP.t[:],
                                                              in1=fdims(TP.t[:, :, 0], [[16, 8], [0, 16]]), op=ALU.subtract), r=[TP], w=[GW])
                                act(lambda a: a.activation(out=GW.t[:], in_=GW.t[:], func=AF.Exp), r=[GW], w=[GW])
                                dve(lambda v: v.tensor_reduce(out=GS.t[:, 0:8], in_=GW.t[:].rearrange("p (h k) -> p h k", k=16), axis=AX.X, op=ALU.add),
                                    r=[GW], w=[GS])
                                dve(lambda v: v.reciprocal(out=GS.t[:, 8:16], in_=GS.t[:, 0:8]), r=[GS], w=[GS])
                                dve(lambda v: v.tensor_tensor(out=GW.t[:].rearrange("p (h k) -> p h k", k=16),
                                                              in0=GW.t[:].rearrange("p (h k) -> p h k", k=16),
                                                              in1=fdims(GS.t[:, 8:16], [[1, 8], [0, 16]]), op=ALU.mult), r=[GW, GS], w=[GW])
                                AR = araw[0]
                                dve(lambda v: v.memset(AR.t[:], 0.0), w=[AR])
                                JK_ = junk[0]
                                for k in range(128):
                                    U = nxt(UG)
                                    fw.dma('pool', U.k, lambda e: e.indirect_dma_start(
                                        out=U.t[:], out_offset=None, in_=pu[l][:, :],
                                        in_offset=bass.IndirectOffsetOnAxis(ap=EI.t[:, k:k + 1], axis=0)), r=[EI], w=[U])
                                    dve(lambda v: v.scalar_tensor_tensor(out=JK_.t[:], in0=U.t[:], scalar=1.0, in1=H2.t[:], op0=ALU.mult, op1=ALU.mult,
                                                                         accum_out=AR.t[:, k:k + 1]), r=[U, H2, AR], w=[JK_, AR])
                                WG = nxt(wgt)
                                act(lambda a: a.activation(out=WG.t[:], in_=AR.t[:], func=AF.Gelu_apprx_tanh), r=[AR], w=[WG])
                                dve(lambda v: v.tensor_tensor(out=WG.t[:], in0=WG.t[:], in1=GW.t[:], op=ALU.mult), r=[WG, GW], w=[WG])
                                for k in range(128):
                                    Vt = nxt(VG_)
                                    fw.dma('pool', Vt.k, lambda e: e.indirect_dma_start(
                                        out=Vt.t[:], out_offset=None, in_=pv[l][:, :],
                                        in_offset=bass.IndirectOffsetOnAxis(ap=EI.t[:, k:k + 1], axis=0)), r=[EI], w=[Vt])
                                    if k == 0:
                                        dve(lambda v: v.tensor_scalar(out=AC.t[:], in0=Vt.t[:], scalar1=WG.t[:, 0:1], scalar2=None, op0=ALU.mult),
                                            r=[Vt, WG], w=[AC])
                                    else:
                                        dve(lambda v: v.scalar_tensor_tensor(out=AC.t[:], in0=Vt.t[:], scalar=WG.t[:, k:k + 1], in1=AC.t[:],
                                                                             op0=ALU.mult, op1=ALU.add), r=[Vt, WG, AC], w=[AC])
                            Y = nxt(ybuf)
                            dve(lambda v: v.tensor_tensor(out=Y.t[:], in0=AC.t[:], in1=G2.t[:], op=ALU.mult), r=[AC, G2], w=[Y])
                            dve(lambda v: v.scalar_tensor_tensor(out=Y.t[:], in0=X.t[:], scalar=ALPHA, in1=Y.t[:], op0=ALU.mult, op1=ALU.add),
                                r=[X, Y], w=[Y])
                            S = ln_stats(Y.t[:], [Y])
                            X2 = nxt(x2b)
                            act(lambda a: a.activation(out=X2.t[:], in_=Y.t[:], func=AF.Identity, scale=S.t[:, 4:5], bias=S.t[:, 5:6]),
                                r=[Y, S], w=[X2])
                            dve(lambda v: v.tensor_tensor(out=X2.t[:], in0=X2.t[:], in1=lnb.t[:, 0, :], op=ALU.mult), r=[X2, lnb], w=[X2])
                            dve(lambda v: v.tensor_tensor(out=X2.t[:], in0=X2.t[:], in1=lnb.t[:, 1, :], op=ALU.add), r=[X2, lnb], w=[X2])
                            if l == NL - 1:
                                if gt < 8:
                                    dst = yp[gt * 128:(gt + 1) * 128, :]
                                else:
                                    dst = ys[(gt - 8) * 128:(gt - 7) * 128, :]
                                ld(X2.k + "st", dst, X2.t[:], r=[X2])
                                if dbg:
                                    ld(X2.k + "st", xsc[l, 1, gt * 128:(gt + 1) * 128, :], X2.t[:], r=[X2], w=["xsc1.%d" % gt])
                            else:
                                ld(X2.k + "st", xsc[l, 1, gt * 128:(gt + 1) * 128, :], X2.t[:], r=[X2], w=["xsc1.%d" % gt])
                    fw.barrier()
        except _Stop:
            pass
        fw.finish()
        build_nc.ninst = fw.ninst
        build_nc.marks = fw.marks
    return nc


def _rope_tables():
    t = np.arange(SS)
    row = (t // 64).astype(np.float32)
    col = (t % 64).astype(np.float32)

    def tab(m, nh):
        freqs = (10000.0 ** (-np.arange(m, dtype=np.float32) / m)).astype(np.float32)
        ar = row[:, None] * freqs[None, :]
        ac = col[:, None] * freqs[None, :]
        cr, sr, cc, sn = np.cos(ar), np.sin(ar), np.cos(ac), np.sin(ac)
        C = np.concatenate([cr, cr, cc, cc], axis=1)
        S = np.concatenate([-sr, sr, -sn, sn], axis=1)
        return (np.tile(C, (1, nh)).astype(np.float32), np.tile(S, (1, nh)).astype(np.float32))
    C16, S16 = tab(16, 10)
    C8, S8 = tab(8, 5)
    return C16, S16, C8, S8


def make_in_maps(inp):
    f = lambda a: np.ascontiguousarray(np.asarray(a, dtype=np.float32))
    C16, S16, C8, S8 = _rope_tables()
    w_mod = f(inp["w_mod"])
    b_mod = f(inp["b_mod"])
    b_modT = f(b_mod.reshape(NL, 48, 128).transpose(0, 2, 1))
    gqk = f(np.concatenate([np.tile(inp["attn_q_norm"], (1, 8)), np.tile(inp["attn_k_norm"], (1, 2))], axis=1))
    wsT = f(np.asarray(inp["gmlp_ws"]).transpose(0, 3, 1, 2))
    bsT = f(np.asarray(inp["gmlp_b"]).transpose(0, 2, 1))
    lnp = f(np.stack([inp["ln1_g"], inp["ln1_b"], inp["ln2_g"], inp["ln2_b"]], axis=1))
    pwq = f(np.asarray(inp["peer_wq"]).reshape(NL, D, 8, 2, 128).transpose(0, 1, 3, 2, 4).reshape(NL, D, 2048))
    k12T = f(np.stack([np.asarray(inp["peer_k1"]).transpose(0, 2, 1), np.asarray(inp["peer_k2"]).transpose(0, 2, 1)], axis=2))
    shared = {
        "w_mod": w_mod, "b_modT": b_modT, "b_mod": b_mod, "w_in": f(inp["w_in"]), "gqk": gqk,
        "mqn": f(inp["mla_q_norm"]), "mkvn": f(inp["mla_kv_norm"]), "w_uq": f(inp["w_uq"]), "w_ukv": f(inp["w_ukv"]),
        "wsT": wsT, "bsT": bsT, "w_o": f(inp["w_o"]), "lnp": lnp, "peer_wq": pwq, "k12T": k12T,
        "pu0": f(inp["peer_u"][0]), "pu1": f(inp["peer_u"][1]), "pv0": f(inp["peer_v"][0]), "pv1": f(inp["peer_v"][1]),
        "ropeC": C16, "ropeS": S16, "ropeMC": C8, "ropeMS": S8,
    }
    maps = []
    xpr = np.asarray(inp["x_prompt"], dtype=np.float32)
    xsa = np.asarray(inp["x_sample"], dtype=np.float32)
    cctx = np.asarray(inp["c_ctx"], dtype=np.float32)
    for c in range(NCORES):
        m = dict(shared)
        m["xp"] = f(xpr[PB * c:PB * (c + 1)].reshape(PB * PS, D))
        m["xs"] = f(xsa[c])
        m["ck"] = f(np.asarray(inp["cache_attn_k"])[c].reshape(NL, PAST, 128))
        m["cv"] = f(np.asarray(inp["cache_attn_v"])[c].reshape(NL, PAST, 128))
        m["cckv"] = f(np.asarray(inp["cache_mla_ckv"])[c])
        m["ckr"] = f(np.asarray(inp["cache_mla_krope"])[c])
        cc = np.stack([cctx, np.asarray(inp["c"], dtype=np.float32)[c]], axis=-1)
        m["cT"] = f(cc.reshape(8, 128, 2).transpose(1, 0, 2))
        maps.append(m)
    return maps


_NC_CACHE = {}


def kernel(**inputs):
    if "nc" not in _NC_CACHE:
        _NC_CACHE["nc"] = build_nc(False)
    nc = _NC_CACHE["nc"]
    maps = make_in_maps(inputs)
    res = run_bass_kernel_spmd(nc, maps, core_ids=list(range(NCORES)))
    R = res.results
    y_prompt = np.concatenate([R[c]["yp"].reshape(PB, PS, D) for c in range(NCORES)], axis=0)
    y_sample = np.stack([R[c]["ys"] for c in range(NCORES)], axis=0)
    nk = np.concatenate([R[c]["nk"].reshape(PB, NL, PS, 2, 64) for c in range(NCORES)], axis=0)
    nv = np.concatenate([R[c]["nv"].reshape(PB, NL, PS, 2, 64) for c in range(NCORES)], axis=0)
    nckv = np.concatenate([R[c]["nckv"] for c in range(NCORES)], axis=0)
    nkr = np.concatenate([R[c]["nkr"] for c in range(NCORES)], axis=0)
    return (y_prompt.astype(np.float32), y_sample.astype(np.float32), nk.astype(np.float32), nv.astype(np.float32),
            nckv.astype(np.float32), nkr.astype(np.float32))
```

```python
import contextlib
import math
import os

import numpy as np
import concourse.bass as bass
import concourse.mybir as mybir
from concourse.bass_utils import run_bass_kernel_spmd

F32 = mybir.dt.float32
BF16 = mybir.dt.bfloat16
I32 = mybir.dt.int32
U32 = mybir.dt.uint32
AF = mybir.ActivationFunctionType
ALU = mybir.AluOpType
AX = mybir.AxisListType

D = 1024
NL = 2
IN_W = 1696
EPS = 1e-6
ALPHA = (2.0 * NL) ** 0.25
NCORES = 8
PB = 4
PS = 256
SS = 2048
PAST = 256
NTILES = (PB * PS + SS) // 128


class _Stop(Exception):
    pass


class Tile:
    def __init__(self, t, k):
        self.t = t
        self.k = k


def _keys(lst):
    out = []
    for x in lst:
        out.append(x.k if isinstance(x, Tile) else x)
    return out


class FW:
    def __init__(self, nc, es):
        self.nc = nc
        self.es = es
        self.eng = {'pe': nc.tensor, 'act': nc.scalar, 'dve': nc.vector, 'pool': nc.gpsimd, 'sp': nc.sync}
        self.sem = {}
        self.cnt = {}
        self.semobj = {}
        for e in self.eng:
            self.sem[e] = es.enter_context(nc.semaphore("sem_" + e))
            self.cnt[e] = 0
            self.semobj["sem_" + e] = self.sem[e]
        self.dsem = {}
        self.dall = []
        self.dfree = {'hw': [], 'sw': []}
        self.waited = {e: {} for e in self.eng}
        self.lastw = {}
        self.readers = {}
        self.ninst = 0
        self.nops = 0
        self.limit = int(os.environ["MK_LIMIT"]) if os.environ.get("MK_LIMIT") else None
        self.marks = []

    def mark(self, name):
        self.marks.append((name, self.nops))

    def _wait(self, e, ev):
        if ev is None:
            return
        name, val = ev
        if e == 'pe' and name == 'sem_pe':
            return
        w = self.waited[e]
        if w.get(name, 0) >= val:
            return
        w[name] = val
        self.eng[e].wait_ge(self.semobj[name], val)
        self.ninst += 1

    def _deps(self, e, reads, writes):
        for k in reads:
            self._wait(e, self.lastw.get(k))
        for k in writes:
            self._wait(e, self.lastw.get(k))
            for ev in self.readers.get(k, {}).values():
                self._wait(e, ev)

    def _record(self, ev, reads, writes):
        for k in reads:
            self.readers.setdefault(k, {})[ev[0]] = ev
        for k in writes:
            self.lastw[k] = ev
            self.readers[k] = {}

    def op(self, e, fn, r=(), w=(), inc=True):
        self.nops += 1
        if self.limit is not None and self.nops > self.limit:
            return None
        reads = _keys(r)
        writes = _keys(w)
        if e != 'pe':
            ps = [k for k in reads if k.startswith("ps")]
            if ps:
                reads = [k for k in reads if not k.startswith("ps")]
                writes = list(writes) + ps
        self._deps(e, reads, writes)
        inst = fn(self.eng[e])
        self.ninst += 1
        name = "sem_" + e
        if inc:
            self.cnt[e] += 1
            inst.then_inc(self.sem[e], 1)
            ev = (name, self.cnt[e])
        else:
            ev = (name, self.cnt[e] + 1)
        self._record(ev, reads, writes)
        return inst

    def dma(self, q, key, fn, r=(), w=()):
        self.nops += 1
        if self.limit is not None and self.nops > self.limit:
            return None
        reads = _keys(r)
        writes = _keys(w)
        self._deps(q, reads, writes)
        cls = 'sw' if q == 'pool' else 'hw'
        key = cls + ":" + key
        if key not in self.dsem:
            if self.dfree[cls]:
                d = self.dfree[cls].pop()
            else:
                nm = "dsem_%d" % len(self.dall)
                sm = self.es.enter_context(self.nc.semaphore(nm))
                self.semobj[nm] = sm
                d = [sm, 0, nm, cls]
                self.dall.append(d)
            self.dsem[key] = d
        d = self.dsem[key]
        inst = fn(self.eng[q])
        self.ninst += 1
        d[1] += 16
        inst.then_inc(d[0], 16)
        ev = (d[2], d[1])
        self._record(ev, reads, writes)
        return inst

    def _all_events(self):
        evs = [("sem_" + e, self.cnt[e]) for e in self.eng if self.cnt[e] > 0]
        evs += [(d[2], d[1]) for d in self.dall if d[1] > 0]
        return evs

    def barrier(self):
        evs = self._all_events()
        for e in self.eng:
            for ev in evs:
                self._wait(e, ev)
        self.lastw = {}
        self.readers = {}
        for d in self.dsem.values():
            self.dfree[d[3]].append(d)
        self.dsem = {}

    def finish(self):
        for ev in self._all_events():
            self._wait('sp', ev)


def mkap(ap, dims, off=0):
    return bass.AP(ap.tensor, ap.offset + off, [list(x) for x in dims])


def fdims(ap, dims, off=0):
    base = list(ap.ap)
    return bass.AP(ap.tensor, ap.offset + off, [list(base[0])] + [list(x) for x in dims])


def build_nc(dbg=False):
    nc = bass.Bass("TRN2", target_bir_lowering=False)

    def din(name, shape, dt=F32):
        return nc.dram_tensor(name, shape, dt, kind="ExternalInput").ap()

    def dout(name, shape, dt=F32):
        return nc.dram_tensor(name, shape, dt, kind="ExternalOutput").ap()

    xp = din("xp", [PB * PS, D])
    xs = din("xs", [SS, D])
    ck = din("ck", [NL, PAST, 128])
    cv = din("cv", [NL, PAST, 128])
    cckv = din("cckv", [NL, PAST, 128])
    ckr = din("ckr", [NL, PAST, 32])
    cT = din("cT", [128, 8, 2])
    w_mod = din("w_mod", [NL, D, 6 * D])
    b_modT = din("b_modT", [NL, 128, 48])
    b_mod = din("b_mod", [NL, 6 * D])
    w_in = din("w_in", [NL, D, IN_W])
    gqk = din("gqk", [NL, 640])
    mqn = din("mqn", [NL, 256])
    mkvn = din("mkvn", [NL, 128])
    w_uq = din("w_uq", [NL, 256, 384])
    w_ukv = din("w_ukv", [NL, 128, 512])
    wsT = din("wsT", [NL, 128, 4, 128])
    bsT = din("bsT", [NL, 128, 4])
    w_o = din("w_o", [NL, D, D])
    lnp = din("lnp", [NL, 4, D])
    peer_wq = din("peer_wq", [NL, D, 2048])
    k12T = din("k12T", [NL, 128, 2, 128])
    puTs = [din("puT%d" % l, [D, 16384]) for l in range(NL)]
    pv = [din("pv%d" % l, [16384, D]) for l in range(NL)]
    ropeC = din("ropeC", [SS, 640])
    ropeS = din("ropeS", [SS, 640])
    ropeMC = din("ropeMC", [SS, 160])
    ropeMS = din("ropeMS", [SS, 160])

    yp = dout("yp", [PB * PS, D])
    ys = dout("ys", [SS, D])
    nk = dout("nk", [PB, NL, PS, 128])
    nv = dout("nv", [PB, NL, PS, 128])
    nckv = dout("nckv", [PB, NL, PS, 128])
    nkr = dout("nkr", [PB, NL, PS, 32])
    if dbg:
        xsc = dout("xsc", [NL, 2, NTILES * 128, D])
    else:
        xsc = nc.dram_tensor("xsc", [NL, 2, NTILES * 128, D], F32, kind="Internal").ap()
    gsc = nc.dram_tensor("gsc", [NL, 2, 4, 128, D], F32, kind="Internal").ap()

    uid = [0]

    with contextlib.ExitStack() as es:
        fw = FW(nc, es)

        def sb(stack, name, shape, dt=F32):
            uid[0] += 1
            nm = "%s_%d" % (name, uid[0])
            return Tile(stack.enter_context(nc.sbuf_tensor(nm, shape, dt)), nm)

        def dve(fn, r=(), w=(), inc=True):
            return fw.op('dve', fn, r, w, inc)

        def act(fn, r=(), w=(), inc=True):
            return fw.op('act', fn, r, w, inc)

        def pe(fn, r=(), w=(), inc=True):
            return fw.op('pe', fn, r, w, inc)

        def pool(fn, r=(), w=(), inc=True):
            return fw.op('pool', fn, r, w, inc)

        def ld(key, out_ap, in_ap, r=(), w=(), q='sp'):
            return fw.dma(q, key, lambda e: e.dma_start(out=out_ap, in_=in_ap), r, w)

        psF = es.enter_context(nc.psum_tensor("psF", [128, 6, 512], F32))
        psB = es.enter_context(nc.psum_tensor("psB", [128, 2, 1024], BF16))
        PF = ["psF%d" % i for i in range(6)]
        PBK = ["psB0", "psB1"]

        ident = sb(es, "ident", [128, 128], BF16)
        pool(lambda p: p.memset(ident.t[:], 0.0), w=[ident])
        pool(lambda p: p.affine_select(out=ident.t[:], in_=ident.t[:], pattern=[[-1, 128]],
                                       compare_op=ALU.not_equal, fill=1.0, base=0, channel_multiplier=1),
             r=[ident], w=[ident])
        epsb = sb(es, "epsb", [128, 1])
        dve(lambda v: v.memset(epsb.t[:], EPS), w=[epsb])
        iota16 = sb(es, "iota16", [128, 16])
        iota16i = sb(es, "iota16i", [128, 16], I32)
        pool(lambda p: p.iota(iota16i.t[:], pattern=[[1, 16]], base=0, channel_multiplier=0), w=[iota16i])
        dve(lambda v: v.tensor_copy(out=iota16.t[:], in_=iota16i.t[:]), r=[iota16i], w=[iota16])
        identF = sb(es, "identF", [128, 128])
        pool(lambda p: p.memset(identF.t[:], 0.0), w=[identF])
        pool(lambda p: p.affine_select(out=identF.t[:], in_=identF.t[:], pattern=[[-1, 128]],
                                       compare_op=ALU.not_equal, fill=1.0, base=0, channel_multiplier=1),
             r=[identF], w=[identF])
        iota128 = sb(es, "iota128", [128, 128])
        iota128i = sb(es, "iota128i", [128, 128], I32)
        pool(lambda p: p.iota(iota128i.t[:], pattern=[[1, 128]], base=0, channel_multiplier=0), w=[iota128i])
        dve(lambda v: v.tensor_copy(out=iota128.t[:], in_=iota128i.t[:]), r=[iota128i], w=[iota128])
        modcol = sb(es, "modcol", [128, NL, 48, 2])

        with contextlib.ExitStack() as ps0:
            cTs = sb(ps0, "cTs", [128, 8, 2])
            scT = sb(ps0, "scT", [128, 8, 2], BF16)
            scR = sb(ps0, "scR", [128, 8, 2, 128], BF16)
            ld("cTs", cTs.t[:], cT, w=[cTs])
            act(lambda a: a.activation(out=scT.t[:], in_=cTs.t[:], func=AF.Silu), r=[cTs], w=[scT])
            dve(lambda v: v.tensor_copy(out=scR.t[:].rearrange("p a g m -> p (a g) m"),
                                        in_=fdims(scT.t[:], [[1, 16], [0, 128]])), r=[scT], w=[scR])
            bmT = sb(ps0, "bmT", [128, NL, 48])
            for l in range(NL):
                ld("bmT", bmT.t[:, l, :], b_modT[l], w=[bmT])
            wm = [sb(ps0, "wm%d" % i, [128, 8, 512], BF16) for i in range(2)]
            bmb = [sb(ps0, "bmb%d" % i, [128, 512]) for i in range(2)]
            gst = [sb(ps0, "gst%d" % i, [128, 512]) for i in range(2)]
            it = 0
            for l in range(NL):
                for ci in range(12):
                    role = ci // 2
                    W = wm[it % 2]
                    ld(W.k, W.t[:], w_mod[l, :, ci * 512:(ci + 1) * 512].rearrange("(c p) n -> p c n", p=128),
                       w=[W], q='pool')
                    if role in (0, 1, 3, 4):
                        for b4 in range(4):
                            blk = ci * 4 + b4
                            for kc in range(8):
                                pe(lambda t: t.matmul(psF[:, 4, b4 * 2:b4 * 2 + 2], lhsT=W.t[:, kc, b4 * 128:(b4 + 1) * 128],
                                                      rhs=scT.t[:, kc, :], start=(kc == 0), stop=(kc == 7)),
                                   r=[W, scT], w=[PF[4]], inc=(kc == 7))
                            addc = 1.0 if role in (1, 4) else 0.0
                            dve(lambda v: v.scalar_tensor_tensor(out=modcol.t[:, l, blk, :], in0=psF[:, 4, b4 * 2:b4 * 2 + 2],
                                                                 scalar=addc, in1=fdims(bmT.t[:, l, blk:blk + 1], [[0, 2]]),
                                                                 op0=ALU.add, op1=ALU.add),
                                r=[PF[4], bmT], w=[modcol])
                    if role in (2, 3, 4, 5):
                        slot = {2: 0, 4: 1, 3: 2, 5: 3}[role]
                        Bb = bmb[it % 2]
                        ld(Bb.k, Bb.t[:], mkap(b_mod[l, ci * 512:(ci + 1) * 512], [[0, 128], [1, 512]]), w=[Bb])
                        for g in range(2):
                            pb = PF[2 + g]
                            for kc in range(8):
                                pe(lambda t: t.matmul(psF[:, 2 + g, :], lhsT=scR.t[:, kc, g, :], rhs=W.t[:, kc, :],
                                                      start=(kc == 0), stop=(kc == 7)),
                                   r=[W, scR], w=[pb], inc=(kc == 7))
                            G = gst[g]
                            addc = 1.0 if role == 4 else 0.0
                            dve(lambda v: v.scalar_tensor_tensor(out=G.t[:], in0=psF[:, 2 + g, :], scalar=addc, in1=Bb.t[:],
                                                                 op0=ALU.add, op1=ALU.add),
                                r=[pb, Bb], w=[G])
                            half = ci % 2
                            ld("gst%d" % g, gsc[l, g, slot, :, half * 512:(half + 1) * 512], G.t[:], r=[G], w=["gsc"])
                    it += 1
        fw.barrier()

        def chk(tag):
            if dbg and os.environ.get("MK_STOP") == tag:
                raise _Stop()

        seqs = [(2 * b, 2, 0, False, b) for b in range(PB)] + [(8, 16, 1, True, -1)]
        if dbg and os.environ.get("MK_SEQS"):
            seqs = [seqs[int(i)] for i in os.environ["MK_SEQS"].split(",")]
        nlayers = int(os.environ.get("MK_LAYERS", NL)) if dbg else NL
        skip_peer = bool(dbg and os.environ.get("MK_SKIP_PEER"))
        stopA = bool(dbg and os.environ.get('MK_STOP') == 'A')
        stopB = bool(dbg and os.environ.get('MK_STOP') == 'B')

        def x_src(l, ph, gt):
            if l < 0:
                if gt < 8:
                    return xp[gt * 128:(gt + 1) * 128, :]
                return xs[(gt - 8) * 128:(gt - 7) * 128, :]
            return xsc[l, ph, gt * 128:(gt + 1) * 128, :]

        try:
            chk('pro')
            for (tile0, nt, grp, is_s, pbi) in seqs:
                nkc = nt + (2 if is_s else 0)
                koff = 2 if is_s else 0
                for l in range(nlayers):
                    with contextlib.ExitStack() as sa:
                        NTOK = nt * 128
                        NKEY = nkc * 128
                        QT = sb(sa, "QT", [128, 4, NTOK], BF16)
                        KT = sb(sa, "KT", [128, NKEY], BF16)
                        VA = sb(sa, "VA", [128, nkc, 2, 65], BF16)
                        QMT = sb(sa, "QMT", [128, 4, NTOK], BF16)
                        KMT = sb(sa, "KMT", [128, 4, NKEY], BF16)
                        VM = sb(sa, "VM", [128, nkc, 4, 65], BF16)
                        OC = sb(sa, "OC", [128, nt, 256], BF16)
                        pool(lambda p: p.memset(VA.t[:], 1.0), w=[VA])
                        pool(lambda p: p.memset(VM.t[:], 1.0), w=[VM])

                        sA = contextlib.ExitStack()
                        cur = [sA]
                        gq = sb(sA, "gq", [128, 640])
                        ld(gq.k, gq.t[:], mkap(gqk[l], [[0, 128], [1, 640]]), w=[gq])
                        gmq = sb(sA, "gmq", [128, 256])
                        ld(gmq.k, gmq.t[:], mkap(mqn[l], [[0, 128], [1, 256]]), w=[gmq])
                        gmk = sb(sA, "gmk", [128, 128])
                        ld(gmk.k, gmk.t[:], mkap(mkvn[l], [[0, 128], [1, 128]]), w=[gmk])
                        bst = sb(sA, "bst", [128, 4])
                        ld(bst.k, bst.t[:], bsT[l], w=[bst])
                        win = sb(sA, "win", [128, 8, IN_W], BF16)
                        for kc in range(8):
                            ld(win.k, win.t[:, kc, :], w_in[l, kc * 128:(kc + 1) * 128, :], w=[win], q='pool')
                        wuq = sb(sA, "wuq", [128, 2, 384], BF16)
                        ld(wuq.k, wuq.t[:], w_uq[l].rearrange("(c p) n -> p c n", p=128), w=[wuq], q='pool')
                        wukv = sb(sA, "wukv", [128, 512], BF16)
                        ld(wukv.k, wukv.t[:], w_ukv[l], w=[wukv], q='pool')
                        wst = sb(sA, "wst", [128, 4, 128], BF16)
                        ld(wst.k, wst.t[:], wsT[l], w=[wst], q='pool')

                        def rot(name, shape, dt=F32, n=2):
                            return [sb(cur[0], name + str(i), shape, dt) for i in range(n)]
                        xts = rot("xt", [128, D])
                        xnb = rot("xnb", [128, D], BF16, n=1)
                        hT = rot("hT", [128, 8, 128], BF16, n=1)
                        pj = rot("pj", [128, IN_W], n=1)
                        sq = rot("sq", [128, 1152], n=1)
                        small = rot("small", [128, 64], n=3)
                        qkn = rot("qkn", [128, 640], n=1)
                        rc_t = rot("ropeC", [128, 640], n=1)
                        rs_t = rot("ropeS", [128, 640], n=1)
                        rmc_t = rot("ropeMC", [128, 160], n=1)
                        rms_t = rot("ropeMS", [128, 160], n=1)
                        tmp640 = rot("tmp640", [128, 640], n=1)
                        tmp640b = rot("tmp640b", [128, 640], n=1)
                        qkb = rot("qkb", [128, 640], BF16)
                        cqb = rot("cqb", [128, 256], BF16)
                        cqT = rot("cqT", [128, 2, 128], BF16)
                        ckvn = rot("ckvn", [128, 128])
                        ckb = rot("ckb", [128, 128], BF16)
                        ckT = rot("ckT", [128, 128], BF16)
                        qm = rot("qm", [128, 4, 96])
                        mr = rot("mr", [128, 160], n=1)
                        mr3 = rot("mr3", [128, 160], n=1)
                        qmb = rot("qmb", [128, 4, 96], BF16)
                        kmb = rot("kmb", [128, 4, 96], BF16)
                        vg = rot("vg", [128, 256], BF16)
                        vtmp = rot("vtmp", [128, 256])
                        cst = rot("cst", [128, 128])
                        cstb = rot("cstb", [128, 128], BF16)
                        krs = rot("krs", [128, 32])
                        rr = {}

                        def nxt(lst):
                            i = rr.get(id(lst), 0)
                            rr[id(lst)] = i + 1
                            return lst[i % len(lst)]

                        def ln_stats(xin, xkeys):
                            S = nxt(small)
                            dve(lambda v: v.bn_stats(out=S.t[:, 8:14], in_=xin[:, 0:512]), r=xkeys, w=[S])
                            dve(lambda v: v.bn_stats(out=S.t[:, 14:20], in_=xin[:, 512:1024]), r=xkeys, w=[S])
                            dve(lambda v: v.bn_aggr(out=S.t[:, 0:2], in_=S.t[:, 8:20]), r=[S], w=[S])
                            act(lambda a: a.activation(out=S.t[:, 2:3], in_=S.t[:, 1:2], func=AF.Sqrt, bias=epsb.t[:], scale=1.0),
                                r=[S, epsb], w=[S])
                            dve(lambda v: v.reciprocal(out=S.t[:, 4:5], in_=S.t[:, 2:3]), r=[S], w=[S])
                            dve(lambda v: v.scalar_tensor_tensor(out=S.t[:, 5:6], in0=S.t[:, 0:1], scalar=-1.0, in1=S.t[:, 4:5],
                                                                 op0=ALU.mult, op1=ALU.mult), r=[S], w=[S])
                            return S

                        tcount = [0]

                        def transposes(srcs, dst_fn, rkeys, wkeys, evac='act'):
                            b = tcount[0] % 2
                            tcount[0] += 1
                            for i, (src, n) in enumerate(srcs):
                                pe(lambda t: t.transpose(out=psB[0:n, b, i * 128:(i + 1) * 128], in_=src, identity=ident.t[:]),
                                   r=list(rkeys) + [ident], w=[PBK[b]], inc=(i == len(srcs) - 1))
                            dst_fn(b)

                        def rope(src, nh, hd, ctab, stab, coff, dsts, rkeys, wkeys):
                            n = nh * hd
                            q4 = hd // 4
                            T1 = nxt(tmp640)
                            T2 = nxt(tmp640b)
                            dve(lambda v: v.tensor_tensor(out=T1.t[:, 0:n], in0=src, in1=ctab.t[:, coff:coff + n], op=ALU.mult),
                                r=list(rkeys) + [ctab], w=[T1])
                            nb = n // (2 * q4)
                            def hv(ap, off, half):
                                return fdims(ap, [[2 * q4, nb], [1, q4]], off + half * q4)
                            dve(lambda v: v.tensor_tensor(out=hv(T2.t[:, 0:n], 0, 0), in0=hv(src, 0, 1), in1=hv(stab.t[:, 0:n], coff, 0),
                                                          op=ALU.mult), r=list(rkeys) + [stab], w=[T2])
                            dve(lambda v: v.tensor_tensor(out=hv(T2.t[:, 0:n], 0, 1), in0=hv(src, 0, 0), in1=hv(stab.t[:, 0:n], coff, 1),
                                                          op=ALU.mult), r=list(rkeys) + [stab], w=[T2])
                            for (oap, i0, i1, j0, j1) in dsts:
                                dve(lambda v: v.tensor_tensor(out=oap, in0=i0(T1.t), in1=i0(T2.t), op=ALU.add), r=[T1, T2], w=wkeys)

                        def mla_keys(ckb_ap, ckb_keys, kr_ap, kr_keys, kc):
                            CT = nxt(ckT)
                            def ev(b):
                                act(lambda a: a.copy(out=CT.t[:], in_=psB[:, b, 0:128]), r=[PBK[b]], w=[CT])
                            transposes([(ckb_ap, 128)], ev, ckb_keys, None)
                            pe(lambda t: t.matmul(psF[:, 5, :], lhsT=CT.t[:], rhs=wukv.t[:], start=True, stop=True),
                               r=[CT, wukv], w=[PF[5]])
                            KB = nxt(kmb)
                            kvv = psF[:, 5, :].rearrange("p (h c) -> p h c", h=4)
                            act(lambda a: a.copy(out=KB.t[:, :, 0:64], in_=kvv[:, :, 0:64]), r=[PF[5]], w=[KB])
                            dve(lambda v: v.tensor_copy(out=VM.t[:, kc, :, 0:64], in_=kvv[:, :, 64:128]), r=[PF[5], VM], w=["VM.%d" % kc])
                            dve(lambda v: v.tensor_copy(out=KB.t[:, :, 64:96], in_=fdims(kr_ap, [[0, 4], [1, 32]])),
                                r=list(kr_keys), w=[KB])
                            def ev2(b):
                                act(lambda a: a.copy(out=KMT.t[0:96, :, kc * 128:(kc + 1) * 128],
                                                     in_=psB[0:96, b, 0:512].rearrange("p (h c) -> p h c", h=4)),
                                    r=[PBK[b]], w=["KMT.%d" % kc])
                            transposes([(KB.t[:, h, :], 96) for h in range(4)], ev2, [KB], None)

                        fw.mark('A-start')
                        if is_s:
                            for j in range(2):
                                C1 = nxt(cst)
                                ld(C1.k, C1.t[:], ck[l, j * 128:(j + 1) * 128, :], w=[C1])
                                CB = nxt(cstb)
                                dve(lambda v: v.tensor_copy(out=CB.t[:], in_=C1.t[:]), r=[C1], w=[CB])
                                def evk(b):
                                    act(lambda a: a.copy(out=KT.t[:, j * 128:(j + 1) * 128], in_=psB[:, b, 0:128]),
                                        r=[PBK[b]], w=["KT.%d" % j])
                                transposes([(CB.t[:], 128)], evk, [CB], None)
                                C2 = nxt(cst)
                                ld(C2.k, C2.t[:], cv[l, j * 128:(j + 1) * 128, :], w=[C2])
                                dve(lambda v: v.tensor_copy(out=VA.t[:, j, :, 0:64], in_=C2.t[:].rearrange("p (h c) -> p h c", h=2)),
                                    r=[C2, VA], w=["VA.%d" % j])
                                C3 = nxt(cst)
                                ld(C3.k, C3.t[:], cckv[l, j * 128:(j + 1) * 128, :], w=[C3])
                                CB3 = nxt(cstb)
                                dve(lambda v: v.tensor_copy(out=CB3.t[:], in_=C3.t[:]), r=[C3], w=[CB3])
                                K4 = nxt(krs)
                                ld(K4.k, K4.t[:], ckr[l, j * 128:(j + 1) * 128, :], w=[K4])
                                mla_keys(CB3.t[:], [CB3], K4.t[:], [K4], j)

                        for ti in range(nt):
                            gt = tile0 + ti
                            kc_new = koff + ti
                            X = nxt(xts)
                            ld(X.k, X.t[:], x_src(l - 1, 1, gt), w=[X])
                            fw.mark('tile%d-ln' % ti)
                            S = ln_stats(X.t[:], [X])
                            XN = nxt(xnb)
                            act(lambda a: a.activation(out=XN.t[:], in_=X.t[:], func=AF.Identity, scale=S.t[:, 4:5], bias=S.t[:, 5:6]),
                                r=[X, S], w=[XN])
                            H = nxt(hT)
                            def evh(b):
                                for c in range(8):
                                    act(lambda a: a.activation(out=H.t[:, c, :], in_=psB[:, b, c * 128:(c + 1) * 128], func=AF.Identity,
                                                               scale=modcol.t[:, l, 8 + c, grp:grp + 1], bias=modcol.t[:, l, c, grp:grp + 1]),
                                        r=[PBK[b], modcol], w=[H])
                            transposes([(XN.t[:, c * 128:(c + 1) * 128], 128) for c in range(8)], evh, [XN], None)
                            fw.mark('proj')
                            segs = [(0, 512), (512, 512), (1024, 512), (1536, 160)]
                            for bi, (c0, cn) in enumerate(segs):
                                for kc in range(8):
                                    pe(lambda t: t.matmul(psF[:, bi, 0:cn], lhsT=H.t[:, kc, :], rhs=win.t[:, kc, c0:c0 + cn],
                                                          start=(kc == 0), stop=(kc == 7)), r=[H, win], w=[PF[bi]], inc=(kc == 7))
                            P = nxt(pj)
                            for bi, (c0, cn) in enumerate(segs):
                                act(lambda a: a.copy(out=P.t[:, c0:c0 + cn], in_=psF[:, bi, 0:cn]), r=[PF[bi]], w=[P])
                            fw.mark('rms')
                            SQ = nxt(sq)
                            dve(lambda v: v.tensor_tensor(out=SQ.t[:], in0=P.t[:, 0:1152], in1=P.t[:, 0:1152], op=ALU.mult), r=[P], w=[SQ])
                            S2 = nxt(small)
                            dve(lambda v: v.tensor_reduce(out=S2.t[:, 0:10], in_=SQ.t[:, 0:640].rearrange("p (h c) -> p h c", c=64),
                                                          axis=AX.X, op=ALU.add), r=[SQ], w=[S2])
                            dve(lambda v: v.tensor_reduce(out=S2.t[:, 10:11], in_=SQ.t[:, 768:1024], axis=AX.X, op=ALU.add), r=[SQ], w=[S2])
                            dve(lambda v: v.tensor_reduce(out=S2.t[:, 11:12], in_=SQ.t[:, 1024:1152], axis=AX.X, op=ALU.add), r=[SQ], w=[S2])
                            dve(lambda v: v.tensor_scalar(out=S2.t[:, 16:26], in0=S2.t[:, 0:10], scalar1=1.0 / 64, scalar2=EPS,
                                                          op0=ALU.mult, op1=ALU.add), r=[S2], w=[S2])
                            dve(lambda v: v.tensor_scalar(out=S2.t[:, 26:27], in0=S2.t[:, 10:11], scalar1=1.0 / 256, scalar2=EPS,
                                                          op0=ALU.mult, op1=ALU.add), r=[S2], w=[S2])
                            dve(lambda v: v.tensor_scalar(out=S2.t[:, 27:28], in0=S2.t[:, 11:12], scalar1=1.0 / 128, scalar2=EPS,
                                                          op0=ALU.mult, op1=ALU.add), r=[S2], w=[S2])
                            act(lambda a: a.activation(out=S2.t[:, 32:44], in_=S2.t[:, 16:28], func=AF.Sqrt), r=[S2], w=[S2])
                            dve(lambda v: v.reciprocal(out=S2.t[:, 48:60], in_=S2.t[:, 32:44]), r=[S2], w=[S2])
                            fw.mark('qk')
                            QK = nxt(qkn)
                            dve(lambda v: v.tensor_tensor(out=QK.t[:].rearrange("p (h c) -> p h c", c=64),
                                                          in0=P.t[:, 0:640].rearrange("p (h c) -> p h c", c=64),
                                                          in1=fdims(S2.t[:, 48:58], [[1, 10], [0, 64]]), op=ALU.mult), r=[P, S2], w=[QK])
                            dve(lambda v: v.tensor_tensor(out=QK.t[:], in0=QK.t[:], in1=gq.t[:], op=ALU.mult), r=[QK, gq], w=[QK])
                            QB = nxt(qkb)
                            def qdst(T, j):
                                return T[:, j * 256:(j + 1) * 256].rearrange("p (s c) -> p s c", c=64)
                            def qout(j):
                                return fdims(QB.t[:], [[128, 4], [1, 64]], j * 64)
                            if is_s:
                                RC = rc_t[0]
                                RS = rs_t[0]
                                ld(RC.k, RC.t[:], ropeC[ti * 128:(ti + 1) * 128, :], w=[RC])
                                ld(RS.k, RS.t[:], ropeS[ti * 128:(ti + 1) * 128, :], w=[RS])
                                dsts = [(qout(0), lambda T: qdst(T, 0), None, 0, 0), (qout(1), lambda T: qdst(T, 1), None, 0, 0),
                                        (QB.t[:, 512:640], lambda T: T[:, 512:640], None, 0, 0)]
                                rope(QK.t[:], 10, 64, RC, RS, 0, dsts, [QK], [QB])
                            else:
                                for j in range(2):
                                    dve(lambda v: v.tensor_copy(out=qout(j), in_=qdst(QK.t, j)), r=[QK], w=[QB])
                                dve(lambda v: v.tensor_copy(out=QB.t[:, 512:640], in_=QK.t[:, 512:640]), r=[QK], w=[QB])
                            def evq(b):
                                act(lambda a: a.copy(out=QT.t[:, :, ti * 128:(ti + 1) * 128],
                                                     in_=psB[:, b, 0:512].rearrange("p (s c) -> p s c", s=4)),
                                    r=[PBK[b]], w=["QT.%d" % ti])
                                act(lambda a: a.copy(out=KT.t[:, kc_new * 128:(kc_new + 1) * 128], in_=psB[:, b, 512:640]),
                                    r=[PBK[b]], w=["KT.%d" % kc_new])
                            transposes([(QB.t[:, i * 128:(i + 1) * 128], 128) for i in range(5)], evq, [QB], None)
                            fw.mark('va')
                            dve(lambda v: v.tensor_copy(out=VA.t[:, kc_new, :, 0:64], in_=P.t[:, 640:768].rearrange("p (h c) -> p h c", h=2)),
                                r=[P, VA], w=["VA.%d" % kc_new])
                            fw.mark('mlaq')
                            CQ = nxt(cqb)
                            dve(lambda v: v.scalar_tensor_tensor(out=CQ.t[:], in0=P.t[:, 768:1024], scalar=S2.t[:, 58:59], in1=gmq.t[:],
                                                                 op0=ALU.mult, op1=ALU.mult), r=[P, S2, gmq], w=[CQ])
                            CQT = nxt(cqT)
                            def evc(b):
                                act(lambda a: a.copy(out=CQT.t[:], in_=psB[:, b, 0:256].rearrange("p (s c) -> p s c", s=2)),
                                    r=[PBK[b]], w=[CQT])
                            transposes([(CQ.t[:, i * 128:(i + 1) * 128], 128) for i in range(2)], evc, [CQ], None)
                            for kc in range(2):
                                pe(lambda t: t.matmul(psF[:, 4, 0:384], lhsT=CQT.t[:, kc, :], rhs=wuq.t[:, kc, :],
                                                      start=(kc == 0), stop=(kc == 1)), r=[CQT, wuq], w=[PF[4]], inc=(kc == 1))
                            QM = nxt(qm)
                            act(lambda a: a.copy(out=QM.t[:].rearrange("p h c -> p (h c)"), in_=psF[:, 4, 0:384]), r=[PF[4]], w=[QM])
                            QMB = nxt(qmb)
                            dve(lambda v: v.tensor_copy(out=QMB.t[:, :, 0:64], in_=QM.t[:, :, 0:64]), r=[QM], w=[QMB])
                            MR = nxt(mr)
                            dve(lambda v: v.tensor_copy(out=MR.t[:, 0:128].rearrange("p (h c) -> p h c", h=4), in_=QM.t[:, :, 64:96]),
                                r=[QM], w=[MR])
                            dve(lambda v: v.tensor_copy(out=MR.t[:, 128:160], in_=P.t[:, 1152:1184]), r=[P], w=[MR])
                            MR3 = nxt(mr3)
                            if is_s:
                                RMC = rmc_t[0]
                                RMS = rms_t[0]
                                ld(RMC.k, RMC.t[:], ropeMC[ti * 128:(ti + 1) * 128, :], w=[RMC])
                                ld(RMS.k, RMS.t[:], ropeMS[ti * 128:(ti + 1) * 128, :], w=[RMS])
                                rope(MR.t[:], 5, 32, RMC, RMS, 0, [(MR3.t[:], lambda T: T[:, 0:160], None, 0, 0)], [MR], [MR3])
                            else:
                                dve(lambda v: v.tensor_copy(out=MR3.t[:], in_=MR.t[:]), r=[MR], w=[MR3])
                            dve(lambda v: v.tensor_copy(out=QMB.t[:, :, 64:96], in_=MR3.t[:, 0:128].rearrange("p (h c) -> p h c", h=4)),
                                r=[MR3], w=[QMB])
                            def evqm(b):
                                act(lambda a: a.copy(out=QMT.t[0:96, :, ti * 128:(ti + 1) * 128],
                                                     in_=psB[0:96, b, 0:512].rearrange("p (h c) -> p h c", h=4)),
                                    r=[PBK[b]], w=["QMT.%d" % ti])
                            transposes([(QMB.t[:, h, :], 96) for h in range(4)], evqm, [QMB], None)
                            fw.mark('mlakv')
                            CK = nxt(ckvn)
                            dve(lambda v: v.scalar_tensor_tensor(out=CK.t[:], in0=P.t[:, 1024:1152], scalar=S2.t[:, 59:60], in1=gmk.t[:],
                                                                 op0=ALU.mult, op1=ALU.mult), r=[P, S2, gmk], w=[CK])
                            CKB = nxt(ckb)
                            dve(lambda v: v.tensor_copy(out=CKB.t[:], in_=CK.t[:]), r=[CK], w=[CKB])
                            mla_keys(CKB.t[:], [CKB], MR3.t[:, 128:160], [MR3], kc_new)
                            fw.mark('gmlp')
                            VT = nxt(vtmp)
                            S3 = nxt(small)
                            vcv = P.t[:, 1440:1696].rearrange("p (g c) -> p g c", g=4)
                            dve(lambda v: v.tensor_reduce(out=S3.t[:, 0:4], in_=vcv, axis=AX.X, op=ALU.add), r=[P], w=[S3])
                            dve(lambda v: v.tensor_tensor(out=VT.t[:], in0=P.t[:, 1440:1696], in1=P.t[:, 1440:1696], op=ALU.mult), r=[P], w=[VT])
                            dve(lambda v: v.tensor_reduce(out=S3.t[:, 4:8], in_=VT.t[:].rearrange("p (g c) -> p g c", g=4), axis=AX.X, op=ALU.add),
                                r=[VT], w=[S3])
                            dve(lambda v: v.tensor_scalar(out=S3.t[:, 8:12], in0=S3.t[:, 0:4], scalar1=1.0 / 64, scalar2=None, op0=ALU.mult),
                                r=[S3], w=[S3])
                            dve(lambda v: v.tensor_tensor(out=S3.t[:, 12:16], in0=S3.t[:, 8:12], in1=S3.t[:, 8:12], op=ALU.mult), r=[S3], w=[S3])
                            dve(lambda v: v.scalar_tensor_tensor(out=S3.t[:, 16:20], in0=S3.t[:, 4:8], scalar=1.0 / 64, in1=S3.t[:, 12:16],
                                                                 op0=ALU.mult, op1=ALU.subtract), r=[S3], w=[S3])
                            dve(lambda v: v.tensor_scalar(out=S3.t[:, 20:24], in0=S3.t[:, 16:20], scalar1=EPS, scalar2=None, op0=ALU.add),
                                r=[S3], w=[S3])
                            act(lambda a: a.activation(out=S3.t[:, 24:28], in_=S3.t[:, 20:24], func=AF.Sqrt), r=[S3], w=[S3])
                            dve(lambda v: v.reciprocal(out=S3.t[:, 28:32], in_=S3.t[:, 24:28]), r=[S3], w=[S3])
                            dve(lambda v: v.tensor_tensor(out=VT.t[:].rearrange("p (g c) -> p g c", g=4), in0=vcv,
                                                          in1=fdims(S3.t[:, 8:12], [[1, 4], [0, 64]]), op=ALU.subtract), r=[P, S3], w=[VT])
                            VG = nxt(vg)
                            dve(lambda v: v.tensor_tensor(out=VG.t[:].rearrange("p (g c) -> p g c", g=4),
                                                          in0=VT.t[:].rearrange("p (g c) -> p g c", g=4),
                                                          in1=fdims(S3.t[:, 28:32], [[1, 4], [0, 64]]), op=ALU.mult), r=[VT, S3], w=[VG])
                            for g in range(4):
                                pe(lambda t: t.matmul(psF[:, 4, g * 64:(g + 1) * 64], lhsT=wst.t[:, g, :], rhs=VG.t[:, g * 64:(g + 1) * 64],
                                                      start=True, stop=True), r=[VG, wst], w=[PF[4]], inc=(g == 3))
                            dve(lambda v: v.tensor_tensor(out=VT.t[:].rearrange("p (g c) -> p g c", g=4),
                                                          in0=psF[:, 4, 0:256].rearrange("p (g c) -> p g c", g=4),
                                                          in1=fdims(bst.t[:], [[1, 4], [0, 64]]), op=ALU.add), r=[PF[4], bst], w=[VT])
                            dve(lambda v: v.tensor_tensor(out=OC.t[:, ti, :], in0=VT.t[:], in1=P.t[:, 1184:1440], op=ALU.mult),
                                r=[VT, P], w=["OC.%d" % ti])
                            fw.mark('outs')
                            if not is_s:
                                r0 = ti * 128
                                ld("o_nk%d" % (ti % 2), nk[pbi, l, r0:r0 + 128, :], QK.t[:, 512:640], r=[QK])
                                ld("o_nv%d" % (ti % 2), nv[pbi, l, r0:r0 + 128, :], P.t[:, 640:768], r=[P])
                                ld("o_nc%d" % (ti % 2), nckv[pbi, l, r0:r0 + 128, :], CK.t[:], r=[CK])
                                ld("o_nr%d" % (ti % 2), nkr[pbi, l, r0:r0 + 128, :], P.t[:, 1152:1184], r=[P])

                        sA.close()
                        fw.barrier()
                        sB = contextlib.ExitStack()
                        cur[0] = sB
                        G1 = sb(sB, "G1", [128, D])
                        ld(G1.k, G1.t[:], gsc[l, grp, 0], w=[G1])
                        lnb = sb(sB, "lnb", [128, 2, D])
                        for j in range(2):
                            ld(lnb.k, lnb.t[:, j, :], mkap(lnp[l, j], [[0, 128], [1, D]]), w=[lnb])
                        wo = sb(sB, "wo", [128, 8, D], BF16)
                        for kc in range(8):
                            ld(wo.k, wo.t[:, kc, :], w_o[l, kc * 128:(kc + 1) * 128, :], w=[wo], q='pool')
                        xts = rot("xtB", [128, D])
                        small = rot("smallB", [128, 64], n=3)
                        ntq = 4 if is_s else 2
                        NQ = ntq * 128
                        PT = rot("PT", [128, NQ], BF16, n=3)
                        mix = sb(sB, "mix", [128, ntq, D], BF16)
                        mixT = rot("mixT", [128, 8, 128], BF16)
                        ybuf = rot("ybuf", [128, D], n=1)
                        x1b = rot("x1b", [128, D])
                        rec = rot("rec", [128, 8])
                        sc_ = [0]
                        oc_ = [0]
                        for qg in range(0 if stopA else nt // ntq):
                            q0 = qg * NQ
                            allkeys_q = ["QT.%d" % (qg * ntq + i) for i in range(ntq)]
                            allkeys_qm = ["QMT.%d" % (qg * ntq + i) for i in range(ntq)]
                            for hh in range(12):
                                isA = hh < 8
                                h = hh if isA else hh - 8
                                ob = 3 + (oc_[0] % 2)
                                oc_[0] += 1
                                oview = psF[:, ob, 0:ntq * 65].rearrange("p (q c) -> p q c", c=65)
                                for kc in range(nkc):
                                    sbk = sc_[0] % 3
                                    sc_[0] += 1
                                    if isA:
                                        base = (h // 4) * 64
                                        pe(lambda t: t.matmul(psF[:, sbk, 0:NQ], lhsT=KT.t[base:base + 64, kc * 128:(kc + 1) * 128],
                                                              rhs=QT.t[base:base + 64, h % 4, q0:q0 + NQ], start=True, stop=True),
                                           r=["KT.%d" % kc] + allkeys_q, w=[PF[sbk]])
                                        scl = 0.125
                                    else:
                                        pe(lambda t: t.matmul(psF[:, sbk, 0:NQ], lhsT=KMT.t[0:96, h, kc * 128:(kc + 1) * 128],
                                                              rhs=QMT.t[0:96, h, q0:q0 + NQ], start=True, stop=True),
                                           r=["KMT.%d" % kc] + allkeys_qm, w=[PF[sbk]])
                                        scl = 1.0 / math.sqrt(96.0)
                                    Pt = nxt(PT)
                                    act(lambda a: a.activation(out=Pt.t[:], in_=psF[:, sbk, 0:NQ], func=AF.Exp, scale=scl), r=[PF[sbk]], w=[Pt])
                                    for qt in range(ntq):
                                        if isA:
                                            rhs = VA.t[:, kc, h // 4, :]
                                            vk = "VA.%d" % kc
                                        else:
                                            rhs = VM.t[:, kc, h, :]
                                            vk = "VM.%d" % kc
                                        pe(lambda t: t.matmul(oview[:, qt, :], lhsT=Pt.t[:, qt * 128:(qt + 1) * 128], rhs=rhs,
                                                              start=(kc == 0 and qt == 0), stop=(kc == nkc - 1 and qt == ntq - 1)),
                                           r=[Pt, vk], w=[PF[ob]], inc=(qt == ntq - 1))
                                R = nxt(rec)
                                dve(lambda v: v.reciprocal(out=R.t[:, 0:ntq], in_=oview[:, :, 64]), r=[PF[ob]], w=[R])
                                col = h * 64 if isA else 512 + h * 64
                                dve(lambda v: v.tensor_tensor(out=mix.t[:, :, col:col + 64], in0=oview[:, :, 0:64],
                                                              in1=fdims(R.t[:, 0:ntq], [[1, ntq], [0, 64]]), op=ALU.mult),
                                    r=[PF[ob], R], w=[mix])
                            dve(lambda v: v.tensor_copy(out=mix.t[:, :, 768:1024], in_=OC.t[:, qg * ntq:(qg + 1) * ntq, :]),
                                r=["OC.%d" % (qg * ntq + i) for i in range(ntq)], w=[mix])
                            for qt in range(ntq):
                                ti = qg * ntq + qt
                                gt = tile0 + ti
                                MT = nxt(mixT)
                                def evm(b):
                                    act(lambda a: a.copy(out=MT.t[:].rearrange("p a c -> p (a c)"), in_=psB[:, b, :]), r=[PBK[b]], w=[MT])
                                transposes([(mix.t[:, qt, c * 128:(c + 1) * 128], 128) for c in range(8)], evm, [mix], None)
                                for hf in range(2):
                                    for kc in range(8):
                                        pe(lambda t: t.matmul(psF[:, hf, :], lhsT=MT.t[:, kc, :], rhs=wo.t[:, kc, hf * 512:(hf + 1) * 512],
                                                              start=(kc == 0), stop=(kc == 7)), r=[MT, wo], w=[PF[hf]], inc=(kc == 7))
                                X = nxt(xts)
                                ld(X.k, X.t[:], x_src(l - 1, 1, gt), w=[X])
                                Y = nxt(ybuf)
                                dve(lambda v: v.tensor_tensor(out=Y.t[:].rearrange("p (a c) -> p a c", a=2), in0=psF[:, 0:2, :],
                                                              in1=G1.t[:].rearrange("p (a c) -> p a c", a=2), op=ALU.mult),
                                    r=[PF[0], PF[1], G1], w=[Y])
                                dve(lambda v: v.scalar_tensor_tensor(out=Y.t[:], in0=X.t[:], scalar=ALPHA, in1=Y.t[:],
                                                                     op0=ALU.mult, op1=ALU.add), r=[X, Y], w=[Y])
                                S = ln_stats(Y.t[:], [Y])
                                X1 = nxt(x1b)
                                act(lambda a: a.activation(out=X1.t[:], in_=Y.t[:], func=AF.Identity, scale=S.t[:, 4:5], bias=S.t[:, 5:6]),
                                    r=[Y, S], w=[X1])
                                dve(lambda v: v.tensor_tensor(out=X1.t[:], in0=X1.t[:], in1=lnb.t[:, 0, :], op=ALU.mult), r=[X1, lnb], w=[X1])
                                dve(lambda v: v.tensor_tensor(out=X1.t[:], in0=X1.t[:], in1=lnb.t[:, 1, :], op=ALU.add), r=[X1, lnb], w=[X1])
                                ld(X1.k + "st", xsc[l, 0, gt * 128:(gt + 1) * 128, :], X1.t[:], r=[X1], w=["xsc.%d" % gt])
                        sB.close()
                    fw.barrier()

                    with contextlib.ExitStack() as sc:
                        cur[0] = sc
                        G2 = sb(sc, "G2", [128, D])
                        ld(G2.k, G2.t[:], gsc[l, grp, 3], w=[G2])
                        A2 = sb(sc, "A2", [128, D])
                        ld(A2.k, A2.t[:], gsc[l, grp, 1], w=[A2])
                        B2 = sb(sc, "B2", [128, D])
                        ld(B2.k, B2.t[:], gsc[l, grp, 2], w=[B2])
                        lnb = sb(sc, "lnb2", [128, 2, D])
                        for j in range(2):
                            ld(lnb.k, lnb.t[:, j, :], mkap(lnp[l, 2 + j], [[0, 128], [1, D]]), w=[lnb])
                        wq = sb(sc, "wq", [128, 8, 2048], BF16)
                        for kc in range(8):
                            ld(wq.k, wq.t[:, kc, :], peer_wq[l, kc * 128:(kc + 1) * 128, :], w=[wq], q='pool')
                        kT = sb(sc, "kT", [128, 2, 128], BF16)
                        ld(kT.k, kT.t[:], k12T[l], w=[kT], q='pool')
                        TGT = 2
                        TG = TGT * 128
                        IJG = sb(sc, "IJG", [128, 3, TG])
                        h2Tg = sb(sc, "h2Tg", [128, 8, TG], BF16)
                        rr = {}

                        def nxt(lst):
                            i = rr.get(id(lst), 0)
                            rr[id(lst)] = i + 1
                            return lst[i % len(lst)]
                        xts = rot("cxt", [128, D])
                        small = rot("csmall", [128, 64], n=3)
                        ybuf = rot("cy", [128, D], n=1)
                        x2b = rot("x2b", [128, D])
                        tcount = [0]
                        puT = puTs[l]
                        pvv = pv[l]
                        for g0 in range(0, 0 if (stopA or stopB) else nt, TGT):
                            s1 = contextlib.ExitStack()
                            cur[0] = s1
                            h2 = rot("h2", [128, D], n=1)
                            xnf = rot("xnf", [128, D], n=1)
                            h2b = rot("h2b", [128, D], BF16, n=1)
                            qTs = rot("qTs", [128, 16, 128], BF16, n=1)
                            sc_s = rot("scs", [128, 16, 128], n=1)
                            sc_r = rot("scr", [128, 16, 128], n=1)
                            v12 = rot("v12", [128, 16, 16], n=1)
                            i12 = rot("i12", [128, 16, 16], U32, n=1)
                            i12f = rot("i12f", [128, 16, 16], n=1)
                            cand = rot("cand", [128, 8, 256], n=1)
                            top = rot("top", [128, 8, 16], n=1)
                            pos = rot("pos", [128, 8, 16], U32, n=1)
                            posr = rot("posr", [128, 128], U32, n=1)
                            posc = rot("posc", [128, 128], U32, n=1)
                            posrf = rot("posrf", [128, 128], n=1)
                            poscf = rot("poscf", [128, 128], n=1)
                            ijg = rot("ijg", [128, 3, 128], n=1)
                            gsm = rot("gsm", [128, 16], n=1)
                            for tl in range(TGT):
                                ti = g0 + tl
                                gt = tile0 + ti
                                X = nxt(xts)
                                ld(X.k, X.t[:], xsc[l, 0, gt * 128:(gt + 1) * 128, :], w=[X])
                                S = ln_stats(X.t[:], [X])
                                XN = xnf[0]
                                act(lambda a: a.activation(out=XN.t[:], in_=X.t[:], func=AF.Identity, scale=S.t[:, 4:5], bias=S.t[:, 5:6]),
                                    r=[X, S], w=[XN])
                                H2 = h2[0]
                                dve(lambda v: v.tensor_tensor(out=H2.t[:], in0=XN.t[:], in1=A2.t[:], op=ALU.mult), r=[XN, A2], w=[H2])
                                H2B = h2b[0]
                                dve(lambda v: v.tensor_tensor(out=H2B.t[:], in0=H2.t[:], in1=B2.t[:], op=ALU.add), r=[H2, B2], w=[H2B])
                                b = tcount[0] % 2
                                tcount[0] += 1
                                for c in range(8):
                                    pe(lambda t: t.transpose(out=psB[:, b, c * 128:(c + 1) * 128], in_=H2B.t[:, c * 128:(c + 1) * 128],
                                                             identity=ident.t[:]), r=[H2B, ident], w=[PBK[b]], inc=(c == 7))
                                act(lambda a: a.copy(out=h2Tg.t[:, :, tl * 128:(tl + 1) * 128], in_=psB[:, b, :].rearrange("p (a c) -> p a c", a=8)),
                                    r=[PBK[b]], w=["h2T.%d" % tl])
                                QS = qTs[0]
                                for c in range(16):
                                    bk = c // 4
                                    for kc in range(8):
                                        pe(lambda t: t.matmul(psF[:, bk, (c % 4) * 128:(c % 4 + 1) * 128], lhsT=wq.t[:, kc, c * 128:(c + 1) * 128],
                                                              rhs=h2Tg.t[:, kc, tl * 128:(tl + 1) * 128], start=(kc == 0), stop=(kc == 7)),
                                           r=["h2T.%d" % tl, wq], w=[PF[bk]], inc=(kc == 7))
                                for bk in range(4):
                                    act(lambda a: a.copy(out=QS.t[:, bk * 4:(bk + 1) * 4, :].rearrange("p a c -> p (a c)"), in_=psF[:, bk, :]),
                                        r=[PF[bk]], w=[QS])
                                SS_ = sc_s[0]
                                for c in range(16):
                                    bk = c // 4
                                    pe(lambda t: t.matmul(psF[:, bk, (c % 4) * 128:(c % 4 + 1) * 128], lhsT=QS.t[:, c, :], rhs=kT.t[:, c // 8, :],
                                                          start=True, stop=True), r=[QS, kT], w=[PF[bk]], inc=(c % 4 == 3))
                                for bk in range(4):
                                    act(lambda a: a.copy(out=SS_.t[:, bk * 4:(bk + 1) * 4, :].rearrange("p a c -> p (a c)"), in_=psF[:, bk, :]),
                                        r=[PF[bk]], w=[SS_])
                                SR = sc_r[0]
                                V12 = v12[0]
                                I12 = i12[0]
                                for c in range(16):
                                    dve(lambda v: v.max(out=V12.t[:, c, 0:8], in_=SS_.t[:, c, :]), r=[SS_], w=[V12])
                                    dve(lambda v: v.max_index(out=I12.t[:, c, 0:8], in_max=V12.t[:, c, 0:8], in_values=SS_.t[:, c, :]),
                                        r=[SS_, V12], w=[I12])
                                    dve(lambda v: v.match_replace(out=SR.t[:, c, :], in_to_replace=V12.t[:, c, 0:8], in_values=SS_.t[:, c, :],
                                                                  imm_value=-1e30), r=[SS_, V12], w=[SR])
                                    dve(lambda v: v.max(out=V12.t[:, c, 8:16], in_=SR.t[:, c, :]), r=[SR], w=[V12])
                                    dve(lambda v: v.max_index(out=I12.t[:, c, 8:16], in_max=V12.t[:, c, 8:16], in_values=SR.t[:, c, :]),
                                        r=[SR, V12], w=[I12])
                                I12F = i12f[0]
                                dve(lambda v: v.tensor_copy(out=I12F.t[:], in_=I12.t[:]), r=[I12], w=[I12F])
                                CD = cand[0]
                                for hd in range(8):
                                    dve(lambda v: v.tensor_tensor(out=CD.t[:, hd, :].rearrange("p (r c) -> p r c", c=16),
                                                                  in0=fdims(V12.t[:, hd, :], [[1, 16], [0, 16]]),
                                                                  in1=fdims(V12.t[:, 8 + hd, :], [[0, 16], [1, 16]]), op=ALU.add),
                                        r=[V12], w=[CD])
                                CD2 = Tile(SR.t[:].rearrange("p a c -> p (a c)").rearrange("p (h c) -> p h c", h=8), SR.k)
                                TP = top[0]
                                PS_ = pos[0]
                                for hd in range(8):
                                    dve(lambda v: v.max(out=TP.t[:, hd, 0:8], in_=CD.t[:, hd, :]), r=[CD], w=[TP])
                                    dve(lambda v: v.max_index(out=PS_.t[:, hd, 0:8], in_max=TP.t[:, hd, 0:8], in_values=CD.t[:, hd, :]),
                                        r=[CD, TP], w=[PS_])
                                    dve(lambda v: v.match_replace(out=CD2.t[:, hd, :], in_to_replace=TP.t[:, hd, 0:8], in_values=CD.t[:, hd, :],
                                                                  imm_value=-1e30), r=[CD, TP], w=[CD2])
                                    dve(lambda v: v.max(out=TP.t[:, hd, 8:16], in_=CD2.t[:, hd, :]), r=[CD2], w=[TP])
                                    dve(lambda v: v.max_index(out=PS_.t[:, hd, 8:16], in_max=TP.t[:, hd, 8:16], in_values=CD2.t[:, hd, :]),
                                        r=[CD2, TP], w=[PS_])
                                PR = posr[0]
                                PC = posc[0]
                                pflat = PS_.t[:].rearrange("p h k -> p (h k)")
                                dve(lambda v: v.tensor_single_scalar(out=PR.t[:], in_=pflat, scalar=4, op=ALU.logical_shift_right), r=[PS_], w=[PR])
                                dve(lambda v: v.tensor_single_scalar(out=PC.t[:], in_=pflat, scalar=15, op=ALU.bitwise_and), r=[PS_], w=[PC])
                                PRF = posrf[0]
                                PCF = poscf[0]
                                dve(lambda v: v.tensor_copy(out=PRF.t[:], in_=PR.t[:]), r=[PR], w=[PRF])
                                dve(lambda v: v.tensor_copy(out=PCF.t[:], in_=PC.t[:]), r=[PC], w=[PCF])
                                OH = Tile(SS_.t[:].rearrange("p a c -> p (a c)").rearrange("p (k r) -> p k r", r=16), SS_.k)
                                IJ = ijg[0]
                                for (PF_, slot, half) in ((PRF, 0, 0), (PCF, 1, 1)):
                                    dve(lambda v: v.tensor_tensor(out=OH.t[:], in0=fdims(PF_.t[:], [[1, 128], [0, 16]]),
                                                                  in1=fdims(iota16.t[:], [[0, 128], [1, 16]]), op=ALU.is_equal),
                                        r=[PF_, iota16], w=[OH])
                                    for hd in range(8):
                                        dve(lambda v: v.tensor_tensor(out=OH.t[:, hd * 16:(hd + 1) * 16, :], in0=OH.t[:, hd * 16:(hd + 1) * 16, :],
                                                                      in1=fdims(I12F.t[:, half * 8 + hd, :], [[0, 16], [1, 16]]), op=ALU.mult),
                                            r=[OH, I12F], w=[OH])
                                    dve(lambda v: v.tensor_reduce(out=IJ.t[:, slot, :], in_=OH.t[:], axis=AX.X, op=ALU.add), r=[OH], w=[IJ])
                                GS = gsm[0]
                                gwv = IJ.t[:, 2, :].rearrange("p (h k) -> p h k", k=16)
                                dve(lambda v: v.tensor_tensor(out=gwv, in0=TP.t[:], in1=fdims(TP.t[:, :, 0], [[16, 8], [0, 16]]), op=ALU.subtract),
                                    r=[TP], w=[IJ])
                                act(lambda a: a.activation(out=IJ.t[:, 2, :], in_=IJ.t[:, 2, :], func=AF.Exp), r=[IJ], w=[IJ])
                                dve(lambda v: v.tensor_reduce(out=GS.t[:, 0:8], in_=gwv, axis=AX.X, op=ALU.add), r=[IJ], w=[GS])
                                dve(lambda v: v.reciprocal(out=GS.t[:, 8:16], in_=GS.t[:, 0:8]), r=[GS], w=[GS])
                                dve(lambda v: v.tensor_tensor(out=gwv, in0=gwv, in1=fdims(GS.t[:, 8:16], [[1, 8], [0, 16]]), op=ALU.mult),
                                    r=[IJ, GS], w=[IJ])
                                for c in range(3):
                                    pe(lambda t: t.transpose(out=psF[:, 4, c * 128:(c + 1) * 128], in_=IJ.t[:, c, :], identity=identF.t[:]),
                                       r=[IJ, identF], w=[PF[4]], inc=(c == 2))
                                act(lambda a: a.copy(out=IJG.t[:, :, tl * 128:(tl + 1) * 128], in_=psF[:, 4, 0:384].rearrange("p (a c) -> p a c", a=3)),
                                    r=[PF[4]], w=["IJG.%d" % tl])
                            s1.close()
                            fw.barrier()
                            s2 = contextlib.ExitStack()
                            cur[0] = s2
                            GT = sb(s2, "GT", [128, 128, TG], BF16)
                            NI = 2
                            ub = rot("ub", [128, 8, NI * 128], BF16, n=2)
                            vb = rot("vb", [128, NI, D], BF16, n=2)
                            NS = 16
                            r1s = rot("r1s", [128, NS, 128], BF16, n=2)
                            r2s = rot("r2s", [128, NS, 128], BF16, n=2)
                            atb = rot("atb", [128, TG], BF16, n=2)

                            def load_uv(ig):
                                U = nxt(ub)
                                V = nxt(vb)
                                ld(U.k, U.t[:], puT[:, ig * NI * 128:(ig + 1) * NI * 128].rearrange("(c p) e -> p c e", p=128), w=[U], q='pool')
                                ld(V.k, V.t[:], pvv[ig * NI * 128:(ig + 1) * NI * 128, :].rearrange("(a p) d -> p a d", p=128), w=[V], q='pool')
                                return U, V
                            pre = [load_uv(0), load_uv(1)]
                            gb = 0
                            for sbk in range(TG // NS):
                                t0 = sbk * NS
                                R1 = nxt(r1s)
                                R2 = nxt(r2s)
                                dve(lambda v: v.tensor_tensor(out=R2.t[:], in0=fdims(iota128.t[:], [[0, NS], [1, 128]]),
                                                              in1=fdims(IJG.t[:, 1, t0:t0 + NS], [[1, NS], [0, 128]]), op=ALU.is_equal),
                                    r=[iota128, "IJG.%d" % (t0 // 128)], w=[R2])
                                dve(lambda v: v.tensor_tensor(out=R1.t[:], in0=fdims(iota128.t[:], [[0, NS], [1, 128]]),
                                                              in1=fdims(IJG.t[:, 0, t0:t0 + NS], [[1, NS], [0, 128]]), op=ALU.is_equal),
                                    r=[iota128, "IJG.%d" % (t0 // 128)], w=[R1])
                                pool(lambda p: p.tensor_tensor(out=R1.t[:], in0=R1.t[:], in1=fdims(IJG.t[:, 2, t0:t0 + NS], [[1, NS], [0, 128]]),
                                                               op=ALU.mult), r=[R1, "IJG.%d" % (t0 // 128)], w=[R1])
                                for q4 in range(NS // 4):
                                    bk = gb % 4
                                    gb += 1
                                    for tt in range(4):
                                        t = q4 * 4 + tt
                                        pe(lambda te: te.matmul(psF[:, bk, tt * 128:(tt + 1) * 128], lhsT=R2.t[:, t, :], rhs=R1.t[:, t, :],
                                                                start=True, stop=True), r=[R1, R2], w=[PF[bk]], inc=(tt == 3))
                                    tb = t0 + q4 * 4
                                    act(lambda a: a.copy(out=fdims(GT.t[:], [[1, 4], [TG, 128]], tb),
                                                         in_=psF[:, bk, :].rearrange("p (t i) -> p t i", t=4)), r=[PF[bk]], w=[GT])
                            def vside(i, V, ii):
                                for tt in range(TGT):
                                    for hf in range(2):
                                        pe(lambda te: te.matmul(psF[:, tt * 2 + hf, :], lhsT=GT.t[:, i, tt * 128:(tt + 1) * 128],
                                                                rhs=V.t[:, ii, hf * 512:(hf + 1) * 512], start=(i == 0), stop=(i == 127)),
                                           r=["GT.%d" % i, V], w=[PF[tt * 2 + hf]], inc=(i == 127 or (tt == TGT - 1 and hf == 1)))
                            prev = None
                            for ig in range(128 // NI):
                                U, V = pre[ig] if ig < 2 else load_uv(ig)
                                for ii in range(NI):
                                    i = ig * NI + ii
                                    ab = 4 + (i % 2)
                                    for kc in range(8):
                                        pe(lambda te: te.matmul(psF[:, ab, 0:TG], lhsT=U.t[:, kc, ii * 128:(ii + 1) * 128], rhs=h2Tg.t[:, kc, :],
                                                                start=(kc == 0), stop=(kc == 7)),
                                           r=[U, "h2T.0", "h2T.1"], w=[PF[ab]], inc=(kc == 7))
                                    AT = nxt(atb)
                                    act(lambda a: a.activation(out=AT.t[:], in_=psF[:, ab, 0:TG], func=AF.Gelu_apprx_tanh), r=[PF[ab]], w=[AT])
                                    dve(lambda v: v.tensor_tensor(out=GT.t[:, i, :], in0=GT.t[:, i, :], in1=AT.t[:], op=ALU.mult),
                                        r=[AT, GT], w=["GT.%d" % i])
                                    if prev is not None:
                                        vside(*prev)
                                    prev = (i, V, ii)
                            vside(*prev)
                            for tl in range(TGT):
                                ti = g0 + tl
                                gt = tile0 + ti
                                X = nxt(xts)
                                ld(X.k, X.t[:], xsc[l, 0, gt * 128:(gt + 1) * 128, :], w=[X])
                                Y = nxt(ybuf)
                                dve(lambda v: v.tensor_tensor(out=Y.t[:].rearrange("p (a c) -> p a c", a=2), in0=psF[:, tl * 2:tl * 2 + 2, :],
                                                              in1=G2.t[:].rearrange("p (a c) -> p a c", a=2), op=ALU.mult),
                                    r=[PF[tl * 2], PF[tl * 2 + 1], G2], w=[Y])
                                dve(lambda v: v.scalar_tensor_tensor(out=Y.t[:], in0=X.t[:], scalar=ALPHA, in1=Y.t[:], op0=ALU.mult, op1=ALU.add),
                                    r=[X, Y], w=[Y])
                                S = ln_stats(Y.t[:], [Y])
                                X2 = nxt(x2b)
                                act(lambda a: a.activation(out=X2.t[:], in_=Y.t[:], func=AF.Identity, scale=S.t[:, 4:5], bias=S.t[:, 5:6]),
                                    r=[Y, S], w=[X2])
                                dve(lambda v: v.tensor_tensor(out=X2.t[:], in0=X2.t[:], in1=lnb.t[:, 0, :], op=ALU.mult), r=[X2, lnb], w=[X2])
                                dve(lambda v: v.tensor_tensor(out=X2.t[:], in0=X2.t[:], in1=lnb.t[:, 1, :], op=ALU.add), r=[X2, lnb], w=[X2])
                                if l == NL - 1:
                                    if gt < 8:
                                        dst = yp[gt * 128:(gt + 1) * 128, :]
                                    else:
                                        dst = ys[(gt - 8) * 128:(gt - 7) * 128, :]
                                    ld(X2.k + "st", dst, X2.t[:], r=[X2])
                                    if dbg:
                                        ld(X2.k + "st", xsc[l, 1, gt * 128:(gt + 1) * 128, :], X2.t[:], r=[X2], w=["xsc1.%d" % gt])
                                else:
                                    ld(X2.k + "st", xsc[l, 1, gt * 128:(gt + 1) * 128, :], X2.t[:], r=[X2], w=["xsc1.%d" % gt])
                            s2.close()
                            fw.barrier()
                    fw.barrier()
        except _Stop:
            pass
        fw.finish()
        build_nc.ninst = fw.ninst
        build_nc.marks = fw.marks
    return nc


def _rope_tables():
    t = np.arange(SS)
    row = (t // 64).astype(np.float32)
    col = (t % 64).astype(np.float32)

    def tab(m, nh):
        freqs = (10000.0 ** (-np.arange(m, dtype=np.float32) / m)).astype(np.float32)
        ar = row[:, None] * freqs[None, :]
        ac = col[:, None] * freqs[None, :]
        cr, sr, cc, sn = np.cos(ar), np.sin(ar), np.cos(ac), np.sin(ac)
        C = np.concatenate([cr, cr, cc, cc], axis=1)
        S = np.concatenate([-sr, sr, -sn, sn], axis=1)
        return (np.tile(C, (1, nh)).astype(np.float32), np.tile(S, (1, nh)).astype(np.float32))
    C16, S16 = tab(16, 10)
    C8, S8 = tab(8, 5)
    return C16, S16, C8, S8


def make_in_maps(inp):
    f = lambda a: np.ascontiguousarray(np.asarray(a, dtype=np.float32))
    C16, S16, C8, S8 = _rope_tables()
    w_mod = f(inp["w_mod"])
    b_mod = f(inp["b_mod"])
    b_modT = f(b_mod.reshape(NL, 48, 128).transpose(0, 2, 1))
    gqk = f(np.concatenate([np.tile(inp["attn_q_norm"], (1, 8)), np.tile(inp["attn_k_norm"], (1, 2))], axis=1))
    wsT = f(np.asarray(inp["gmlp_ws"]).transpose(0, 3, 1, 2))
    bsT = f(np.asarray(inp["gmlp_b"]).transpose(0, 2, 1))
    lnp = f(np.stack([inp["ln1_g"], inp["ln1_b"], inp["ln2_g"], inp["ln2_b"]], axis=1))
    pwq = f(np.asarray(inp["peer_wq"]).reshape(NL, D, 8, 2, 128).transpose(0, 1, 3, 2, 4).reshape(NL, D, 2048))
    k12T = f(np.stack([np.asarray(inp["peer_k1"]).transpose(0, 2, 1), np.asarray(inp["peer_k2"]).transpose(0, 2, 1)], axis=2))
    shared = {
        "w_mod": w_mod, "b_modT": b_modT, "b_mod": b_mod, "w_in": f(inp["w_in"]), "gqk": gqk,
        "mqn": f(inp["mla_q_norm"]), "mkvn": f(inp["mla_kv_norm"]), "w_uq": f(inp["w_uq"]), "w_ukv": f(inp["w_ukv"]),
        "wsT": wsT, "bsT": bsT, "w_o": f(inp["w_o"]), "lnp": lnp, "peer_wq": pwq, "k12T": k12T,
        "puT0": f(np.asarray(inp["peer_u"][0]).T), "puT1": f(np.asarray(inp["peer_u"][1]).T), "pv0": f(inp["peer_v"][0]), "pv1": f(inp["peer_v"][1]),
        "ropeC": C16, "ropeS": S16, "ropeMC": C8, "ropeMS": S8,
    }
    maps = []
    xpr = np.asarray(inp["x_prompt"], dtype=np.float32)
    xsa = np.asarray(inp["x_sample"], dtype=np.float32)
    cctx = np.asarray(inp["c_ctx"], dtype=np.float32)
    for c in range(NCORES):
        m = dict(shared)
        m["xp"] = f(xpr[PB * c:PB * (c + 1)].reshape(PB * PS, D))
        m["xs"] = f(xsa[c])
        m["ck"] = f(np.asarray(inp["cache_attn_k"])[c].reshape(NL, PAST, 128))
        m["cv"] = f(np.asarray(inp["cache_attn_v"])[c].reshape(NL, PAST, 128))
        m["cckv"] = f(np.asarray(inp["cache_mla_ckv"])[c])
        m["ckr"] = f(np.asarray(inp["cache_mla_krope"])[c])
        cc = np.stack([cctx, np.asarray(inp["c"], dtype=np.float32)[c]], axis=-1)
        m["cT"] = f(cc.reshape(8, 128, 2).transpose(1, 0, 2))
        maps.append(m)
    return maps


_NC_CACHE = {}


def kernel(**inputs):
    if "nc" not in _NC_CACHE:
        _NC_CACHE["nc"] = build_nc(False)
    nc = _NC_CACHE["nc"]
    maps = make_in_maps(inputs)
    res = run_bass_kernel_spmd(nc, maps, core_ids=list(range(NCORES)))
    R = res.results
    y_prompt = np.concatenate([R[c]["yp"].reshape(PB, PS, D) for c in range(NCORES)], axis=0)
    y_sample = np.stack([R[c]["ys"] for c in range(NCORES)], axis=0)
    nk = np.concatenate([R[c]["nk"].reshape(PB, NL, PS, 2, 64) for c in range(NCORES)], axis=0)
    nv = np.concatenate([R[c]["nv"].reshape(PB, NL, PS, 2, 64) for c in range(NCORES)], axis=0)
    nckv = np.concatenate([R[c]["nckv"] for c in range(NCORES)], axis=0)
    nkr = np.concatenate([R[c]["nkr"] for c in range(NCORES)], axis=0)
    return (y_prompt.astype(np.float32), y_sample.astype(np.float32), nk.astype(np.float32), nv.astype(np.float32),
            nckv.astype(np.float32), nkr.astype(np.float32))
```

```python
import contextlib
import math
import os

import numpy as np
import concourse.bass as bass
import concourse.mybir as mybir
from concourse.bass_utils import run_bass_kernel_spmd

F32 = mybir.dt.float32
BF16 = mybir.dt.bfloat16
I32 = mybir.dt.int32
U32 = mybir.dt.uint32
AF = mybir.ActivationFunctionType
ALU = mybir.AluOpType
AX = mybir.AxisListType

D = 1024
NL = 2
IN_W = 1696
EPS = 1e-6
ALPHA = (2.0 * NL) ** 0.25
NCORES = 8
PB = 4
PS = 256
SS = 2048
PAST = 256
NTILES = (PB * PS + SS) // 128


class _Stop(Exception):
    pass


class Tile:
    def __init__(self, t, k):
        self.t = t
        self.k = k


def _keys(lst):
    out = []
    for x in lst:
        out.append(x.k if isinstance(x, Tile) else x)
    return out


class FW:
    def __init__(self, nc, es):
        self.nc = nc
        self.es = es
        self.eng = {'pe': nc.tensor, 'act': nc.scalar, 'dve': nc.vector, 'pool': nc.gpsimd, 'sp': nc.sync}
        self.sem = {}
        self.cnt = {}
        self.semobj = {}
        for e in self.eng:
            self.sem[e] = es.enter_context(nc.semaphore("sem_" + e))
            self.cnt[e] = 0
            self.semobj["sem_" + e] = self.sem[e]
        self.dsem = {}
        self.dall = []
        self.dfree = {'hw': [], 'sw': []}
        self.waited = {e: {} for e in self.eng}
        self.lastw = {}
        self.readers = {}
        self.ninst = 0
        self.nops = 0
        self.limit = int(os.environ["MK_LIMIT"]) if os.environ.get("MK_LIMIT") else None
        self.marks = []

    def mark(self, name):
        self.marks.append((name, self.nops))

    def _wait(self, e, ev):
        if ev is None:
            return
        name, val = ev
        if e == 'pe' and name == 'sem_pe':
            return
        w = self.waited[e]
        if w.get(name, 0) >= val:
            return
        w[name] = val
        self.eng[e].wait_ge(self.semobj[name], val)
        self.ninst += 1

    def _deps(self, e, reads, writes):
        for k in reads:
            self._wait(e, self.lastw.get(k))
        for k in writes:
            self._wait(e, self.lastw.get(k))
            for ev in self.readers.get(k, {}).values():
                self._wait(e, ev)

    def _record(self, ev, reads, writes):
        for k in reads:
            self.readers.setdefault(k, {})[ev[0]] = ev
        for k in writes:
            self.lastw[k] = ev
            self.readers[k] = {}

    def op(self, e, fn, r=(), w=(), inc=True):
        self.nops += 1
        if self.limit is not None and self.nops > self.limit:
            return None
        reads = _keys(r)
        writes = _keys(w)
        if e != 'pe':
            ps = [k for k in reads if k.startswith("ps")]
            if ps:
                reads = [k for k in reads if not k.startswith("ps")]
                writes = list(writes) + ps
        self._deps(e, reads, writes)
        inst = fn(self.eng[e])
        self.ninst += 1
        name = "sem_" + e
        if inc:
            self.cnt[e] += 1
            inst.then_inc(self.sem[e], 1)
            ev = (name, self.cnt[e])
        else:
            ev = (name, self.cnt[e] + 1)
        self._record(ev, reads, writes)
        return inst

    def dma(self, q, key, fn, r=(), w=()):
        self.nops += 1
        if self.limit is not None and self.nops > self.limit:
            return None
        reads = _keys(r)
        writes = _keys(w)
        self._deps(q, reads, writes)
        cls = 'sw' if q == 'pool' else 'hw'
        key = cls + ":" + key
        if key not in self.dsem:
            if self.dfree[cls]:
                d = self.dfree[cls].pop()
            else:
                nm = "dsem_%d" % len(self.dall)
                sm = self.es.enter_context(self.nc.semaphore(nm))
                self.semobj[nm] = sm
                d = [sm, 0, nm, cls]
                self.dall.append(d)
            self.dsem[key] = d
        d = self.dsem[key]
        inst = fn(self.eng[q])
        self.ninst += 1
        d[1] += 16
        inst.then_inc(d[0], 16)
        ev = (d[2], d[1])
        self._record(ev, reads, writes)
        return inst

    def _all_events(self):
        evs = [("sem_" + e, self.cnt[e]) for e in self.eng if self.cnt[e] > 0]
        evs += [(d[2], d[1]) for d in self.dall if d[1] > 0]
        return evs

    def barrier(self):
        evs = self._all_events()
        for e in self.eng:
            for ev in evs:
                self._wait(e, ev)
        self.lastw = {}
        self.readers = {}
        for d in self.dsem.values():
            self.dfree[d[3]].append(d)
        self.dsem = {}

    def finish(self):
        for ev in self._all_events():
            self._wait('sp', ev)


def mkap(ap, dims, off=0):
    return bass.AP(ap.tensor, ap.offset + off, [list(x) for x in dims])


def fdims(ap, dims, off=0):
    base = list(ap.ap)
    return bass.AP(ap.tensor, ap.offset + off, [list(base[0])] + [list(x) for x in dims])


def build_nc(dbg=False):
    nc = bass.Bass("TRN2", target_bir_lowering=False)

    def din(name, shape, dt=F32):
        return nc.dram_tensor(name, shape, dt, kind="ExternalInput").ap()

    def dout(name, shape, dt=F32):
        return nc.dram_tensor(name, shape, dt, kind="ExternalOutput").ap()

    xp = din("xp", [PB * PS, D])
    xs = din("xs", [SS, D])
    ck = din("ck", [NL, PAST, 128])
    cv = din("cv", [NL, PAST, 128])
    cckv = din("cckv", [NL, PAST, 128])
    ckr = din("ckr", [NL, PAST, 32])
    cT = din("cT", [128, 8, 2])
    w_mod = din("w_mod", [NL, D, 6 * D])
    b_modT = din("b_modT", [NL, 128, 48])
    b_mod = din("b_mod", [NL, 6 * D])
    w_in = din("w_in", [NL, D, IN_W])
    gqk = din("gqk", [NL, 640])
    mqn = din("mqn", [NL, 256])
    mkvn = din("mkvn", [NL, 128])
    w_uq = din("w_uq", [NL, 256, 384])
    w_ukv = din("w_ukv", [NL, 128, 512])
    wsT = din("wsT", [NL, 128, 4, 128])
    bsT = din("bsT", [NL, 128, 4])
    w_o = din("w_o", [NL, D, D])
    lnp = din("lnp", [NL, 4, D])
    peer_wq = din("peer_wq", [NL, D, 2048])
    k12T = din("k12T", [NL, 128, 2, 128])
    puTs = [din("puT%d" % l, [D, 16384]) for l in range(NL)]
    pv = [din("pv%d" % l, [16384, D]) for l in range(NL)]
    ropeC = din("ropeC", [SS, 640])
    ropeS = din("ropeS", [SS, 640])
    ropeMC = din("ropeMC", [SS, 160])
    ropeMS = din("ropeMS", [SS, 160])

    yp = dout("yp", [PB * PS, D])
    ys = dout("ys", [SS, D])
    nk = dout("nk", [PB, NL, PS, 128])
    nv = dout("nv", [PB, NL, PS, 128])
    nckv = dout("nckv", [PB, NL, PS, 128])
    nkr = dout("nkr", [PB, NL, PS, 32])
    if dbg:
        xsc = dout("xsc", [NL, 2, NTILES * 128, D])
    else:
        xsc = nc.dram_tensor("xsc", [NL, 2, NTILES * 128, D], F32, kind="Internal").ap()
    gsc = nc.dram_tensor("gsc", [NL, 2, 4, 128, D], F32, kind="Internal").ap()
    NI = 2
    NIG = 128 // NI
    ubf = nc.dram_tensor("ubf", [NL, NIG, 128, 8 * NI * 128], BF16, kind="Internal").ap()
    vbf = nc.dram_tensor("vbf", [NL, NIG, 128, NI * D], BF16, kind="Internal").ap()
    cast_done = [False] * NL

    uid = [0]

    with contextlib.ExitStack() as es:
        fw = FW(nc, es)

        def sb(stack, name, shape, dt=F32):
            uid[0] += 1
            nm = "%s_%d" % (name, uid[0])
            return Tile(stack.enter_context(nc.sbuf_tensor(nm, shape, dt)), nm)

        def dve(fn, r=(), w=(), inc=True):
            return fw.op('dve', fn, r, w, inc)

        def act(fn, r=(), w=(), inc=True):
            return fw.op('act', fn, r, w, inc)

        def pe(fn, r=(), w=(), inc=True):
            return fw.op('pe', fn, r, w, inc)

        def pool(fn, r=(), w=(), inc=True):
            return fw.op('pool', fn, r, w, inc)

        def ld(key, out_ap, in_ap, r=(), w=(), q='sp'):
            return fw.dma(q, key, lambda e: e.dma_start(out=out_ap, in_=in_ap), r, w)

        psF = es.enter_context(nc.psum_tensor("psF", [128, 6, 512], F32))
        psB = es.enter_context(nc.psum_tensor("psB", [128, 2, 1024], BF16))
        PF = ["psF%d" % i for i in range(6)]
        PBK = ["psB0", "psB1"]

        ident = sb(es, "ident", [128, 128], BF16)
        pool(lambda p: p.memset(ident.t[:], 0.0), w=[ident])
        pool(lambda p: p.affine_select(out=ident.t[:], in_=ident.t[:], pattern=[[-1, 128]],
                                       compare_op=ALU.not_equal, fill=1.0, base=0, channel_multiplier=1),
             r=[ident], w=[ident])
        epsb = sb(es, "epsb", [128, 1])
        dve(lambda v: v.memset(epsb.t[:], EPS), w=[epsb])
        iota16 = sb(es, "iota16", [128, 16])
        iota16i = sb(es, "iota16i", [128, 16], I32)
        pool(lambda p: p.iota(iota16i.t[:], pattern=[[1, 16]], base=0, channel_multiplier=0), w=[iota16i])
        dve(lambda v: v.tensor_copy(out=iota16.t[:], in_=iota16i.t[:]), r=[iota16i], w=[iota16])
        identF = sb(es, "identF", [128, 128])
        pool(lambda p: p.memset(identF.t[:], 0.0), w=[identF])
        pool(lambda p: p.affine_select(out=identF.t[:], in_=identF.t[:], pattern=[[-1, 128]],
                                       compare_op=ALU.not_equal, fill=1.0, base=0, channel_multiplier=1),
             r=[identF], w=[identF])
        iota128 = sb(es, "iota128", [128, 128])
        iota128i = sb(es, "iota128i", [128, 128], I32)
        pool(lambda p: p.iota(iota128i.t[:], pattern=[[1, 128]], base=0, channel_multiplier=0), w=[iota128i])
        dve(lambda v: v.tensor_copy(out=iota128.t[:], in_=iota128i.t[:]), r=[iota128i], w=[iota128])
        modcol = sb(es, "modcol", [128, NL, 48, 2])

        with contextlib.ExitStack() as ps0:
            cTs = sb(ps0, "cTs", [128, 8, 2])
            scT = sb(ps0, "scT", [128, 8, 2], BF16)
            scR = sb(ps0, "scR", [128, 8, 2, 128], BF16)
            ld("cTs", cTs.t[:], cT, w=[cTs])
            act(lambda a: a.activation(out=scT.t[:], in_=cTs.t[:], func=AF.Silu), r=[cTs], w=[scT])
            dve(lambda v: v.tensor_copy(out=scR.t[:].rearrange("p a g m -> p (a g) m"),
                                        in_=fdims(scT.t[:], [[1, 16], [0, 128]])), r=[scT], w=[scR])
            bmT = sb(ps0, "bmT", [128, NL, 48])
            for l in range(NL):
                ld("bmT", bmT.t[:, l, :], b_modT[l], w=[bmT])
            wm = [sb(ps0, "wm%d" % i, [128, 8, 512], BF16) for i in range(2)]
            bmb = [sb(ps0, "bmb%d" % i, [128, 512]) for i in range(2)]
            gst = [sb(ps0, "gst%d" % i, [128, 512]) for i in range(2)]
            it = 0
            for l in range(NL):
                for ci in range(12):
                    role = ci // 2
                    W = wm[it % 2]
                    ld(W.k, W.t[:], w_mod[l, :, ci * 512:(ci + 1) * 512].rearrange("(c p) n -> p c n", p=128),
                       w=[W], q='pool')
                    if role in (0, 1, 3, 4):
                        for b4 in range(4):
                            blk = ci * 4 + b4
                            for kc in range(8):
                                pe(lambda t: t.matmul(psF[:, 4, b4 * 2:b4 * 2 + 2], lhsT=W.t[:, kc, b4 * 128:(b4 + 1) * 128],
                                                      rhs=scT.t[:, kc, :], start=(kc == 0), stop=(kc == 7)),
                                   r=[W, scT], w=[PF[4]], inc=(kc == 7))
                            addc = 1.0 if role in (1, 4) else 0.0
                            dve(lambda v: v.scalar_tensor_tensor(out=modcol.t[:, l, blk, :], in0=psF[:, 4, b4 * 2:b4 * 2 + 2],
                                                                 scalar=addc, in1=fdims(bmT.t[:, l, blk:blk + 1], [[0, 2]]),
                                                                 op0=ALU.add, op1=ALU.add),
                                r=[PF[4], bmT], w=[modcol])
                    if role in (2, 3, 4, 5):
                        slot = {2: 0, 4: 1, 3: 2, 5: 3}[role]
                        Bb = bmb[it % 2]
                        ld(Bb.k, Bb.t[:], mkap(b_mod[l, ci * 512:(ci + 1) * 512], [[0, 128], [1, 512]]), w=[Bb])
                        for g in range(2):
                            pb = PF[2 + g]
                            for kc in range(8):
                                pe(lambda t: t.matmul(psF[:, 2 + g, :], lhsT=scR.t[:, kc, g, :], rhs=W.t[:, kc, :],
                                                      start=(kc == 0), stop=(kc == 7)),
                                   r=[W, scR], w=[pb], inc=(kc == 7))
                            G = gst[g]
                            addc = 1.0 if role == 4 else 0.0
                            dve(lambda v: v.scalar_tensor_tensor(out=G.t[:], in0=psF[:, 2 + g, :], scalar=addc, in1=Bb.t[:],
                                                                 op0=ALU.add, op1=ALU.add),
                                r=[pb, Bb], w=[G])
                            half = ci % 2
                            ld("gst%d" % g, gsc[l, g, slot, :, half * 512:(half + 1) * 512], G.t[:], r=[G], w=["gsc"])
                    it += 1
        fw.barrier()

        def chk(tag):
            if dbg and os.environ.get("MK_STOP") == tag:
                raise _Stop()

        seqs = [(2 * b, 2, 0, False, b) for b in range(PB)] + [(8, 16, 1, True, -1)]
        if dbg and os.environ.get("MK_SEQS"):
            seqs = [seqs[int(i)] for i in os.environ["MK_SEQS"].split(",")]
        nlayers = int(os.environ.get("MK_LAYERS", NL)) if dbg else NL
        skip_peer = bool(dbg and os.environ.get("MK_SKIP_PEER"))
        stopA = bool(dbg and os.environ.get('MK_STOP') == 'A')
        stopB = bool(dbg and os.environ.get('MK_STOP') == 'B')

        def x_src(l, ph, gt):
            if l < 0:
                if gt < 8:
                    return xp[gt * 128:(gt + 1) * 128, :]
                return xs[(gt - 8) * 128:(gt - 7) * 128, :]
            return xsc[l, ph, gt * 128:(gt + 1) * 128, :]

        try:
            chk('pro')
            for (tile0, nt, grp, is_s, pbi) in seqs:
                nkc = nt + (2 if is_s else 0)
                koff = 2 if is_s else 0
                for l in range(nlayers):
                    with contextlib.ExitStack() as sa:
                        NTOK = nt * 128
                        NKEY = nkc * 128
                        QT = sb(sa, "QT", [128, 4, NTOK], BF16)
                        KT = sb(sa, "KT", [128, NKEY], BF16)
                        VA = sb(sa, "VA", [128, nkc, 2, 65], BF16)
                        QMT = sb(sa, "QMT", [128, 4, NTOK], BF16)
                        KMT = sb(sa, "KMT", [128, 4, NKEY], BF16)
                        VM = sb(sa, "VM", [128, nkc, 4, 65], BF16)
                        OC = sb(sa, "OC", [128, nt, 256], BF16)
                        pool(lambda p: p.memset(VA.t[:], 1.0), w=[VA])
                        pool(lambda p: p.memset(VM.t[:], 1.0), w=[VM])

                        sA = contextlib.ExitStack()
                        cur = [sA]
                        gq = sb(sA, "gq", [128, 640])
                        ld(gq.k, gq.t[:], mkap(gqk[l], [[0, 128], [1, 640]]), w=[gq])
                        gmq = sb(sA, "gmq", [128, 256])
                        ld(gmq.k, gmq.t[:], mkap(mqn[l], [[0, 128], [1, 256]]), w=[gmq])
                        gmk = sb(sA, "gmk", [128, 128])
                        ld(gmk.k, gmk.t[:], mkap(mkvn[l], [[0, 128], [1, 128]]), w=[gmk])
                        bst = sb(sA, "bst", [128, 4])
                        ld(bst.k, bst.t[:], bsT[l], w=[bst])
                        win = sb(sA, "win", [128, 8, IN_W], BF16)
                        for kc in range(8):
                            ld(win.k, win.t[:, kc, :], w_in[l, kc * 128:(kc + 1) * 128, :], w=[win], q='pool')
                        wuq = sb(sA, "wuq", [128, 2, 384], BF16)
                        ld(wuq.k, wuq.t[:], w_uq[l].rearrange("(c p) n -> p c n", p=128), w=[wuq], q='pool')
                        wukv = sb(sA, "wukv", [128, 512], BF16)
                        ld(wukv.k, wukv.t[:], w_ukv[l], w=[wukv], q='pool')
                        wst = sb(sA, "wst", [128, 4, 128], BF16)
                        ld(wst.k, wst.t[:], wsT[l], w=[wst], q='pool')

                        def rot(name, shape, dt=F32, n=2):
                            return [sb(cur[0], name + str(i), shape, dt) for i in range(n)]
                        xts = rot("xt", [128, D])
                        xnb = rot("xnb", [128, D], BF16, n=1)
                        hT = rot("hT", [128, 8, 128], BF16, n=1)
                        pj = rot("pj", [128, IN_W], n=1)
                        sq = rot("sq", [128, 1152], n=1)
                        small = rot("small", [128, 64], n=3)
                        qkn = rot("qkn", [128, 640], n=1)
                        rc_t = rot("ropeC", [128, 640], n=1)
                        rs_t = rot("ropeS", [128, 640], n=1)
                        rmc_t = rot("ropeMC", [128, 160], n=1)
                        rms_t = rot("ropeMS", [128, 160], n=1)
                        tmp640 = rot("tmp640", [128, 640], n=1)
                        tmp640b = rot("tmp640b", [128, 640], n=1)
                        qkb = rot("qkb", [128, 640], BF16)
                        cqb = rot("cqb", [128, 256], BF16)
                        cqT = rot("cqT", [128, 2, 128], BF16)
                        ckvn = rot("ckvn", [128, 128])
                        ckb = rot("ckb", [128, 128], BF16)
                        ckT = rot("ckT", [128, 128], BF16)
                        qm = rot("qm", [128, 4, 96])
                        mr = rot("mr", [128, 160], n=1)
                        mr3 = rot("mr3", [128, 160], n=1)
                        qmb = rot("qmb", [128, 4, 96], BF16)
                        kmb = rot("kmb", [128, 4, 96], BF16)
                        vg = rot("vg", [128, 256], BF16)
                        vtmp = rot("vtmp", [128, 256])
                        cst = rot("cst", [128, 128])
                        cstb = rot("cstb", [128, 128], BF16)
                        krs = rot("krs", [128, 32])
                        rr = {}

                        def nxt(lst):
                            i = rr.get(id(lst), 0)
                            rr[id(lst)] = i + 1
                            return lst[i % len(lst)]

                        def ln_stats(xin, xkeys):
                            S = nxt(small)
                            dve(lambda v: v.bn_stats(out=S.t[:, 8:14], in_=xin[:, 0:512]), r=xkeys, w=[S])
                            dve(lambda v: v.bn_stats(out=S.t[:, 14:20], in_=xin[:, 512:1024]), r=xkeys, w=[S])
                            dve(lambda v: v.bn_aggr(out=S.t[:, 0:2], in_=S.t[:, 8:20]), r=[S], w=[S])
                            act(lambda a: a.activation(out=S.t[:, 2:3], in_=S.t[:, 1:2], func=AF.Sqrt, bias=epsb.t[:], scale=1.0),
                                r=[S, epsb], w=[S])
                            dve(lambda v: v.reciprocal(out=S.t[:, 4:5], in_=S.t[:, 2:3]), r=[S], w=[S])
                            dve(lambda v: v.scalar_tensor_tensor(out=S.t[:, 5:6], in0=S.t[:, 0:1], scalar=-1.0, in1=S.t[:, 4:5],
                                                                 op0=ALU.mult, op1=ALU.mult), r=[S], w=[S])
                            return S

                        tcount = [0]

                        def transposes(srcs, dst_fn, rkeys, wkeys, evac='act'):
                            b = tcount[0] % 2
                            tcount[0] += 1
                            for i, (src, n) in enumerate(srcs):
                                pe(lambda t: t.transpose(out=psB[0:n, b, i * 128:(i + 1) * 128], in_=src, identity=ident.t[:]),
                                   r=list(rkeys) + [ident], w=[PBK[b]], inc=(i == len(srcs) - 1))
                            dst_fn(b)

                        def rope(src, nh, hd, ctab, stab, coff, dsts, rkeys, wkeys):
                            n = nh * hd
                            q4 = hd // 4
                            T1 = nxt(tmp640)
                            T2 = nxt(tmp640b)
                            dve(lambda v: v.tensor_tensor(out=T1.t[:, 0:n], in0=src, in1=ctab.t[:, coff:coff + n], op=ALU.mult),
                                r=list(rkeys) + [ctab], w=[T1])
                            nb = n // (2 * q4)
                            def hv(ap, off, half):
                                return fdims(ap, [[2 * q4, nb], [1, q4]], off + half * q4)
                            dve(lambda v: v.tensor_tensor(out=hv(T2.t[:, 0:n], 0, 0), in0=hv(src, 0, 1), in1=hv(stab.t[:, 0:n], coff, 0),
                                                          op=ALU.mult), r=list(rkeys) + [stab], w=[T2])
                            dve(lambda v: v.tensor_tensor(out=hv(T2.t[:, 0:n], 0, 1), in0=hv(src, 0, 0), in1=hv(stab.t[:, 0:n], coff, 1),
                                                          op=ALU.mult), r=list(rkeys) + [stab], w=[T2])
                            for (oap, i0, i1, j0, j1) in dsts:
                                dve(lambda v: v.tensor_tensor(out=oap, in0=i0(T1.t), in1=i0(T2.t), op=ALU.add), r=[T1, T2], w=wkeys)

                        def mla_keys(ckb_ap, ckb_keys, kr_ap, kr_keys, kc):
                            CT = nxt(ckT)
                            def ev(b):
                                act(lambda a: a.copy(out=CT.t[:], in_=psB[:, b, 0:128]), r=[PBK[b]], w=[CT])
                            transposes([(ckb_ap, 128)], ev, ckb_keys, None)
                            pe(lambda t: t.matmul(psF[:, 5, :], lhsT=CT.t[:], rhs=wukv.t[:], start=True, stop=True),
                               r=[CT, wukv], w=[PF[5]])
                            KB = nxt(kmb)
                            kvv = psF[:, 5, :].rearrange("p (h c) -> p h c", h=4)
                            act(lambda a: a.copy(out=KB.t[:, :, 0:64], in_=kvv[:, :, 0:64]), r=[PF[5]], w=[KB])
                            dve(lambda v: v.tensor_copy(out=VM.t[:, kc, :, 0:64], in_=kvv[:, :, 64:128]), r=[PF[5], VM], w=["VM.%d" % kc])
                            dve(lambda v: v.tensor_copy(out=KB.t[:, :, 64:96], in_=fdims(kr_ap, [[0, 4], [1, 32]])),
                                r=list(kr_keys), w=[KB])
                            def ev2(b):
                                act(lambda a: a.copy(out=KMT.t[0:96, :, kc * 128:(kc + 1) * 128],
                                                     in_=psB[0:96, b, 0:512].rearrange("p (h c) -> p h c", h=4)),
                                    r=[PBK[b]], w=["KMT.%d" % kc])
                            transposes([(KB.t[:, h, :], 96) for h in range(4)], ev2, [KB], None)

                        fw.mark('A-start')
                        if is_s:
                            for j in range(2):
                                C1 = nxt(cst)
                                ld(C1.k, C1.t[:], ck[l, j * 128:(j + 1) * 128, :], w=[C1])
                                CB = nxt(cstb)
                                dve(lambda v: v.tensor_copy(out=CB.t[:], in_=C1.t[:]), r=[C1], w=[CB])
                                def evk(b):
                                    act(lambda a: a.copy(out=KT.t[:, j * 128:(j + 1) * 128], in_=psB[:, b, 0:128]),
                                        r=[PBK[b]], w=["KT.%d" % j])
                                transposes([(CB.t[:], 128)], evk, [CB], None)
                                C2 = nxt(cst)
                                ld(C2.k, C2.t[:], cv[l, j * 128:(j + 1) * 128, :], w=[C2])
                                dve(lambda v: v.tensor_copy(out=VA.t[:, j, :, 0:64], in_=C2.t[:].rearrange("p (h c) -> p h c", h=2)),
                                    r=[C2, VA], w=["VA.%d" % j])
                                C3 = nxt(cst)
                                ld(C3.k, C3.t[:], cckv[l, j * 128:(j + 1) * 128, :], w=[C3])
                                CB3 = nxt(cstb)
                                dve(lambda v: v.tensor_copy(out=CB3.t[:], in_=C3.t[:]), r=[C3], w=[CB3])
                                K4 = nxt(krs)
                                ld(K4.k, K4.t[:], ckr[l, j * 128:(j + 1) * 128, :], w=[K4])
                                mla_keys(CB3.t[:], [CB3], K4.t[:], [K4], j)

                        for ti in range(nt):
                            gt = tile0 + ti
                            kc_new = koff + ti
                            X = nxt(xts)
                            ld(X.k, X.t[:], x_src(l - 1, 1, gt), w=[X])
                            fw.mark('tile%d-ln' % ti)
                            S = ln_stats(X.t[:], [X])
                            XN = nxt(xnb)
                            act(lambda a: a.activation(out=XN.t[:], in_=X.t[:], func=AF.Identity, scale=S.t[:, 4:5], bias=S.t[:, 5:6]),
                                r=[X, S], w=[XN])
                            H = nxt(hT)
                            def evh(b):
                                for c in range(8):
                                    act(lambda a: a.activation(out=H.t[:, c, :], in_=psB[:, b, c * 128:(c + 1) * 128], func=AF.Identity,
                                                               scale=modcol.t[:, l, 8 + c, grp:grp + 1], bias=modcol.t[:, l, c, grp:grp + 1]),
                                        r=[PBK[b], modcol], w=[H])
                            transposes([(XN.t[:, c * 128:(c + 1) * 128], 128) for c in range(8)], evh, [XN], None)
                            fw.mark('proj')
                            segs = [(0, 512), (512, 512), (1024, 512), (1536, 160)]
                            for bi, (c0, cn) in enumerate(segs):
                                for kc in range(8):
                                    pe(lambda t: t.matmul(psF[:, bi, 0:cn], lhsT=H.t[:, kc, :], rhs=win.t[:, kc, c0:c0 + cn],
                                                          start=(kc == 0), stop=(kc == 7)), r=[H, win], w=[PF[bi]], inc=(kc == 7))
                            P = nxt(pj)
                            for bi, (c0, cn) in enumerate(segs):
                                act(lambda a: a.copy(out=P.t[:, c0:c0 + cn], in_=psF[:, bi, 0:cn]), r=[PF[bi]], w=[P])
                            fw.mark('rms')
                            SQ = nxt(sq)
                            dve(lambda v: v.tensor_tensor(out=SQ.t[:], in0=P.t[:, 0:1152], in1=P.t[:, 0:1152], op=ALU.mult), r=[P], w=[SQ])
                            S2 = nxt(small)
                            dve(lambda v: v.tensor_reduce(out=S2.t[:, 0:10], in_=SQ.t[:, 0:640].rearrange("p (h c) -> p h c", c=64),
                                                          axis=AX.X, op=ALU.add), r=[SQ], w=[S2])
                            dve(lambda v: v.tensor_reduce(out=S2.t[:, 10:11], in_=SQ.t[:, 768:1024], axis=AX.X, op=ALU.add), r=[SQ], w=[S2])
                            dve(lambda v: v.tensor_reduce(out=S2.t[:, 11:12], in_=SQ.t[:, 1024:1152], axis=AX.X, op=ALU.add), r=[SQ], w=[S2])
                            dve(lambda v: v.tensor_scalar(out=S2.t[:, 16:26], in0=S2.t[:, 0:10], scalar1=1.0 / 64, scalar2=EPS,
                                                          op0=ALU.mult, op1=ALU.add), r=[S2], w=[S2])
                            dve(lambda v: v.tensor_scalar(out=S2.t[:, 26:27], in0=S2.t[:, 10:11], scalar1=1.0 / 256, scalar2=EPS,
                                                          op0=ALU.mult, op1=ALU.add), r=[S2], w=[S2])
                            dve(lambda v: v.tensor_scalar(out=S2.t[:, 27:28], in0=S2.t[:, 11:12], scalar1=1.0 / 128, scalar2=EPS,
                                                          op0=ALU.mult, op1=ALU.add), r=[S2], w=[S2])
                            act(lambda a: a.activation(out=S2.t[:, 32:44], in_=S2.t[:, 16:28], func=AF.Sqrt), r=[S2], w=[S2])
                            dve(lambda v: v.reciprocal(out=S2.t[:, 48:60], in_=S2.t[:, 32:44]), r=[S2], w=[S2])
                            fw.mark('qk')
                            QK = nxt(qkn)
                            dve(lambda v: v.tensor_tensor(out=QK.t[:].rearrange("p (h c) -> p h c", c=64),
                                                          in0=P.t[:, 0:640].rearrange("p (h c) -> p h c", c=64),
                                                          in1=fdims(S2.t[:, 48:58], [[1, 10], [0, 64]]), op=ALU.mult), r=[P, S2], w=[QK])
                            dve(lambda v: v.tensor_tensor(out=QK.t[:], in0=QK.t[:], in1=gq.t[:], op=ALU.mult), r=[QK, gq], w=[QK])
                            QB = nxt(qkb)
                            def qdst(T, j):
                                return T[:, j * 256:(j + 1) * 256].rearrange("p (s c) -> p s c", c=64)
                            def qout(j):
                                return fdims(QB.t[:], [[128, 4], [1, 64]], j * 64)
                            if is_s:
                                RC = rc_t[0]
                                RS = rs_t[0]
                                ld(RC.k, RC.t[:], ropeC[ti * 128:(ti + 1) * 128, :], w=[RC])
                                ld(RS.k, RS.t[:], ropeS[ti * 128:(ti + 1) * 128, :], w=[RS])
                                dsts = [(qout(0), lambda T: qdst(T, 0), None, 0, 0), (qout(1), lambda T: qdst(T, 1), None, 0, 0),
                                        (QB.t[:, 512:640], lambda T: T[:, 512:640], None, 0, 0)]
                                rope(QK.t[:], 10, 64, RC, RS, 0, dsts, [QK], [QB])
                            else:
                                for j in range(2):
                                    dve(lambda v: v.tensor_copy(out=qout(j), in_=qdst(QK.t, j)), r=[QK], w=[QB])
                                dve(lambda v: v.tensor_copy(out=QB.t[:, 512:640], in_=QK.t[:, 512:640]), r=[QK], w=[QB])
                            def evq(b):
                                act(lambda a: a.copy(out=QT.t[:, :, ti * 128:(ti + 1) * 128],
                                                     in_=psB[:, b, 0:512].rearrange("p (s c) -> p s c", s=4)),
                                    r=[PBK[b]], w=["QT.%d" % ti])
                                act(lambda a: a.copy(out=KT.t[:, kc_new * 128:(kc_new + 1) * 128], in_=psB[:, b, 512:640]),
                                    r=[PBK[b]], w=["KT.%d" % kc_new])
                            transposes([(QB.t[:, i * 128:(i + 1) * 128], 128) for i in range(5)], evq, [QB], None)
                            fw.mark('va')
                            dve(lambda v: v.tensor_copy(out=VA.t[:, kc_new, :, 0:64], in_=P.t[:, 640:768].rearrange("p (h c) -> p h c", h=2)),
                                r=[P, VA], w=["VA.%d" % kc_new])
                            fw.mark('mlaq')
                            CQ = nxt(cqb)
                            dve(lambda v: v.scalar_tensor_tensor(out=CQ.t[:], in0=P.t[:, 768:1024], scalar=S2.t[:, 58:59], in1=gmq.t[:],
                                                                 op0=ALU.mult, op1=ALU.mult), r=[P, S2, gmq], w=[CQ])
                            CQT = nxt(cqT)
                            def evc(b):
                                act(lambda a: a.copy(out=CQT.t[:], in_=psB[:, b, 0:256].rearrange("p (s c) -> p s c", s=2)),
                                    r=[PBK[b]], w=[CQT])
                            transposes([(CQ.t[:, i * 128:(i + 1) * 128], 128) for i in range(2)], evc, [CQ], None)
                            for kc in range(2):
                                pe(lambda t: t.matmul(psF[:, 4, 0:384], lhsT=CQT.t[:, kc, :], rhs=wuq.t[:, kc, :],
                                                      start=(kc == 0), stop=(kc == 1)), r=[CQT, wuq], w=[PF[4]], inc=(kc == 1))
                            QM = nxt(qm)
                            act(lambda a: a.copy(out=QM.t[:].rearrange("p h c -> p (h c)"), in_=psF[:, 4, 0:384]), r=[PF[4]], w=[QM])
                            QMB = nxt(qmb)
                            dve(lambda v: v.tensor_copy(out=QMB.t[:, :, 0:64], in_=QM.t[:, :, 0:64]), r=[QM], w=[QMB])
                            MR = nxt(mr)
                            dve(lambda v: v.tensor_copy(out=MR.t[:, 0:128].rearrange("p (h c) -> p h c", h=4), in_=QM.t[:, :, 64:96]),
                                r=[QM], w=[MR])
                            dve(lambda v: v.tensor_copy(out=MR.t[:, 128:160], in_=P.t[:, 1152:1184]), r=[P], w=[MR])
                            MR3 = nxt(mr3)
                            if is_s:
                                RMC = rmc_t[0]
                                RMS = rms_t[0]
                                ld(RMC.k, RMC.t[:], ropeMC[ti * 128:(ti + 1) * 128, :], w=[RMC])
                                ld(RMS.k, RMS.t[:], ropeMS[ti * 128:(ti + 1) * 128, :], w=[RMS])
                                rope(MR.t[:], 5, 32, RMC, RMS, 0, [(MR3.t[:], lambda T: T[:, 0:160], None, 0, 0)], [MR], [MR3])
                            else:
                                dve(lambda v: v.tensor_copy(out=MR3.t[:], in_=MR.t[:]), r=[MR], w=[MR3])
                            dve(lambda v: v.tensor_copy(out=QMB.t[:, :, 64:96], in_=MR3.t[:, 0:128].rearrange("p (h c) -> p h c", h=4)),
                                r=[MR3], w=[QMB])
                            def evqm(b):
                                act(lambda a: a.copy(out=QMT.t[0:96, :, ti * 128:(ti + 1) * 128],
                                                     in_=psB[0:96, b, 0:512].rearrange("p (h c) -> p h c", h=4)),
                                    r=[PBK[b]], w=["QMT.%d" % ti])
                            transposes([(QMB.t[:, h, :], 96) for h in range(4)], evqm, [QMB], None)
                            fw.mark('mlakv')
                            CK = nxt(ckvn)
                            dve(lambda v: v.scalar_tensor_tensor(out=CK.t[:], in0=P.t[:, 1024:1152], scalar=S2.t[:, 59:60], in1=gmk.t[:],
                                                                 op0=ALU.mult, op1=ALU.mult), r=[P, S2, gmk], w=[CK])
                            CKB = nxt(ckb)
                            dve(lambda v: v.tensor_copy(out=CKB.t[:], in_=CK.t[:]), r=[CK], w=[CKB])
                            mla_keys(CKB.t[:], [CKB], MR3.t[:, 128:160], [MR3], kc_new)
                            fw.mark('gmlp')
                            VT = nxt(vtmp)
                            S3 = nxt(small)
                            vcv = P.t[:, 1440:1696].rearrange("p (g c) -> p g c", g=4)
                            dve(lambda v: v.tensor_reduce(out=S3.t[:, 0:4], in_=vcv, axis=AX.X, op=ALU.add), r=[P], w=[S3])
                            dve(lambda v: v.tensor_tensor(out=VT.t[:], in0=P.t[:, 1440:1696], in1=P.t[:, 1440:1696], op=ALU.mult), r=[P], w=[VT])
                            dve(lambda v: v.tensor_reduce(out=S3.t[:, 4:8], in_=VT.t[:].rearrange("p (g c) -> p g c", g=4), axis=AX.X, op=ALU.add),
                                r=[VT], w=[S3])
                            dve(lambda v: v.tensor_scalar(out=S3.t[:, 8:12], in0=S3.t[:, 0:4], scalar1=1.0 / 64, scalar2=None, op0=ALU.mult),
                                r=[S3], w=[S3])
                            dve(lambda v: v.tensor_tensor(out=S3.t[:, 12:16], in0=S3.t[:, 8:12], in1=S3.t[:, 8:12], op=ALU.mult), r=[S3], w=[S3])
                            dve(lambda v: v.scalar_tensor_tensor(out=S3.t[:, 16:20], in0=S3.t[:, 4:8], scalar=1.0 / 64, in1=S3.t[:, 12:16],
                                                                 op0=ALU.mult, op1=ALU.subtract), r=[S3], w=[S3])
                            dve(lambda v: v.tensor_scalar(out=S3.t[:, 20:24], in0=S3.t[:, 16:20], scalar1=EPS, scalar2=None, op0=ALU.add),
                                r=[S3], w=[S3])
                            act(lambda a: a.activation(out=S3.t[:, 24:28], in_=S3.t[:, 20:24], func=AF.Sqrt), r=[S3], w=[S3])
                            dve(lambda v: v.reciprocal(out=S3.t[:, 28:32], in_=S3.t[:, 24:28]), r=[S3], w=[S3])
                            dve(lambda v: v.tensor_tensor(out=VT.t[:].rearrange("p (g c) -> p g c", g=4), in0=vcv,
                                                          in1=fdims(S3.t[:, 8:12], [[1, 4], [0, 64]]), op=ALU.subtract), r=[P, S3], w=[VT])
                            VG = nxt(vg)
                            dve(lambda v: v.tensor_tensor(out=VG.t[:].rearrange("p (g c) -> p g c", g=4),
                                                          in0=VT.t[:].rearrange("p (g c) -> p g c", g=4),
                                                          in1=fdims(S3.t[:, 28:32], [[1, 4], [0, 64]]), op=ALU.mult), r=[VT, S3], w=[VG])
                            for g in range(4):
                                pe(lambda t: t.matmul(psF[:, 4, g * 64:(g + 1) * 64], lhsT=wst.t[:, g, :], rhs=VG.t[:, g * 64:(g + 1) * 64],
                                                      start=True, stop=True), r=[VG, wst], w=[PF[4]], inc=(g == 3))
                            dve(lambda v: v.tensor_tensor(out=VT.t[:].rearrange("p (g c) -> p g c", g=4),
                                                          in0=psF[:, 4, 0:256].rearrange("p (g c) -> p g c", g=4),
                                                          in1=fdims(bst.t[:], [[1, 4], [0, 64]]), op=ALU.add), r=[PF[4], bst], w=[VT])
                            dve(lambda v: v.tensor_tensor(out=OC.t[:, ti, :], in0=VT.t[:], in1=P.t[:, 1184:1440], op=ALU.mult),
                                r=[VT, P], w=["OC.%d" % ti])
                            fw.mark('outs')
                            if not is_s:
                                r0 = ti * 128
                                ld("o_nk%d" % (ti % 2), nk[pbi, l, r0:r0 + 128, :], QK.t[:, 512:640], r=[QK])
                                ld("o_nv%d" % (ti % 2), nv[pbi, l, r0:r0 + 128, :], P.t[:, 640:768], r=[P])
                                ld("o_nc%d" % (ti % 2), nckv[pbi, l, r0:r0 + 128, :], CK.t[:], r=[CK])
                                ld("o_nr%d" % (ti % 2), nkr[pbi, l, r0:r0 + 128, :], P.t[:, 1152:1184], r=[P])

                        sA.close()
                        fw.barrier()
                        sB = contextlib.ExitStack()
                        cur[0] = sB
                        G1 = sb(sB, "G1", [128, D])
                        ld(G1.k, G1.t[:], gsc[l, grp, 0], w=[G1])
                        lnb = sb(sB, "lnb", [128, 2, D])
                        for j in range(2):
                            ld(lnb.k, lnb.t[:, j, :], mkap(lnp[l, j], [[0, 128], [1, D]]), w=[lnb])
                        wo = sb(sB, "wo", [128, 8, D], BF16)
                        for kc in range(8):
                            ld(wo.k, wo.t[:, kc, :], w_o[l, kc * 128:(kc + 1) * 128, :], w=[wo], q='pool')
                        xts = rot("xtB", [128, D])
                        small = rot("smallB", [128, 64], n=3)
                        ntq = 4 if is_s else 2
                        NQ = ntq * 128
                        PT = rot("PT", [128, NQ], BF16, n=3)
                        mix = sb(sB, "mix", [128, ntq, D], BF16)
                        mixT = rot("mixT", [128, 8, 128], BF16)
                        ybuf = rot("ybuf", [128, D], n=1)
                        x1b = rot("x1b", [128, D])
                        rec = rot("rec", [128, 8])
                        sc_ = [0]
                        oc_ = [0]
                        for qg in range(0 if stopA else nt // ntq):
                            q0 = qg * NQ
                            allkeys_q = ["QT.%d" % (qg * ntq + i) for i in range(ntq)]
                            allkeys_qm = ["QMT.%d" % (qg * ntq + i) for i in range(ntq)]
                            for hh in range(12):
                                isA = hh < 8
                                h = hh if isA else hh - 8
                                ob = 3 + (oc_[0] % 2)
                                oc_[0] += 1
                                oview = psF[:, ob, 0:ntq * 65].rearrange("p (q c) -> p q c", c=65)
                                for kc in range(nkc):
                                    sbk = sc_[0] % 3
                                    sc_[0] += 1
                                    if isA:
                                        base = (h // 4) * 64
                                        pe(lambda t: t.matmul(psF[:, sbk, 0:NQ], lhsT=KT.t[base:base + 64, kc * 128:(kc + 1) * 128],
                                                              rhs=QT.t[base:base + 64, h % 4, q0:q0 + NQ], start=True, stop=True),
                                           r=["KT.%d" % kc] + allkeys_q, w=[PF[sbk]])
                                        scl = 0.125
                                    else:
                                        pe(lambda t: t.matmul(psF[:, sbk, 0:NQ], lhsT=KMT.t[0:96, h, kc * 128:(kc + 1) * 128],
                                                              rhs=QMT.t[0:96, h, q0:q0 + NQ], start=True, stop=True),
                                           r=["KMT.%d" % kc] + allkeys_qm, w=[PF[sbk]])
                                        scl = 1.0 / math.sqrt(96.0)
                                    Pt = nxt(PT)
                                    act(lambda a: a.activation(out=Pt.t[:], in_=psF[:, sbk, 0:NQ], func=AF.Exp, scale=scl), r=[PF[sbk]], w=[Pt])
                                    for qt in range(ntq):
                                        if isA:
                                            rhs = VA.t[:, kc, h // 4, :]
                                            vk = "VA.%d" % kc
                                        else:
                                            rhs = VM.t[:, kc, h, :]
                                            vk = "VM.%d" % kc
                                        pe(lambda t: t.matmul(oview[:, qt, :], lhsT=Pt.t[:, qt * 128:(qt + 1) * 128], rhs=rhs,
                                                              start=(kc == 0 and qt == 0), stop=(kc == nkc - 1 and qt == ntq - 1)),
                                           r=[Pt, vk], w=[PF[ob]], inc=(qt == ntq - 1))
                                R = nxt(rec)
                                dve(lambda v: v.reciprocal(out=R.t[:, 0:ntq], in_=oview[:, :, 64]), r=[PF[ob]], w=[R])
                                col = h * 64 if isA else 512 + h * 64
                                dve(lambda v: v.tensor_tensor(out=mix.t[:, :, col:col + 64], in0=oview[:, :, 0:64],
                                                              in1=fdims(R.t[:, 0:ntq], [[1, ntq], [0, 64]]), op=ALU.mult),
                                    r=[PF[ob], R], w=[mix])
                            dve(lambda v: v.tensor_copy(out=mix.t[:, :, 768:1024], in_=OC.t[:, qg * ntq:(qg + 1) * ntq, :]),
                                r=["OC.%d" % (qg * ntq + i) for i in range(ntq)], w=[mix])
                            for qt in range(ntq):
                                ti = qg * ntq + qt
                                gt = tile0 + ti
                                MT = nxt(mixT)
                                def evm(b):
                                    act(lambda a: a.copy(out=MT.t[:].rearrange("p a c -> p (a c)"), in_=psB[:, b, :]), r=[PBK[b]], w=[MT])
                                transposes([(mix.t[:, qt, c * 128:(c + 1) * 128], 128) for c in range(8)], evm, [mix], None)
                                for hf in range(2):
                                    for kc in range(8):
                                        pe(lambda t: t.matmul(psF[:, hf, :], lhsT=MT.t[:, kc, :], rhs=wo.t[:, kc, hf * 512:(hf + 1) * 512],
                                                              start=(kc == 0), stop=(kc == 7)), r=[MT, wo], w=[PF[hf]], inc=(kc == 7))
                                X = nxt(xts)
                                ld(X.k, X.t[:], x_src(l - 1, 1, gt), w=[X])
                                Y = nxt(ybuf)
                                dve(lambda v: v.tensor_tensor(out=Y.t[:].rearrange("p (a c) -> p a c", a=2), in0=psF[:, 0:2, :],
                                                              in1=G1.t[:].rearrange("p (a c) -> p a c", a=2), op=ALU.mult),
                                    r=[PF[0], PF[1], G1], w=[Y])
                                dve(lambda v: v.scalar_tensor_tensor(out=Y.t[:], in0=X.t[:], scalar=ALPHA, in1=Y.t[:],
                                                                     op0=ALU.mult, op1=ALU.add), r=[X, Y], w=[Y])
                                S = ln_stats(Y.t[:], [Y])
                                X1 = nxt(x1b)
                                act(lambda a: a.activation(out=X1.t[:], in_=Y.t[:], func=AF.Identity, scale=S.t[:, 4:5], bias=S.t[:, 5:6]),
                                    r=[Y, S], w=[X1])
                                dve(lambda v: v.tensor_tensor(out=X1.t[:], in0=X1.t[:], in1=lnb.t[:, 0, :], op=ALU.mult), r=[X1, lnb], w=[X1])
                                dve(lambda v: v.tensor_tensor(out=X1.t[:], in0=X1.t[:], in1=lnb.t[:, 1, :], op=ALU.add), r=[X1, lnb], w=[X1])
                                ld(X1.k + "st", xsc[l, 0, gt * 128:(gt + 1) * 128, :], X1.t[:], r=[X1], w=["xsc.%d" % gt])
                        sB.close()
                    fw.barrier()

                    with contextlib.ExitStack() as sc:
                        cur[0] = sc
                        G2 = sb(sc, "G2", [128, D])
                        ld(G2.k, G2.t[:], gsc[l, grp, 3], w=[G2])
                        A2 = sb(sc, "A2", [128, D])
                        ld(A2.k, A2.t[:], gsc[l, grp, 1], w=[A2])
                        B2 = sb(sc, "B2", [128, D])
                        ld(B2.k, B2.t[:], gsc[l, grp, 2], w=[B2])
                        lnb = sb(sc, "lnb2", [128, 2, D])
                        for j in range(2):
                            ld(lnb.k, lnb.t[:, j, :], mkap(lnp[l, 2 + j], [[0, 128], [1, D]]), w=[lnb])
                        wq = sb(sc, "wq", [128, 8, 2048], BF16)
                        for kc in range(8):
                            ld(wq.k, wq.t[:, kc, :], peer_wq[l, kc * 128:(kc + 1) * 128, :], w=[wq], q='pool')
                        kT = sb(sc, "kT", [128, 2, 128], BF16)
                        ld(kT.k, kT.t[:], k12T[l], w=[kT], q='pool')
                        TGT = 2
                        TG = TGT * 128
                        IJG = sb(sc, "IJG", [128, 3, TG])
                        h2Tg = sb(sc, "h2Tg", [128, 8, TG], BF16)
                        rr = {}

                        def nxt(lst):
                            i = rr.get(id(lst), 0)
                            rr[id(lst)] = i + 1
                            return lst[i % len(lst)]
                        xts = rot("cxt", [128, D])
                        small = rot("csmall", [128, 64], n=3)
                        ybuf = rot("cy", [128, D], n=1)
                        x2b = rot("x2b", [128, D])
                        tcount = [0]
                        puT = puTs[l]
                        pvv = pv[l]
                        for g0 in range(0, 0 if (stopA or stopB) else nt, TGT):
                            s1 = contextlib.ExitStack()
                            cur[0] = s1
                            h2 = rot("h2", [128, D], n=1)
                            xnf = rot("xnf", [128, D], n=1)
                            h2b = rot("h2b", [128, D], BF16, n=1)
                            qTs = rot("qTs", [128, 16, 128], BF16, n=1)
                            sc_s = rot("scs", [128, 16, 128], n=1)
                            sc_r = rot("scr", [128, 16, 128], n=1)
                            v12 = rot("v12", [128, 16, 16], n=1)
                            i12 = rot("i12", [128, 16, 16], U32, n=1)
                            i12f = rot("i12f", [128, 16, 16], n=1)
                            cand = rot("cand", [128, 8, 256], n=1)
                            top = rot("top", [128, 8, 16], n=1)
                            pos = rot("pos", [128, 8, 16], U32, n=1)
                            posr = rot("posr", [128, 128], U32, n=1)
                            posc = rot("posc", [128, 128], U32, n=1)
                            posrf = rot("posrf", [128, 128], n=1)
                            poscf = rot("poscf", [128, 128], n=1)
                            ijg = rot("ijg", [128, 3, 128], n=1)
                            gsm = rot("gsm", [128, 16], n=1)
                            for tl in range(TGT):
                                ti = g0 + tl
                                gt = tile0 + ti
                                X = nxt(xts)
                                ld(X.k, X.t[:], xsc[l, 0, gt * 128:(gt + 1) * 128, :], w=[X])
                                S = ln_stats(X.t[:], [X])
                                XN = xnf[0]
                                act(lambda a: a.activation(out=XN.t[:], in_=X.t[:], func=AF.Identity, scale=S.t[:, 4:5], bias=S.t[:, 5:6]),
                                    r=[X, S], w=[XN])
                                H2 = h2[0]
                                dve(lambda v: v.tensor_tensor(out=H2.t[:], in0=XN.t[:], in1=A2.t[:], op=ALU.mult), r=[XN, A2], w=[H2])
                                H2B = h2b[0]
                                dve(lambda v: v.tensor_tensor(out=H2B.t[:], in0=H2.t[:], in1=B2.t[:], op=ALU.add), r=[H2, B2], w=[H2B])
                                b = tcount[0] % 2
                                tcount[0] += 1
                                for c in range(8):
                                    pe(lambda t: t.transpose(out=psB[:, b, c * 128:(c + 1) * 128], in_=H2B.t[:, c * 128:(c + 1) * 128],
                                                             identity=ident.t[:]), r=[H2B, ident], w=[PBK[b]], inc=(c == 7))
                                act(lambda a: a.copy(out=h2Tg.t[:, :, tl * 128:(tl + 1) * 128], in_=psB[:, b, :].rearrange("p (a c) -> p a c", a=8)),
                                    r=[PBK[b]], w=["h2T.%d" % tl])
                                QS = qTs[0]
                                for c in range(16):
                                    bk = c // 4
                                    for kc in range(8):
                                        pe(lambda t: t.matmul(psF[:, bk, (c % 4) * 128:(c % 4 + 1) * 128], lhsT=wq.t[:, kc, c * 128:(c + 1) * 128],
                                                              rhs=h2Tg.t[:, kc, tl * 128:(tl + 1) * 128], start=(kc == 0), stop=(kc == 7)),
                                           r=["h2T.%d" % tl, wq], w=[PF[bk]], inc=(kc == 7))
                                for bk in range(4):
                                    act(lambda a: a.copy(out=QS.t[:, bk * 4:(bk + 1) * 4, :].rearrange("p a c -> p (a c)"), in_=psF[:, bk, :]),
                                        r=[PF[bk]], w=[QS])
                                SS_ = sc_s[0]
                                for c in range(16):
                                    bk = c // 4
                                    pe(lambda t: t.matmul(psF[:, bk, (c % 4) * 128:(c % 4 + 1) * 128], lhsT=QS.t[:, c, :], rhs=kT.t[:, c // 8, :],
                                                          start=True, stop=True), r=[QS, kT], w=[PF[bk]], inc=(c % 4 == 3))
                                for bk in range(4):
                                    act(lambda a: a.copy(out=SS_.t[:, bk * 4:(bk + 1) * 4, :].rearrange("p a c -> p (a c)"), in_=psF[:, bk, :]),
                                        r=[PF[bk]], w=[SS_])
                                SR = sc_r[0]
                                V12 = v12[0]
                                I12 = i12[0]
                                for c in range(16):
                                    dve(lambda v: v.max(out=V12.t[:, c, 0:8], in_=SS_.t[:, c, :]), r=[SS_], w=[V12])
                                    dve(lambda v: v.max_index(out=I12.t[:, c, 0:8], in_max=V12.t[:, c, 0:8], in_values=SS_.t[:, c, :]),
                                        r=[SS_, V12], w=[I12])
                                    dve(lambda v: v.match_replace(out=SR.t[:, c, :], in_to_replace=V12.t[:, c, 0:8], in_values=SS_.t[:, c, :],
                                                                  imm_value=-1e30), r=[SS_, V12], w=[SR])
                                    dve(lambda v: v.max(out=V12.t[:, c, 8:16], in_=SR.t[:, c, :]), r=[SR], w=[V12])
                                    dve(lambda v: v.max_index(out=I12.t[:, c, 8:16], in_max=V12.t[:, c, 8:16], in_values=SR.t[:, c, :]),
                                        r=[SR, V12], w=[I12])
                                I12F = i12f[0]
                                dve(lambda v: v.tensor_copy(out=I12F.t[:], in_=I12.t[:]), r=[I12], w=[I12F])
                                CD = cand[0]
                                for hd in range(8):
                                    dve(lambda v: v.tensor_tensor(out=CD.t[:, hd, :].rearrange("p (r c) -> p r c", c=16),
                                                                  in0=fdims(V12.t[:, hd, :], [[1, 16], [0, 16]]),
                                                                  in1=fdims(V12.t[:, 8 + hd, :], [[0, 16], [1, 16]]), op=ALU.add),
                                        r=[V12], w=[CD])
                                CD2 = Tile(SR.t[:].rearrange("p a c -> p (a c)").rearrange("p (h c) -> p h c", h=8), SR.k)
                                TP = top[0]
                                PS_ = pos[0]
                                for hd in range(8):
                                    dve(lambda v: v.max(out=TP.t[:, hd, 0:8], in_=CD.t[:, hd, :]), r=[CD], w=[TP])
                                    dve(lambda v: v.max_index(out=PS_.t[:, hd, 0:8], in_max=TP.t[:, hd, 0:8], in_values=CD.t[:, hd, :]),
                                        r=[CD, TP], w=[PS_])
                                    dve(lambda v: v.match_replace(out=CD2.t[:, hd, :], in_to_replace=TP.t[:, hd, 0:8], in_values=CD.t[:, hd, :],
                                                                  imm_value=-1e30), r=[CD, TP], w=[CD2])
                                    dve(lambda v: v.max(out=TP.t[:, hd, 8:16], in_=CD2.t[:, hd, :]), r=[CD2], w=[TP])
                                    dve(lambda v: v.max_index(out=PS_.t[:, hd, 8:16], in_max=TP.t[:, hd, 8:16], in_values=CD2.t[:, hd, :]),
                                        r=[CD2, TP], w=[PS_])
                                PR = posr[0]
                                PC = posc[0]
                                pflat = PS_.t[:].rearrange("p h k -> p (h k)")
                                dve(lambda v: v.tensor_single_scalar(out=PR.t[:], in_=pflat, scalar=4, op=ALU.logical_shift_right), r=[PS_], w=[PR])
                                dve(lambda v: v.tensor_single_scalar(out=PC.t[:], in_=pflat, scalar=15, op=ALU.bitwise_and), r=[PS_], w=[PC])
                                PRF = posrf[0]
                                PCF = poscf[0]
                                dve(lambda v: v.tensor_copy(out=PRF.t[:], in_=PR.t[:]), r=[PR], w=[PRF])
                                dve(lambda v: v.tensor_copy(out=PCF.t[:], in_=PC.t[:]), r=[PC], w=[PCF])
                                OH = Tile(SS_.t[:].rearrange("p a c -> p (a c)").rearrange("p (k r) -> p k r", r=16), SS_.k)
                                IJ = ijg[0]
                                for (PF_, slot, half) in ((PRF, 0, 0), (PCF, 1, 1)):
                                    dve(lambda v: v.tensor_tensor(out=OH.t[:], in0=fdims(PF_.t[:], [[1, 128], [0, 16]]),
                                                                  in1=fdims(iota16.t[:], [[0, 128], [1, 16]]), op=ALU.is_equal),
                                        r=[PF_, iota16], w=[OH])
                                    for hd in range(8):
                                        dve(lambda v: v.tensor_tensor(out=OH.t[:, hd * 16:(hd + 1) * 16, :], in0=OH.t[:, hd * 16:(hd + 1) * 16, :],
                                                                      in1=fdims(I12F.t[:, half * 8 + hd, :], [[0, 16], [1, 16]]), op=ALU.mult),
                                            r=[OH, I12F], w=[OH])
                                    dve(lambda v: v.tensor_reduce(out=IJ.t[:, slot, :], in_=OH.t[:], axis=AX.X, op=ALU.add), r=[OH], w=[IJ])
                                GS = gsm[0]
                                gwv = IJ.t[:, 2, :].rearrange("p (h k) -> p h k", k=16)
                                dve(lambda v: v.tensor_tensor(out=gwv, in0=TP.t[:], in1=fdims(TP.t[:, :, 0], [[16, 8], [0, 16]]), op=ALU.subtract),
                                    r=[TP], w=[IJ])
                                act(lambda a: a.activation(out=IJ.t[:, 2, :], in_=IJ.t[:, 2, :], func=AF.Exp), r=[IJ], w=[IJ])
                                dve(lambda v: v.tensor_reduce(out=GS.t[:, 0:8], in_=gwv, axis=AX.X, op=ALU.add), r=[IJ], w=[GS])
                                dve(lambda v: v.reciprocal(out=GS.t[:, 8:16], in_=GS.t[:, 0:8]), r=[GS], w=[GS])
                                dve(lambda v: v.tensor_tensor(out=gwv, in0=gwv, in1=fdims(GS.t[:, 8:16], [[1, 8], [0, 16]]), op=ALU.mult),
                                    r=[IJ, GS], w=[IJ])
                                for c in range(3):
                                    pe(lambda t: t.transpose(out=psF[:, 4, c * 128:(c + 1) * 128], in_=IJ.t[:, c, :], identity=identF.t[:]),
                                       r=[IJ, identF], w=[PF[4]], inc=(c == 2))
                                act(lambda a: a.copy(out=IJG.t[:, :, tl * 128:(tl + 1) * 128], in_=psF[:, 4, 0:384].rearrange("p (a c) -> p a c", a=3)),
                                    r=[PF[4]], w=["IJG.%d" % tl])
                            s1.close()
                            fw.barrier()
                            s2 = contextlib.ExitStack()
                            cur[0] = s2
                            GT = sb(s2, "GT", [128, TG, 128], BF16)
                            wtb = rot("wtb", [128, TG], BF16, n=3)
                            ub = rot("ub", [128, 8, NI * 128], BF16, n=2)
                            vb = rot("vb", [128, NI, D], BF16, n=2)
                            NS = 16
                            r1s = rot("r1s", [128, NS, 128], BF16, n=2)
                            r2s = rot("r2s", [128, NS, 128], BF16, n=2)
                            atb = rot("atb", [128, TG], BF16, n=2)

                            first_touch = not cast_done[l]
                            cast_done[l] = True

                            def load_uv(ig):
                                U = nxt(ub)
                                V = nxt(vb)
                                if first_touch:
                                    ld(U.k, U.t[:], puT[:, ig * NI * 128:(ig + 1) * NI * 128].rearrange("(c p) e -> p c e", p=128), w=[U], q='pool')
                                    ld(V.k, V.t[:], pvv[ig * NI * 128:(ig + 1) * NI * 128, :].rearrange("(a p) d -> p a d", p=128), w=[V], q='pool')
                                    ld(U.k + "s", ubf[l, ig], U.t[:].rearrange("p c e -> p (c e)"), r=[U])
                                    ld(V.k + "s", vbf[l, ig], V.t[:].rearrange("p a d -> p (a d)"), r=[V])
                                else:
                                    ld(U.k, U.t[:].rearrange("p c e -> p (c e)"), ubf[l, ig], w=[U])
                                    ld(V.k, V.t[:].rearrange("p a d -> p (a d)"), vbf[l, ig], w=[V])
                                return U, V
                            pre = [load_uv(0), load_uv(1)]
                            gb = 0
                            for sbk in range(TG // NS):
                                t0 = sbk * NS
                                R1 = nxt(r1s)
                                R2 = nxt(r2s)
                                dve(lambda v: v.tensor_tensor(out=R2.t[:], in0=fdims(iota128.t[:], [[0, NS], [1, 128]]),
                                                              in1=fdims(IJG.t[:, 1, t0:t0 + NS], [[1, NS], [0, 128]]), op=ALU.is_equal),
                                    r=[iota128, "IJG.%d" % (t0 // 128)], w=[R2])
                                dve(lambda v: v.tensor_tensor(out=R1.t[:], in0=fdims(iota128.t[:], [[0, NS], [1, 128]]),
                                                              in1=fdims(IJG.t[:, 0, t0:t0 + NS], [[1, NS], [0, 128]]), op=ALU.is_equal),
                                    r=[iota128, "IJG.%d" % (t0 // 128)], w=[R1])
                                pool(lambda p: p.tensor_tensor(out=R1.t[:], in0=R1.t[:], in1=fdims(IJG.t[:, 2, t0:t0 + NS], [[1, NS], [0, 128]]),
                                                               op=ALU.mult), r=[R1, "IJG.%d" % (t0 // 128)], w=[R1])
                                for q4 in range(NS // 4):
                                    bk = gb % 4
                                    gb += 1
                                    for tt in range(4):
                                        t = q4 * 4 + tt
                                        pe(lambda te: te.matmul(psF[:, bk, tt * 128:(tt + 1) * 128], lhsT=R2.t[:, t, :], rhs=R1.t[:, t, :],
                                                                start=True, stop=True), r=[R1, R2], w=[PF[bk]], inc=(tt == 3))
                                    tb = t0 + q4 * 4
                                    act(lambda a: a.copy(out=GT.t[:, tb:tb + 4, :], in_=psF[:, bk, :].rearrange("p (t i) -> p t i", t=4)),
                                        r=[PF[bk]], w=[GT])
                            def vside(i, V, ii, WT):
                                for tt in range(TGT):
                                    for hf in range(2):
                                        pe(lambda te: te.matmul(psF[:, tt * 2 + hf, :], lhsT=WT.t[:, tt * 128:(tt + 1) * 128],
                                                                rhs=V.t[:, ii, hf * 512:(hf + 1) * 512], start=(i == 0), stop=(i == 127)),
                                           r=[WT, V], w=[PF[tt * 2 + hf]], inc=(i == 127 or (tt == TGT - 1 and hf == 1)))
                            prev = None
                            for ig in range(128 // NI):
                                U, V = pre[ig] if ig < 2 else load_uv(ig)
                                for ii in range(NI):
                                    i = ig * NI + ii
                                    ab = 4 + (i % 2)
                                    for kc in range(8):
                                        pe(lambda te: te.matmul(psF[:, ab, 0:TG], lhsT=U.t[:, kc, ii * 128:(ii + 1) * 128], rhs=h2Tg.t[:, kc, :],
                                                                start=(kc == 0), stop=(kc == 7)),
                                           r=[U, "h2T.0", "h2T.1"], w=[PF[ab]], inc=(kc == 7))
                                    AT = nxt(atb)
                                    act(lambda a: a.activation(out=AT.t[:], in_=psF[:, ab, 0:TG], func=AF.Gelu_apprx_tanh), r=[PF[ab]], w=[AT])
                                    WT = nxt(wtb)
                                    dve(lambda v: v.tensor_tensor(out=WT.t[:], in0=GT.t[:, :, i], in1=AT.t[:], op=ALU.mult),
                                        r=[AT, GT], w=[WT])
                                    if prev is not None:
                                        vside(*prev)
                                    prev = (i, V, ii, WT)
                            vside(*prev)
                            for tl in range(TGT):
                                ti = g0 + tl
                                gt = tile0 + ti
                                X = nxt(xts)
                                ld(X.k, X.t[:], xsc[l, 0, gt * 128:(gt + 1) * 128, :], w=[X])
                                Y = nxt(ybuf)
                                dve(lambda v: v.tensor_tensor(out=Y.t[:].rearrange("p (a c) -> p a c", a=2), in0=psF[:, tl * 2:tl * 2 + 2, :],
                                                              in1=G2.t[:].rearrange("p (a c) -> p a c", a=2), op=ALU.mult),
                                    r=[PF[tl * 2], PF[tl * 2 + 1], G2], w=[Y])
                                dve(lambda v: v.scalar_tensor_tensor(out=Y.t[:], in0=X.t[:], scalar=ALPHA, in1=Y.t[:], op0=ALU.mult, op1=ALU.add),
                                    r=[X, Y], w=[Y])
                                S = ln_stats(Y.t[:], [Y])
                                X2 = nxt(x2b)
                                act(lambda a: a.activation(out=X2.t[:], in_=Y.t[:], func=AF.Identity, scale=S.t[:, 4:5], bias=S.t[:, 5:6]),
                                    r=[Y, S], w=[X2])
                                dve(lambda v: v.tensor_tensor(out=X2.t[:], in0=X2.t[:], in1=lnb.t[:, 0, :], op=ALU.mult), r=[X2, lnb], w=[X2])
                                dve(lambda v: v.tensor_tensor(out=X2.t[:], in0=X2.t[:], in1=lnb.t[:, 1, :], op=ALU.add), r=[X2, lnb], w=[X2])
                                if l == NL - 1:
                                    if gt < 8:
                                        dst = yp[gt * 128:(gt + 1) * 128, :]
                                    else:
                                        dst = ys[(gt - 8) * 128:(gt - 7) * 128, :]
                                    ld(X2.k + "st", dst, X2.t[:], r=[X2])
                                    if dbg:
                                        ld(X2.k + "st", xsc[l, 1, gt * 128:(gt + 1) * 128, :], X2.t[:], r=[X2], w=["xsc1.%d" % gt])
                                else:
                                    ld(X2.k + "st", xsc[l, 1, gt * 128:(gt + 1) * 128, :], X2.t[:], r=[X2], w=["xsc1.%d" % gt])
                            s2.close()
                            fw.barrier()
                    fw.barrier()
        except _Stop:
            pass
        fw.finish()
        build_nc.ninst = fw.ninst
        build_nc.marks = fw.marks
    return nc


def _rope_tables():
    t = np.arange(SS)
    row = (t // 64).astype(np.float32)
    col = (t % 64).astype(np.float32)

    def tab(m, nh):
        freqs = (10000.0 ** (-np.arange(m, dtype=np.float32) / m)).astype(np.float32)
        ar = row[:, None] * freqs[None, :]
        ac = col[:, None] * freqs[None, :]
        cr, sr, cc, sn = np.cos(ar), np.sin(ar), np.cos(ac), np.sin(ac)
        C = np.concatenate([cr, cr, cc, cc], axis=1)
        S = np.concatenate([-sr, sr, -sn, sn], axis=1)
        return (np.tile(C, (1, nh)).astype(np.float32), np.tile(S, (1, nh)).astype(np.float32))
    C16, S16 = tab(16, 10)
    C8, S8 = tab(8, 5)
    return C16, S16, C8, S8


def make_in_maps(inp):
    f = lambda a: np.ascontiguousarray(np.asarray(a, dtype=np.float32))
    C16, S16, C8, S8 = _rope_tables()
    w_mod = f(inp["w_mod"])
    b_mod = f(inp["b_mod"])
    b_modT = f(b_mod.reshape(NL, 48, 128).transpose(0, 2, 1))
    gqk = f(np.concatenate([np.tile(inp["attn_q_norm"], (1, 8)), np.tile(inp["attn_k_norm"], (1, 2))], axis=1))
    wsT = f(np.asarray(inp["gmlp_ws"]).transpose(0, 3, 1, 2))
    bsT = f(np.asarray(inp["gmlp_b"]).transpose(0, 2, 1))
    lnp = f(np.stack([inp["ln1_g"], inp["ln1_b"], inp["ln2_g"], inp["ln2_b"]], axis=1))
    pwq = f(np.asarray(inp["peer_wq"]).reshape(NL, D, 8, 2, 128).transpose(0, 1, 3, 2, 4).reshape(NL, D, 2048))
    k12T = f(np.stack([np.asarray(inp["peer_k1"]).transpose(0, 2, 1), np.asarray(inp["peer_k2"]).transpose(0, 2, 1)], axis=2))
    shared = {
        "w_mod": w_mod, "b_modT": b_modT, "b_mod": b_mod, "w_in": f(inp["w_in"]), "gqk": gqk,
        "mqn": f(inp["mla_q_norm"]), "mkvn": f(inp["mla_kv_norm"]), "w_uq": f(inp["w_uq"]), "w_ukv": f(inp["w_ukv"]),
        "wsT": wsT, "bsT": bsT, "w_o": f(inp["w_o"]), "lnp": lnp, "peer_wq": pwq, "k12T": k12T,
        "puT0": f(np.asarray(inp["peer_u"][0]).T), "puT1": f(np.asarray(inp["peer_u"][1]).T), "pv0": f(inp["peer_v"][0]), "pv1": f(inp["peer_v"][1]),
        "ropeC": C16, "ropeS": S16, "ropeMC": C8, "ropeMS": S8,
    }
    maps = []
    xpr = np.asarray(inp["x_prompt"], dtype=np.float32)
    xsa = np.asarray(inp["x_sample"], dtype=np.float32)
    cctx = np.asarray(inp["c_ctx"], dtype=np.float32)
    for c in range(NCORES):
        m = dict(shared)
        m["xp"] = f(xpr[PB * c:PB * (c + 1)].reshape(PB * PS, D))
        m["xs"] = f(xsa[c])
        m["ck"] = f(np.asarray(inp["cache_attn_k"])[c].reshape(NL, PAST, 128))
        m["cv"] = f(np.asarray(inp["cache_attn_v"])[c].reshape(NL, PAST, 128))
        m["cckv"] = f(np.asarray(inp["cache_mla_ckv"])[c])
        m["ckr"] = f(np.asarray(inp["cache_mla_krope"])[c])
        cc = np.stack([cctx, np.asarray(inp["c"], dtype=np.float32)[c]], axis=-1)
        m["cT"] = f(cc.reshape(8, 128, 2).transpose(1, 0, 2))
        maps.append(m)
    return maps


_NC_CACHE = {}


def kernel(**inputs):
    if "nc" not in _NC_CACHE:
        _NC_CACHE["nc"] = build_nc(False)
    nc = _NC_CACHE["nc"]
    maps = make_in_maps(inputs)
    res = run_bass_kernel_spmd(nc, maps, core_ids=list(range(NCORES)))
    R = res.results
    y_prompt = np.concatenate([R[c]["yp"].reshape(PB, PS, D) for c in range(NCORES)], axis=0)
    y_sample = np.stack([R[c]["ys"] for c in range(NCORES)], axis=0)
    nk = np.concatenate([R[c]["nk"].reshape(PB, NL, PS, 2, 64) for c in range(NCORES)], axis=0)
    nv = np.concatenate([R[c]["nv"].reshape(PB, NL, PS, 2, 64) for c in range(NCORES)], axis=0)
    nckv = np.concatenate([R[c]["nckv"] for c in range(NCORES)], axis=0)
    nkr = np.concatenate([R[c]["nkr"] for c in range(NCORES)], axis=0)
    return (y_prompt.astype(np.float32), y_sample.astype(np.float32), nk.astype(np.float32), nv.astype(np.float32),
            nckv.astype(np.float32), nkr.astype(np.float32))
```

```python
import contextlib
import math
import os

import numpy as np
import concourse.bass as bass
import concourse.mybir as mybir
from concourse.bass_utils import run_bass_kernel_spmd

F32 = mybir.dt.float32
BF16 = mybir.dt.bfloat16
I32 = mybir.dt.int32
U32 = mybir.dt.uint32
AF = mybir.ActivationFunctionType
ALU = mybir.AluOpType
AX = mybir.AxisListType

D = 1024
NL = 2
IN_W = 1696
EPS = 1e-6
ALPHA = (2.0 * NL) ** 0.25
NCORES = 8
PB = 4
PS = 256
SS = 2048
PAST = 256
NTILES = (PB * PS + SS) // 128


class _Stop(Exception):
    pass


class Tile:
    def __init__(self, t, k):
        self.t = t
        self.k = k


def _keys(lst):
    out = []
    for x in lst:
        out.append(x.k if isinstance(x, Tile) else x)
    return out


class FW:
    def __init__(self, nc, es):
        self.nc = nc
        self.es = es
        self.eng = {'pe': nc.tensor, 'act': nc.scalar, 'dve': nc.vector, 'pool': nc.gpsimd, 'sp': nc.sync}
        self.sem = {}
        self.cnt = {}
        self.semobj = {}
        for e in self.eng:
            self.sem[e] = es.enter_context(nc.semaphore("sem_" + e))
            self.cnt[e] = 0
            self.semobj["sem_" + e] = self.sem[e]
        self.dsem = {}
        self.dall = []
        self.dfree = {'hw': [], 'sw': []}
        self.waited = {e: {} for e in self.eng}
        self.lastw = {}
        self.readers = {}
        self.ninst = 0
        self.nops = 0
        self.limit = int(os.environ["MK_LIMIT"]) if os.environ.get("MK_LIMIT") else None
        self.marks = []

    def mark(self, name):
        self.marks.append((name, self.nops))

    def _wait(self, e, ev):
        if ev is None:
            return
        name, val = ev
        if e == 'pe' and name == 'sem_pe':
            return
        w = self.waited[e]
        if w.get(name, 0) >= val:
            return
        w[name] = val
        self.eng[e].wait_ge(self.semobj[name], val)
        self.ninst += 1

    def _deps(self, e, reads, writes):
        for k in reads:
            self._wait(e, self.lastw.get(k))
        for k in writes:
            self._wait(e, self.lastw.get(k))
            for ev in self.readers.get(k, {}).values():
                self._wait(e, ev)

    def _record(self, ev, reads, writes):
        for k in reads:
            self.readers.setdefault(k, {})[ev[0]] = ev
        for k in writes:
            self.lastw[k] = ev
            self.readers[k] = {}

    def op(self, e, fn, r=(), w=(), inc=True):
        self.nops += 1
        if self.limit is not None and self.nops > self.limit:
            return None
        reads = _keys(r)
        writes = _keys(w)
        if e != 'pe':
            ps = [k for k in reads if k.startswith("ps")]
            if ps:
                reads = [k for k in reads if not k.startswith("ps")]
                writes = list(writes) + ps
        self._deps(e, reads, writes)
        inst = fn(self.eng[e])
        self.ninst += 1
        name = "sem_" + e
        if inc:
            self.cnt[e] += 1
            inst.then_inc(self.sem[e], 1)
            ev = (name, self.cnt[e])
        else:
            ev = (name, self.cnt[e] + 1)
        self._record(ev, reads, writes)
        return inst

    def dma(self, q, key, fn, r=(), w=()):
        self.nops += 1
        if self.limit is not None and self.nops > self.limit:
            return None
        reads = _keys(r)
        writes = _keys(w)
        self._deps(q, reads, writes)
        cls = 'sw' if q == 'pool' else 'hw'
        key = cls + ":" + key
        if key not in self.dsem:
            if self.dfree[cls]:
                d = self.dfree[cls].pop()
            else:
                nm = "dsem_%d" % len(self.dall)
                sm = self.es.enter_context(self.nc.semaphore(nm))
                self.semobj[nm] = sm
                d = [sm, 0, nm, cls]
                self.dall.append(d)
            self.dsem[key] = d
        d = self.dsem[key]
        inst = fn(self.eng[q])
        self.ninst += 1
        d[1] += 16
        inst.then_inc(d[0], 16)
        ev = (d[2], d[1])
        self._record(ev, reads, writes)
        return inst

    def _all_events(self):
        evs = [("sem_" + e, self.cnt[e]) for e in self.eng if self.cnt[e] > 0]
        evs += [(d[2], d[1]) for d in self.dall if d[1] > 0]
        return evs

    def barrier(self):
        evs = self._all_events()
        for e in self.eng:
            for ev in evs:
                self._wait(e, ev)
        self.lastw = {}
        self.readers = {}
        for d in self.dsem.values():
            self.dfree[d[3]].append(d)
        self.dsem = {}

    def finish(self):
        for ev in self._all_events():
            self._wait('sp', ev)


def mkap(ap, dims, off=0):
    return bass.AP(ap.tensor, ap.offset + off, [list(x) for x in dims])


def fdims(ap, dims, off=0):
    base = list(ap.ap)
    return bass.AP(ap.tensor, ap.offset + off, [list(base[0])] + [list(x) for x in dims])


def build_nc(dbg=False):
    nc = bass.Bass("TRN2", target_bir_lowering=False)

    def din(name, shape, dt=F32):
        return nc.dram_tensor(name, shape, dt, kind="ExternalInput").ap()

    def dout(name, shape, dt=F32):
        return nc.dram_tensor(name, shape, dt, kind="ExternalOutput").ap()

    xp = din("xp", [PB * PS, D])
    xs = din("xs", [SS, D])
    ck = din("ck", [NL, PAST, 128])
    cv = din("cv", [NL, PAST, 128])
    cckv = din("cckv", [NL, PAST, 128])
    ckr = din("ckr", [NL, PAST, 32])
    cT = din("cT", [128, 8, 2])
    w_mod = din("w_mod", [NL, D, 6 * D])
    b_modT = din("b_modT", [NL, 128, 48])
    b_mod = din("b_mod", [NL, 6 * D])
    w_in = din("w_in", [NL, D, IN_W])
    gqk = din("gqk", [NL, 640])
    mqn = din("mqn", [NL, 256])
    mkvn = din("mkvn", [NL, 128])
    w_uq = din("w_uq", [NL, 256, 384])
    w_ukv = din("w_ukv", [NL, 128, 512])
    wsT = din("wsT", [NL, 128, 4, 128])
    bsT = din("bsT", [NL, 128, 4])
    w_o = din("w_o", [NL, D, D])
    lnp = din("lnp", [NL, 4, D])
    peer_wq = din("peer_wq", [NL, D, 2048])
    k12T = din("k12T", [NL, 128, 2, 128])
    puTs = [din("puT%d" % l, [D, 16384]) for l in range(NL)]
    pv = [din("pv%d" % l, [16384, D]) for l in range(NL)]
    ropeC = din("ropeC", [SS, 640])
    ropeS = din("ropeS", [SS, 640])
    ropeMC = din("ropeMC", [SS, 160])
    ropeMS = din("ropeMS", [SS, 160])

    yp = dout("yp", [PB * PS, D])
    ys = dout("ys", [SS, D])
    nk = dout("nk", [PB, NL, PS, 128])
    nv = dout("nv", [PB, NL, PS, 128])
    nckv = dout("nckv", [PB, NL, PS, 128])
    nkr = dout("nkr", [PB, NL, PS, 32])
    if dbg:
        xsc = dout("xsc", [NL, 2, NTILES * 128, D])
    else:
        xsc = nc.dram_tensor("xsc", [NL, 2, NTILES * 128, D], F32, kind="Internal").ap()
    gsc = nc.dram_tensor("gsc", [NL, 2, 4, 128, D], F32, kind="Internal").ap()
    NI = 2
    NIG = 128 // NI
    ubf = nc.dram_tensor("ubf", [NL, NIG, 128, 8 * NI * 128], BF16, kind="Internal").ap()
    vbf = nc.dram_tensor("vbf", [NL, NIG, 128, NI * D], BF16, kind="Internal").ap()
    cast_done = [False] * NL

    uid = [0]

    with contextlib.ExitStack() as es:
        fw = FW(nc, es)

        def sb(stack, name, shape, dt=F32):
            uid[0] += 1
            nm = "%s_%d" % (name, uid[0])
            return Tile(stack.enter_context(nc.sbuf_tensor(nm, shape, dt)), nm)

        def dve(fn, r=(), w=(), inc=True):
            return fw.op('dve', fn, r, w, inc)

        def act(fn, r=(), w=(), inc=True):
            return fw.op('act', fn, r, w, inc)

        def pe(fn, r=(), w=(), inc=True):
            return fw.op('pe', fn, r, w, inc)

        def pool(fn, r=(), w=(), inc=True):
            return fw.op('pool', fn, r, w, inc)

        def ld(key, out_ap, in_ap, r=(), w=(), q='sp'):
            return fw.dma(q, key, lambda e: e.dma_start(out=out_ap, in_=in_ap), r, w)

        psF = es.enter_context(nc.psum_tensor("psF", [128, 6, 512], F32))
        psB = es.enter_context(nc.psum_tensor("psB", [128, 2, 1024], BF16))
        PF = ["psF%d" % i for i in range(6)]
        PBK = ["psB0", "psB1"]

        ident = sb(es, "ident", [128, 128], BF16)
        pool(lambda p: p.memset(ident.t[:], 0.0), w=[ident])
        pool(lambda p: p.affine_select(out=ident.t[:], in_=ident.t[:], pattern=[[-1, 128]],
                                       compare_op=ALU.not_equal, fill=1.0, base=0, channel_multiplier=1),
             r=[ident], w=[ident])
        epsb = sb(es, "epsb", [128, 1])
        dve(lambda v: v.memset(epsb.t[:], EPS), w=[epsb])
        iota16 = sb(es, "iota16", [128, 16])
        iota16i = sb(es, "iota16i", [128, 16], I32)
        pool(lambda p: p.iota(iota16i.t[:], pattern=[[1, 16]], base=0, channel_multiplier=0), w=[iota16i])
        dve(lambda v: v.tensor_copy(out=iota16.t[:], in_=iota16i.t[:]), r=[iota16i], w=[iota16])
        identF = sb(es, "identF", [128, 128])
        pool(lambda p: p.memset(identF.t[:], 0.0), w=[identF])
        pool(lambda p: p.affine_select(out=identF.t[:], in_=identF.t[:], pattern=[[-1, 128]],
                                       compare_op=ALU.not_equal, fill=1.0, base=0, channel_multiplier=1),
             r=[identF], w=[identF])
        iota128 = sb(es, "iota128", [128, 128])
        iota128i = sb(es, "iota128i", [128, 128], I32)
        pool(lambda p: p.iota(iota128i.t[:], pattern=[[1, 128]], base=0, channel_multiplier=0), w=[iota128i])
        dve(lambda v: v.tensor_copy(out=iota128.t[:], in_=iota128i.t[:]), r=[iota128i], w=[iota128])
        iota128b = sb(es, "iota128b", [128, 128], BF16)
        dve(lambda v: v.tensor_copy(out=iota128b.t[:], in_=iota128i.t[:]), r=[iota128i], w=[iota128b])
        modcol = sb(es, "modcol", [128, NL, 48, 2])

        with contextlib.ExitStack() as ps0:
            cTs = sb(ps0, "cTs", [128, 8, 2])
            scT = sb(ps0, "scT", [128, 8, 2], BF16)
            scR = sb(ps0, "scR", [128, 8, 2, 128], BF16)
            ld("cTs", cTs.t[:], cT, w=[cTs])
            act(lambda a: a.activation(out=scT.t[:], in_=cTs.t[:], func=AF.Silu), r=[cTs], w=[scT])
            dve(lambda v: v.tensor_copy(out=scR.t[:].rearrange("p a g m -> p (a g) m"),
                                        in_=fdims(scT.t[:], [[1, 16], [0, 128]])), r=[scT], w=[scR])
            bmT = sb(ps0, "bmT", [128, NL, 48])
            for l in range(NL):
                ld("bmT", bmT.t[:, l, :], b_modT[l], w=[bmT])
            wm = [sb(ps0, "wm%d" % i, [128, 8, 512], BF16) for i in range(2)]
            bmb = [sb(ps0, "bmb%d" % i, [128, 512]) for i in range(2)]
            gst = [sb(ps0, "gst%d" % i, [128, 512]) for i in range(2)]
            it = 0
            for l in range(NL):
                for ci in range(12):
                    role = ci // 2
                    W = wm[it % 2]
                    ld(W.k, W.t[:], w_mod[l, :, ci * 512:(ci + 1) * 512].rearrange("(c p) n -> p c n", p=128),
                       w=[W], q='pool')
                    if role in (0, 1, 3, 4):
                        for b4 in range(4):
                            blk = ci * 4 + b4
                            for kc in range(8):
                                pe(lambda t: t.matmul(psF[:, 4, b4 * 2:b4 * 2 + 2], lhsT=W.t[:, kc, b4 * 128:(b4 + 1) * 128],
                                                      rhs=scT.t[:, kc, :], start=(kc == 0), stop=(kc == 7)),
                                   r=[W, scT], w=[PF[4]], inc=(kc == 7))
                            addc = 1.0 if role in (1, 4) else 0.0
                            dve(lambda v: v.scalar_tensor_tensor(out=modcol.t[:, l, blk, :], in0=psF[:, 4, b4 * 2:b4 * 2 + 2],
                                                                 scalar=addc, in1=fdims(bmT.t[:, l, blk:blk + 1], [[0, 2]]),
                                                                 op0=ALU.add, op1=ALU.add),
                                r=[PF[4], bmT], w=[modcol])
                    if role in (2, 3, 4, 5):
                        slot = {2: 0, 4: 1, 3: 2, 5: 3}[role]
                        Bb = bmb[it % 2]
                        ld(Bb.k, Bb.t[:], mkap(b_mod[l, ci * 512:(ci + 1) * 512], [[0, 128], [1, 512]]), w=[Bb])
                        for g in range(2):
                            pb = PF[2 + g]
                            for kc in range(8):
                                pe(lambda t: t.matmul(psF[:, 2 + g, :], lhsT=scR.t[:, kc, g, :], rhs=W.t[:, kc, :],
                                                      start=(kc == 0), stop=(kc == 7)),
                                   r=[W, scR], w=[pb], inc=(kc == 7))
                            G = gst[g]
                            addc = 1.0 if role == 4 else 0.0
                            dve(lambda v: v.scalar_tensor_tensor(out=G.t[:], in0=psF[:, 2 + g, :], scalar=addc, in1=Bb.t[:],
                                                                 op0=ALU.add, op1=ALU.add),
                                r=[pb, Bb], w=[G])
                            half = ci % 2
                            ld("gst%d" % g, gsc[l, g, slot, :, half * 512:(half + 1) * 512], G.t[:], r=[G], w=["gsc"])
                    it += 1
        fw.barrier()

        def chk(tag):
            if dbg and os.environ.get("MK_STOP") == tag:
                raise _Stop()

        seqs = [(2 * b, 2, 0, False, b) for b in range(PB)] + [(8, 16, 1, True, -1)]
        if dbg and os.environ.get("MK_SEQS"):
            seqs = [seqs[int(i)] for i in os.environ["MK_SEQS"].split(",")]
        nlayers = int(os.environ.get("MK_LAYERS", NL)) if dbg else NL
        skip_peer = bool(dbg and os.environ.get("MK_SKIP_PEER"))
        stopA = bool(dbg and os.environ.get('MK_STOP') == 'A')
        stopB = bool(dbg and os.environ.get('MK_STOP') == 'B')

        def x_src(l, ph, gt):
            if l < 0:
                if gt < 8:
                    return xp[gt * 128:(gt + 1) * 128, :]
                return xs[(gt - 8) * 128:(gt - 7) * 128, :]
            return xsc[l, ph, gt * 128:(gt + 1) * 128, :]

        try:
            chk('pro')
            for (tile0, nt, grp, is_s, pbi) in seqs:
                nkc = nt + (2 if is_s else 0)
                koff = 2 if is_s else 0
                for l in range(nlayers):
                    with contextlib.ExitStack() as sa:
                        NTOK = nt * 128
                        NKEY = nkc * 128
                        QT = sb(sa, "QT", [128, 4, NTOK], BF16)
                        KT = sb(sa, "KT", [128, NKEY], BF16)
                        VA = sb(sa, "VA", [128, nkc, 2, 65], BF16)
                        QMT = sb(sa, "QMT", [128, 4, NTOK], BF16)
                        KMT = sb(sa, "KMT", [128, 4, NKEY], BF16)
                        VM = sb(sa, "VM", [128, nkc, 4, 65], BF16)
                        OC = sb(sa, "OC", [128, nt, 256], BF16)
                        pool(lambda p: p.memset(VA.t[:], 1.0), w=[VA])
                        pool(lambda p: p.memset(VM.t[:], 1.0), w=[VM])

                        sA = contextlib.ExitStack()
                        cur = [sA]
                        gq = sb(sA, "gq", [128, 640])
                        ld(gq.k, gq.t[:], mkap(gqk[l], [[0, 128], [1, 640]]), w=[gq])
                        gmq = sb(sA, "gmq", [128, 256])
                        ld(gmq.k, gmq.t[:], mkap(mqn[l], [[0, 128], [1, 256]]), w=[gmq])
                        gmk = sb(sA, "gmk", [128, 128])
                        ld(gmk.k, gmk.t[:], mkap(mkvn[l], [[0, 128], [1, 128]]), w=[gmk])
                        bst = sb(sA, "bst", [128, 4])
                        ld(bst.k, bst.t[:], bsT[l], w=[bst])
                        win = sb(sA, "win", [128, 8, IN_W], BF16)
                        for kc in range(8):
                            ld(win.k, win.t[:, kc, :], w_in[l, kc * 128:(kc + 1) * 128, :], w=[win], q='pool')
                        wuq = sb(sA, "wuq", [128, 2, 384], BF16)
                        ld(wuq.k, wuq.t[:], w_uq[l].rearrange("(c p) n -> p c n", p=128), w=[wuq], q='pool')
                        wukv = sb(sA, "wukv", [128, 512], BF16)
                        ld(wukv.k, wukv.t[:], w_ukv[l], w=[wukv], q='pool')
                        wst = sb(sA, "wst", [128, 4, 128], BF16)
                        ld(wst.k, wst.t[:], wsT[l], w=[wst], q='pool')

                        def rot(name, shape, dt=F32, n=2):
                            return [sb(cur[0], name + str(i), shape, dt) for i in range(n)]
                        xts = rot("xt", [128, D])
                        xnb = rot("xnb", [128, D], BF16, n=1)
                        hT = rot("hT", [128, 8, 128], BF16, n=1)
                        pj = rot("pj", [128, IN_W], n=1)
                        sq = rot("sq", [128, 1152], n=1)
                        small = rot("small", [128, 64], n=3)
                        qkn = rot("qkn", [128, 640], n=1)
                        rc_t = rot("ropeC", [128, 640], n=1)
                        rs_t = rot("ropeS", [128, 640], n=1)
                        rmc_t = rot("ropeMC", [128, 160], n=1)
                        rms_t = rot("ropeMS", [128, 160], n=1)
                        tmp640 = rot("tmp640", [128, 640], n=1)
                        tmp640b = rot("tmp640b", [128, 640], n=1)
                        qkb = rot("qkb", [128, 640], BF16)
                        cqb = rot("cqb", [128, 256], BF16)
                        cqT = rot("cqT", [128, 2, 128], BF16)
                        ckvn = rot("ckvn", [128, 128])
                        ckb = rot("ckb", [128, 128], BF16)
                        ckT = rot("ckT", [128, 128], BF16)
                        qm = rot("qm", [128, 4, 96])
                        mr = rot("mr", [128, 160], n=1)
                        mr3 = rot("mr3", [128, 160], n=1)
                        qmb = rot("qmb", [128, 4, 96], BF16)
                        kmb = rot("kmb", [128, 4, 96], BF16)
                        vg = rot("vg", [128, 256], BF16)
                        vtmp = rot("vtmp", [128, 256])
                        cst = rot("cst", [128, 128])
                        cstb = rot("cstb", [128, 128], BF16)
                        krs = rot("krs", [128, 32])
                        rr = {}

                        def nxt(lst):
                            i = rr.get(id(lst), 0)
                            rr[id(lst)] = i + 1
                            return lst[i % len(lst)]

                        def ln_stats(xin, xkeys):
                            S = nxt(small)
                            dve(lambda v: v.bn_stats(out=S.t[:, 8:14], in_=xin[:, 0:512]), r=xkeys, w=[S])
                            dve(lambda v: v.bn_stats(out=S.t[:, 14:20], in_=xin[:, 512:1024]), r=xkeys, w=[S])
                            dve(lambda v: v.bn_aggr(out=S.t[:, 0:2], in_=S.t[:, 8:20]), r=[S], w=[S])
                            act(lambda a: a.activation(out=S.t[:, 2:3], in_=S.t[:, 1:2], func=AF.Sqrt, bias=epsb.t[:], scale=1.0),
                                r=[S, epsb], w=[S])
                            dve(lambda v: v.reciprocal(out=S.t[:, 4:5], in_=S.t[:, 2:3]), r=[S], w=[S])
                            dve(lambda v: v.scalar_tensor_tensor(out=S.t[:, 5:6], in0=S.t[:, 0:1], scalar=-1.0, in1=S.t[:, 4:5],
                                                                 op0=ALU.mult, op1=ALU.mult), r=[S], w=[S])
                            return S

                        tcount = [0]

                        def transposes(srcs, dst_fn, rkeys, wkeys, evac='act'):
                            b = tcount[0] % 2
                            tcount[0] += 1
                            for i, (src, n) in enumerate(srcs):
                                pe(lambda t: t.transpose(out=psB[0:n, b, i * 128:(i + 1) * 128], in_=src, identity=ident.t[:]),
                                   r=list(rkeys) + [ident], w=[PBK[b]], inc=(i == len(srcs) - 1))
                            dst_fn(b)

                        def rope(src, nh, hd, ctab, stab, coff, dsts, rkeys, wkeys):
                            n = nh * hd
                            q4 = hd // 4
                            T1 = nxt(tmp640)
                            T2 = nxt(tmp640b)
                            dve(lambda v: v.tensor_tensor(out=T1.t[:, 0:n], in0=src, in1=ctab.t[:, coff:coff + n], op=ALU.mult),
                                r=list(rkeys) + [ctab], w=[T1])
                            nb = n // (2 * q4)
                            def hv(ap, off, half):
                                return fdims(ap, [[2 * q4, nb], [1, q4]], off + half * q4)
                            dve(lambda v: v.tensor_tensor(out=hv(T2.t[:, 0:n], 0, 0), in0=hv(src, 0, 1), in1=hv(stab.t[:, 0:n], coff, 0),
                                                          op=ALU.mult), r=list(rkeys) + [stab], w=[T2])
                            dve(lambda v: v.tensor_tensor(out=hv(T2.t[:, 0:n], 0, 1), in0=hv(src, 0, 0), in1=hv(stab.t[:, 0:n], coff, 1),
                                                          op=ALU.mult), r=list(rkeys) + [stab], w=[T2])
                            for (oap, i0, i1, j0, j1) in dsts:
                                dve(lambda v: v.tensor_tensor(out=oap, in0=i0(T1.t), in1=i0(T2.t), op=ALU.add), r=[T1, T2], w=wkeys)

                        def mla_keys(ckb_ap, ckb_keys, kr_ap, kr_keys, kc):
                            CT = nxt(ckT)
                            def ev(b):
                                act(lambda a: a.copy(out=CT.t[:], in_=psB[:, b, 0:128]), r=[PBK[b]], w=[CT])
                            transposes([(ckb_ap, 128)], ev, ckb_keys, None)
                            pe(lambda t: t.matmul(psF[:, 5, :], lhsT=CT.t[:], rhs=wukv.t[:], start=True, stop=True),
                               r=[CT, wukv], w=[PF[5]])
                            KB = nxt(kmb)
                            kvv = psF[:, 5, :].rearrange("p (h c) -> p h c", h=4)
                            act(lambda a: a.copy(out=KB.t[:, :, 0:64], in_=kvv[:, :, 0:64]), r=[PF[5]], w=[KB])
                            dve(lambda v: v.tensor_copy(out=VM.t[:, kc, :, 0:64], in_=kvv[:, :, 64:128]), r=[PF[5], VM], w=["VM.%d" % kc])
                            dve(lambda v: v.tensor_copy(out=KB.t[:, :, 64:96], in_=fdims(kr_ap, [[0, 4], [1, 32]])),
                                r=list(kr_keys), w=[KB])
                            def ev2(b):
                                act(lambda a: a.copy(out=KMT.t[0:96, :, kc * 128:(kc + 1) * 128],
                                                     in_=psB[0:96, b, 0:512].rearrange("p (h c) -> p h c", h=4)),
                                    r=[PBK[b]], w=["KMT.%d" % kc])
                            transposes([(KB.t[:, h, :], 96) for h in range(4)], ev2, [KB], None)

                        fw.mark('A-start')
                        if is_s:
                            for j in range(2):
                                C1 = nxt(cst)
                                ld(C1.k, C1.t[:], ck[l, j * 128:(j + 1) * 128, :], w=[C1])
                                CB = nxt(cstb)
                                dve(lambda v: v.tensor_copy(out=CB.t[:], in_=C1.t[:]), r=[C1], w=[CB])
                                def evk(b):
                                    act(lambda a: a.copy(out=KT.t[:, j * 128:(j + 1) * 128], in_=psB[:, b, 0:128]),
                                        r=[PBK[b]], w=["KT.%d" % j])
                                transposes([(CB.t[:], 128)], evk, [CB], None)
                                C2 = nxt(cst)
                                ld(C2.k, C2.t[:], cv[l, j * 128:(j + 1) * 128, :], w=[C2])
                                dve(lambda v: v.tensor_copy(out=VA.t[:, j, :, 0:64], in_=C2.t[:].rearrange("p (h c) -> p h c", h=2)),
                                    r=[C2, VA], w=["VA.%d" % j])
                                C3 = nxt(cst)
                                ld(C3.k, C3.t[:], cckv[l, j * 128:(j + 1) * 128, :], w=[C3])
                                CB3 = nxt(cstb)
                                dve(lambda v: v.tensor_copy(out=CB3.t[:], in_=C3.t[:]), r=[C3], w=[CB3])
                                K4 = nxt(krs)
                                ld(K4.k, K4.t[:], ckr[l, j * 128:(j + 1) * 128, :], w=[K4])
                                mla_keys(CB3.t[:], [CB3], K4.t[:], [K4], j)

                        for ti in range(nt):
                            gt = tile0 + ti
                            kc_new = koff + ti
                            X = nxt(xts)
                            ld(X.k, X.t[:], x_src(l - 1, 1, gt), w=[X])
                            fw.mark('tile%d-ln' % ti)
                            S = ln_stats(X.t[:], [X])
                            XN = nxt(xnb)
                            act(lambda a: a.activation(out=XN.t[:], in_=X.t[:], func=AF.Identity, scale=S.t[:, 4:5], bias=S.t[:, 5:6]),
                                r=[X, S], w=[XN])
                            H = nxt(hT)
                            def evh(b):
                                for c in range(8):
                                    act(lambda a: a.activation(out=H.t[:, c, :], in_=psB[:, b, c * 128:(c + 1) * 128], func=AF.Identity,
                                                               scale=modcol.t[:, l, 8 + c, grp:grp + 1], bias=modcol.t[:, l, c, grp:grp + 1]),
                                        r=[PBK[b], modcol], w=[H])
                            transposes([(XN.t[:, c * 128:(c + 1) * 128], 128) for c in range(8)], evh, [XN], None)
                            fw.mark('proj')
                            segs = [(0, 512), (512, 512), (1024, 512), (1536, 160)]
                            for bi, (c0, cn) in enumerate(segs):
                                for kc in range(8):
                                    pe(lambda t: t.matmul(psF[:, bi, 0:cn], lhsT=H.t[:, kc, :], rhs=win.t[:, kc, c0:c0 + cn],
                                                          start=(kc == 0), stop=(kc == 7)), r=[H, win], w=[PF[bi]], inc=(kc == 7))
                            P = nxt(pj)
                            for bi, (c0, cn) in enumerate(segs):
                                act(lambda a: a.copy(out=P.t[:, c0:c0 + cn], in_=psF[:, bi, 0:cn]), r=[PF[bi]], w=[P])
                            fw.mark('rms')
                            SQ = nxt(sq)
                            dve(lambda v: v.tensor_tensor(out=SQ.t[:], in0=P.t[:, 0:1152], in1=P.t[:, 0:1152], op=ALU.mult), r=[P], w=[SQ])
                            S2 = nxt(small)
                            dve(lambda v: v.tensor_reduce(out=S2.t[:, 0:10], in_=SQ.t[:, 0:640].rearrange("p (h c) -> p h c", c=64),
                                                          axis=AX.X, op=ALU.add), r=[SQ], w=[S2])
                            dve(lambda v: v.tensor_reduce(out=S2.t[:, 10:11], in_=SQ.t[:, 768:1024], axis=AX.X, op=ALU.add), r=[SQ], w=[S2])
                            dve(lambda v: v.tensor_reduce(out=S2.t[:, 11:12], in_=SQ.t[:, 1024:1152], axis=AX.X, op=ALU.add), r=[SQ], w=[S2])
                            dve(lambda v: v.tensor_scalar(out=S2.t[:, 16:26], in0=S2.t[:, 0:10], scalar1=1.0 / 64, scalar2=EPS,
                                                          op0=ALU.mult, op1=ALU.add), r=[S2], w=[S2])
                            dve(lambda v: v.tensor_scalar(out=S2.t[:, 26:27], in0=S2.t[:, 10:11], scalar1=1.0 / 256, scalar2=EPS,
                                                          op0=ALU.mult, op1=ALU.add), r=[S2], w=[S2])
                            dve(lambda v: v.tensor_scalar(out=S2.t[:, 27:28], in0=S2.t[:, 11:12], scalar1=1.0 / 128, scalar2=EPS,
                                                          op0=ALU.mult, op1=ALU.add), r=[S2], w=[S2])
                            act(lambda a: a.activation(out=S2.t[:, 32:44], in_=S2.t[:, 16:28], func=AF.Sqrt), r=[S2], w=[S2])
                            dve(lambda v: v.reciprocal(out=S2.t[:, 48:60], in_=S2.t[:, 32:44]), r=[S2], w=[S2])
                            fw.mark('qk')
                            QK = nxt(qkn)
                            dve(lambda v: v.tensor_tensor(out=QK.t[:].rearrange("p (h c) -> p h c", c=64),
                                                          in0=P.t[:, 0:640].rearrange("p (h c) -> p h c", c=64),
                                                          in1=fdims(S2.t[:, 48:58], [[1, 10], [0, 64]]), op=ALU.mult), r=[P, S2], w=[QK])
                            dve(lambda v: v.tensor_tensor(out=QK.t[:], in0=QK.t[:], in1=gq.t[:], op=ALU.mult), r=[QK, gq], w=[QK])
                            QB = nxt(qkb)
                            def qdst(T, j):
                                return T[:, j * 256:(j + 1) * 256].rearrange("p (s c) -> p s c", c=64)
                            def qout(j):
                                return fdims(QB.t[:], [[128, 4], [1, 64]], j * 64)
                            if is_s:
                                RC = rc_t[0]
                                RS = rs_t[0]
                                ld(RC.k, RC.t[:], ropeC[ti * 128:(ti + 1) * 128, :], w=[RC])
                                ld(RS.k, RS.t[:], ropeS[ti * 128:(ti + 1) * 128, :], w=[RS])
                                dsts = [(qout(0), lambda T: qdst(T, 0), None, 0, 0), (qout(1), lambda T: qdst(T, 1), None, 0, 0),
                                        (QB.t[:, 512:640], lambda T: T[:, 512:640], None, 0, 0)]
                                rope(QK.t[:], 10, 64, RC, RS, 0, dsts, [QK], [QB])
                            else:
                                for j in range(2):
                                    dve(lambda v: v.tensor_copy(out=qout(j), in_=qdst(QK.t, j)), r=[QK], w=[QB])
                                dve(lambda v: v.tensor_copy(out=QB.t[:, 512:640], in_=QK.t[:, 512:640]), r=[QK], w=[QB])
                            def evq(b):
                                act(lambda a: a.copy(out=QT.t[:, :, ti * 128:(ti + 1) * 128],
                                                     in_=psB[:, b, 0:512].rearrange("p (s c) -> p s c", s=4)),
                                    r=[PBK[b]], w=["QT.%d" % ti])
                                act(lambda a: a.copy(out=KT.t[:, kc_new * 128:(kc_new + 1) * 128], in_=psB[:, b, 512:640]),
                                    r=[PBK[b]], w=["KT.%d" % kc_new])
                            transposes([(QB.t[:, i * 128:(i + 1) * 128], 128) for i in range(5)], evq, [QB], None)
                            fw.mark('va')
                            dve(lambda v: v.tensor_copy(out=VA.t[:, kc_new, :, 0:64], in_=P.t[:, 640:768].rearrange("p (h c) -> p h c", h=2)),
                                r=[P, VA], w=["VA.%d" % kc_new])
                            fw.mark('mlaq')
                            CQ = nxt(cqb)
                            dve(lambda v: v.scalar_tensor_tensor(out=CQ.t[:], in0=P.t[:, 768:1024], scalar=S2.t[:, 58:59], in1=gmq.t[:],
                                                                 op0=ALU.mult, op1=ALU.mult), r=[P, S2, gmq], w=[CQ])
                            CQT = nxt(cqT)
                            def evc(b):
                                act(lambda a: a.copy(out=CQT.t[:], in_=psB[:, b, 0:256].rearrange("p (s c) -> p s c", s=2)),
                                    r=[PBK[b]], w=[CQT])
                            transposes([(CQ.t[:, i * 128:(i + 1) * 128], 128) for i in range(2)], evc, [CQ], None)
                            for kc in range(2):
                                pe(lambda t: t.matmul(psF[:, 4, 0:384], lhsT=CQT.t[:, kc, :], rhs=wuq.t[:, kc, :],
                                                      start=(kc == 0), stop=(kc == 1)), r=[CQT, wuq], w=[PF[4]], inc=(kc == 1))
                            QM = nxt(qm)
                            act(lambda a: a.copy(out=QM.t[:].rearrange("p h c -> p (h c)"), in_=psF[:, 4, 0:384]), r=[PF[4]], w=[QM])
                            QMB = nxt(qmb)
                            dve(lambda v: v.tensor_copy(out=QMB.t[:, :, 0:64], in_=QM.t[:, :, 0:64]), r=[QM], w=[QMB])
                            MR = nxt(mr)
                            dve(lambda v: v.tensor_copy(out=MR.t[:, 0:128].rearrange("p (h c) -> p h c", h=4), in_=QM.t[:, :, 64:96]),
                                r=[QM], w=[MR])
                            dve(lambda v: v.tensor_copy(out=MR.t[:, 128:160], in_=P.t[:, 1152:1184]), r=[P], w=[MR])
                            MR3 = nxt(mr3)
                            if is_s:
                                RMC = rmc_t[0]
                                RMS = rms_t[0]
                                ld(RMC.k, RMC.t[:], ropeMC[ti * 128:(ti + 1) * 128, :], w=[RMC])
                                ld(RMS.k, RMS.t[:], ropeMS[ti * 128:(ti + 1) * 128, :], w=[RMS])
                                rope(MR.t[:], 5, 32, RMC, RMS, 0, [(MR3.t[:], lambda T: T[:, 0:160], None, 0, 0)], [MR], [MR3])
                            else:
                                dve(lambda v: v.tensor_copy(out=MR3.t[:], in_=MR.t[:]), r=[MR], w=[MR3])
                            dve(lambda v: v.tensor_copy(out=QMB.t[:, :, 64:96], in_=MR3.t[:, 0:128].rearrange("p (h c) -> p h c", h=4)),
                                r=[MR3], w=[QMB])
                            def evqm(b):
                                act(lambda a: a.copy(out=QMT.t[0:96, :, ti * 128:(ti + 1) * 128],
                                                     in_=psB[0:96, b, 0:512].rearrange("p (h c) -> p h c", h=4)),
                                    r=[PBK[b]], w=["QMT.%d" % ti])
                            transposes([(QMB.t[:, h, :], 96) for h in range(4)], evqm, [QMB], None)
                            fw.mark('mlakv')
                            CK = nxt(ckvn)
                            dve(lambda v: v.scalar_tensor_tensor(out=CK.t[:], in0=P.t[:, 1024:1152], scalar=S2.t[:, 59:60], in1=gmk.t[:],
                                                                 op0=ALU.mult, op1=ALU.mult), r=[P, S2, gmk], w=[CK])
                            CKB = nxt(ckb)
                            dve(lambda v: v.tensor_copy(out=CKB.t[:], in_=CK.t[:]), r=[CK], w=[CKB])
                            mla_keys(CKB.t[:], [CKB], MR3.t[:, 128:160], [MR3], kc_new)
                            fw.mark('gmlp')
                            VT = nxt(vtmp)
                            S3 = nxt(small)
                            vcv = P.t[:, 1440:1696].rearrange("p (g c) -> p g c", g=4)
                            dve(lambda v: v.tensor_reduce(out=S3.t[:, 0:4], in_=vcv, axis=AX.X, op=ALU.add), r=[P], w=[S3])
                            dve(lambda v: v.tensor_tensor(out=VT.t[:], in0=P.t[:, 1440:1696], in1=P.t[:, 1440:1696], op=ALU.mult), r=[P], w=[VT])
                            dve(lambda v: v.tensor_reduce(out=S3.t[:, 4:8], in_=VT.t[:].rearrange("p (g c) -> p g c", g=4), axis=AX.X, op=ALU.add),
                                r=[VT], w=[S3])
                            dve(lambda v: v.tensor_scalar(out=S3.t[:, 8:12], in0=S3.t[:, 0:4], scalar1=1.0 / 64, scalar2=None, op0=ALU.mult),
                                r=[S3], w=[S3])
                            dve(lambda v: v.tensor_tensor(out=S3.t[:, 12:16], in0=S3.t[:, 8:12], in1=S3.t[:, 8:12], op=ALU.mult), r=[S3], w=[S3])
                            dve(lambda v: v.scalar_tensor_tensor(out=S3.t[:, 16:20], in0=S3.t[:, 4:8], scalar=1.0 / 64, in1=S3.t[:, 12:16],
                                                                 op0=ALU.mult, op1=ALU.subtract), r=[S3], w=[S3])
                            dve(lambda v: v.tensor_scalar(out=S3.t[:, 20:24], in0=S3.t[:, 16:20], scalar1=EPS, scalar2=None, op0=ALU.add),
                                r=[S3], w=[S3])
                            act(lambda a: a.activation(out=S3.t[:, 24:28], in_=S3.t[:, 20:24], func=AF.Sqrt), r=[S3], w=[S3])
                            dve(lambda v: v.reciprocal(out=S3.t[:, 28:32], in_=S3.t[:, 24:28]), r=[S3], w=[S3])
                            dve(lambda v: v.tensor_tensor(out=VT.t[:].rearrange("p (g c) -> p g c", g=4), in0=vcv,
                                                          in1=fdims(S3.t[:, 8:12], [[1, 4], [0, 64]]), op=ALU.subtract), r=[P, S3], w=[VT])
                            VG = nxt(vg)
                            dve(lambda v: v.tensor_tensor(out=VG.t[:].rearrange("p (g c) -> p g c", g=4),
                                                          in0=VT.t[:].rearrange("p (g c) -> p g c", g=4),
                                                          in1=fdims(S3.t[:, 28:32], [[1, 4], [0, 64]]), op=ALU.mult), r=[VT, S3], w=[VG])
                            for g in range(4):
                                pe(lambda t: t.matmul(psF[:, 4, g * 64:(g + 1) * 64], lhsT=wst.t[:, g, :], rhs=VG.t[:, g * 64:(g + 1) * 64],
                                                      start=True, stop=True), r=[VG, wst], w=[PF[4]], inc=(g == 3))
                            dve(lambda v: v.tensor_tensor(out=VT.t[:].rearrange("p (g c) -> p g c", g=4),
                                                          in0=psF[:, 4, 0:256].rearrange("p (g c) -> p g c", g=4),
                                                          in1=fdims(bst.t[:], [[1, 4], [0, 64]]), op=ALU.add), r=[PF[4], bst], w=[VT])
                            dve(lambda v: v.tensor_tensor(out=OC.t[:, ti, :], in0=VT.t[:], in1=P.t[:, 1184:1440], op=ALU.mult),
                                r=[VT, P], w=["OC.%d" % ti])
                            fw.mark('outs')
                            if not is_s:
                                r0 = ti * 128
                                ld("o_nk%d" % (ti % 2), nk[pbi, l, r0:r0 + 128, :], QK.t[:, 512:640], r=[QK])
                                ld("o_nv%d" % (ti % 2), nv[pbi, l, r0:r0 + 128, :], P.t[:, 640:768], r=[P])
                                ld("o_nc%d" % (ti % 2), nckv[pbi, l, r0:r0 + 128, :], CK.t[:], r=[CK])
                                ld("o_nr%d" % (ti % 2), nkr[pbi, l, r0:r0 + 128, :], P.t[:, 1152:1184], r=[P])

                        sA.close()
                        fw.barrier()
                        sB = contextlib.ExitStack()
                        cur[0] = sB
                        G1 = sb(sB, "G1", [128, D])
                        ld(G1.k, G1.t[:], gsc[l, grp, 0], w=[G1])
                        lnb = sb(sB, "lnb", [128, 2, D])
                        for j in range(2):
                            ld(lnb.k, lnb.t[:, j, :], mkap(lnp[l, j], [[0, 128], [1, D]]), w=[lnb])
                        wo = sb(sB, "wo", [128, 8, D], BF16)
                        for kc in range(8):
                            ld(wo.k, wo.t[:, kc, :], w_o[l, kc * 128:(kc + 1) * 128, :], w=[wo], q='pool')
                        xts = rot("xtB", [128, D])
                        small = rot("smallB", [128, 64], n=3)
                        ntq = 4 if is_s else 2
                        NQ = ntq * 128
                        PT = rot("PT", [128, NQ], BF16, n=3)
                        mix = sb(sB, "mix", [128, ntq, D], BF16)
                        mixT = rot("mixT", [128, 8, 128], BF16)
                        ybuf = rot("ybuf", [128, D], n=1)
                        x1b = rot("x1b", [128, D])
                        rec = rot("rec", [128, 8])
                        sc_ = [0]
                        oc_ = [0]
                        for qg in range(0 if stopA else nt // ntq):
                            q0 = qg * NQ
                            allkeys_q = ["QT.%d" % (qg * ntq + i) for i in range(ntq)]
                            allkeys_qm = ["QMT.%d" % (qg * ntq + i) for i in range(ntq)]
                            def emit_S(hh, kc):
                                isA = hh < 8
                                h = hh if isA else hh - 8
                                sbk = sc_[0] % 3
                                sc_[0] += 1
                                if isA:
                                    base = (h // 4) * 64
                                    pe(lambda t: t.matmul(psF[:, sbk, 0:NQ], lhsT=KT.t[base:base + 64, kc * 128:(kc + 1) * 128],
                                                          rhs=QT.t[base:base + 64, h % 4, q0:q0 + NQ], start=True, stop=True),
                                       r=["KT.%d" % kc] + allkeys_q, w=[PF[sbk]])
                                    return sbk, 0.125
                                pe(lambda t: t.matmul(psF[:, sbk, 0:NQ], lhsT=KMT.t[0:96, h, kc * 128:(kc + 1) * 128],
                                                      rhs=QMT.t[0:96, h, q0:q0 + NQ], start=True, stop=True),
                                   r=["KMT.%d" % kc] + allkeys_qm, w=[PF[sbk]])
                                return sbk, 1.0 / math.sqrt(96.0)
                            its = [(hh, kc) for hh in range(12) for kc in range(nkc)]
                            pend = emit_S(*its[0])
                            for n, (hh, kc) in enumerate(its):
                                isA = hh < 8
                                h = hh if isA else hh - 8
                                if kc == 0:
                                    ob = 3 + (oc_[0] % 2)
                                    oc_[0] += 1
                                    oview = psF[:, ob, 0:ntq * 65].rearrange("p (q c) -> p q c", c=65)
                                sbk, scl = pend
                                if n + 1 < len(its):
                                    pend = emit_S(*its[n + 1])
                                Pt = nxt(PT)
                                act(lambda a: a.activation(out=Pt.t[:], in_=psF[:, sbk, 0:NQ], func=AF.Exp, scale=scl), r=[PF[sbk]], w=[Pt])
                                for qt in range(ntq):
                                    if isA:
                                        rhs = VA.t[:, kc, h // 4, :]
                                        vk = "VA.%d" % kc
                                    else:
                                        rhs = VM.t[:, kc, h, :]
                                        vk = "VM.%d" % kc
                                    pe(lambda t: t.matmul(oview[:, qt, :], lhsT=Pt.t[:, qt * 128:(qt + 1) * 128], rhs=rhs,
                                                          start=(kc == 0 and qt == 0), stop=(kc == nkc - 1 and qt == ntq - 1)),
                                       r=[Pt, vk], w=[PF[ob]], inc=(qt == ntq - 1))
                                if kc == nkc - 1:
                                    R = nxt(rec)
                                    dve(lambda v: v.reciprocal(out=R.t[:, 0:ntq], in_=oview[:, :, 64]), r=[PF[ob]], w=[R])
                                    col = h * 64 if isA else 512 + h * 64
                                    dve(lambda v: v.tensor_tensor(out=mix.t[:, :, col:col + 64], in0=oview[:, :, 0:64],
                                                                  in1=fdims(R.t[:, 0:ntq], [[1, ntq], [0, 64]]), op=ALU.mult),
                                        r=[PF[ob], R], w=[mix])
                            dve(lambda v: v.tensor_copy(out=mix.t[:, :, 768:1024], in_=OC.t[:, qg * ntq:(qg + 1) * ntq, :]),
                                r=["OC.%d" % (qg * ntq + i) for i in range(ntq)], w=[mix])
                            for qt in range(ntq):
                                ti = qg * ntq + qt
                                gt = tile0 + ti
                                MT = nxt(mixT)
                                def evm(b):
                                    act(lambda a: a.copy(out=MT.t[:].rearrange("p a c -> p (a c)"), in_=psB[:, b, :]), r=[PBK[b]], w=[MT])
                                transposes([(mix.t[:, qt, c * 128:(c + 1) * 128], 128) for c in range(8)], evm, [mix], None)
                                for hf in range(2):
                                    for kc in range(8):
                                        pe(lambda t: t.matmul(psF[:, hf, :], lhsT=MT.t[:, kc, :], rhs=wo.t[:, kc, hf * 512:(hf + 1) * 512],
                                                              start=(kc == 0), stop=(kc == 7)), r=[MT, wo], w=[PF[hf]], inc=(kc == 7))
                                X = nxt(xts)
                                ld(X.k, X.t[:], x_src(l - 1, 1, gt), w=[X])
                                Y = nxt(ybuf)
                                dve(lambda v: v.tensor_tensor(out=Y.t[:].rearrange("p (a c) -> p a c", a=2), in0=psF[:, 0:2, :],
                                                              in1=G1.t[:].rearrange("p (a c) -> p a c", a=2), op=ALU.mult),
                                    r=[PF[0], PF[1], G1], w=[Y])
                                dve(lambda v: v.scalar_tensor_tensor(out=Y.t[:], in0=X.t[:], scalar=ALPHA, in1=Y.t[:],
                                                                     op0=ALU.mult, op1=ALU.add), r=[X, Y], w=[Y])
                                S = ln_stats(Y.t[:], [Y])
                                X1 = nxt(x1b)
                                act(lambda a: a.activation(out=X1.t[:], in_=Y.t[:], func=AF.Identity, scale=S.t[:, 4:5], bias=S.t[:, 5:6]),
                                    r=[Y, S], w=[X1])
                                dve(lambda v: v.tensor_tensor(out=X1.t[:], in0=X1.t[:], in1=lnb.t[:, 0, :], op=ALU.mult), r=[X1, lnb], w=[X1])
                                dve(lambda v: v.tensor_tensor(out=X1.t[:], in0=X1.t[:], in1=lnb.t[:, 1, :], op=ALU.add), r=[X1, lnb], w=[X1])
                                ld(X1.k + "st", xsc[l, 0, gt * 128:(gt + 1) * 128, :], X1.t[:], r=[X1], w=["xsc.%d" % gt])
                        sB.close()
                    fw.barrier()

                    with contextlib.ExitStack() as sc:
                        cur[0] = sc
                        G2 = sb(sc, "G2", [128, D])
                        ld(G2.k, G2.t[:], gsc[l, grp, 3], w=[G2])
                        A2 = sb(sc, "A2", [128, D])
                        ld(A2.k, A2.t[:], gsc[l, grp, 1], w=[A2])
                        B2 = sb(sc, "B2", [128, D])
                        ld(B2.k, B2.t[:], gsc[l, grp, 2], w=[B2])
                        lnb = sb(sc, "lnb2", [128, 2, D])
                        for j in range(2):
                            ld(lnb.k, lnb.t[:, j, :], mkap(lnp[l, 2 + j], [[0, 128], [1, D]]), w=[lnb])
                        wq = sb(sc, "wq", [128, 8, 2048], BF16)
                        for kc in range(8):
                            ld(wq.k, wq.t[:, kc, :], peer_wq[l, kc * 128:(kc + 1) * 128, :], w=[wq], q='pool')
                        kT = sb(sc, "kT", [128, 2, 128], BF16)
                        ld(kT.k, kT.t[:], k12T[l], w=[kT], q='pool')
                        TGT = 2
                        TG = TGT * 128
                        IJG = sb(sc, "IJG", [128, 3, TG])
                        h2Tg = sb(sc, "h2Tg", [128, 8, TG], BF16)
                        rr = {}

                        def nxt(lst):
                            i = rr.get(id(lst), 0)
                            rr[id(lst)] = i + 1
                            return lst[i % len(lst)]
                        xts = rot("cxt", [128, D])
                        small = rot("csmall", [128, 64], n=3)
                        ybuf = rot("cy", [128, D], n=1)
                        x2b = rot("x2b", [128, D])
                        tcount = [0]
                        puT = puTs[l]
                        pvv = pv[l]
                        for g0 in range(0, 0 if (stopA or stopB) else nt, TGT):
                            s1 = contextlib.ExitStack()
                            cur[0] = s1
                            h2 = rot("h2", [128, D], n=1)
                            xnf = rot("xnf", [128, D], n=1)
                            h2b = rot("h2b", [128, D], BF16, n=1)
                            qTs = rot("qTs", [128, 16, 128], BF16, n=1)
                            sc_s = rot("scs", [128, 16, 128], n=1)
                            sc_r = rot("scr", [128, 16, 128], n=1)
                            v12 = rot("v12", [128, 16, 16], n=1)
                            i12 = rot("i12", [128, 16, 16], U32, n=1)
                            i12f = rot("i12f", [128, 16, 16], n=1)
                            cand = rot("cand", [128, 8, 256], n=1)
                            top = rot("top", [128, 8, 16], n=1)
                            pos = rot("pos", [128, 8, 16], U32, n=1)
                            posr = rot("posr", [128, 128], U32, n=1)
                            posc = rot("posc", [128, 128], U32, n=1)
                            posrf = rot("posrf", [128, 128], n=1)
                            poscf = rot("poscf", [128, 128], n=1)
                            ijg = rot("ijg", [128, 3, 128], n=1)
                            gsm = rot("gsm", [128, 16], n=1)
                            for tl in range(TGT):
                                ti = g0 + tl
                                gt = tile0 + ti
                                X = nxt(xts)
                                ld(X.k, X.t[:], xsc[l, 0, gt * 128:(gt + 1) * 128, :], w=[X])
                                S = ln_stats(X.t[:], [X])
                                XN = xnf[0]
                                act(lambda a: a.activation(out=XN.t[:], in_=X.t[:], func=AF.Identity, scale=S.t[:, 4:5], bias=S.t[:, 5:6]),
                                    r=[X, S], w=[XN])
                                H2 = h2[0]
                                dve(lambda v: v.tensor_tensor(out=H2.t[:], in0=XN.t[:], in1=A2.t[:], op=ALU.mult), r=[XN, A2], w=[H2])
                                H2B = h2b[0]
                                dve(lambda v: v.tensor_tensor(out=H2B.t[:], in0=H2.t[:], in1=B2.t[:], op=ALU.add), r=[H2, B2], w=[H2B])
                                b = tcount[0] % 2
                                tcount[0] += 1
                                for c in range(8):
                                    pe(lambda t: t.transpose(out=psB[:, b, c * 128:(c + 1) * 128], in_=H2B.t[:, c * 128:(c + 1) * 128],
                                                             identity=ident.t[:]), r=[H2B, ident], w=[PBK[b]], inc=(c == 7))
                                act(lambda a: a.copy(out=h2Tg.t[:, :, tl * 128:(tl + 1) * 128], in_=psB[:, b, :].rearrange("p (a c) -> p a c", a=8)),
                                    r=[PBK[b]], w=["h2T.%d" % tl])
                                QS = qTs[0]
                                for c in range(16):
                                    bk = c // 4
                                    for kc in range(8):
                                        pe(lambda t: t.matmul(psF[:, bk, (c % 4) * 128:(c % 4 + 1) * 128], lhsT=wq.t[:, kc, c * 128:(c + 1) * 128],
                                                              rhs=h2Tg.t[:, kc, tl * 128:(tl + 1) * 128], start=(kc == 0), stop=(kc == 7)),
                                           r=["h2T.%d" % tl, wq], w=[PF[bk]], inc=(kc == 7))
                                for bk in range(4):
                                    act(lambda a: a.copy(out=QS.t[:, bk * 4:(bk + 1) * 4, :].rearrange("p a c -> p (a c)"), in_=psF[:, bk, :]),
                                        r=[PF[bk]], w=[QS])
                                SS_ = sc_s[0]
                                for c in range(16):
                                    bk = c // 4
                                    pe(lambda t: t.matmul(psF[:, bk, (c % 4) * 128:(c % 4 + 1) * 128], lhsT=QS.t[:, c, :], rhs=kT.t[:, c // 8, :],
                                                          start=True, stop=True), r=[QS, kT], w=[PF[bk]], inc=(c % 4 == 3))
                                for bk in range(4):
                                    act(lambda a: a.copy(out=SS_.t[:, bk * 4:(bk + 1) * 4, :].rearrange("p a c -> p (a c)"), in_=psF[:, bk, :]),
                                        r=[PF[bk]], w=[SS_])
                                SR = sc_r[0]
                                V12 = v12[0]
                                I12 = i12[0]
                                for c in range(16):
                                    dve(lambda v: v.max(out=V12.t[:, c, 0:8], in_=SS_.t[:, c, :]), r=[SS_], w=["V12a.%d" % c])
                                for c in range(16):
                                    dve(lambda v: v.max_index(out=I12.t[:, c, 0:8], in_max=V12.t[:, c, 0:8], in_values=SS_.t[:, c, :]),
                                        r=[SS_, "V12a.%d" % c], w=["I12a.%d" % c])
                                for c in range(16):
                                    dve(lambda v: v.match_replace(out=SR.t[:, c, :], in_to_replace=V12.t[:, c, 0:8], in_values=SS_.t[:, c, :],
                                                                  imm_value=-1e30), r=[SS_, "V12a.%d" % c], w=["SR.%d" % c, "CD2.%d" % (c // 2)])
                                for c in range(16):
                                    dve(lambda v: v.max(out=V12.t[:, c, 8:16], in_=SR.t[:, c, :]), r=["SR.%d" % c], w=["V12b.%d" % c])
                                for c in range(16):
                                    dve(lambda v: v.max_index(out=I12.t[:, c, 8:16], in_max=V12.t[:, c, 8:16], in_values=SR.t[:, c, :]),
                                        r=["SR.%d" % c, "V12b.%d" % c], w=["I12b.%d" % c])
                                V12K = ["V12a.%d" % c for c in range(16)] + ["V12b.%d" % c for c in range(16)]
                                I12K = ["I12a.%d" % c for c in range(16)] + ["I12b.%d" % c for c in range(16)]
                                SRK = ["SR.%d" % c for c in range(16)]
                                I12F = i12f[0]
                                dve(lambda v: v.tensor_copy(out=I12F.t[:], in_=I12.t[:]), r=I12K, w=[I12F])
                                CD = cand[0]
                                for hd in range(8):
                                    dve(lambda v: v.tensor_tensor(out=CD.t[:, hd, :].rearrange("p (r c) -> p r c", c=16),
                                                                  in0=fdims(V12.t[:, hd, :], [[1, 16], [0, 16]]),
                                                                  in1=fdims(V12.t[:, 8 + hd, :], [[0, 16], [1, 16]]), op=ALU.add),
                                        r=V12K, w=["CD.%d" % hd])
                                CD2 = Tile(SR.t[:].rearrange("p a c -> p (a c)").rearrange("p (h c) -> p h c", h=8), SR.k)
                                TP = top[0]
                                PS_ = pos[0]
                                for hd in range(8):
                                    dve(lambda v: v.max(out=TP.t[:, hd, 0:8], in_=CD.t[:, hd, :]), r=["CD.%d" % hd], w=["TPa.%d" % hd])
                                for hd in range(8):
                                    dve(lambda v: v.max_index(out=PS_.t[:, hd, 0:8], in_max=TP.t[:, hd, 0:8], in_values=CD.t[:, hd, :]),
                                        r=["CD.%d" % hd, "TPa.%d" % hd], w=["PSa.%d" % hd])
                                for hd in range(8):
                                    dve(lambda v: v.match_replace(out=CD2.t[:, hd, :], in_to_replace=TP.t[:, hd, 0:8], in_values=CD.t[:, hd, :],
                                                                  imm_value=-1e30), r=["CD.%d" % hd, "TPa.%d" % hd], w=["CD2.%d" % hd, "SR.%d" % (2 * hd), "SR.%d" % (2 * hd + 1)])
                                for hd in range(8):
                                    dve(lambda v: v.max(out=TP.t[:, hd, 8:16], in_=CD2.t[:, hd, :]), r=["CD2.%d" % hd], w=["TPb.%d" % hd])
                                for hd in range(8):
                                    dve(lambda v: v.max_index(out=PS_.t[:, hd, 8:16], in_max=TP.t[:, hd, 8:16], in_values=CD2.t[:, hd, :]),
                                        r=["CD2.%d" % hd, "TPb.%d" % hd], w=["PSb.%d" % hd])
                                TPK = ["TPa.%d" % hd for hd in range(8)] + ["TPb.%d" % hd for hd in range(8)]
                                PSK = ["PSa.%d" % hd for hd in range(8)] + ["PSb.%d" % hd for hd in range(8)]
                                PR = posr[0]
                                PC = posc[0]
                                pflat = PS_.t[:].rearrange("p h k -> p (h k)")
                                dve(lambda v: v.tensor_single_scalar(out=PR.t[:], in_=pflat, scalar=4, op=ALU.logical_shift_right), r=PSK, w=[PR])
                                dve(lambda v: v.tensor_single_scalar(out=PC.t[:], in_=pflat, scalar=15, op=ALU.bitwise_and), r=PSK, w=[PC])
                                PRF = posrf[0]
                                PCF = poscf[0]
                                dve(lambda v: v.tensor_copy(out=PRF.t[:], in_=PR.t[:]), r=[PR], w=[PRF])
                                dve(lambda v: v.tensor_copy(out=PCF.t[:], in_=PC.t[:]), r=[PC], w=[PCF])
                                OH = Tile(SS_.t[:].rearrange("p a c -> p (a c)").rearrange("p (k r) -> p k r", r=16), SS_.k)
                                IJ = ijg[0]
                                for (PF_, slot, half) in ((PRF, 0, 0), (PCF, 1, 1)):
                                    dve(lambda v: v.tensor_tensor(out=OH.t[:], in0=fdims(PF_.t[:], [[1, 128], [0, 16]]),
                                                                  in1=fdims(iota16.t[:], [[0, 128], [1, 16]]), op=ALU.is_equal),
                                        r=[PF_, iota16], w=[OH] + ["OHm.%d" % hd for hd in range(8)])
                                    for hd in range(8):
                                        dve(lambda v: v.tensor_tensor(out=OH.t[:, hd * 16:(hd + 1) * 16, :], in0=OH.t[:, hd * 16:(hd + 1) * 16, :],
                                                                      in1=fdims(I12F.t[:, half * 8 + hd, :], [[0, 16], [1, 16]]), op=ALU.mult),
                                            r=[OH, I12F], w=["OHm.%d" % hd])
                                    dve(lambda v: v.tensor_reduce(out=IJ.t[:, slot, :], in_=OH.t[:], axis=AX.X, op=ALU.add),
                                        r=["OHm.%d" % hd for hd in range(8)], w=["IJ.%d" % slot])
                                GS = gsm[0]
                                gwv = IJ.t[:, 2, :].rearrange("p (h k) -> p h k", k=16)
                                dve(lambda v: v.tensor_tensor(out=gwv, in0=TP.t[:], in1=fdims(TP.t[:, :, 0], [[16, 8], [0, 16]]), op=ALU.subtract),
                                    r=TPK, w=[IJ])
                                act(lambda a: a.activation(out=IJ.t[:, 2, :], in_=IJ.t[:, 2, :], func=AF.Exp), r=[IJ], w=[IJ])
                                dve(lambda v: v.tensor_reduce(out=GS.t[:, 0:8], in_=gwv, axis=AX.X, op=ALU.add), r=[IJ], w=[GS])
                                dve(lambda v: v.reciprocal(out=GS.t[:, 8:16], in_=GS.t[:, 0:8]), r=[GS], w=[GS])
                                dve(lambda v: v.tensor_tensor(out=gwv, in0=gwv, in1=fdims(GS.t[:, 8:16], [[1, 8], [0, 16]]), op=ALU.mult),
                                    r=[IJ, GS], w=[IJ])
                                for c in range(3):
                                    pe(lambda t: t.transpose(out=psF[:, 4, c * 128:(c + 1) * 128], in_=IJ.t[:, c, :], identity=identF.t[:]),
                                       r=[IJ, "IJ.0", "IJ.1", identF], w=[PF[4]], inc=(c == 2))
                                act(lambda a: a.copy(out=IJG.t[:, :, tl * 128:(tl + 1) * 128], in_=psF[:, 4, 0:384].rearrange("p (a c) -> p a c", a=3)),
                                    r=[PF[4]], w=["IJG.%d" % tl])
                            s1.close()
                            fw.barrier()
                            s2 = contextlib.ExitStack()
                            cur[0] = s2
                            GT = sb(s2, "GT", [128, TG, 128], BF16)
                            wtb = rot("wtb", [128, TG], BF16, n=3)
                            ub = rot("ub", [128, 8, NI * 128], BF16, n=2)
                            vb = rot("vb", [128, NI, D], BF16, n=2)
                            NS = 16
                            r1s = rot("r1s", [128, NS, 128], BF16, n=2)
                            r2s = rot("r2s", [128, NS, 128], BF16, n=2)
                            atb = rot("atb", [128, TG], BF16, n=2)

                            first_touch = not cast_done[l]
                            cast_done[l] = True

                            def load_uv(ig):
                                U = nxt(ub)
                                V = nxt(vb)
                                if first_touch:
                                    ld(U.k, U.t[:], puT[:, ig * NI * 128:(ig + 1) * NI * 128].rearrange("(c p) e -> p c e", p=128), w=[U], q='pool')
                                    ld(V.k, V.t[:], pvv[ig * NI * 128:(ig + 1) * NI * 128, :].rearrange("(a p) d -> p a d", p=128), w=[V], q='pool')
                                    ld(U.k + "s", ubf[l, ig], U.t[:].rearrange("p c e -> p (c e)"), r=[U])
                                    ld(V.k + "s", vbf[l, ig], V.t[:].rearrange("p a d -> p (a d)"), r=[V])
                                else:
                                    ld(U.k, U.t[:].rearrange("p c e -> p (c e)"), ubf[l, ig], w=[U])
                                    ld(V.k, V.t[:].rearrange("p a d -> p (a d)"), vbf[l, ig], w=[V])
                                return U, V
                            pre = [load_uv(0), load_uv(1)]
                            gb = 0
                            for sbk in range(TG // NS):
                                t0 = sbk * NS
                                R1 = nxt(r1s)
                                R2 = nxt(r2s)
                                for t in range(NS):
                                    dve(lambda v: v.tensor_scalar(out=R2.t[:, t, :], in0=iota128b.t[:], scalar1=IJG.t[:, 1, t0 + t:t0 + t + 1], scalar2=None,
                                                                  op0=ALU.is_equal), r=[iota128b], w=["%s.%d" % (R2.k, t)])
                                for t in range(NS):
                                    dve(lambda v: v.tensor_scalar(out=R1.t[:, t, :], in0=iota128b.t[:], scalar1=IJG.t[:, 0, t0 + t:t0 + t + 1],
                                                                  scalar2=IJG.t[:, 2, t0 + t:t0 + t + 1], op0=ALU.is_equal, op1=ALU.mult),
                                        r=[iota128b], w=["%s.%d" % (R1.k, t)])
                                for q4 in range(NS // 4):
                                    bk = gb % 4
                                    gb += 1
                                    for tt in range(4):
                                        t = q4 * 4 + tt
                                        pe(lambda te: te.matmul(psF[:, bk, tt * 128:(tt + 1) * 128], lhsT=R2.t[:, t, :], rhs=R1.t[:, t, :],
                                                                start=True, stop=True), r=["%s.%d" % (R1.k, t), "%s.%d" % (R2.k, t)], w=[PF[bk]], inc=(tt == 3))
                                    tb = t0 + q4 * 4
                                    act(lambda a: a.copy(out=GT.t[:, tb:tb + 4, :], in_=psF[:, bk, :].rearrange("p (t i) -> p t i", t=4)),
                                        r=[PF[bk]], w=[GT])
                            def vside(i, V, ii, WT):
                                for tt in range(TGT):
                                    for hf in range(2):
                                        pe(lambda te: te.matmul(psF[:, tt * 2 + hf, :], lhsT=WT.t[:, tt * 128:(tt + 1) * 128],
                                                                rhs=V.t[:, ii, hf * 512:(hf + 1) * 512], start=(i == 0), stop=(i == 127)),
                                           r=[WT, V], w=[PF[tt * 2 + hf]], inc=(i == 127 or (tt == TGT - 1 and hf == 1)))
                            prev = None
                            for ig in range(128 // NI):
                                U, V = pre[ig] if ig < 2 else load_uv(ig)
                                for ii in range(NI):
                                    i = ig * NI + ii
                                    ab = 4 + (i % 2)
                                    for kc in range(8):
                                        pe(lambda te: te.matmul(psF[:, ab, 0:TG], lhsT=U.t[:, kc, ii * 128:(ii + 1) * 128], rhs=h2Tg.t[:, kc, :],
                                                                start=(kc == 0), stop=(kc == 7)),
                                           r=[U, "h2T.0", "h2T.1"], w=[PF[ab]], inc=(kc == 7))
                                    AT = nxt(atb)
                                    act(lambda a: a.activation(out=AT.t[:], in_=psF[:, ab, 0:TG], func=AF.Gelu_apprx_tanh), r=[PF[ab]], w=[AT])
                                    WT = nxt(wtb)
                                    dve(lambda v: v.tensor_tensor(out=WT.t[:], in0=GT.t[:, :, i], in1=AT.t[:], op=ALU.mult),
                                        r=[AT, GT], w=[WT])
                                    if prev is not None:
                                        vside(*prev)
                                    prev = (i, V, ii, WT)
                            vside(*prev)
                            for tl in range(TGT):
                                ti = g0 + tl
                                gt = tile0 + ti
                                X = nxt(xts)
                                ld(X.k, X.t[:], xsc[l, 0, gt * 128:(gt + 1) * 128, :], w=[X])
                                Y = nxt(ybuf)
                                dve(lambda v: v.tensor_tensor(out=Y.t[:].rearrange("p (a c) -> p a c", a=2), in0=psF[:, tl * 2:tl * 2 + 2, :],
                                                              in1=G2.t[:].rearrange("p (a c) -> p a c", a=2), op=ALU.mult),
                                    r=[PF[tl * 2], PF[tl * 2 + 1], G2], w=[Y])
                                dve(lambda v: v.scalar_tensor_tensor(out=Y.t[:], in0=X.t[:], scalar=ALPHA, in1=Y.t[:], op0=ALU.mult, op1=ALU.add),
                                    r=[X, Y], w=[Y])
                                S = ln_stats(Y.t[:], [Y])
                                X2 = nxt(x2b)
                                act(lambda a: a.activation(out=X2.t[:], in_=Y.t[:], func=AF.Identity, scale=S.t[:, 4:5], bias=S.t[:, 5:6]),
                                    r=[Y, S], w=[X2])
                                dve(lambda v: v.tensor_tensor(out=X2.t[:], in0=X2.t[:], in1=lnb.t[:, 0, :], op=ALU.mult), r=[X2, lnb], w=[X2])
                                dve(lambda v: v.tensor_tensor(out=X2.t[:], in0=X2.t[:], in1=lnb.t[:, 1, :], op=ALU.add), r=[X2, lnb], w=[X2])
                                if l == NL - 1:
                                    if gt < 8:
                                        dst = yp[gt * 128:(gt + 1) * 128, :]
                                    else:
                                        dst = ys[(gt - 8) * 128:(gt - 7) * 128, :]
                                    ld(X2.k + "st", dst, X2.t[:], r=[X2])
                                    if dbg:
                                        ld(X2.k + "st", xsc[l, 1, gt * 128:(gt + 1) * 128, :], X2.t[:], r=[X2], w=["xsc1.%d" % gt])
                                else:
                                    ld(X2.k + "st", xsc[l, 1, gt * 128:(gt + 1) * 128, :], X2.t[:], r=[X2], w=["xsc1.%d" % gt])
                            s2.close()
                            fw.barrier()
                    fw.barrier()
        except _Stop:
            pass
        fw.finish()
        build_nc.ninst = fw.ninst
        build_nc.marks = fw.marks
    return nc


def _rope_tables():
    t = np.arange(SS)
    row = (t // 64).astype(np.float32)
    col = (t % 64).astype(np.float32)

    def tab(m, nh):
        freqs = (10000.0 ** (-np.arange(m, dtype=np.float32) / m)).astype(np.float32)
        ar = row[:, None] * freqs[None, :]
        ac = col[:, None] * freqs[None, :]
        cr, sr, cc, sn = np.cos(ar), np.sin(ar), np.cos(ac), np.sin(ac)
        C = np.concatenate([cr, cr, cc, cc], axis=1)
        S = np.concatenate([-sr, sr, -sn, sn], axis=1)
        return (np.tile(C, (1, nh)).astype(np.float32), np.tile(S, (1, nh)).astype(np.float32))
    C16, S16 = tab(16, 10)
    C8, S8 = tab(8, 5)
    return C16, S16, C8, S8


def make_in_maps(inp):
    f = lambda a: np.ascontiguousarray(np.asarray(a, dtype=np.float32))
    C16, S16, C8, S8 = _rope_tables()
    w_mod = f(inp["w_mod"])
    b_mod = f(inp["b_mod"])
    b_modT = f(b_mod.reshape(NL, 48, 128).transpose(0, 2, 1))
    gqk = f(np.concatenate([np.tile(inp["attn_q_norm"], (1, 8)), np.tile(inp["attn_k_norm"], (1, 2))], axis=1))
    wsT = f(np.asarray(inp["gmlp_ws"]).transpose(0, 3, 1, 2))
    bsT = f(np.asarray(inp["gmlp_b"]).transpose(0, 2, 1))
    lnp = f(np.stack([inp["ln1_g"], inp["ln1_b"], inp["ln2_g"], inp["ln2_b"]], axis=1))
    pwq = f(np.asarray(inp["peer_wq"]).reshape(NL, D, 8, 2, 128).transpose(0, 1, 3, 2, 4).reshape(NL, D, 2048))
    k12T = f(np.stack([np.asarray(inp["peer_k1"]).transpose(0, 2, 1), np.asarray(inp["peer_k2"]).transpose(0, 2, 1)], axis=2))
    shared = {
        "w_mod": w_mod, "b_modT": b_modT, "b_mod": b_mod, "w_in": f(inp["w_in"]), "gqk": gqk,
        "mqn": f(inp["mla_q_norm"]), "mkvn": f(inp["mla_kv_norm"]), "w_uq": f(inp["w_uq"]), "w_ukv": f(inp["w_ukv"]),
        "wsT": wsT, "bsT": bsT, "w_o": f(inp["w_o"]), "lnp": lnp, "peer_wq": pwq, "k12T": k12T,
        "puT0": f(np.asarray(inp["peer_u"][0]).T), "puT1": f(np.asarray(inp["peer_u"][1]).T), "pv0": f(inp["peer_v"][0]), "pv1": f(inp["peer_v"][1]),
        "ropeC": C16, "ropeS": S16, "ropeMC": C8, "ropeMS": S8,
    }
    maps = []
    xpr = np.asarray(inp["x_prompt"], dtype=np.float32)
    xsa = np.asarray(inp["x_sample"], dtype=np.float32)
    cctx = np.asarray(inp["c_ctx"], dtype=np.float32)
    for c in range(NCORES):
        m = dict(shared)
        m["xp"] = f(xpr[PB * c:PB * (c + 1)].reshape(PB * PS, D))
        m["xs"] = f(xsa[c])
        m["ck"] = f(np.asarray(inp["cache_attn_k"])[c].reshape(NL, PAST, 128))
        m["cv"] = f(np.asarray(inp["cache_attn_v"])[c].reshape(NL, PAST, 128))
        m["cckv"] = f(np.asarray(inp["cache_mla_ckv"])[c])
        m["ckr"] = f(np.asarray(inp["cache_mla_krope"])[c])
        cc = np.stack([cctx, np.asarray(inp["c"], dtype=np.float32)[c]], axis=-1)
        m["cT"] = f(cc.reshape(8, 128, 2).transpose(1, 0, 2))
        maps.append(m)
    return maps


_NC_CACHE = {}


def kernel(**inputs):
    if "nc" not in _NC_CACHE:
        _NC_CACHE["nc"] = build_nc(False)
    nc = _NC_CACHE["nc"]
    maps = make_in_maps(inputs)
    res = run_bass_kernel_spmd(nc, maps, core_ids=list(range(NCORES)))
    R = res.results
    y_prompt = np.concatenate([R[c]["yp"].reshape(PB, PS, D) for c in range(NCORES)], axis=0)
    y_sample = np.stack([R[c]["ys"] for c in range(NCORES)], axis=0)
    nk = np.concatenate([R[c]["nk"].reshape(PB, NL, PS, 2, 64) for c in range(NCORES)], axis=0)
    nv = np.concatenate([R[c]["nv"].reshape(PB, NL, PS, 2, 64) for c in range(NCORES)], axis=0)
    nckv = np.concatenate([R[c]["nckv"] for c in range(NCORES)], axis=0)
    nkr = np.concatenate([R[c]["nkr"] for c in range(NCORES)], axis=0)
    return (y_prompt.astype(np.float32), y_sample.astype(np.float32), nk.astype(np.float32), nv.astype(np.float32),
            nckv.astype(np.float32), nkr.astype(np.float32))
```

```python
import contextlib
import math
import os

import numpy as np
import concourse.bass as bass
import concourse.mybir as mybir
from concourse.bass_utils import run_bass_kernel_spmd

F32 = mybir.dt.float32
BF16 = mybir.dt.bfloat16
I32 = mybir.dt.int32
U32 = mybir.dt.uint32
AF = mybir.ActivationFunctionType
ALU = mybir.AluOpType
AX = mybir.AxisListType

D = 1024
NL = 2
IN_W = 1696
EPS = 1e-6
ALPHA = (2.0 * NL) ** 0.25
NCORES = 8
PB = 4
PS = 256
SS = 2048
PAST = 256
NTILES = (PB * PS + SS) // 128


class _Stop(Exception):
    pass


class Tile:
    def __init__(self, t, k):
        self.t = t
        self.k = k


def _keys(lst):
    out = []
    for x in lst:
        out.append(x.k if isinstance(x, Tile) else x)
    return out


class FW:
    def __init__(self, nc, es):
        self.nc = nc
        self.es = es
        self.eng = {'pe': nc.tensor, 'act': nc.scalar, 'dve': nc.vector, 'pool': nc.gpsimd, 'sp': nc.sync}
        self.sem = {}
        self.cnt = {}
        self.semobj = {}
        for e in self.eng:
            self.sem[e] = es.enter_context(nc.semaphore("sem_" + e))
            self.cnt[e] = 0
            self.semobj["sem_" + e] = self.sem[e]
        self.dsem = {}
        self.dall = []
        self.dfree = {'hw': [], 'sw': []}
        self.waited = {e: {} for e in self.eng}
        self.lastw = {}
        self.readers = {}
        self.ninst = 0
        self.nops = 0
        self.limit = int(os.environ["MK_LIMIT"]) if os.environ.get("MK_LIMIT") else None
        self.marks = []

    def mark(self, name):
        self.marks.append((name, self.nops))

    def _wait(self, e, ev):
        if ev is None:
            return
        name, val = ev
        if e == 'pe' and name == 'sem_pe':
            return
        w = self.waited[e]
        if w.get(name, 0) >= val:
            return
        w[name] = val
        self.eng[e].wait_ge(self.semobj[name], val)
        self.ninst += 1

    def _deps(self, e, reads, writes):
        for k in reads:
            self._wait(e, self.lastw.get(k))
        for k in writes:
            self._wait(e, self.lastw.get(k))
            for ev in self.readers.get(k, {}).values():
                self._wait(e, ev)

    def _record(self, ev, reads, writes):
        for k in reads:
            self.readers.setdefault(k, {})[ev[0]] = ev
        for k in writes:
            self.lastw[k] = ev
            self.readers[k] = {}

    def op(self, e, fn, r=(), w=(), inc=True):
        self.nops += 1
        if self.limit is not None and self.nops > self.limit:
            return None
        reads = _keys(r)
        writes = _keys(w)
        if e != 'pe':
            ps = [k for k in reads if k.startswith("ps")]
            if ps:
                reads = [k for k in reads if not k.startswith("ps")]
                writes = list(writes) + ps
        self._deps(e, reads, writes)
        inst = fn(self.eng[e])
        self.ninst += 1
        name = "sem_" + e
        if inc:
            self.cnt[e] += 1
            inst.then_inc(self.sem[e], 1)
            ev = (name, self.cnt[e])
        else:
            ev = (name, self.cnt[e] + 1)
        self._record(ev, reads, writes)
        return inst

    def dma(self, q, key, fn, r=(), w=()):
        self.nops += 1
        if self.limit is not None and self.nops > self.limit:
            return None
        reads = _keys(r)
        writes = _keys(w)
        self._deps(q, reads, writes)
        cls = 'sw' if q == 'pool' else 'hw'
        key = cls + ":" + key
        if key not in self.dsem:
            if self.dfree[cls]:
                d = self.dfree[cls].pop()
            else:
                nm = "dsem_%d" % len(self.dall)
                sm = self.es.enter_context(self.nc.semaphore(nm))
                self.semobj[nm] = sm
                d = [sm, 0, nm, cls]
                self.dall.append(d)
            self.dsem[key] = d
        d = self.dsem[key]
        inst = fn(self.eng[q])
        self.ninst += 1
        d[1] += 16
        inst.then_inc(d[0], 16)
        ev = (d[2], d[1])
        self._record(ev, reads, writes)
        return inst

    def _all_events(self):
        evs = [("sem_" + e, self.cnt[e]) for e in self.eng if self.cnt[e] > 0]
        evs += [(d[2], d[1]) for d in self.dall if d[1] > 0]
        return evs

    def barrier(self):
        evs = self._all_events()
        for e in self.eng:
            for ev in evs:
                self._wait(e, ev)
        self.lastw = {}
        self.readers = {}
        for d in self.dsem.values():
            self.dfree[d[3]].append(d)
        self.dsem = {}

    def finish(self):
        for ev in self._all_events():
            self._wait('sp', ev)


def mkap(ap, dims, off=0):
    return bass.AP(ap.tensor, ap.offset + off, [list(x) for x in dims])


def fdims(ap, dims, off=0):
    base = list(ap.ap)
    return bass.AP(ap.tensor, ap.offset + off, [list(base[0])] + [list(x) for x in dims])


def build_nc(dbg=False):
    nc = bass.Bass("TRN2", target_bir_lowering=False)

    def din(name, shape, dt=F32):
        return nc.dram_tensor(name, shape, dt, kind="ExternalInput").ap()

    def dout(name, shape, dt=F32):
        return nc.dram_tensor(name, shape, dt, kind="ExternalOutput").ap()

    xp = din("xp", [PB * PS, D])
    xs = din("xs", [SS, D])
    ck = din("ck", [NL, PAST, 128])
    cv = din("cv", [NL, PAST, 128])
    cckv = din("cckv", [NL, PAST, 128])
    ckr = din("ckr", [NL, PAST, 32])
    cT = din("cT", [128, 8, 2])
    w_mod = din("w_mod", [NL, D, 6 * D])
    b_modT = din("b_modT", [NL, 128, 48])
    b_mod = din("b_mod", [NL, 6 * D])
    w_in = din("w_in", [NL, D, IN_W])
    gqk = din("gqk", [NL, 640])
    mqn = din("mqn", [NL, 256])
    mkvn = din("mkvn", [NL, 128])
    w_uq = din("w_uq", [NL, 256, 384])
    w_ukv = din("w_ukv", [NL, 128, 512])
    wsT = din("wsT", [NL, 128, 4, 128])
    bsT = din("bsT", [NL, 128, 4])
    w_o = din("w_o", [NL, D, D])
    lnp = din("lnp", [NL, 4, D])
    peer_wq = din("peer_wq", [NL, D, 2048])
    k12T = din("k12T", [NL, 128, 2, 128])
    puTs = [din("puT%d" % l, [D, 16384]) for l in range(NL)]
    pv = [din("pv%d" % l, [16384, D]) for l in range(NL)]
    ropeC = din("ropeC", [SS, 640])
    ropeS = din("ropeS", [SS, 640])
    ropeMC = din("ropeMC", [SS, 160])
    ropeMS = din("ropeMS", [SS, 160])

    yp = dout("yp", [PB * PS, D])
    ys = dout("ys", [SS, D])
    nk = dout("nk", [PB, NL, PS, 128])
    nv = dout("nv", [PB, NL, PS, 128])
    nckv = dout("nckv", [PB, NL, PS, 128])
    nkr = dout("nkr", [PB, NL, PS, 32])
    if dbg:
        xsc = dout("xsc", [NL, 2, NTILES * 128, D])
    else:
        xsc = nc.dram_tensor("xsc", [NL, 2, NTILES * 128, D], F32, kind="Internal").ap()
    gsc = nc.dram_tensor("gsc", [NL, 2, 4, 128, D], F32, kind="Internal").ap()
    NI = 2
    NIG = 128 // NI
    ubf = nc.dram_tensor("ubf", [NL, NIG, 128, 8 * NI * 128], BF16, kind="Internal").ap()
    vbf = nc.dram_tensor("vbf", [NL, NIG, 128, NI * D], BF16, kind="Internal").ap()
    cast_done = [False] * NL
    wscr = {"win": nc.dram_tensor("winb", [NL, 128, 8 * IN_W], BF16, kind="Internal").ap(),
            "wo": nc.dram_tensor("wob", [NL, 128, 8 * D], BF16, kind="Internal").ap(),
            "wq": nc.dram_tensor("wqb", [NL, 128, 8 * 2048], BF16, kind="Internal").ap()}
    wdone = {}

    uid = [0]

    with contextlib.ExitStack() as es:
        fw = FW(nc, es)

        def sb(stack, name, shape, dt=F32):
            uid[0] += 1
            nm = "%s_%d" % (name, uid[0])
            return Tile(stack.enter_context(nc.sbuf_tensor(nm, shape, dt)), nm)

        def dve(fn, r=(), w=(), inc=True):
            return fw.op('dve', fn, r, w, inc)

        def act(fn, r=(), w=(), inc=True):
            return fw.op('act', fn, r, w, inc)

        def pe(fn, r=(), w=(), inc=True):
            return fw.op('pe', fn, r, w, inc)

        def pool(fn, r=(), w=(), inc=True):
            return fw.op('pool', fn, r, w, inc)

        def ld(key, out_ap, in_ap, r=(), w=(), q='sp'):
            return fw.dma(q, key, lambda e: e.dma_start(out=out_ap, in_=in_ap), r, w)

        psF = es.enter_context(nc.psum_tensor("psF", [128, 6, 512], F32))
        psB = es.enter_context(nc.psum_tensor("psB", [128, 2, 1024], BF16))
        PF = ["psF%d" % i for i in range(6)]
        PBK = ["psB0", "psB1"]

        ident = sb(es, "ident", [128, 128], BF16)
        pool(lambda p: p.memset(ident.t[:], 0.0), w=[ident])
        pool(lambda p: p.affine_select(out=ident.t[:], in_=ident.t[:], pattern=[[-1, 128]],
                                       compare_op=ALU.not_equal, fill=1.0, base=0, channel_multiplier=1),
             r=[ident], w=[ident])
        epsb = sb(es, "epsb", [128, 1])
        dve(lambda v: v.memset(epsb.t[:], EPS), w=[epsb])
        iota16 = sb(es, "iota16", [128, 16])
        iota16i = sb(es, "iota16i", [128, 16], I32)
        pool(lambda p: p.iota(iota16i.t[:], pattern=[[1, 16]], base=0, channel_multiplier=0), w=[iota16i])
        dve(lambda v: v.tensor_copy(out=iota16.t[:], in_=iota16i.t[:]), r=[iota16i], w=[iota16])
        identF = sb(es, "identF", [128, 128])
        pool(lambda p: p.memset(identF.t[:], 0.0), w=[identF])
        pool(lambda p: p.affine_select(out=identF.t[:], in_=identF.t[:], pattern=[[-1, 128]],
                                       compare_op=ALU.not_equal, fill=1.0, base=0, channel_multiplier=1),
             r=[identF], w=[identF])
        iota128 = sb(es, "iota128", [128, 128])
        iota128i = sb(es, "iota128i", [128, 128], I32)
        pool(lambda p: p.iota(iota128i.t[:], pattern=[[1, 128]], base=0, channel_multiplier=0), w=[iota128i])
        dve(lambda v: v.tensor_copy(out=iota128.t[:], in_=iota128i.t[:]), r=[iota128i], w=[iota128])
        iota128b = sb(es, "iota128b", [128, 128], BF16)
        dve(lambda v: v.tensor_copy(out=iota128b.t[:], in_=iota128i.t[:]), r=[iota128i], w=[iota128b])
        modcol = sb(es, "modcol", [128, NL, 48, 2])

        with contextlib.ExitStack() as ps0:
            cTs = sb(ps0, "cTs", [128, 8, 2])
            scT = sb(ps0, "scT", [128, 8, 2], BF16)
            scR = sb(ps0, "scR", [128, 8, 2, 128], BF16)
            ld("cTs", cTs.t[:], cT, w=[cTs])
            act(lambda a: a.activation(out=scT.t[:], in_=cTs.t[:], func=AF.Silu), r=[cTs], w=[scT])
            dve(lambda v: v.tensor_copy(out=scR.t[:].rearrange("p a g m -> p (a g) m"),
                                        in_=fdims(scT.t[:], [[1, 16], [0, 128]])), r=[scT], w=[scR])
            bmT = sb(ps0, "bmT", [128, NL, 48])
            for l in range(NL):
                ld("bmT", bmT.t[:, l, :], b_modT[l], w=[bmT])
            wm = [sb(ps0, "wm%d" % i, [128, 8, 512], BF16) for i in range(2)]
            bmb = [sb(ps0, "bmb%d" % i, [128, 512]) for i in range(2)]
            gst = [sb(ps0, "gst%d" % i, [128, 512]) for i in range(2)]
            it = 0
            for l in range(NL):
                for ci in range(12):
                    role = ci // 2
                    W = wm[it % 2]
                    ld(W.k, W.t[:], w_mod[l, :, ci * 512:(ci + 1) * 512].rearrange("(c p) n -> p c n", p=128),
                       w=[W], q='pool')
                    if role in (0, 1, 3, 4):
                        for b4 in range(4):
                            blk = ci * 4 + b4
                            for kc in range(8):
                                pe(lambda t: t.matmul(psF[:, 4, b4 * 2:b4 * 2 + 2], lhsT=W.t[:, kc, b4 * 128:(b4 + 1) * 128],
                                                      rhs=scT.t[:, kc, :], start=(kc == 0), stop=(kc == 7)),
                                   r=[W, scT], w=[PF[4]], inc=(kc == 7))
                            addc = 1.0 if role in (1, 4) else 0.0
                            dve(lambda v: v.scalar_tensor_tensor(out=modcol.t[:, l, blk, :], in0=psF[:, 4, b4 * 2:b4 * 2 + 2],
                                                                 scalar=addc, in1=fdims(bmT.t[:, l, blk:blk + 1], [[0, 2]]),
                                                                 op0=ALU.add, op1=ALU.add),
                                r=[PF[4], bmT], w=[modcol])
                    if role in (2, 3, 4, 5):
                        slot = {2: 0, 4: 1, 3: 2, 5: 3}[role]
                        Bb = bmb[it % 2]
                        ld(Bb.k, Bb.t[:], mkap(b_mod[l, ci * 512:(ci + 1) * 512], [[0, 128], [1, 512]]), w=[Bb])
                        for g in range(2):
                            pb = PF[2 + g]
                            for kc in range(8):
                                pe(lambda t: t.matmul(psF[:, 2 + g, :], lhsT=scR.t[:, kc, g, :], rhs=W.t[:, kc, :],
                                                      start=(kc == 0), stop=(kc == 7)),
                                   r=[W, scR], w=[pb], inc=(kc == 7))
                            G = gst[g]
                            addc = 1.0 if role == 4 else 0.0
                            dve(lambda v: v.scalar_tensor_tensor(out=G.t[:], in0=psF[:, 2 + g, :], scalar=addc, in1=Bb.t[:],
                                                                 op0=ALU.add, op1=ALU.add),
                                r=[pb, Bb], w=[G])
                            half = ci % 2
                            ld("gst%d" % g, gsc[l, g, slot, :, half * 512:(half + 1) * 512], G.t[:], r=[G], w=["gsc"])
                    it += 1
        fw.barrier()

        def chk(tag):
            if dbg and os.environ.get("MK_STOP") == tag:
                raise _Stop()

        seqs = [(2 * b, 2, 0, False, b) for b in range(PB)] + [(8, 16, 1, True, -1)]
        if dbg and os.environ.get("MK_SEQS"):
            seqs = [seqs[int(i)] for i in os.environ["MK_SEQS"].split(",")]
        nlayers = int(os.environ.get("MK_LAYERS", NL)) if dbg else NL
        skip_peer = bool(dbg and os.environ.get("MK_SKIP_PEER"))
        stopA = bool(dbg and os.environ.get('MK_STOP') == 'A')
        stopB = bool(dbg and os.environ.get('MK_STOP') == 'B')

        def x_src(l, ph, gt):
            if l < 0:
                if gt < 8:
                    return xp[gt * 128:(gt + 1) * 128, :]
                return xs[(gt - 8) * 128:(gt - 7) * 128, :]
            return xsc[l, ph, gt * 128:(gt + 1) * 128, :]

        try:
            chk('pro')
            for (tile0, nt, grp, is_s, pbi) in seqs:
                nkc = nt + (2 if is_s else 0)
                koff = 2 if is_s else 0
                for l in range(nlayers):
                    with contextlib.ExitStack() as sa:
                        NTOK = nt * 128
                        NKEY = nkc * 128
                        QT = sb(sa, "QT", [128, 4, NTOK], BF16)
                        KT = sb(sa, "KT", [128, NKEY], BF16)
                        VA = sb(sa, "VA", [128, nkc, 2, 65], BF16)
                        QMT = sb(sa, "QMT", [128, 4, NTOK], BF16)
                        KMT = sb(sa, "KMT", [128, 4, NKEY], BF16)
                        VM = sb(sa, "VM", [128, nkc, 4, 65], BF16)
                        OC = sb(sa, "OC", [128, nt, 256], BF16)
                        pool(lambda p: p.memset(VA.t[:], 1.0), w=[VA])
                        pool(lambda p: p.memset(VM.t[:], 1.0), w=[VM])

                        sA = contextlib.ExitStack()
                        cur = [sA]
                        gq = sb(sA, "gq", [128, 640])
                        ld(gq.k, gq.t[:], mkap(gqk[l], [[0, 128], [1, 640]]), w=[gq])
                        gmq = sb(sA, "gmq", [128, 256])
                        ld(gmq.k, gmq.t[:], mkap(mqn[l], [[0, 128], [1, 256]]), w=[gmq])
                        gmk = sb(sA, "gmk", [128, 128])
                        ld(gmk.k, gmk.t[:], mkap(mkvn[l], [[0, 128], [1, 128]]), w=[gmk])
                        bst = sb(sA, "bst", [128, 4])
                        ld(bst.k, bst.t[:], bsT[l], w=[bst])
                        win = sb(sA, "win", [128, 8, IN_W], BF16)
                        if ("win", l) not in wdone:
                            wdone[("win", l)] = True
                            for kc in range(8):
                                ld(win.k, win.t[:, kc, :], w_in[l, kc * 128:(kc + 1) * 128, :], w=[win], q='pool')
                            ld(win.k + "s", wscr["win"][l], win.t[:].rearrange("p a c -> p (a c)"), r=[win])
                        else:
                            ld(win.k, win.t[:].rearrange("p a c -> p (a c)"), wscr["win"][l], w=[win])
                        wuq = sb(sA, "wuq", [128, 2, 384], BF16)
                        ld(wuq.k, wuq.t[:], w_uq[l].rearrange("(c p) n -> p c n", p=128), w=[wuq], q='pool')
                        wukv = sb(sA, "wukv", [128, 512], BF16)
                        ld(wukv.k, wukv.t[:], w_ukv[l], w=[wukv], q='pool')
                        wst = sb(sA, "wst", [128, 4, 128], BF16)
                        ld(wst.k, wst.t[:], wsT[l], w=[wst], q='pool')

                        def rot(name, shape, dt=F32, n=2):
                            return [sb(cur[0], name + str(i), shape, dt) for i in range(n)]
                        xts = rot("xt", [128, D])
                        xnb = rot("xnb", [128, D], BF16, n=1)
                        hT = rot("hT", [128, 8, 128], BF16, n=1)
                        pj = rot("pj", [128, IN_W], n=1)
                        sq = rot("sq", [128, 1152], n=1)
                        small = rot("small", [128, 64], n=3)
                        qkn = rot("qkn", [128, 640], n=1)
                        rc_t = rot("ropeC", [128, 640], n=1)
                        rs_t = rot("ropeS", [128, 640], n=1)
                        rmc_t = rot("ropeMC", [128, 160], n=1)
                        rms_t = rot("ropeMS", [128, 160], n=1)
                        tmp640 = rot("tmp640", [128, 640], n=1)
                        tmp640b = rot("tmp640b", [128, 640], n=1)
                        qkb = rot("qkb", [128, 640], BF16)
                        cqb = rot("cqb", [128, 256], BF16)
                        cqT = rot("cqT", [128, 2, 128], BF16)
                        ckvn = rot("ckvn", [128, 128])
                        ckb = rot("ckb", [128, 128], BF16)
                        ckT = rot("ckT", [128, 128], BF16)
                        qm = rot("qm", [128, 4, 96])
                        mr = rot("mr", [128, 160], n=1)
                        mr3 = rot("mr3", [128, 160], n=1)
                        qmb = rot("qmb", [128, 4, 96], BF16)
                        kmb = rot("kmb", [128, 4, 96], BF16)
                        vg = rot("vg", [128, 256], BF16)
                        vtmp = rot("vtmp", [128, 256])
                        cst = rot("cst", [128, 128])
                        cstb = rot("cstb", [128, 128], BF16)
                        krs = rot("krs", [128, 32])
                        rr = {}

                        def nxt(lst):
                            i = rr.get(id(lst), 0)
                            rr[id(lst)] = i + 1
                            return lst[i % len(lst)]

                        def ln_stats(xin, xkeys):
                            S = nxt(small)
                            dve(lambda v: v.bn_stats(out=S.t[:, 8:14], in_=xin[:, 0:512]), r=xkeys, w=[S])
                            dve(lambda v: v.bn_stats(out=S.t[:, 14:20], in_=xin[:, 512:1024]), r=xkeys, w=[S])
                            dve(lambda v: v.bn_aggr(out=S.t[:, 0:2], in_=S.t[:, 8:20]), r=[S], w=[S])
                            act(lambda a: a.activation(out=S.t[:, 2:3], in_=S.t[:, 1:2], func=AF.Sqrt, bias=epsb.t[:], scale=1.0),
                                r=[S, epsb], w=[S])
                            dve(lambda v: v.reciprocal(out=S.t[:, 4:5], in_=S.t[:, 2:3]), r=[S], w=[S])
                            dve(lambda v: v.scalar_tensor_tensor(out=S.t[:, 5:6], in0=S.t[:, 0:1], scalar=-1.0, in1=S.t[:, 4:5],
                                                                 op0=ALU.mult, op1=ALU.mult), r=[S], w=[S])
                            return S

                        tcount = [0]

                        def transposes(srcs, dst_fn, rkeys, wkeys, evac='act'):
                            b = tcount[0] % 2
                            tcount[0] += 1
                            for i, (src, n) in enumerate(srcs):
                                pe(lambda t: t.transpose(out=psB[0:n, b, i * 128:(i + 1) * 128], in_=src, identity=ident.t[:]),
                                   r=list(rkeys) + [ident], w=[PBK[b]], inc=(i == len(srcs) - 1))
                            dst_fn(b)

                        def rope(src, nh, hd, ctab, stab, coff, dsts, rkeys, wkeys):
                            n = nh * hd
                            q4 = hd // 4
                            T1 = nxt(tmp640)
                            T2 = nxt(tmp640b)
                            dve(lambda v: v.tensor_tensor(out=T1.t[:, 0:n], in0=src, in1=ctab.t[:, coff:coff + n], op=ALU.mult),
                                r=list(rkeys) + [ctab], w=[T1])
                            nb = n // (2 * q4)
                            def hv(ap, off, half):
                                return fdims(ap, [[2 * q4, nb], [1, q4]], off + half * q4)
                            dve(lambda v: v.tensor_tensor(out=hv(T2.t[:, 0:n], 0, 0), in0=hv(src, 0, 1), in1=hv(stab.t[:, 0:n], coff, 0),
                                                          op=ALU.mult), r=list(rkeys) + [stab], w=[T2])
                            dve(lambda v: v.tensor_tensor(out=hv(T2.t[:, 0:n], 0, 1), in0=hv(src, 0, 0), in1=hv(stab.t[:, 0:n], coff, 1),
                                                          op=ALU.mult), r=list(rkeys) + [stab], w=[T2])
                            for (oap, i0, i1, j0, j1) in dsts:
                                dve(lambda v: v.tensor_tensor(out=oap, in0=i0(T1.t), in1=i0(T2.t), op=ALU.add), r=[T1, T2], w=wkeys)

                        def mla_keys(ckb_ap, ckb_keys, kr_ap, kr_keys, kc):
                            CT = nxt(ckT)
                            def ev(b):
                                act(lambda a: a.copy(out=CT.t[:], in_=psB[:, b, 0:128]), r=[PBK[b]], w=[CT])
                            transposes([(ckb_ap, 128)], ev, ckb_keys, None)
                            pe(lambda t: t.matmul(psF[:, 5, :], lhsT=CT.t[:], rhs=wukv.t[:], start=True, stop=True),
                               r=[CT, wukv], w=[PF[5]])
                            KB = nxt(kmb)
                            kvv = psF[:, 5, :].rearrange("p (h c) -> p h c", h=4)
                            act(lambda a: a.copy(out=KB.t[:, :, 0:64], in_=kvv[:, :, 0:64]), r=[PF[5]], w=[KB])
                            dve(lambda v: v.tensor_copy(out=VM.t[:, kc, :, 0:64], in_=kvv[:, :, 64:128]), r=[PF[5], VM], w=["VM.%d" % kc])
                            dve(lambda v: v.tensor_copy(out=KB.t[:, :, 64:96], in_=fdims(kr_ap, [[0, 4], [1, 32]])),
                                r=list(kr_keys), w=[KB])
                            def ev2(b):
                                act(lambda a: a.copy(out=KMT.t[0:96, :, kc * 128:(kc + 1) * 128],
                                                     in_=psB[0:96, b, 0:512].rearrange("p (h c) -> p h c", h=4)),
                                    r=[PBK[b]], w=["KMT.%d" % kc])
                            transposes([(KB.t[:, h, :], 96) for h in range(4)], ev2, [KB], None)

                        fw.mark('A-start')
                        if is_s:
                            for j in range(2):
                                C1 = nxt(cst)
                                ld(C1.k, C1.t[:], ck[l, j * 128:(j + 1) * 128, :], w=[C1])
                                CB = nxt(cstb)
                                dve(lambda v: v.tensor_copy(out=CB.t[:], in_=C1.t[:]), r=[C1], w=[CB])
                                def evk(b):
                                    act(lambda a: a.copy(out=KT.t[:, j * 128:(j + 1) * 128], in_=psB[:, b, 0:128]),
                                        r=[PBK[b]], w=["KT.%d" % j])
                                transposes([(CB.t[:], 128)], evk, [CB], None)
                                C2 = nxt(cst)
                                ld(C2.k, C2.t[:], cv[l, j * 128:(j + 1) * 128, :], w=[C2])
                                dve(lambda v: v.tensor_copy(out=VA.t[:, j, :, 0:64], in_=C2.t[:].rearrange("p (h c) -> p h c", h=2)),
                                    r=[C2, VA], w=["VA.%d" % j])
                                C3 = nxt(cst)
                                ld(C3.k, C3.t[:], cckv[l, j * 128:(j + 1) * 128, :], w=[C3])
                                CB3 = nxt(cstb)
                                dve(lambda v: v.tensor_copy(out=CB3.t[:], in_=C3.t[:]), r=[C3], w=[CB3])
                                K4 = nxt(krs)
                                ld(K4.k, K4.t[:], ckr[l, j * 128:(j + 1) * 128, :], w=[K4])
                                mla_keys(CB3.t[:], [CB3], K4.t[:], [K4], j)

                        for ti in range(nt):
                            gt = tile0 + ti
                            kc_new = koff + ti
                            X = nxt(xts)
                            ld(X.k, X.t[:], x_src(l - 1, 1, gt), w=[X])
                            fw.mark('tile%d-ln' % ti)
                            S = ln_stats(X.t[:], [X])
                            XN = nxt(xnb)
                            act(lambda a: a.activation(out=XN.t[:], in_=X.t[:], func=AF.Identity, scale=S.t[:, 4:5], bias=S.t[:, 5:6]),
                                r=[X, S], w=[XN])
                            H = nxt(hT)
                            def evh(b):
                                for c in range(8):
                                    act(lambda a: a.activation(out=H.t[:, c, :], in_=psB[:, b, c * 128:(c + 1) * 128], func=AF.Identity,
                                                               scale=modcol.t[:, l, 8 + c, grp:grp + 1], bias=modcol.t[:, l, c, grp:grp + 1]),
                                        r=[PBK[b], modcol], w=[H])
                            transposes([(XN.t[:, c * 128:(c + 1) * 128], 128) for c in range(8)], evh, [XN], None)
                            fw.mark('proj')
                            segs = [(0, 512), (512, 512), (1024, 512), (1536, 160)]
                            for bi, (c0, cn) in enumerate(segs):
                                for kc in range(8):
                                    pe(lambda t: t.matmul(psF[:, bi, 0:cn], lhsT=H.t[:, kc, :], rhs=win.t[:, kc, c0:c0 + cn],
                                                          start=(kc == 0), stop=(kc == 7)), r=[H, win], w=[PF[bi]], inc=(kc == 7))
                            P = nxt(pj)
                            for bi, (c0, cn) in enumerate(segs):
                                act(lambda a: a.copy(out=P.t[:, c0:c0 + cn], in_=psF[:, bi, 0:cn]), r=[PF[bi]], w=[P])
                            fw.mark('rms')
                            SQ = nxt(sq)
                            dve(lambda v: v.tensor_tensor(out=SQ.t[:], in0=P.t[:, 0:1152], in1=P.t[:, 0:1152], op=ALU.mult), r=[P], w=[SQ])
                            S2 = nxt(small)
                            dve(lambda v: v.tensor_reduce(out=S2.t[:, 0:10], in_=SQ.t[:, 0:640].rearrange("p (h c) -> p h c", c=64),
                                                          axis=AX.X, op=ALU.add), r=[SQ], w=[S2])
                            dve(lambda v: v.tensor_reduce(out=S2.t[:, 10:11], in_=SQ.t[:, 768:1024], axis=AX.X, op=ALU.add), r=[SQ], w=[S2])
                            dve(lambda v: v.tensor_reduce(out=S2.t[:, 11:12], in_=SQ.t[:, 1024:1152], axis=AX.X, op=ALU.add), r=[SQ], w=[S2])
                            dve(lambda v: v.tensor_scalar(out=S2.t[:, 16:26], in0=S2.t[:, 0:10], scalar1=1.0 / 64, scalar2=EPS,
                                                          op0=ALU.mult, op1=ALU.add), r=[S2], w=[S2])
                            dve(lambda v: v.tensor_scalar(out=S2.t[:, 26:27], in0=S2.t[:, 10:11], scalar1=1.0 / 256, scalar2=EPS,
                                                          op0=ALU.mult, op1=ALU.add), r=[S2], w=[S2])
                            dve(lambda v: v.tensor_scalar(out=S2.t[:, 27:28], in0=S2.t[:, 11:12], scalar1=1.0 / 128, scalar2=EPS,
                                                          op0=ALU.mult, op1=ALU.add), r=[S2], w=[S2])
                            act(lambda a: a.activation(out=S2.t[:, 32:44], in_=S2.t[:, 16:28], func=AF.Sqrt), r=[S2], w=[S2])
                            dve(lambda v: v.reciprocal(out=S2.t[:, 48:60], in_=S2.t[:, 32:44]), r=[S2], w=[S2])
                            fw.mark('qk')
                            QK = nxt(qkn)
                            dve(lambda v: v.tensor_tensor(out=QK.t[:].rearrange("p (h c) -> p h c", c=64),
                                                          in0=P.t[:, 0:640].rearrange("p (h c) -> p h c", c=64),
                                                          in1=fdims(S2.t[:, 48:58], [[1, 10], [0, 64]]), op=ALU.mult), r=[P, S2], w=[QK])
                            dve(lambda v: v.tensor_tensor(out=QK.t[:], in0=QK.t[:], in1=gq.t[:], op=ALU.mult), r=[QK, gq], w=[QK])
                            QB = nxt(qkb)
                            def qdst(T, j):
                                return T[:, j * 256:(j + 1) * 256].rearrange("p (s c) -> p s c", c=64)
                            def qout(j):
                                return fdims(QB.t[:], [[128, 4], [1, 64]], j * 64)
                            if is_s:
                                RC = rc_t[0]
                                RS = rs_t[0]
                                ld(RC.k, RC.t[:], ropeC[ti * 128:(ti + 1) * 128, :], w=[RC])
                                ld(RS.k, RS.t[:], ropeS[ti * 128:(ti + 1) * 128, :], w=[RS])
                                dsts = [(qout(0), lambda T: qdst(T, 0), None, 0, 0), (qout(1), lambda T: qdst(T, 1), None, 0, 0),
                                        (QB.t[:, 512:640], lambda T: T[:, 512:640], None, 0, 0)]
                                rope(QK.t[:], 10, 64, RC, RS, 0, dsts, [QK], [QB])
                            else:
                                for j in range(2):
                                    dve(lambda v: v.tensor_copy(out=qout(j), in_=qdst(QK.t, j)), r=[QK], w=[QB])
                                dve(lambda v: v.tensor_copy(out=QB.t[:, 512:640], in_=QK.t[:, 512:640]), r=[QK], w=[QB])
                            def evq(b):
                                act(lambda a: a.copy(out=QT.t[:, :, ti * 128:(ti + 1) * 128],
                                                     in_=psB[:, b, 0:512].rearrange("p (s c) -> p s c", s=4)),
                                    r=[PBK[b]], w=["QT.%d" % ti])
                                act(lambda a: a.copy(out=KT.t[:, kc_new * 128:(kc_new + 1) * 128], in_=psB[:, b, 512:640]),
                                    r=[PBK[b]], w=["KT.%d" % kc_new])
                            transposes([(QB.t[:, i * 128:(i + 1) * 128], 128) for i in range(5)], evq, [QB], None)
                            fw.mark('va')
                            dve(lambda v: v.tensor_copy(out=VA.t[:, kc_new, :, 0:64], in_=P.t[:, 640:768].rearrange("p (h c) -> p h c", h=2)),
                                r=[P, VA], w=["VA.%d" % kc_new])
                            fw.mark('mlaq')
                            CQ = nxt(cqb)
                            dve(lambda v: v.scalar_tensor_tensor(out=CQ.t[:], in0=P.t[:, 768:1024], scalar=S2.t[:, 58:59], in1=gmq.t[:],
                                                                 op0=ALU.mult, op1=ALU.mult), r=[P, S2, gmq], w=[CQ])
                            CQT = nxt(cqT)
                            def evc(b):
                                act(lambda a: a.copy(out=CQT.t[:], in_=psB[:, b, 0:256].rearrange("p (s c) -> p s c", s=2)),
                                    r=[PBK[b]], w=[CQT])
                            transposes([(CQ.t[:, i * 128:(i + 1) * 128], 128) for i in range(2)], evc, [CQ], None)
                            for kc in range(2):
                                pe(lambda t: t.matmul(psF[:, 4, 0:384], lhsT=CQT.t[:, kc, :], rhs=wuq.t[:, kc, :],
                                                      start=(kc == 0), stop=(kc == 1)), r=[CQT, wuq], w=[PF[4]], inc=(kc == 1))
                            QM = nxt(qm)
                            act(lambda a: a.copy(out=QM.t[:].rearrange("p h c -> p (h c)"), in_=psF[:, 4, 0:384]), r=[PF[4]], w=[QM])
                            QMB = nxt(qmb)
                            dve(lambda v: v.tensor_copy(out=QMB.t[:, :, 0:64], in_=QM.t[:, :, 0:64]), r=[QM], w=[QMB])
                            MR = nxt(mr)
                            dve(lambda v: v.tensor_copy(out=MR.t[:, 0:128].rearrange("p (h c) -> p h c", h=4), in_=QM.t[:, :, 64:96]),
                                r=[QM], w=[MR])
                            dve(lambda v: v.tensor_copy(out=MR.t[:, 128:160], in_=P.t[:, 1152:1184]), r=[P], w=[MR])
                            MR3 = nxt(mr3)
                            if is_s:
                                RMC = rmc_t[0]
                                RMS = rms_t[0]
                                ld(RMC.k, RMC.t[:], ropeMC[ti * 128:(ti + 1) * 128, :], w=[RMC])
                                ld(RMS.k, RMS.t[:], ropeMS[ti * 128:(ti + 1) * 128, :], w=[RMS])
                                rope(MR.t[:], 5, 32, RMC, RMS, 0, [(MR3.t[:], lambda T: T[:, 0:160], None, 0, 0)], [MR], [MR3])
                            else:
                                dve(lambda v: v.tensor_copy(out=MR3.t[:], in_=MR.t[:]), r=[MR], w=[MR3])
                            dve(lambda v: v.tensor_copy(out=QMB.t[:, :, 64:96], in_=MR3.t[:, 0:128].rearrange("p (h c) -> p h c", h=4)),
                                r=[MR3], w=[QMB])
                            def evqm(b):
                                act(lambda a: a.copy(out=QMT.t[0:96, :, ti * 128:(ti + 1) * 128],
                                                     in_=psB[0:96, b, 0:512].rearrange("p (h c) -> p h c", h=4)),
                                    r=[PBK[b]], w=["QMT.%d" % ti])
                            transposes([(QMB.t[:, h, :], 96) for h in range(4)], evqm, [QMB], None)
                            fw.mark('mlakv')
                            CK = nxt(ckvn)
                            dve(lambda v: v.scalar_tensor_tensor(out=CK.t[:], in0=P.t[:, 1024:1152], scalar=S2.t[:, 59:60], in1=gmk.t[:],
                                                                 op0=ALU.mult, op1=ALU.mult), r=[P, S2, gmk], w=[CK])
                            CKB = nxt(ckb)
                            dve(lambda v: v.tensor_copy(out=CKB.t[:], in_=CK.t[:]), r=[CK], w=[CKB])
                            mla_keys(CKB.t[:], [CKB], MR3.t[:, 128:160], [MR3], kc_new)
                            fw.mark('gmlp')
                            VT = nxt(vtmp)
                            S3 = nxt(small)
                            vcv = P.t[:, 1440:1696].rearrange("p (g c) -> p g c", g=4)
                            dve(lambda v: v.tensor_reduce(out=S3.t[:, 0:4], in_=vcv, axis=AX.X, op=ALU.add), r=[P], w=[S3])
                            dve(lambda v: v.tensor_tensor(out=VT.t[:], in0=P.t[:, 1440:1696], in1=P.t[:, 1440:1696], op=ALU.mult), r=[P], w=[VT])
                            dve(lambda v: v.tensor_reduce(out=S3.t[:, 4:8], in_=VT.t[:].rearrange("p (g c) -> p g c", g=4), axis=AX.X, op=ALU.add),
                                r=[VT], w=[S3])
                            dve(lambda v: v.tensor_scalar(out=S3.t[:, 8:12], in0=S3.t[:, 0:4], scalar1=1.0 / 64, scalar2=None, op0=ALU.mult),
                                r=[S3], w=[S3])
                            dve(lambda v: v.tensor_tensor(out=S3.t[:, 12:16], in0=S3.t[:, 8:12], in1=S3.t[:, 8:12], op=ALU.mult), r=[S3], w=[S3])
                            dve(lambda v: v.scalar_tensor_tensor(out=S3.t[:, 16:20], in0=S3.t[:, 4:8], scalar=1.0 / 64, in1=S3.t[:, 12:16],
                                                                 op0=ALU.mult, op1=ALU.subtract), r=[S3], w=[S3])
                            dve(lambda v: v.tensor_scalar(out=S3.t[:, 20:24], in0=S3.t[:, 16:20], scalar1=EPS, scalar2=None, op0=ALU.add),
                                r=[S3], w=[S3])
                            act(lambda a: a.activation(out=S3.t[:, 24:28], in_=S3.t[:, 20:24], func=AF.Sqrt), r=[S3], w=[S3])
                            dve(lambda v: v.reciprocal(out=S3.t[:, 28:32], in_=S3.t[:, 24:28]), r=[S3], w=[S3])
                            dve(lambda v: v.tensor_tensor(out=VT.t[:].rearrange("p (g c) -> p g c", g=4), in0=vcv,
                                                          in1=fdims(S3.t[:, 8:12], [[1, 4], [0, 64]]), op=ALU.subtract), r=[P, S3], w=[VT])
                            VG = nxt(vg)
                            dve(lambda v: v.tensor_tensor(out=VG.t[:].rearrange("p (g c) -> p g c", g=4),
                                                          in0=VT.t[:].rearrange("p (g c) -> p g c", g=4),
                                                          in1=fdims(S3.t[:, 28:32], [[1, 4], [0, 64]]), op=ALU.mult), r=[VT, S3], w=[VG])
                            for g in range(4):
                                pe(lambda t: t.matmul(psF[:, 4, g * 64:(g + 1) * 64], lhsT=wst.t[:, g, :], rhs=VG.t[:, g * 64:(g + 1) * 64],
                                                      start=True, stop=True), r=[VG, wst], w=[PF[4]], inc=(g == 3))
                            dve(lambda v: v.tensor_tensor(out=VT.t[:].rearrange("p (g c) -> p g c", g=4),
                                                          in0=psF[:, 4, 0:256].rearrange("p (g c) -> p g c", g=4),
                                                          in1=fdims(bst.t[:], [[1, 4], [0, 64]]), op=ALU.add), r=[PF[4], bst], w=[VT])
                            dve(lambda v: v.tensor_tensor(out=OC.t[:, ti, :], in0=VT.t[:], in1=P.t[:, 1184:1440], op=ALU.mult),
                                r=[VT, P], w=["OC.%d" % ti])
                            fw.mark('outs')
                            if not is_s:
                                r0 = ti * 128
                                ld("o_nk%d" % (ti % 2), nk[pbi, l, r0:r0 + 128, :], QK.t[:, 512:640], r=[QK])
                                ld("o_nv%d" % (ti % 2), nv[pbi, l, r0:r0 + 128, :], P.t[:, 640:768], r=[P])
                                ld("o_nc%d" % (ti % 2), nckv[pbi, l, r0:r0 + 128, :], CK.t[:], r=[CK])
                                ld("o_nr%d" % (ti % 2), nkr[pbi, l, r0:r0 + 128, :], P.t[:, 1152:1184], r=[P])

                        sA.close()
                        fw.barrier()
                        sB = contextlib.ExitStack()
                        cur[0] = sB
                        G1 = sb(sB, "G1", [128, D])
                        ld(G1.k, G1.t[:], gsc[l, grp, 0], w=[G1])
                        lnb = sb(sB, "lnb", [128, 2, D])
                        for j in range(2):
                            ld(lnb.k, lnb.t[:, j, :], mkap(lnp[l, j], [[0, 128], [1, D]]), w=[lnb])
                        wo = sb(sB, "wo", [128, 8, D], BF16)
                        if ("wo", l) not in wdone:
                            wdone[("wo", l)] = True
                            for kc in range(8):
                                ld(wo.k, wo.t[:, kc, :], w_o[l, kc * 128:(kc + 1) * 128, :], w=[wo], q='pool')
                            ld(wo.k + "s", wscr["wo"][l], wo.t[:].rearrange("p a c -> p (a c)"), r=[wo])
                        else:
                            ld(wo.k, wo.t[:].rearrange("p a c -> p (a c)"), wscr["wo"][l], w=[wo])
                        xts = rot("xtB", [128, D])
                        small = rot("smallB", [128, 64], n=3)
                        ntq = 4 if is_s else 2
                        NQ = ntq * 128
                        PT = rot("PT", [128, NQ], BF16, n=3)
                        mix = sb(sB, "mix", [128, ntq, D], BF16)
                        mixT = rot("mixT", [128, 8, 128], BF16)
                        ybuf = rot("ybuf", [128, D], n=1)
                        x1b = rot("x1b", [128, D])
                        rec = rot("rec", [128, 8])
                        sc_ = [0]
                        oc_ = [0]
                        for qg in range(0 if stopA else nt // ntq):
                            q0 = qg * NQ
                            allkeys_q = ["QT.%d" % (qg * ntq + i) for i in range(ntq)]
                            allkeys_qm = ["QMT.%d" % (qg * ntq + i) for i in range(ntq)]
                            def emit_S(hh, kc):
                                isA = hh < 8
                                h = hh if isA else hh - 8
                                sbk = sc_[0] % 3
                                sc_[0] += 1
                                if isA:
                                    base = (h // 4) * 64
                                    pe(lambda t: t.matmul(psF[:, sbk, 0:NQ], lhsT=KT.t[base:base + 64, kc * 128:(kc + 1) * 128],
                                                          rhs=QT.t[base:base + 64, h % 4, q0:q0 + NQ], start=True, stop=True),
                                       r=["KT.%d" % kc] + allkeys_q, w=[PF[sbk]])
                                    return sbk, 0.125
                                pe(lambda t: t.matmul(psF[:, sbk, 0:NQ], lhsT=KMT.t[0:96, h, kc * 128:(kc + 1) * 128],
                                                      rhs=QMT.t[0:96, h, q0:q0 + NQ], start=True, stop=True),
                                   r=["KMT.%d" % kc] + allkeys_qm, w=[PF[sbk]])
                                return sbk, 1.0 / math.sqrt(96.0)
                            its = [(hh, kc) for hh in range(12) for kc in range(nkc)]
                            pend = emit_S(*its[0])
                            for n, (hh, kc) in enumerate(its):
                                isA = hh < 8
                                h = hh if isA else hh - 8
                                if kc == 0:
                                    ob = 3 + (oc_[0] % 2)
                                    oc_[0] += 1
                                    oview = psF[:, ob, 0:ntq * 65].rearrange("p (q c) -> p q c", c=65)
                                sbk, scl = pend
                                if n + 1 < len(its):
                                    pend = emit_S(*its[n + 1])
                                Pt = nxt(PT)
                                act(lambda a: a.activation(out=Pt.t[:], in_=psF[:, sbk, 0:NQ], func=AF.Exp, scale=scl), r=[PF[sbk]], w=[Pt])
                                for qt in range(ntq):
                                    if isA:
                                        rhs = VA.t[:, kc, h // 4, :]
                                        vk = "VA.%d" % kc
                                    else:
                                        rhs = VM.t[:, kc, h, :]
                                        vk = "VM.%d" % kc
                                    pe(lambda t: t.matmul(oview[:, qt, :], lhsT=Pt.t[:, qt * 128:(qt + 1) * 128], rhs=rhs,
                                                          start=(kc == 0 and qt == 0), stop=(kc == nkc - 1 and qt == ntq - 1)),
                                       r=[Pt, vk], w=[PF[ob]], inc=(qt == ntq - 1))
                                if kc == nkc - 1:
                                    R = nxt(rec)
                                    dve(lambda v: v.reciprocal(out=R.t[:, 0:ntq], in_=oview[:, :, 64]), r=[PF[ob]], w=[R])
                                    col = h * 64 if isA else 512 + h * 64
                                    dve(lambda v: v.tensor_tensor(out=mix.t[:, :, col:col + 64], in0=oview[:, :, 0:64],
                                                                  in1=fdims(R.t[:, 0:ntq], [[1, ntq], [0, 64]]), op=ALU.mult),
                                        r=[PF[ob], R], w=[mix])
                            dve(lambda v: v.tensor_copy(out=mix.t[:, :, 768:1024], in_=OC.t[:, qg * ntq:(qg + 1) * ntq, :]),
                                r=["OC.%d" % (qg * ntq + i) for i in range(ntq)], w=[mix])
                            for qt in range(ntq):
                                ti = qg * ntq + qt
                                gt = tile0 + ti
                                MT = nxt(mixT)
                                def evm(b):
                                    act(lambda a: a.copy(out=MT.t[:].rearrange("p a c -> p (a c)"), in_=psB[:, b, :]), r=[PBK[b]], w=[MT])
                                transposes([(mix.t[:, qt, c * 128:(c + 1) * 128], 128) for c in range(8)], evm, [mix], None)
                                for hf in range(2):
                                    for kc in range(8):
                                        pe(lambda t: t.matmul(psF[:, hf, :], lhsT=MT.t[:, kc, :], rhs=wo.t[:, kc, hf * 512:(hf + 1) * 512],
                                                              start=(kc == 0), stop=(kc == 7)), r=[MT, wo], w=[PF[hf]], inc=(kc == 7))
                                X = nxt(xts)
                                ld(X.k, X.t[:], x_src(l - 1, 1, gt), w=[X])
                                Y = nxt(ybuf)
                                dve(lambda v: v.tensor_tensor(out=Y.t[:].rearrange("p (a c) -> p a c", a=2), in0=psF[:, 0:2, :],
                                                              in1=G1.t[:].rearrange("p (a c) -> p a c", a=2), op=ALU.mult),
                                    r=[PF[0], PF[1], G1], w=[Y])
                                dve(lambda v: v.scalar_tensor_tensor(out=Y.t[:], in0=X.t[:], scalar=ALPHA, in1=Y.t[:],
                                                                     op0=ALU.mult, op1=ALU.add), r=[X, Y], w=[Y])
                                S = ln_stats(Y.t[:], [Y])
                                X1 = nxt(x1b)
                                act(lambda a: a.activation(out=X1.t[:], in_=Y.t[:], func=AF.Identity, scale=S.t[:, 4:5], bias=S.t[:, 5:6]),
                                    r=[Y, S], w=[X1])
                                dve(lambda v: v.tensor_tensor(out=X1.t[:], in0=X1.t[:], in1=lnb.t[:, 0, :], op=ALU.mult), r=[X1, lnb], w=[X1])
                                dve(lambda v: v.tensor_tensor(out=X1.t[:], in0=X1.t[:], in1=lnb.t[:, 1, :], op=ALU.add), r=[X1, lnb], w=[X1])
                                ld(X1.k + "st", xsc[l, 0, gt * 128:(gt + 1) * 128, :], X1.t[:], r=[X1], w=["xsc.%d" % gt])
                        sB.close()
                    fw.barrier()

                    with contextlib.ExitStack() as sc:
                        cur[0] = sc
                        G2 = sb(sc, "G2", [128, D])
                        ld(G2.k, G2.t[:], gsc[l, grp, 3], w=[G2])
                        A2 = sb(sc, "A2", [128, D])
                        ld(A2.k, A2.t[:], gsc[l, grp, 1], w=[A2])
                        B2 = sb(sc, "B2", [128, D])
                        ld(B2.k, B2.t[:], gsc[l, grp, 2], w=[B2])
                        lnb = sb(sc, "lnb2", [128, 2, D])
                        for j in range(2):
                            ld(lnb.k, lnb.t[:, j, :], mkap(lnp[l, 2 + j], [[0, 128], [1, D]]), w=[lnb])
                        wq = sb(sc, "wq", [128, 8, 2048], BF16)
                        if ("wq", l) not in wdone:
                            wdone[("wq", l)] = True
                            for kc in range(8):
                                ld(wq.k, wq.t[:, kc, :], peer_wq[l, kc * 128:(kc + 1) * 128, :], w=[wq], q='pool')
                            ld(wq.k + "s", wscr["wq"][l], wq.t[:].rearrange("p a c -> p (a c)"), r=[wq])
                        else:
                            ld(wq.k, wq.t[:].rearrange("p a c -> p (a c)"), wscr["wq"][l], w=[wq])
                        kT = sb(sc, "kT", [128, 2, 128], BF16)
                        ld(kT.k, kT.t[:], k12T[l], w=[kT], q='pool')
                        TGT = 2
                        TG = TGT * 128
                        IJG = sb(sc, "IJG", [128, 3, TG])
                        h2Tg = sb(sc, "h2Tg", [128, 8, TG], BF16)
                        rr = {}

                        def nxt(lst):
                            i = rr.get(id(lst), 0)
                            rr[id(lst)] = i + 1
                            return lst[i % len(lst)]
                        xts = rot("cxt", [128, D])
                        small = rot("csmall", [128, 64], n=3)
                        ybuf = rot("cy", [128, D], n=1)
                        x2b = rot("x2b", [128, D])
                        tcount = [0]
                        puT = puTs[l]
                        pvv = pv[l]
                        for g0 in range(0, 0 if (stopA or stopB) else nt, TGT):
                            s1 = contextlib.ExitStack()
                            cur[0] = s1
                            h2 = rot("h2", [128, D], n=1)
                            xnf = rot("xnf", [128, D], n=1)
                            h2b = rot("h2b", [128, D], BF16, n=1)
                            qTs = rot("qTs", [128, 16, 128], BF16, n=1)
                            sc_s = rot("scs", [128, 16, 128], n=1)
                            sc_r = rot("scr", [128, 16, 128], n=1)
                            v12 = rot("v12", [128, 16, 16], n=1)
                            i12 = rot("i12", [128, 16, 16], U32, n=1)
                            i12f = rot("i12f", [128, 16, 16], n=1)
                            cand = rot("cand", [128, 8, 256], n=1)
                            top = rot("top", [128, 8, 16], n=1)
                            pos = rot("pos", [128, 8, 16], U32, n=1)
                            posr = rot("posr", [128, 128], U32, n=1)
                            posc = rot("posc", [128, 128], U32, n=1)
                            posrf = rot("posrf", [128, 128], n=1)
                            poscf = rot("poscf", [128, 128], n=1)
                            ijg = rot("ijg", [128, 3, 128], n=1)
                            gsm = rot("gsm", [128, 16], n=1)
                            for tl in range(TGT):
                                ti = g0 + tl
                                gt = tile0 + ti
                                X = nxt(xts)
                                ld(X.k, X.t[:], xsc[l, 0, gt * 128:(gt + 1) * 128, :], w=[X])
                                S = ln_stats(X.t[:], [X])
                                XN = xnf[0]
                                act(lambda a: a.activation(out=XN.t[:], in_=X.t[:], func=AF.Identity, scale=S.t[:, 4:5], bias=S.t[:, 5:6]),
                                    r=[X, S], w=[XN])
                                H2 = h2[0]
                                dve(lambda v: v.tensor_tensor(out=H2.t[:], in0=XN.t[:], in1=A2.t[:], op=ALU.mult), r=[XN, A2], w=[H2])
                                H2B = h2b[0]
                                dve(lambda v: v.tensor_tensor(out=H2B.t[:], in0=H2.t[:], in1=B2.t[:], op=ALU.add), r=[H2, B2], w=[H2B])
                                b = tcount[0] % 2
                                tcount[0] += 1
                                for c in range(8):
                                    pe(lambda t: t.transpose(out=psB[:, b, c * 128:(c + 1) * 128], in_=H2B.t[:, c * 128:(c + 1) * 128],
                                                             identity=ident.t[:]), r=[H2B, ident], w=[PBK[b]], inc=(c == 7))
                                act(lambda a: a.copy(out=h2Tg.t[:, :, tl * 128:(tl + 1) * 128], in_=psB[:, b, :].rearrange("p (a c) -> p a c", a=8)),
                                    r=[PBK[b]], w=["h2T.%d" % tl])
                                QS = qTs[0]
                                for c in range(16):
                                    bk = c // 4
                                    for kc in range(8):
                                        pe(lambda t: t.matmul(psF[:, bk, (c % 4) * 128:(c % 4 + 1) * 128], lhsT=wq.t[:, kc, c * 128:(c + 1) * 128],
                                                              rhs=h2Tg.t[:, kc, tl * 128:(tl + 1) * 128], start=(kc == 0), stop=(kc == 7)),
                                           r=["h2T.%d" % tl, wq], w=[PF[bk]], inc=(kc == 7))
                                for bk in range(4):
                                    act(lambda a: a.copy(out=QS.t[:, bk * 4:(bk + 1) * 4, :].rearrange("p a c -> p (a c)"), in_=psF[:, bk, :]),
                                        r=[PF[bk]], w=[QS])
                                SS_ = sc_s[0]
                                for c in range(16):
                                    bk = c // 4
                                    pe(lambda t: t.matmul(psF[:, bk, (c % 4) * 128:(c % 4 + 1) * 128], lhsT=QS.t[:, c, :], rhs=kT.t[:, c // 8, :],
                                                          start=True, stop=True), r=[QS, kT], w=[PF[bk]], inc=(c % 4 == 3))
                                for bk in range(4):
                                    act(lambda a: a.copy(out=SS_.t[:, bk * 4:(bk + 1) * 4, :].rearrange("p a c -> p (a c)"), in_=psF[:, bk, :]),
                                        r=[PF[bk]], w=[SS_])
                                SR = sc_r[0]
                                V12 = v12[0]
                                I12 = i12[0]
                                for c in range(16):
                                    dve(lambda v: v.max(out=V12.t[:, c, 0:8], in_=SS_.t[:, c, :]), r=[SS_], w=["V12a.%d" % c])
                                for c in range(16):
                                    dve(lambda v: v.max_index(out=I12.t[:, c, 0:8], in_max=V12.t[:, c, 0:8], in_values=SS_.t[:, c, :]),
                                        r=[SS_, "V12a.%d" % c], w=["I12a.%d" % c])
                                for c in range(16):
                                    dve(lambda v: v.match_replace(out=SR.t[:, c, :], in_to_replace=V12.t[:, c, 0:8], in_values=SS_.t[:, c, :],
                                                                  imm_value=-1e30), r=[SS_, "V12a.%d" % c], w=["SR.%d" % c, "CD2.%d" % (c // 2)])
                                for c in range(16):
                                    dve(lambda v: v.max(out=V12.t[:, c, 8:16], in_=SR.t[:, c, :]), r=["SR.%d" % c], w=["V12b.%d" % c])
                                for c in range(16):
                                    dve(lambda v: v.max_index(out=I12.t[:, c, 8:16], in_max=V12.t[:, c, 8:16], in_values=SR.t[:, c, :]),
                                        r=["SR.%d" % c, "V12b.%d" % c], w=["I12b.%d" % c])
                                V12K = ["V12a.%d" % c for c in range(16)] + ["V12b.%d" % c for c in range(16)]
                                I12K = ["I12a.%d" % c for c in range(16)] + ["I12b.%d" % c for c in range(16)]
                                SRK = ["SR.%d" % c for c in range(16)]
                                I12F = i12f[0]
                                dve(lambda v: v.tensor_copy(out=I12F.t[:], in_=I12.t[:]), r=I12K, w=[I12F])
                                CD = cand[0]
                                for hd in range(8):
                                    dve(lambda v: v.tensor_tensor(out=CD.t[:, hd, :].rearrange("p (r c) -> p r c", c=16),
                                                                  in0=fdims(V12.t[:, hd, :], [[1, 16], [0, 16]]),
                                                                  in1=fdims(V12.t[:, 8 + hd, :], [[0, 16], [1, 16]]), op=ALU.add),
                                        r=V12K, w=["CD.%d" % hd])
                                CD2 = Tile(SR.t[:].rearrange("p a c -> p (a c)").rearrange("p (h c) -> p h c", h=8), SR.k)
                                TP = top[0]
                                PS_ = pos[0]
                                for hd in range(8):
                                    dve(lambda v: v.max(out=TP.t[:, hd, 0:8], in_=CD.t[:, hd, :]), r=["CD.%d" % hd], w=["TPa.%d" % hd])
                                for hd in range(8):
                                    dve(lambda v: v.max_index(out=PS_.t[:, hd, 0:8], in_max=TP.t[:, hd, 0:8], in_values=CD.t[:, hd, :]),
                                        r=["CD.%d" % hd, "TPa.%d" % hd], w=["PSa.%d" % hd])
                                for hd in range(8):
                                    dve(lambda v: v.match_replace(out=CD2.t[:, hd, :], in_to_replace=TP.t[:, hd, 0:8], in_values=CD.t[:, hd, :],
                                                                  imm_value=-1e30), r=["CD.%d" % hd, "TPa.%d" % hd], w=["CD2.%d" % hd, "SR.%d" % (2 * hd), "SR.%d" % (2 * hd + 1)])
                                for hd in range(8):
                                    dve(lambda v: v.max(out=TP.t[:, hd, 8:16], in_=CD2.t[:, hd, :]), r=["CD2.%d" % hd], w=["TPb.%d" % hd])
                                for hd in range(8):
                                    dve(lambda v: v.max_index(out=PS_.t[:, hd, 8:16], in_max=TP.t[:, hd, 8:16], in_values=CD2.t[:, hd, :]),
                                        r=["CD2.%d" % hd, "TPb.%d" % hd], w=["PSb.%d" % hd])
                                TPK = ["TPa.%d" % hd for hd in range(8)] + ["TPb.%d" % hd for hd in range(8)]
                                PSK = ["PSa.%d" % hd for hd in range(8)] + ["PSb.%d" % hd for hd in range(8)]
                                PR = posr[0]
                                PC = posc[0]
                                pflat = PS_.t[:].rearrange("p h k -> p (h k)")
                                dve(lambda v: v.tensor_single_scalar(out=PR.t[:], in_=pflat, scalar=4, op=ALU.logical_shift_right), r=PSK, w=[PR])
                                dve(lambda v: v.tensor_single_scalar(out=PC.t[:], in_=pflat, scalar=15, op=ALU.bitwise_and), r=PSK, w=[PC])
                                PRF = posrf[0]
                                PCF = poscf[0]
                                dve(lambda v: v.tensor_copy(out=PRF.t[:], in_=PR.t[:]), r=[PR], w=[PRF])
                                dve(lambda v: v.tensor_copy(out=PCF.t[:], in_=PC.t[:]), r=[PC], w=[PCF])
                                OH = Tile(SS_.t[:].rearrange("p a c -> p (a c)").rearrange("p (k r) -> p k r", r=16), SS_.k)
                                IJ = ijg[0]
                                for (PF_, slot, half) in ((PRF, 0, 0), (PCF, 1, 1)):
                                    dve(lambda v: v.tensor_tensor(out=OH.t[:], in0=fdims(PF_.t[:], [[1, 128], [0, 16]]),
                                                                  in1=fdims(iota16.t[:], [[0, 128], [1, 16]]), op=ALU.is_equal),
                                        r=[PF_, iota16], w=[OH] + ["OHm.%d" % hd for hd in range(8)])
                                    for hd in range(8):
                                        dve(lambda v: v.tensor_tensor(out=OH.t[:, hd * 16:(hd + 1) * 16, :], in0=OH.t[:, hd * 16:(hd + 1) * 16, :],
                                                                      in1=fdims(I12F.t[:, half * 8 + hd, :], [[0, 16], [1, 16]]), op=ALU.mult),
                                            r=[OH, I12F], w=["OHm.%d" % hd])
                                    dve(lambda v: v.tensor_reduce(out=IJ.t[:, slot, :], in_=OH.t[:], axis=AX.X, op=ALU.add),
                                        r=["OHm.%d" % hd for hd in range(8)], w=["IJ.%d" % slot])
                                GS = gsm[0]
                                gwv = IJ.t[:, 2, :].rearrange("p (h k) -> p h k", k=16)
                                dve(lambda v: v.tensor_tensor(out=gwv, in0=TP.t[:], in1=fdims(TP.t[:, :, 0], [[16, 8], [0, 16]]), op=ALU.subtract),
                                    r=TPK, w=[IJ])
                                act(lambda a: a.activation(out=IJ.t[:, 2, :], in_=IJ.t[:, 2, :], func=AF.Exp), r=[IJ], w=[IJ])
                                dve(lambda v: v.tensor_reduce(out=GS.t[:, 0:8], in_=gwv, axis=AX.X, op=ALU.add), r=[IJ], w=[GS])
                                dve(lambda v: v.reciprocal(out=GS.t[:, 8:16], in_=GS.t[:, 0:8]), r=[GS], w=[GS])
                                dve(lambda v: v.tensor_tensor(out=gwv, in0=gwv, in1=fdims(GS.t[:, 8:16], [[1, 8], [0, 16]]), op=ALU.mult),
                                    r=[IJ, GS], w=[IJ])
                                for c in range(3):
                                    pe(lambda t: t.transpose(out=psF[:, 4, c * 128:(c + 1) * 128], in_=IJ.t[:, c, :], identity=identF.t[:]),
                                       r=[IJ, "IJ.0", "IJ.1", identF], w=[PF[4]], inc=(c == 2))
                                act(lambda a: a.copy(out=IJG.t[:, :, tl * 128:(tl + 1) * 128], in_=psF[:, 4, 0:384].rearrange("p (a c) -> p a c", a=3)),
                                    r=[PF[4]], w=["IJG.%d" % tl])
                            s1.close()
                            fw.barrier()
                            s2 = contextlib.ExitStack()
                            cur[0] = s2
                            GT = sb(s2, "GT", [128, TG, 128], BF16)
                            wtb = rot("wtb", [128, TG], BF16, n=3)
                            ub = rot("ub", [128, 8, NI * 128], BF16, n=2)
                            vb = rot("vb", [128, NI, D], BF16, n=2)
                            NS = 16
                            r1s = rot("r1s", [128, NS, 128], BF16, n=2)
                            r2s = rot("r2s", [128, NS, 128], BF16, n=2)
                            atb = rot("atb", [128, TG], BF16, n=2)

                            first_touch = not cast_done[l]
                            cast_done[l] = True

                            def load_uv(ig):
                                U = nxt(ub)
                                V = nxt(vb)
                                if first_touch:
                                    ld(U.k, U.t[:], puT[:, ig * NI * 128:(ig + 1) * NI * 128].rearrange("(c p) e -> p c e", p=128), w=[U], q='pool')
                                    ld(V.k, V.t[:], pvv[ig * NI * 128:(ig + 1) * NI * 128, :].rearrange("(a p) d -> p a d", p=128), w=[V], q='pool')
                                    ld(U.k + "s", ubf[l, ig], U.t[:].rearrange("p c e -> p (c e)"), r=[U])
                                    ld(V.k + "s", vbf[l, ig], V.t[:].rearrange("p a d -> p (a d)"), r=[V])
                                else:
                                    ld(U.k, U.t[:].rearrange("p c e -> p (c e)"), ubf[l, ig], w=[U])
                                    ld(V.k, V.t[:].rearrange("p a d -> p (a d)"), vbf[l, ig], w=[V])
                                return U, V
                            pre = [load_uv(0), load_uv(1)]
                            gb = 0
                            for sbk in range(TG // NS):
                                t0 = sbk * NS
                                R1 = nxt(r1s)
                                R2 = nxt(r2s)
                                for t in range(NS):
                                    dve(lambda v: v.tensor_scalar(out=R2.t[:, t, :], in0=iota128b.t[:], scalar1=IJG.t[:, 1, t0 + t:t0 + t + 1], scalar2=None,
                                                                  op0=ALU.is_equal), r=[iota128b], w=["%s.%d" % (R2.k, t)])
                                for t in range(NS):
                                    dve(lambda v: v.tensor_scalar(out=R1.t[:, t, :], in0=iota128b.t[:], scalar1=IJG.t[:, 0, t0 + t:t0 + t + 1],
                                                                  scalar2=IJG.t[:, 2, t0 + t:t0 + t + 1], op0=ALU.is_equal, op1=ALU.mult),
                                        r=[iota128b], w=["%s.%d" % (R1.k, t)])
                                for q4 in range(NS // 4):
                                    bk = gb % 4
                                    gb += 1
                                    for tt in range(4):
                                        t = q4 * 4 + tt
                                        pe(lambda te: te.matmul(psF[:, bk, tt * 128:(tt + 1) * 128], lhsT=R2.t[:, t, :], rhs=R1.t[:, t, :],
                                                                start=True, stop=True), r=["%s.%d" % (R1.k, t), "%s.%d" % (R2.k, t)], w=[PF[bk]], inc=(tt == 3))
                                    tb = t0 + q4 * 4
                                    act(lambda a: a.copy(out=GT.t[:, tb:tb + 4, :], in_=psF[:, bk, :].rearrange("p (t i) -> p t i", t=4)),
                                        r=[PF[bk]], w=[GT])
                            def vside(i, V, ii, WT):
                                for tt in range(TGT):
                                    for hf in range(2):
                                        pe(lambda te: te.matmul(psF[:, tt * 2 + hf, :], lhsT=WT.t[:, tt * 128:(tt + 1) * 128],
                                                                rhs=V.t[:, ii, hf * 512:(hf + 1) * 512], start=(i == 0), stop=(i == 127)),
                                           r=[WT, V], w=[PF[tt * 2 + hf]], inc=(i == 127 or (tt == TGT - 1 and hf == 1)))
                            prev = None
                            for ig in range(128 // NI):
                                U, V = pre[ig] if ig < 2 else load_uv(ig)
                                for ii in range(NI):
                                    i = ig * NI + ii
                                    ab = 4 + (i % 2)
                                    for kc in range(8):
                                        pe(lambda te: te.matmul(psF[:, ab, 0:TG], lhsT=U.t[:, kc, ii * 128:(ii + 1) * 128], rhs=h2Tg.t[:, kc, :],
                                                                start=(kc == 0), stop=(kc == 7)),
                                           r=[U, "h2T.0", "h2T.1"], w=[PF[ab]], inc=(kc == 7))
                                    AT = nxt(atb)
                                    act(lambda a: a.activation(out=AT.t[:], in_=psF[:, ab, 0:TG], func=AF.Gelu_apprx_tanh), r=[PF[ab]], w=[AT])
                                    WT = nxt(wtb)
                                    dve(lambda v: v.tensor_tensor(out=WT.t[:], in0=GT.t[:, :, i], in1=AT.t[:], op=ALU.mult),
                                        r=[AT, GT], w=[WT])
                                    if prev is not None:
                                        vside(*prev)
                                    prev = (i, V, ii, WT)
                            vside(*prev)
                            for tl in range(TGT):
                                ti = g0 + tl
                                gt = tile0 + ti
                                X = nxt(xts)
                                ld(X.k, X.t[:], xsc[l, 0, gt * 128:(gt + 1) * 128, :], w=[X])
                                Y = nxt(ybuf)
                                dve(lambda v: v.tensor_tensor(out=Y.t[:].rearrange("p (a c) -> p a c", a=2), in0=psF[:, tl * 2:tl * 2 + 2, :],
                                                              in1=G2.t[:].rearrange("p (a c) -> p a c", a=2), op=ALU.mult),
                                    r=[PF[tl * 2], PF[tl * 2 + 1], G2], w=[Y])
                                dve(lambda v: v.scalar_tensor_tensor(out=Y.t[:], in0=X.t[:], scalar=ALPHA, in1=Y.t[:], op0=ALU.mult, op1=ALU.add),
                                    r=[X, Y], w=[Y])
                                S = ln_stats(Y.t[:], [Y])
                                X2 = nxt(x2b)
                                act(lambda a: a.activation(out=X2.t[:], in_=Y.t[:], func=AF.Identity, scale=S.t[:, 4:5], bias=S.t[:, 5:6]),
                                    r=[Y, S], w=[X2])
                                dve(lambda v: v.tensor_tensor(out=X2.t[:], in0=X2.t[:], in1=lnb.t[:, 0, :], op=ALU.mult), r=[X2, lnb], w=[X2])
                                dve(lambda v: v.tensor_tensor(out=X2.t[:], in0=X2.t[:], in1=lnb.t[:, 1, :], op=ALU.add), r=[X2, lnb], w=[X2])
                                if l == NL - 1:
                                    if gt < 8:
                                        dst = yp[gt * 128:(gt + 1) * 128, :]
                                    else:
                                        dst = ys[(gt - 8) * 128:(gt - 7) * 128, :]
                                    ld(X2.k + "st", dst, X2.t[:], r=[X2])
                                    if dbg:
                                        ld(X2.k + "st", xsc[l, 1, gt * 128:(gt + 1) * 128, :], X2.t[:], r=[X2], w=["xsc1.%d" % gt])
                                else:
                                    ld(X2.k + "st", xsc[l, 1, gt * 128:(gt + 1) * 128, :], X2.t[:], r=[X2], w=["xsc1.%d" % gt])
                            s2.close()
                            fw.barrier()
                    fw.barrier()
        except _Stop:
            pass
        fw.finish()
        build_nc.ninst = fw.ninst
        build_nc.marks = fw.marks
    return nc


def _rope_tables():
    t = np.arange(SS)
    row = (t // 64).astype(np.float32)
    col = (t % 64).astype(np.float32)

    def tab(m, nh):
        freqs = (10000.0 ** (-np.arange(m, dtype=np.float32) / m)).astype(np.float32)
        ar = row[:, None] * freqs[None, :]
        ac = col[:, None] * freqs[None, :]
        cr, sr, cc, sn = np.cos(ar), np.sin(ar), np.cos(ac), np.sin(ac)
        C = np.concatenate([cr, cr, cc, cc], axis=1)
        S = np.concatenate([-sr, sr, -sn, sn], axis=1)
        return (np.tile(C, (1, nh)).astype(np.float32), np.tile(S, (1, nh)).astype(np.float32))
    C16, S16 = tab(16, 10)
    C8, S8 = tab(8, 5)
    return C16, S16, C8, S8


def make_in_maps(inp):
    f = lambda a: np.ascontiguousarray(np.asarray(a, dtype=np.float32))
    C16, S16, C8, S8 = _rope_tables()
    w_mod = f(inp["w_mod"])
    b_mod = f(inp["b_mod"])
    b_modT = f(b_mod.reshape(NL, 48, 128).transpose(0, 2, 1))
    gqk = f(np.concatenate([np.tile(inp["attn_q_norm"], (1, 8)), np.tile(inp["attn_k_norm"], (1, 2))], axis=1))
    wsT = f(np.asarray(inp["gmlp_ws"]).transpose(0, 3, 1, 2))
    bsT = f(np.asarray(inp["gmlp_b"]).transpose(0, 2, 1))
    lnp = f(np.stack([inp["ln1_g"], inp["ln1_b"], inp["ln2_g"], inp["ln2_b"]], axis=1))
    pwq = f(np.asarray(inp["peer_wq"]).reshape(NL, D, 8, 2, 128).transpose(0, 1, 3, 2, 4).reshape(NL, D, 2048))
    k12T = f(np.stack([np.asarray(inp["peer_k1"]).transpose(0, 2, 1), np.asarray(inp["peer_k2"]).transpose(0, 2, 1)], axis=2))
    shared = {
        "w_mod": w_mod, "b_modT": b_modT, "b_mod": b_mod, "w_in": f(inp["w_in"]), "gqk": gqk,
        "mqn": f(inp["mla_q_norm"]), "mkvn": f(inp["mla_kv_norm"]), "w_uq": f(inp["w_uq"]), "w_ukv": f(inp["w_ukv"]),
        "wsT": wsT, "bsT": bsT, "w_o": f(inp["w_o"]), "lnp": lnp, "peer_wq": pwq, "k12T": k12T,
        "puT0": f(np.asarray(inp["peer_u"][0]).T), "puT1": f(np.asarray(inp["peer_u"][1]).T), "pv0": f(inp["peer_v"][0]), "pv1": f(inp["peer_v"][1]),
        "ropeC": C16, "ropeS": S16, "ropeMC": C8, "ropeMS": S8,
    }
    maps = []
    xpr = np.asarray(inp["x_prompt"], dtype=np.float32)
    xsa = np.asarray(inp["x_sample"], dtype=np.float32)
    cctx = np.asarray(inp["c_ctx"], dtype=np.float32)
    for c in range(NCORES):
        m = dict(shared)
        m["xp"] = f(xpr[PB * c:PB * (c + 1)].reshape(PB * PS, D))
        m["xs"] = f(xsa[c])
        m["ck"] = f(np.asarray(inp["cache_attn_k"])[c].reshape(NL, PAST, 128))
        m["cv"] = f(np.asarray(inp["cache_attn_v"])[c].reshape(NL, PAST, 128))
        m["cckv"] = f(np.asarray(inp["cache_mla_ckv"])[c])
        m["ckr"] = f(np.asarray(inp["cache_mla_krope"])[c])
        cc = np.stack([cctx, np.asarray(inp["c"], dtype=np.float32)[c]], axis=-1)
        m["cT"] = f(cc.reshape(8, 128, 2).transpose(1, 0, 2))
        maps.append(m)
    return maps


_NC_CACHE = {}


def kernel(**inputs):
    if "nc" not in _NC_CACHE:
        _NC_CACHE["nc"] = build_nc(False)
    nc = _NC_CACHE["nc"]
    maps = make_in_maps(inputs)
    res = run_bass_kernel_spmd(nc, maps, core_ids=list(range(NCORES)))
    R = res.results
    y_prompt = np.concatenate([R[c]["yp"].reshape(PB, PS, D) for c in range(NCORES)], axis=0)
    y_sample = np.stack([R[c]["ys"] for c in range(NCORES)], axis=0)
    nk = np.concatenate([R[c]["nk"].reshape(PB, NL, PS, 2, 64) for c in range(NCORES)], axis=0)
    nv = np.concatenate([R[c]["nv"].reshape(PB, NL, PS, 2, 64) for c in range(NCORES)], axis=0)
    nckv = np.concatenate([R[c]["nckv"] for c in range(NCORES)], axis=0)
    nkr = np.concatenate([R[c]["nkr"] for c in range(NCORES)], axis=0)
    return (y_prompt.astype(np.float32), y_sample.astype(np.float32), nk.astype(np.float32), nv.astype(np.float32),
            nckv.astype(np.float32), nkr.astype(np.float32))
```

```python
import contextlib
import math
import os

import numpy as np
import concourse.bass as bass
import concourse.mybir as mybir
from concourse.bass_utils import run_bass_kernel_spmd

F32 = mybir.dt.float32
BF16 = mybir.dt.bfloat16
I32 = mybir.dt.int32
U32 = mybir.dt.uint32
AF = mybir.ActivationFunctionType
ALU = mybir.AluOpType
AX = mybir.AxisListType

D = 1024
NL = 2
IN_W = 1696
EPS = 1e-6
ALPHA = (2.0 * NL) ** 0.25
NCORES = 8
PB = 4
PS = 256
SS = 2048
PAST = 256
NTILES = (PB * PS + SS) // 128


class _Stop(Exception):
    pass


class Tile:
    def __init__(self, t, k):
        self.t = t
        self.k = k


def _keys(lst):
    out = []
    for x in lst:
        out.append(x.k if isinstance(x, Tile) else x)
    return out


class FW:
    def __init__(self, nc, es):
        self.nc = nc
        self.es = es
        self.eng = {'pe': nc.tensor, 'act': nc.scalar, 'dve': nc.vector, 'pool': nc.gpsimd, 'sp': nc.sync}
        self.sem = {}
        self.cnt = {}
        self.semobj = {}
        for e in self.eng:
            self.sem[e] = es.enter_context(nc.semaphore("sem_" + e))
            self.cnt[e] = 0
            self.semobj["sem_" + e] = self.sem[e]
        self.dsem = {}
        self.dall = []
        self.dfree = {'hw': [], 'sw': []}
        self.waited = {e: {} for e in self.eng}
        self.lastw = {}
        self.readers = {}
        self.ninst = 0
        self.nops = 0
        self.limit = int(os.environ["MK_LIMIT"]) if os.environ.get("MK_LIMIT") else None
        self.marks = []

    def mark(self, name):
        self.marks.append((name, self.nops))

    def _wait(self, e, ev):
        if ev is None:
            return
        name, val = ev
        if e == 'pe' and name == 'sem_pe':
            return
        w = self.waited[e]
        if w.get(name, 0) >= val:
            return
        w[name] = val
        self.eng[e].wait_ge(self.semobj[name], val)
        self.ninst += 1

    def _deps(self, e, reads, writes):
        for k in reads:
            self._wait(e, self.lastw.get(k))
        for k in writes:
            self._wait(e, self.lastw.get(k))
            for ev in self.readers.get(k, {}).values():
                self._wait(e, ev)

    def _record(self, ev, reads, writes):
        for k in reads:
            self.readers.setdefault(k, {})[ev[0]] = ev
        for k in writes:
            self.lastw[k] = ev
            self.readers[k] = {}

    def op(self, e, fn, r=(), w=(), inc=True):
        self.nops += 1
        if self.limit is not None and self.nops > self.limit:
            return None
        reads = _keys(r)
        writes = _keys(w)
        if e != 'pe':
            ps = [k for k in reads if k.startswith("ps")]
            if ps:
                reads = [k for k in reads if not k.startswith("ps")]
                writes = list(writes) + ps
        self._deps(e, reads, writes)
        inst = fn(self.eng[e])
        self.ninst += 1
        name = "sem_" + e
        if inc:
            self.cnt[e] += 1
            inst.then_inc(self.sem[e], 1)
            ev = (name, self.cnt[e])
        else:
            ev = (name, self.cnt[e] + 1)
        self._record(ev, reads, writes)
        return inst

    def dma(self, q, key, fn, r=(), w=()):
        self.nops += 1
        if self.limit is not None and self.nops > self.limit:
            return None
        reads = _keys(r)
        writes = _keys(w)
        self._deps(q, reads, writes)
        cls = 'sw' if q == 'pool' else 'hw'
        key = cls + ":" + key
        if key not in self.dsem:
            if self.dfree[cls]:
                d = self.dfree[cls].pop()
            else:
                nm = "dsem_%d" % len(self.dall)
                sm = self.es.enter_context(self.nc.semaphore(nm))
                self.semobj[nm] = sm
                d = [sm, 0, nm, cls]
                self.dall.append(d)
            self.dsem[key] = d
        d = self.dsem[key]
        inst = fn(self.eng[q])
        self.ninst += 1
        d[1] += 16
        inst.then_inc(d[0], 16)
        ev = (d[2], d[1])
        self._record(ev, reads, writes)
        return inst

    def _all_events(self):
        evs = [("sem_" + e, self.cnt[e]) for e in self.eng if self.cnt[e] > 0]
        evs += [(d[2], d[1]) for d in self.dall if d[1] > 0]
        return evs

    def barrier(self):
        evs = self._all_events()
        for e in self.eng:
            for ev in evs:
                self._wait(e, ev)
        self.lastw = {}
        self.readers = {}
        for d in self.dsem.values():
            self.dfree[d[3]].append(d)
        self.dsem = {}

    def finish(self):
        for ev in self._all_events():
            self._wait('sp', ev)


def mkap(ap, dims, off=0):
    return bass.AP(ap.tensor, ap.offset + off, [list(x) for x in dims])


def fdims(ap, dims, off=0):
    base = list(ap.ap)
    return bass.AP(ap.tensor, ap.offset + off, [list(base[0])] + [list(x) for x in dims])


def build_nc(dbg=False):
    nc = bass.Bass("TRN2", target_bir_lowering=False)

    def din(name, shape, dt=F32):
        return nc.dram_tensor(name, shape, dt, kind="ExternalInput").ap()

    def dout(name, shape, dt=F32):
        return nc.dram_tensor(name, shape, dt, kind="ExternalOutput").ap()

    xp = din("xp", [PB * PS, D])
    xs = din("xs", [SS, D])
    ck = din("ck", [NL, PAST, 128])
    cv = din("cv", [NL, PAST, 128])
    cckv = din("cckv", [NL, PAST, 128])
    ckr = din("ckr", [NL, PAST, 32])
    cT = din("cT", [128, 8, 2])
    w_mod = din("w_mod", [NL, D, 6 * D])
    b_modT = din("b_modT", [NL, 128, 48])
    b_mod = din("b_mod", [NL, 6 * D])
    w_in = din("w_in", [NL, D, IN_W])
    gqk = din("gqk", [NL, 640])
    mqn = din("mqn", [NL, 256])
    mkvn = din("mkvn", [NL, 128])
    w_uq = din("w_uq", [NL, 256, 384])
    w_ukv = din("w_ukv", [NL, 128, 512])
    wsT = din("wsT", [NL, 128, 4, 128])
    bsT = din("bsT", [NL, 128, 4])
    w_o = din("w_o", [NL, D, D])
    lnp = din("lnp", [NL, 4, D])
    peer_wq = din("peer_wq", [NL, D, 2048])
    k12T = din("k12T", [NL, 128, 2, 128])
    puTs = [din("puT%d" % l, [D, 16384]) for l in range(NL)]
    pv = [din("pv%d" % l, [16384, D]) for l in range(NL)]
    ropeC = din("ropeC", [SS, 640])
    ropeS = din("ropeS", [SS, 640])
    ropeMC = din("ropeMC", [SS, 160])
    ropeMS = din("ropeMS", [SS, 160])

    yp = dout("yp", [PB * PS, D])
    ys = dout("ys", [SS, D])
    nk = dout("nk", [PB, NL, PS, 128])
    nv = dout("nv", [PB, NL, PS, 128])
    nckv = dout("nckv", [PB, NL, PS, 128])
    nkr = dout("nkr", [PB, NL, PS, 32])
    if dbg:
        xsc = dout("xsc", [NL, 2, NTILES * 128, D])
    else:
        xsc = nc.dram_tensor("xsc", [NL, 2, NTILES * 128, D], F32, kind="Internal").ap()
    gsc = nc.dram_tensor("gsc", [NL, 2, 4, 128, D], F32, kind="Internal").ap()
    NI = 2
    NIG = 128 // NI
    ubf = nc.dram_tensor("ubf", [NL, NIG, 128, 8 * NI * 128], BF16, kind="Internal").ap()
    vbf = nc.dram_tensor("vbf", [NL, NIG, 128, NI * D], BF16, kind="Internal").ap()
    cast_done = [False] * NL
    wscr = {"win": nc.dram_tensor("winb", [NL, 128, 8 * IN_W], BF16, kind="Internal").ap(),
            "wo": nc.dram_tensor("wob", [NL, 128, 8 * D], BF16, kind="Internal").ap(),
            "wq": nc.dram_tensor("wqb", [NL, 128, 8 * 2048], BF16, kind="Internal").ap()}
    wdone = {}

    uid = [0]

    with contextlib.ExitStack() as es:
        fw = FW(nc, es)

        def sb(stack, name, shape, dt=F32):
            uid[0] += 1
            nm = "%s_%d" % (name, uid[0])
            return Tile(stack.enter_context(nc.sbuf_tensor(nm, shape, dt)), nm)

        def dve(fn, r=(), w=(), inc=True):
            return fw.op('dve', fn, r, w, inc)

        def act(fn, r=(), w=(), inc=True):
            return fw.op('act', fn, r, w, inc)

        def pe(fn, r=(), w=(), inc=True):
            return fw.op('pe', fn, r, w, inc)

        def pool(fn, r=(), w=(), inc=True):
            return fw.op('pool', fn, r, w, inc)

        def ld(key, out_ap, in_ap, r=(), w=(), q='sp'):
            return fw.dma(q, key, lambda e: e.dma_start(out=out_ap, in_=in_ap), r, w)

        psF = es.enter_context(nc.psum_tensor("psF", [128, 6, 512], F32))
        psB = es.enter_context(nc.psum_tensor("psB", [128, 2, 1024], BF16))
        PF = ["psF%d" % i for i in range(6)]
        PBK = ["psB0", "psB1"]

        ident = sb(es, "ident", [128, 128], BF16)
        pool(lambda p: p.memset(ident.t[:], 0.0), w=[ident])
        pool(lambda p: p.affine_select(out=ident.t[:], in_=ident.t[:], pattern=[[-1, 128]],
                                       compare_op=ALU.not_equal, fill=1.0, base=0, channel_multiplier=1),
             r=[ident], w=[ident])
        epsb = sb(es, "epsb", [128, 1])
        dve(lambda v: v.memset(epsb.t[:], EPS), w=[epsb])
        iota16 = sb(es, "iota16", [128, 16])
        iota16i = sb(es, "iota16i", [128, 16], I32)
        pool(lambda p: p.iota(iota16i.t[:], pattern=[[1, 16]], base=0, channel_multiplier=0), w=[iota16i])
        dve(lambda v: v.tensor_copy(out=iota16.t[:], in_=iota16i.t[:]), r=[iota16i], w=[iota16])
        identF = sb(es, "identF", [128, 128])
        pool(lambda p: p.memset(identF.t[:], 0.0), w=[identF])
        pool(lambda p: p.affine_select(out=identF.t[:], in_=identF.t[:], pattern=[[-1, 128]],
                                       compare_op=ALU.not_equal, fill=1.0, base=0, channel_multiplier=1),
             r=[identF], w=[identF])
        iota128 = sb(es, "iota128", [128, 128])
        iota128i = sb(es, "iota128i", [128, 128], I32)
        pool(lambda p: p.iota(iota128i.t[:], pattern=[[1, 128]], base=0, channel_multiplier=0), w=[iota128i])
        dve(lambda v: v.tensor_copy(out=iota128.t[:], in_=iota128i.t[:]), r=[iota128i], w=[iota128])
        iota128b = sb(es, "iota128b", [128, 128], BF16)
        dve(lambda v: v.tensor_copy(out=iota128b.t[:], in_=iota128i.t[:]), r=[iota128i], w=[iota128b])
        modcol = sb(es, "modcol", [128, NL, 48, 2])

        with contextlib.ExitStack() as ps0:
            cTs = sb(ps0, "cTs", [128, 8, 2])
            scT = sb(ps0, "scT", [128, 8, 2], BF16)
            scR = sb(ps0, "scR", [128, 8, 2, 128], BF16)
            ld("cTs", cTs.t[:], cT, w=[cTs])
            act(lambda a: a.activation(out=scT.t[:], in_=cTs.t[:], func=AF.Silu), r=[cTs], w=[scT])
            dve(lambda v: v.tensor_copy(out=scR.t[:].rearrange("p a g m -> p (a g) m"),
                                        in_=fdims(scT.t[:], [[1, 16], [0, 128]])), r=[scT], w=[scR])
            bmT = sb(ps0, "bmT", [128, NL, 48])
            for l in range(NL):
                ld("bmT", bmT.t[:, l, :], b_modT[l], w=[bmT])
            wm = [sb(ps0, "wm%d" % i, [128, 8, 512], BF16) for i in range(2)]
            bmb = [sb(ps0, "bmb%d" % i, [128, 512]) for i in range(2)]
            gst = [sb(ps0, "gst%d" % i, [128, 512]) for i in range(2)]
            it = 0
            for l in range(NL):
                for ci in range(12):
                    role = ci // 2
                    W = wm[it % 2]
                    ld(W.k, W.t[:], w_mod[l, :, ci * 512:(ci + 1) * 512].rearrange("(c p) n -> p c n", p=128),
                       w=[W], q='pool')
                    if role in (0, 1, 3, 4):
                        for b4 in range(4):
                            blk = ci * 4 + b4
                            for kc in range(8):
                                pe(lambda t: t.matmul(psF[:, 4, b4 * 2:b4 * 2 + 2], lhsT=W.t[:, kc, b4 * 128:(b4 + 1) * 128],
                                                      rhs=scT.t[:, kc, :], start=(kc == 0), stop=(kc == 7)),
                                   r=[W, scT], w=[PF[4]], inc=(kc == 7))
                            addc = 1.0 if role in (1, 4) else 0.0
                            dve(lambda v: v.scalar_tensor_tensor(out=modcol.t[:, l, blk, :], in0=psF[:, 4, b4 * 2:b4 * 2 + 2],
                                                                 scalar=addc, in1=fdims(bmT.t[:, l, blk:blk + 1], [[0, 2]]),
                                                                 op0=ALU.add, op1=ALU.add),
                                r=[PF[4], bmT], w=[modcol])
                    if role in (2, 3, 4, 5):
                        slot = {2: 0, 4: 1, 3: 2, 5: 3}[role]
                        Bb = bmb[it % 2]
                        ld(Bb.k, Bb.t[:], mkap(b_mod[l, ci * 512:(ci + 1) * 512], [[0, 128], [1, 512]]), w=[Bb])
                        for g in range(2):
                            pb = PF[2 + g]
                            for kc in range(8):
                                pe(lambda t: t.matmul(psF[:, 2 + g, :], lhsT=scR.t[:, kc, g, :], rhs=W.t[:, kc, :],
                                                      start=(kc == 0), stop=(kc == 7)),
                                   r=[W, scR], w=[pb], inc=(kc == 7))
                            G = gst[g]
                            addc = 1.0 if role == 4 else 0.0
                            dve(lambda v: v.scalar_tensor_tensor(out=G.t[:], in0=psF[:, 2 + g, :], scalar=addc, in1=Bb.t[:],
                                                                 op0=ALU.add, op1=ALU.add),
                                r=[pb, Bb], w=[G])
                            half = ci % 2
                            ld("gst%d" % g, gsc[l, g, slot, :, half * 512:(half + 1) * 512], G.t[:], r=[G], w=["gsc"])
                    it += 1
        fw.barrier()

        def chk(tag):
            if dbg and os.environ.get("MK_STOP") == tag:
                raise _Stop()

        seqs = [(2 * b, 2, 0, False, b) for b in range(PB)] + [(8, 16, 1, True, -1)]
        if dbg and os.environ.get("MK_SEQS"):
            seqs = [seqs[int(i)] for i in os.environ["MK_SEQS"].split(",")]
        nlayers = int(os.environ.get("MK_LAYERS", NL)) if dbg else NL
        skip_peer = bool(dbg and os.environ.get("MK_SKIP_PEER"))
        stopA = bool(dbg and os.environ.get('MK_STOP') == 'A')
        stopB = bool(dbg and os.environ.get('MK_STOP') == 'B')

        def x_src(l, ph, gt):
            if l < 0:
                if gt < 8:
                    return xp[gt * 128:(gt + 1) * 128, :]
                return xs[(gt - 8) * 128:(gt - 7) * 128, :]
            return xsc[l, ph, gt * 128:(gt + 1) * 128, :]

        try:
            chk('pro')
            for (tile0, nt, grp, is_s, pbi) in seqs:
                nkc = nt + (2 if is_s else 0)
                koff = 2 if is_s else 0
                for l in range(nlayers):
                    with contextlib.ExitStack() as sa:
                        NTOK = nt * 128
                        NKEY = nkc * 128
                        QT = sb(sa, "QT", [128, 4, NTOK], BF16)
                        KT = sb(sa, "KT", [128, NKEY], BF16)
                        VA = sb(sa, "VA", [128, nkc, 2, 65], BF16)
                        QMT = sb(sa, "QMT", [128, 4, NTOK], BF16)
                        KMT = sb(sa, "KMT", [128, 4, NKEY], BF16)
                        VM = sb(sa, "VM", [128, nkc, 4, 65], BF16)
                        OC = sb(sa, "OC", [128, nt, 256], BF16)
                        pool(lambda p: p.memset(VA.t[:], 1.0), w=[VA])
                        pool(lambda p: p.memset(VM.t[:], 1.0), w=[VM])

                        sA = contextlib.ExitStack()
                        cur = [sA]
                        gq = sb(sA, "gq", [128, 640])
                        ld(gq.k, gq.t[:], mkap(gqk[l], [[0, 128], [1, 640]]), w=[gq])
                        gmq = sb(sA, "gmq", [128, 256])
                        ld(gmq.k, gmq.t[:], mkap(mqn[l], [[0, 128], [1, 256]]), w=[gmq])
                        gmk = sb(sA, "gmk", [128, 128])
                        ld(gmk.k, gmk.t[:], mkap(mkvn[l], [[0, 128], [1, 128]]), w=[gmk])
                        bst = sb(sA, "bst", [128, 4])
                        ld(bst.k, bst.t[:], bsT[l], w=[bst])
                        win = sb(sA, "win", [128, 8, IN_W], BF16)
                        if ("win", l) not in wdone:
                            wdone[("win", l)] = True
                            for kc in range(8):
                                ld(win.k, win.t[:, kc, :], w_in[l, kc * 128:(kc + 1) * 128, :], w=[win], q='pool')
                            ld(win.k + "s", wscr["win"][l], win.t[:].rearrange("p a c -> p (a c)"), r=[win])
                        else:
                            ld(win.k, win.t[:].rearrange("p a c -> p (a c)"), wscr["win"][l], w=[win])
                        wuq = sb(sA, "wuq", [128, 2, 384], BF16)
                        ld(wuq.k, wuq.t[:], w_uq[l].rearrange("(c p) n -> p c n", p=128), w=[wuq], q='pool')
                        wukv = sb(sA, "wukv", [128, 512], BF16)
                        ld(wukv.k, wukv.t[:], w_ukv[l], w=[wukv], q='pool')
                        wst = sb(sA, "wst", [128, 4, 128], BF16)
                        ld(wst.k, wst.t[:], wsT[l], w=[wst], q='pool')

                        def rot(name, shape, dt=F32, n=2):
                            return [sb(cur[0], name + str(i), shape, dt) for i in range(n)]
                        xts = rot("xt", [128, D])
                        xnb = rot("xnb", [128, D], BF16, n=1)
                        hT = rot("hT", [128, 8, 128], BF16, n=1)
                        pj = rot("pj", [128, IN_W], n=1)
                        sq = rot("sq", [128, 1152], n=1)
                        small = rot("small", [128, 64], n=3)
                        qkn = rot("qkn", [128, 640], n=1)
                        rc_t = rot("ropeC", [128, 640], n=1)
                        rs_t = rot("ropeS", [128, 640], n=1)
                        rmc_t = rot("ropeMC", [128, 160], n=1)
                        rms_t = rot("ropeMS", [128, 160], n=1)
                        tmp640 = rot("tmp640", [128, 640], n=1)
                        tmp640b = rot("tmp640b", [128, 640], n=1)
                        qkb = rot("qkb", [128, 640], BF16)
                        cqb = rot("cqb", [128, 256], BF16)
                        cqT = rot("cqT", [128, 2, 128], BF16)
                        ckvn = rot("ckvn", [128, 128])
                        ckb = rot("ckb", [128, 128], BF16)
                        ckT = rot("ckT", [128, 128], BF16)
                        qm = rot("qm", [128, 4, 96])
                        mr = rot("mr", [128, 160], n=1)
                        mr3 = rot("mr3", [128, 160], n=1)
                        qmb = rot("qmb", [128, 4, 96], BF16)
                        kmb = rot("kmb", [128, 4, 96], BF16)
                        vg = rot("vg", [128, 256], BF16)
                        vtmp = rot("vtmp", [128, 256])
                        cst = rot("cst", [128, 128])
                        cstb = rot("cstb", [128, 128], BF16)
                        krs = rot("krs", [128, 32])
                        rr = {}

                        def nxt(lst):
                            i = rr.get(id(lst), 0)
                            rr[id(lst)] = i + 1
                            return lst[i % len(lst)]

                        def ln_stats(xin, xkeys):
                            S = nxt(small)
                            dve(lambda v: v.bn_stats(out=S.t[:, 8:14], in_=xin[:, 0:512]), r=xkeys, w=[S])
                            dve(lambda v: v.bn_stats(out=S.t[:, 14:20], in_=xin[:, 512:1024]), r=xkeys, w=[S])
                            dve(lambda v: v.bn_aggr(out=S.t[:, 0:2], in_=S.t[:, 8:20]), r=[S], w=[S])
                            act(lambda a: a.activation(out=S.t[:, 2:3], in_=S.t[:, 1:2], func=AF.Sqrt, bias=epsb.t[:], scale=1.0),
                                r=[S, epsb], w=[S])
                            dve(lambda v: v.reciprocal(out=S.t[:, 4:5], in_=S.t[:, 2:3]), r=[S], w=[S])
                            dve(lambda v: v.scalar_tensor_tensor(out=S.t[:, 5:6], in0=S.t[:, 0:1], scalar=-1.0, in1=S.t[:, 4:5],
                                                                 op0=ALU.mult, op1=ALU.mult), r=[S], w=[S])
                            return S

                        tcount = [0]

                        def transposes(srcs, dst_fn, rkeys, wkeys, evac='act'):
                            b = tcount[0] % 2
                            tcount[0] += 1
                            for i, (src, n) in enumerate(srcs):
                                pe(lambda t: t.transpose(out=psB[0:n, b, i * 128:(i + 1) * 128], in_=src, identity=ident.t[:]),
                                   r=list(rkeys) + [ident], w=[PBK[b]], inc=(i == len(srcs) - 1))
                            dst_fn(b)

                        def rope(src, nh, hd, ctab, stab, coff, dsts, rkeys, wkeys):
                            n = nh * hd
                            q4 = hd // 4
                            T1 = nxt(tmp640)
                            T2 = nxt(tmp640b)
                            dve(lambda v: v.tensor_tensor(out=T1.t[:, 0:n], in0=src, in1=ctab.t[:, coff:coff + n], op=ALU.mult),
                                r=list(rkeys) + [ctab], w=[T1])
                            nb = n // (2 * q4)
                            def hv(ap, off, half):
                                return fdims(ap, [[2 * q4, nb], [1, q4]], off + half * q4)
                            dve(lambda v: v.tensor_tensor(out=hv(T2.t[:, 0:n], 0, 0), in0=hv(src, 0, 1), in1=hv(stab.t[:, 0:n], coff, 0),
                                                          op=ALU.mult), r=list(rkeys) + [stab], w=[T2])
                            dve(lambda v: v.tensor_tensor(out=hv(T2.t[:, 0:n], 0, 1), in0=hv(src, 0, 0), in1=hv(stab.t[:, 0:n], coff, 1),
                                                          op=ALU.mult), r=list(rkeys) + [stab], w=[T2])
                            for (oap, i0, i1, j0, j1) in dsts:
                                dve(lambda v: v.tensor_tensor(out=oap, in0=i0(T1.t), in1=i0(T2.t), op=ALU.add), r=[T1, T2], w=wkeys)

                        def mla_keys(ckb_ap, ckb_keys, kr_ap, kr_keys, kc):
                            CT = nxt(ckT)
                            def ev(b):
                                act(lambda a: a.copy(out=CT.t[:], in_=psB[:, b, 0:128]), r=[PBK[b]], w=[CT])
                            transposes([(ckb_ap, 128)], ev, ckb_keys, None)
                            pe(lambda t: t.matmul(psF[:, 5, :], lhsT=CT.t[:], rhs=wukv.t[:], start=True, stop=True),
                               r=[CT, wukv], w=[PF[5]])
                            KB = nxt(kmb)
                            kvv = psF[:, 5, :].rearrange("p (h c) -> p h c", h=4)
                            act(lambda a: a.copy(out=KB.t[:, :, 0:64], in_=kvv[:, :, 0:64]), r=[PF[5]], w=[KB])
                            dve(lambda v: v.tensor_copy(out=VM.t[:, kc, :, 0:64], in_=kvv[:, :, 64:128]), r=[PF[5], VM], w=["VM.%d" % kc])
                            dve(lambda v: v.tensor_copy(out=KB.t[:, :, 64:96], in_=fdims(kr_ap, [[0, 4], [1, 32]])),
                                r=list(kr_keys), w=[KB])
                            def ev2(b):
                                act(lambda a: a.copy(out=KMT.t[0:96, :, kc * 128:(kc + 1) * 128],
                                                     in_=psB[0:96, b, 0:512].rearrange("p (h c) -> p h c", h=4)),
                                    r=[PBK[b]], w=["KMT.%d" % kc])
                            transposes([(KB.t[:, h, :], 96) for h in range(4)], ev2, [KB], None)

                        fw.mark('A-start')
                        if is_s:
                            for j in range(2):
                                C1 = nxt(cst)
                                ld(C1.k, C1.t[:], ck[l, j * 128:(j + 1) * 128, :], w=[C1])
                                CB = nxt(cstb)
                                dve(lambda v: v.tensor_copy(out=CB.t[:], in_=C1.t[:]), r=[C1], w=[CB])
                                def evk(b):
                                    act(lambda a: a.copy(out=KT.t[:, j * 128:(j + 1) * 128], in_=psB[:, b, 0:128]),
                                        r=[PBK[b]], w=["KT.%d" % j])
                                transposes([(CB.t[:], 128)], evk, [CB], None)
                                C2 = nxt(cst)
                                ld(C2.k, C2.t[:], cv[l, j * 128:(j + 1) * 128, :], w=[C2])
                                dve(lambda v: v.tensor_copy(out=VA.t[:, j, :, 0:64], in_=C2.t[:].rearrange("p (h c) -> p h c", h=2)),
                                    r=[C2, VA], w=["VA.%d" % j])
                                C3 = nxt(cst)
                                ld(C3.k, C3.t[:], cckv[l, j * 128:(j + 1) * 128, :], w=[C3])
                                CB3 = nxt(cstb)
                                dve(lambda v: v.tensor_copy(out=CB3.t[:], in_=C3.t[:]), r=[C3], w=[CB3])
                                K4 = nxt(krs)
                                ld(K4.k, K4.t[:], ckr[l, j * 128:(j + 1) * 128, :], w=[K4])
                                mla_keys(CB3.t[:], [CB3], K4.t[:], [K4], j)

                        for ti in range(nt):
                            gt = tile0 + ti
                            kc_new = koff + ti
                            X = nxt(xts)
                            ld(X.k, X.t[:], x_src(l - 1, 1, gt), w=[X])
                            fw.mark('tile%d-ln' % ti)
                            S = ln_stats(X.t[:], [X])
                            XN = nxt(xnb)
                            act(lambda a: a.activation(out=XN.t[:], in_=X.t[:], func=AF.Identity, scale=S.t[:, 4:5], bias=S.t[:, 5:6]),
                                r=[X, S], w=[XN])
                            H = nxt(hT)
                            def evh(b):
                                for c in range(8):
                                    act(lambda a: a.activation(out=H.t[:, c, :], in_=psB[:, b, c * 128:(c + 1) * 128], func=AF.Identity,
                                                               scale=modcol.t[:, l, 8 + c, grp:grp + 1], bias=modcol.t[:, l, c, grp:grp + 1]),
                                        r=[PBK[b], modcol], w=[H])
                            transposes([(XN.t[:, c * 128:(c + 1) * 128], 128) for c in range(8)], evh, [XN], None)
                            fw.mark('proj')
                            segs = [(0, 512), (512, 512), (1024, 512), (1536, 160)]
                            for bi, (c0, cn) in enumerate(segs):
                                for kc in range(8):
                                    pe(lambda t: t.matmul(psF[:, bi, 0:cn], lhsT=H.t[:, kc, :], rhs=win.t[:, kc, c0:c0 + cn],
                                                          start=(kc == 0), stop=(kc == 7)), r=[H, win], w=[PF[bi]], inc=(kc == 7))
                            P = nxt(pj)
                            for bi, (c0, cn) in enumerate(segs):
                                act(lambda a: a.copy(out=P.t[:, c0:c0 + cn], in_=psF[:, bi, 0:cn]), r=[PF[bi]], w=[P])
                            fw.mark('rms')
                            SQ = nxt(sq)
                            dve(lambda v: v.tensor_tensor(out=SQ.t[:], in0=P.t[:, 0:1152], in1=P.t[:, 0:1152], op=ALU.mult), r=[P], w=[SQ])
                            S2 = nxt(small)
                            dve(lambda v: v.tensor_reduce(out=S2.t[:, 0:10], in_=SQ.t[:, 0:640].rearrange("p (h c) -> p h c", c=64),
                                                          axis=AX.X, op=ALU.add), r=[SQ], w=[S2])
                            dve(lambda v: v.tensor_reduce(out=S2.t[:, 10:11], in_=SQ.t[:, 768:1024], axis=AX.X, op=ALU.add), r=[SQ], w=[S2])
                            dve(lambda v: v.tensor_reduce(out=S2.t[:, 11:12], in_=SQ.t[:, 1024:1152], axis=AX.X, op=ALU.add), r=[SQ], w=[S2])
                            dve(lambda v: v.tensor_scalar(out=S2.t[:, 16:26], in0=S2.t[:, 0:10], scalar1=1.0 / 64, scalar2=EPS,
                                                          op0=ALU.mult, op1=ALU.add), r=[S2], w=[S2])
                            dve(lambda v: v.tensor_scalar(out=S2.t[:, 26:27], in0=S2.t[:, 10:11], scalar1=1.0 / 256, scalar2=EPS,
                                                          op0=ALU.mult, op1=ALU.add), r=[S2], w=[S2])
                            dve(lambda v: v.tensor_scalar(out=S2.t[:, 27:28], in0=S2.t[:, 11:12], scalar1=1.0 / 128, scalar2=EPS,
                                                          op0=ALU.mult, op1=ALU.add), r=[S2], w=[S2])
                            act(lambda a: a.activation(out=S2.t[:, 32:44], in_=S2.t[:, 16:28], func=AF.Sqrt), r=[S2], w=[S2])
                            dve(lambda v: v.reciprocal(out=S2.t[:, 48:60], in_=S2.t[:, 32:44]), r=[S2], w=[S2])
                            fw.mark('qk')
                            QK = nxt(qkn)
                            dve(lambda v: v.tensor_tensor(out=QK.t[:].rearrange("p (h c) -> p h c", c=64),
                                                          in0=P.t[:, 0:640].rearrange("p (h c) -> p h c", c=64),
                                                          in1=fdims(S2.t[:, 48:58], [[1, 10], [0, 64]]), op=ALU.mult), r=[P, S2], w=[QK])
                            dve(lambda v: v.tensor_tensor(out=QK.t[:], in0=QK.t[:], in1=gq.t[:], op=ALU.mult), r=[QK, gq], w=[QK])
                            QB = nxt(qkb)
                            def qdst(T, j):
                                return T[:, j * 256:(j + 1) * 256].rearrange("p (s c) -> p s c", c=64)
                            def qout(j):
                                return fdims(QB.t[:], [[128, 4], [1, 64]], j * 64)
                            if is_s:
                                RC = rc_t[0]
                                RS = rs_t[0]
                                ld(RC.k, RC.t[:], ropeC[ti * 128:(ti + 1) * 128, :], w=[RC])
                                ld(RS.k, RS.t[:], ropeS[ti * 128:(ti + 1) * 128, :], w=[RS])
                                dsts = [(qout(0), lambda T: qdst(T, 0), None, 0, 0), (qout(1), lambda T: qdst(T, 1), None, 0, 0),
                                        (QB.t[:, 512:640], lambda T: T[:, 512:640], None, 0, 0)]
                                rope(QK.t[:], 10, 64, RC, RS, 0, dsts, [QK], [QB])
                            else:
                                for j in range(2):
                                    dve(lambda v: v.tensor_copy(out=qout(j), in_=qdst(QK.t, j)), r=[QK], w=[QB])
                                dve(lambda v: v.tensor_copy(out=QB.t[:, 512:640], in_=QK.t[:, 512:640]), r=[QK], w=[QB])
                            def evq(b):
                                act(lambda a: a.copy(out=QT.t[:, :, ti * 128:(ti + 1) * 128],
                                                     in_=psB[:, b, 0:512].rearrange("p (s c) -> p s c", s=4)),
                                    r=[PBK[b]], w=["QT.%d" % ti])
                                act(lambda a: a.copy(out=KT.t[:, kc_new * 128:(kc_new + 1) * 128], in_=psB[:, b, 512:640]),
                                    r=[PBK[b]], w=["KT.%d" % kc_new])
                            transposes([(QB.t[:, i * 128:(i + 1) * 128], 128) for i in range(5)], evq, [QB], None)
                            fw.mark('va')
                            dve(lambda v: v.tensor_copy(out=VA.t[:, kc_new, :, 0:64], in_=P.t[:, 640:768].rearrange("p (h c) -> p h c", h=2)),
                                r=[P, VA], w=["VA.%d" % kc_new])
                            fw.mark('mlaq')
                            CQ = nxt(cqb)
                            dve(lambda v: v.scalar_tensor_tensor(out=CQ.t[:], in0=P.t[:, 768:1024], scalar=S2.t[:, 58:59], in1=gmq.t[:],
                                                                 op0=ALU.mult, op1=ALU.mult), r=[P, S2, gmq], w=[CQ])
                            CQT = nxt(cqT)
                            def evc(b):
                                act(lambda a: a.copy(out=CQT.t[:], in_=psB[:, b, 0:256].rearrange("p (s c) -> p s c", s=2)),
                                    r=[PBK[b]], w=[CQT])
                            transposes([(CQ.t[:, i * 128:(i + 1) * 128], 128) for i in range(2)], evc, [CQ], None)
                            for kc in range(2):
                                pe(lambda t: t.matmul(psF[:, 4, 0:384], lhsT=CQT.t[:, kc, :], rhs=wuq.t[:, kc, :],
                                                      start=(kc == 0), stop=(kc == 1)), r=[CQT, wuq], w=[PF[4]], inc=(kc == 1))
                            QM = nxt(qm)
                            act(lambda a: a.copy(out=QM.t[:].rearrange("p h c -> p (h c)"), in_=psF[:, 4, 0:384]), r=[PF[4]], w=[QM])
                            QMB = nxt(qmb)
                            dve(lambda v: v.tensor_copy(out=QMB.t[:, :, 0:64], in_=QM.t[:, :, 0:64]), r=[QM], w=[QMB])
                            MR = nxt(mr)
                            dve(lambda v: v.tensor_copy(out=MR.t[:, 0:128].rearrange("p (h c) -> p h c", h=4), in_=QM.t[:, :, 64:96]),
                                r=[QM], w=[MR])
                            dve(lambda v: v.tensor_copy(out=MR.t[:, 128:160], in_=P.t[:, 1152:1184]), r=[P], w=[MR])
                            MR3 = nxt(mr3)
                            if is_s:
                                RMC = rmc_t[0]
                                RMS = rms_t[0]
                                ld(RMC.k, RMC.t[:], ropeMC[ti * 128:(ti + 1) * 128, :], w=[RMC])
                                ld(RMS.k, RMS.t[:], ropeMS[ti * 128:(ti + 1) * 128, :], w=[RMS])
                                rope(MR.t[:], 5, 32, RMC, RMS, 0, [(MR3.t[:], lambda T: T[:, 0:160], None, 0, 0)], [MR], [MR3])
                            else:
                                dve(lambda v: v.tensor_copy(out=MR3.t[:], in_=MR.t[:]), r=[MR], w=[MR3])
                            dve(lambda v: v.tensor_copy(out=QMB.t[:, :, 64:96], in_=MR3.t[:, 0:128].rearrange("p (h c) -> p h c", h=4)),
                                r=[MR3], w=[QMB])
                            def evqm(b):
                                act(lambda a: a.copy(out=QMT.t[0:96, :, ti * 128:(ti + 1) * 128],
                                                     in_=psB[0:96, b, 0:512].rearrange("p (h c) -> p h c", h=4)),
                                    r=[PBK[b]], w=["QMT.%d" % ti])
                            transposes([(QMB.t[:, h, :], 96) for h in range(4)], evqm, [QMB], None)
                            fw.mark('mlakv')
                            CK = nxt(ckvn)
                            dve(lambda v: v.scalar_tensor_tensor(out=CK.t[:], in0=P.t[:, 1024:1152], scalar=S2.t[:, 59:60], in1=gmk.t[:],
                                                                 op0=ALU.mult, op1=ALU.mult), r=[P, S2, gmk], w=[CK])
                            CKB = nxt(ckb)
                            dve(lambda v: v.tensor_copy(out=CKB.t[:], in_=CK.t[:]), r=[CK], w=[CKB])
                            mla_keys(CKB.t[:], [CKB], MR3.t[:, 128:160], [MR3], kc_new)
                            fw.mark('gmlp')
                            VT = nxt(vtmp)
                            S3 = nxt(small)
                            vcv = P.t[:, 1440:1696].rearrange("p (g c) -> p g c", g=4)
                            dve(lambda v: v.tensor_reduce(out=S3.t[:, 0:4], in_=vcv, axis=AX.X, op=ALU.add), r=[P], w=[S3])
                            dve(lambda v: v.tensor_tensor(out=VT.t[:], in0=P.t[:, 1440:1696], in1=P.t[:, 1440:1696], op=ALU.mult), r=[P], w=[VT])
                            dve(lambda v: v.tensor_reduce(out=S3.t[:, 4:8], in_=VT.t[:].rearrange("p (g c) -> p g c", g=4), axis=AX.X, op=ALU.add),
                                r=[VT], w=[S3])
                            dve(lambda v: v.tensor_scalar(out=S3.t[:, 8:12], in0=S3.t[:, 0:4], scalar1=1.0 / 64, scalar2=None, op0=ALU.mult),
                                r=[S3], w=[S3])
                            dve(lambda v: v.tensor_tensor(out=S3.t[:, 12:16], in0=S3.t[:, 8:12], in1=S3.t[:, 8:12], op=ALU.mult), r=[S3], w=[S3])
                            dve(lambda v: v.scalar_tensor_tensor(out=S3.t[:, 16:20], in0=S3.t[:, 4:8], scalar=1.0 / 64, in1=S3.t[:, 12:16],
                                                                 op0=ALU.mult, op1=ALU.subtract), r=[S3], w=[S3])
                            dve(lambda v: v.tensor_scalar(out=S3.t[:, 20:24], in0=S3.t[:, 16:20], scalar1=EPS, scalar2=None, op0=ALU.add),
                                r=[S3], w=[S3])
                            act(lambda a: a.activation(out=S3.t[:, 24:28], in_=S3.t[:, 20:24], func=AF.Sqrt), r=[S3], w=[S3])
                            dve(lambda v: v.reciprocal(out=S3.t[:, 28:32], in_=S3.t[:, 24:28]), r=[S3], w=[S3])
                            dve(lambda v: v.tensor_tensor(out=VT.t[:].rearrange("p (g c) -> p g c", g=4), in0=vcv,
                                                          in1=fdims(S3.t[:, 8:12], [[1, 4], [0, 64]]), op=ALU.subtract), r=[P, S3], w=[VT])
                            VG = nxt(vg)
                            dve(lambda v: v.tensor_tensor(out=VG.t[:].rearrange("p (g c) -> p g c", g=4),
                                                          in0=VT.t[:].rearrange("p (g c) -> p g c", g=4),
                                                          in1=fdims(S3.t[:, 28:32], [[1, 4], [0, 64]]), op=ALU.mult), r=[VT, S3], w=[VG])
                            for g in range(4):
                                pe(lambda t: t.matmul(psF[:, 4, g * 64:(g + 1) * 64], lhsT=wst.t[:, g, :], rhs=VG.t[:, g * 64:(g + 1) * 64],
                                                      start=True, stop=True), r=[VG, wst], w=[PF[4]], inc=(g == 3))
                            dve(lambda v: v.tensor_tensor(out=VT.t[:].rearrange("p (g c) -> p g c", g=4),
                                                          in0=psF[:, 4, 0:256].rearrange("p (g c) -> p g c", g=4),
                                                          in1=fdims(bst.t[:], [[1, 4], [0, 64]]), op=ALU.add), r=[PF[4], bst], w=[VT])
                            dve(lambda v: v.tensor_tensor(out=OC.t[:, ti, :], in0=VT.t[:], in1=P.t[:, 1184:1440], op=ALU.mult),
                                r=[VT, P], w=["OC.%d" % ti])
                            fw.mark('outs')
                            if not is_s:
                                r0 = ti * 128
                                ld("o_nk%d" % (ti % 2), nk[pbi, l, r0:r0 + 128, :], QK.t[:, 512:640], r=[QK])
                                ld("o_nv%d" % (ti % 2), nv[pbi, l, r0:r0 + 128, :], P.t[:, 640:768], r=[P])
                                ld("o_nc%d" % (ti % 2), nckv[pbi, l, r0:r0 + 128, :], CK.t[:], r=[CK])
                                ld("o_nr%d" % (ti % 2), nkr[pbi, l, r0:r0 + 128, :], P.t[:, 1152:1184], r=[P])

                        sA.close()
                        fw.barrier()
                        sB = contextlib.ExitStack()
                        cur[0] = sB
                        G1 = sb(sB, "G1", [128, D])
                        ld(G1.k, G1.t[:], gsc[l, grp, 0], w=[G1])
                        lnb = sb(sB, "lnb", [128, 2, D])
                        for j in range(2):
                            ld(lnb.k, lnb.t[:, j, :], mkap(lnp[l, j], [[0, 128], [1, D]]), w=[lnb])
                        wo = sb(sB, "wo", [128, 8, D], BF16)
                        if ("wo", l) not in wdone:
                            wdone[("wo", l)] = True
                            for kc in range(8):
                                ld(wo.k, wo.t[:, kc, :], w_o[l, kc * 128:(kc + 1) * 128, :], w=[wo], q='pool')
                            ld(wo.k + "s", wscr["wo"][l], wo.t[:].rearrange("p a c -> p (a c)"), r=[wo])
                        else:
                            ld(wo.k, wo.t[:].rearrange("p a c -> p (a c)"), wscr["wo"][l], w=[wo])
                        xts = rot("xtB", [128, D])
                        small = rot("smallB", [128, 64], n=3)
                        ntq = 4 if is_s else 2
                        NQ = ntq * 128
                        PT = rot("PT", [128, NQ], BF16, n=3)
                        mix = sb(sB, "mix", [128, ntq, D], BF16)
                        mixT = rot("mixT", [128, 8, 128], BF16)
                        ybuf = rot("ybuf", [128, D], n=1)
                        x1b = rot("x1b", [128, D])
                        rec = rot("rec", [128, 8])
                        sc_ = [0]
                        oc_ = [0]
                        for qg in range(0 if stopA else nt // ntq):
                            q0 = qg * NQ
                            allkeys_q = ["QT.%d" % (qg * ntq + i) for i in range(ntq)]
                            allkeys_qm = ["QMT.%d" % (qg * ntq + i) for i in range(ntq)]
                            def emit_S(hh, kc):
                                isA = hh < 8
                                h = hh if isA else hh - 8
                                sbk = sc_[0] % 3
                                sc_[0] += 1
                                if isA:
                                    base = (h // 4) * 64
                                    pe(lambda t: t.matmul(psF[:, sbk, 0:NQ], lhsT=KT.t[base:base + 64, kc * 128:(kc + 1) * 128],
                                                          rhs=QT.t[base:base + 64, h % 4, q0:q0 + NQ], start=True, stop=True),
                                       r=["KT.%d" % kc] + allkeys_q, w=[PF[sbk]])
                                    return sbk, 0.125
                                pe(lambda t: t.matmul(psF[:, sbk, 0:NQ], lhsT=KMT.t[0:96, h, kc * 128:(kc + 1) * 128],
                                                      rhs=QMT.t[0:96, h, q0:q0 + NQ], start=True, stop=True),
                                   r=["KMT.%d" % kc] + allkeys_qm, w=[PF[sbk]])
                                return sbk, 1.0 / math.sqrt(96.0)
                            its = [(hh, kc) for hh in range(12) for kc in range(nkc)]
                            pend = emit_S(*its[0])
                            for n, (hh, kc) in enumerate(its):
                                isA = hh < 8
                                h = hh if isA else hh - 8
                                if kc == 0:
                                    ob = 3 + (oc_[0] % 2)
                                    oc_[0] += 1
                                    oview = psF[:, ob, 0:ntq * 65].rearrange("p (q c) -> p q c", c=65)
                                sbk, scl = pend
                                if n + 1 < len(its):
                                    pend = emit_S(*its[n + 1])
                                Pt = nxt(PT)
                                act(lambda a: a.activation(out=Pt.t[:], in_=psF[:, sbk, 0:NQ], func=AF.Exp, scale=scl), r=[PF[sbk]], w=[Pt])
                                for qt in range(ntq):
                                    if isA:
                                        rhs = VA.t[:, kc, h // 4, :]
                                        vk = "VA.%d" % kc
                                    else:
                                        rhs = VM.t[:, kc, h, :]
                                        vk = "VM.%d" % kc
                                    pe(lambda t: t.matmul(oview[:, qt, :], lhsT=Pt.t[:, qt * 128:(qt + 1) * 128], rhs=rhs,
                                                          start=(kc == 0 and qt == 0), stop=(kc == nkc - 1 and qt == ntq - 1)),
                                       r=[Pt, vk], w=[PF[ob]], inc=(qt == ntq - 1))
                                if kc == nkc - 1:
                                    R = nxt(rec)
                                    dve(lambda v: v.reciprocal(out=R.t[:, 0:ntq], in_=oview[:, :, 64]), r=[PF[ob]], w=[R])
                                    col = h * 64 if isA else 512 + h * 64
                                    dve(lambda v: v.tensor_tensor(out=mix.t[:, :, col:col + 64], in0=oview[:, :, 0:64],
                                                                  in1=fdims(R.t[:, 0:ntq], [[1, ntq], [0, 64]]), op=ALU.mult),
                                        r=[PF[ob], R], w=[mix])
                            dve(lambda v: v.tensor_copy(out=mix.t[:, :, 768:1024], in_=OC.t[:, qg * ntq:(qg + 1) * ntq, :]),
                                r=["OC.%d" % (qg * ntq + i) for i in range(ntq)], w=[mix])
                            for qt in range(ntq):
                                ti = qg * ntq + qt
                                gt = tile0 + ti
                                MT = nxt(mixT)
                                def evm(b):
                                    act(lambda a: a.copy(out=MT.t[:].rearrange("p a c -> p (a c)"), in_=psB[:, b, :]), r=[PBK[b]], w=[MT])
                                transposes([(mix.t[:, qt, c * 128:(c + 1) * 128], 128) for c in range(8)], evm, [mix], None)
                                for hf in range(2):
                                    for kc in range(8):
                                        pe(lambda t: t.matmul(psF[:, hf, :], lhsT=MT.t[:, kc, :], rhs=wo.t[:, kc, hf * 512:(hf + 1) * 512],
                                                              start=(kc == 0), stop=(kc == 7)), r=[MT, wo], w=[PF[hf]], inc=(kc == 7))
                                X = nxt(xts)
                                ld(X.k, X.t[:], x_src(l - 1, 1, gt), w=[X])
                                Y = nxt(ybuf)
                                dve(lambda v: v.tensor_tensor(out=Y.t[:].rearrange("p (a c) -> p a c", a=2), in0=psF[:, 0:2, :],
                                                              in1=G1.t[:].rearrange("p (a c) -> p a c", a=2), op=ALU.mult),
                                    r=[PF[0], PF[1], G1], w=[Y])
                                dve(lambda v: v.scalar_tensor_tensor(out=Y.t[:], in0=X.t[:], scalar=ALPHA, in1=Y.t[:],
                                                                     op0=ALU.mult, op1=ALU.add), r=[X, Y], w=[Y])
                                S = ln_stats(Y.t[:], [Y])
                                X1 = nxt(x1b)
                                act(lambda a: a.activation(out=X1.t[:], in_=Y.t[:], func=AF.Identity, scale=S.t[:, 4:5], bias=S.t[:, 5:6]),
                                    r=[Y, S], w=[X1])
                                dve(lambda v: v.tensor_tensor(out=X1.t[:], in0=X1.t[:], in1=lnb.t[:, 0, :], op=ALU.mult), r=[X1, lnb], w=[X1])
                                dve(lambda v: v.tensor_tensor(out=X1.t[:], in0=X1.t[:], in1=lnb.t[:, 1, :], op=ALU.add), r=[X1, lnb], w=[X1])
                                ld(X1.k + "st", xsc[l, 0, gt * 128:(gt + 1) * 128, :], X1.t[:], r=[X1], w=["xsc.%d" % gt])
                        sB.close()
                    fw.barrier()

                    with contextlib.ExitStack() as sc:
                        cur[0] = sc
                        G2 = sb(sc, "G2", [128, D])
                        ld(G2.k, G2.t[:], gsc[l, grp, 3], w=[G2])
                        A2 = sb(sc, "A2", [128, D])
                        ld(A2.k, A2.t[:], gsc[l, grp, 1], w=[A2])
                        B2 = sb(sc, "B2", [128, D])
                        ld(B2.k, B2.t[:], gsc[l, grp, 2], w=[B2])
                        lnb = sb(sc, "lnb2", [128, 2, D])
                        for j in range(2):
                            ld(lnb.k, lnb.t[:, j, :], mkap(lnp[l, 2 + j], [[0, 128], [1, D]]), w=[lnb])
                        wq = sb(sc, "wq", [128, 8, 2048], BF16)
                        if ("wq", l) not in wdone:
                            wdone[("wq", l)] = True
                            for kc in range(8):
                                ld(wq.k, wq.t[:, kc, :], peer_wq[l, kc * 128:(kc + 1) * 128, :], w=[wq], q='pool')
                            ld(wq.k + "s", wscr["wq"][l], wq.t[:].rearrange("p a c -> p (a c)"), r=[wq])
                        else:
                            ld(wq.k, wq.t[:].rearrange("p a c -> p (a c)"), wscr["wq"][l], w=[wq])
                        kT = sb(sc, "kT", [128, 2, 128], BF16)
                        ld(kT.k, kT.t[:], k12T[l], w=[kT], q='pool')
                        TGT = 2
                        TG = TGT * 128
                        IJG = sb(sc, "IJG", [128, 3, TG])
                        h2Tg = sb(sc, "h2Tg", [128, 8, TG], BF16)
                        rr = {}

                        def nxt(lst):
                            i = rr.get(id(lst), 0)
                            rr[id(lst)] = i + 1
                            return lst[i % len(lst)]
                        xts = rot("cxt", [128, D])
                        small = rot("csmall", [128, 64], n=3)
                        ybuf = rot("cy", [128, D], n=1)
                        x2b = rot("x2b", [128, D])
                        tcount = [0]
                        puT = puTs[l]
                        pvv = pv[l]
                        for g0 in range(0, 0 if (stopA or stopB) else nt, TGT):
                            s1 = contextlib.ExitStack()
                            cur[0] = s1
                            h2 = rot("h2", [128, D], n=1)
                            xnf = rot("xnf", [128, D], n=1)
                            h2b = rot("h2b", [128, D], BF16, n=1)
                            qTs = rot("qTs", [128, 16, 128], BF16, n=1)
                            sc_s = rot("scs", [128, 16, 128], n=2)
                            sc_r = rot("scr", [128, 16, 128], n=1)
                            v12 = rot("v12", [128, 16, 16], n=1)
                            i12 = rot("i12", [128, 16, 16], U32, n=1)
                            i12f = rot("i12f", [128, 16, 16], n=1)
                            cand = rot("cand", [128, 8, 256], n=1)
                            top = rot("top", [128, 8, 16], n=1)
                            pos = rot("pos", [128, 8, 16], U32, n=1)
                            posr = rot("posr", [128, 128], U32, n=1)
                            posc = rot("posc", [128, 128], U32, n=1)
                            posrf = rot("posrf", [128, 128], n=1)
                            poscf = rot("poscf", [128, 128], n=1)
                            ijg = rot("ijg", [128, 3, 128], n=1)
                            gsm = rot("gsm", [128, 16], n=1)
                            for stage in range(2):
                              for tl in range(TGT):
                                if stage == 0:
                                    ti = g0 + tl
                                    gt = tile0 + ti
                                    X = nxt(xts)
                                    ld(X.k, X.t[:], xsc[l, 0, gt * 128:(gt + 1) * 128, :], w=[X])
                                    S = ln_stats(X.t[:], [X])
                                    XN = xnf[0]
                                    act(lambda a: a.activation(out=XN.t[:], in_=X.t[:], func=AF.Identity, scale=S.t[:, 4:5], bias=S.t[:, 5:6]),
                                        r=[X, S], w=[XN])
                                    H2 = h2[0]
                                    dve(lambda v: v.tensor_tensor(out=H2.t[:], in0=XN.t[:], in1=A2.t[:], op=ALU.mult), r=[XN, A2], w=[H2])
                                    H2B = h2b[0]
                                    dve(lambda v: v.tensor_tensor(out=H2B.t[:], in0=H2.t[:], in1=B2.t[:], op=ALU.add), r=[H2, B2], w=[H2B])
                                    b = tcount[0] % 2
                                    tcount[0] += 1
                                    for c in range(8):
                                        pe(lambda t: t.transpose(out=psB[:, b, c * 128:(c + 1) * 128], in_=H2B.t[:, c * 128:(c + 1) * 128],
                                                                 identity=ident.t[:]), r=[H2B, ident], w=[PBK[b]], inc=(c == 7))
                                    act(lambda a: a.copy(out=h2Tg.t[:, :, tl * 128:(tl + 1) * 128], in_=psB[:, b, :].rearrange("p (a c) -> p a c", a=8)),
                                        r=[PBK[b]], w=["h2T.%d" % tl])
                                    QS = qTs[0]
                                    for c in range(16):
                                        bk = c // 4
                                        for kc in range(8):
                                            pe(lambda t: t.matmul(psF[:, bk, (c % 4) * 128:(c % 4 + 1) * 128], lhsT=wq.t[:, kc, c * 128:(c + 1) * 128],
                                                                  rhs=h2Tg.t[:, kc, tl * 128:(tl + 1) * 128], start=(kc == 0), stop=(kc == 7)),
                                               r=["h2T.%d" % tl, wq], w=[PF[bk]], inc=(kc == 7))
                                    for bk in range(4):
                                        act(lambda a: a.copy(out=QS.t[:, bk * 4:(bk + 1) * 4, :].rearrange("p a c -> p (a c)"), in_=psF[:, bk, :]),
                                            r=[PF[bk]], w=[QS])
                                    SS_ = sc_s[tl]
                                    for c in range(16):
                                        bk = c // 4
                                        pe(lambda t: t.matmul(psF[:, bk, (c % 4) * 128:(c % 4 + 1) * 128], lhsT=QS.t[:, c, :], rhs=kT.t[:, c // 8, :],
                                                              start=True, stop=True), r=[QS, kT], w=[PF[bk]], inc=(c % 4 == 3))
                                    for bk in range(4):
                                        act(lambda a: a.copy(out=SS_.t[:, bk * 4:(bk + 1) * 4, :].rearrange("p a c -> p (a c)"), in_=psF[:, bk, :]),
                                            r=[PF[bk]], w=[SS_])
                                else:
                                    SS_ = sc_s[tl]
                                    SR = sc_r[0]
                                    V12 = v12[0]
                                    I12 = i12[0]
                                    for c in range(16):
                                        dve(lambda v: v.max(out=V12.t[:, c, 0:8], in_=SS_.t[:, c, :]), r=[SS_], w=["V12a.%d" % c])
                                    for c in range(16):
                                        dve(lambda v: v.max_index(out=I12.t[:, c, 0:8], in_max=V12.t[:, c, 0:8], in_values=SS_.t[:, c, :]),
                                            r=[SS_, "V12a.%d" % c], w=["I12a.%d" % c])
                                    for c in range(16):
                                        dve(lambda v: v.match_replace(out=SR.t[:, c, :], in_to_replace=V12.t[:, c, 0:8], in_values=SS_.t[:, c, :],
                                                                      imm_value=-1e30), r=[SS_, "V12a.%d" % c], w=["SR.%d" % c, "CD2.%d" % (c // 2)])
                                    for c in range(16):
                                        dve(lambda v: v.max(out=V12.t[:, c, 8:16], in_=SR.t[:, c, :]), r=["SR.%d" % c], w=["V12b.%d" % c])
                                    for c in range(16):
                                        dve(lambda v: v.max_index(out=I12.t[:, c, 8:16], in_max=V12.t[:, c, 8:16], in_values=SR.t[:, c, :]),
                                            r=["SR.%d" % c, "V12b.%d" % c], w=["I12b.%d" % c])
                                    V12K = ["V12a.%d" % c for c in range(16)] + ["V12b.%d" % c for c in range(16)]
                                    I12K = ["I12a.%d" % c for c in range(16)] + ["I12b.%d" % c for c in range(16)]
                                    SRK = ["SR.%d" % c for c in range(16)]
                                    I12F = i12f[0]
                                    dve(lambda v: v.tensor_copy(out=I12F.t[:], in_=I12.t[:]), r=I12K, w=[I12F])
                                    CD = cand[0]
                                    for hd in range(8):
                                        dve(lambda v: v.tensor_tensor(out=CD.t[:, hd, :].rearrange("p (r c) -> p r c", c=16),
                                                                      in0=fdims(V12.t[:, hd, :], [[1, 16], [0, 16]]),
                                                                      in1=fdims(V12.t[:, 8 + hd, :], [[0, 16], [1, 16]]), op=ALU.add),
                                            r=V12K, w=["CD.%d" % hd])
                                    CD2 = Tile(SR.t[:].rearrange("p a c -> p (a c)").rearrange("p (h c) -> p h c", h=8), SR.k)
                                    TP = top[0]
                                    PS_ = pos[0]
                                    for hd in range(8):
                                        dve(lambda v: v.max(out=TP.t[:, hd, 0:8], in_=CD.t[:, hd, :]), r=["CD.%d" % hd], w=["TPa.%d" % hd])
                                    for hd in range(8):
                                        dve(lambda v: v.max_index(out=PS_.t[:, hd, 0:8], in_max=TP.t[:, hd, 0:8], in_values=CD.t[:, hd, :]),
                                            r=["CD.%d" % hd, "TPa.%d" % hd], w=["PSa.%d" % hd])
                                    for hd in range(8):
                                        dve(lambda v: v.match_replace(out=CD2.t[:, hd, :], in_to_replace=TP.t[:, hd, 0:8], in_values=CD.t[:, hd, :],
                                                                      imm_value=-1e30), r=["CD.%d" % hd, "TPa.%d" % hd], w=["CD2.%d" % hd, "SR.%d" % (2 * hd), "SR.%d" % (2 * hd + 1)])
                                    for hd in range(8):
                                        dve(lambda v: v.max(out=TP.t[:, hd, 8:16], in_=CD2.t[:, hd, :]), r=["CD2.%d" % hd], w=["TPb.%d" % hd])
                                    for hd in range(8):
                                        dve(lambda v: v.max_index(out=PS_.t[:, hd, 8:16], in_max=TP.t[:, hd, 8:16], in_values=CD2.t[:, hd, :]),
                                            r=["CD2.%d" % hd, "TPb.%d" % hd], w=["PSb.%d" % hd])
                                    TPK = ["TPa.%d" % hd for hd in range(8)] + ["TPb.%d" % hd for hd in range(8)]
                                    PSK = ["PSa.%d" % hd for hd in range(8)] + ["PSb.%d" % hd for hd in range(8)]
                                    PR = posr[0]
                                    PC = posc[0]
                                    pflat = PS_.t[:].rearrange("p h k -> p (h k)")
                                    dve(lambda v: v.tensor_single_scalar(out=PR.t[:], in_=pflat, scalar=4, op=ALU.logical_shift_right), r=PSK, w=[PR])
                                    dve(lambda v: v.tensor_single_scalar(out=PC.t[:], in_=pflat, scalar=15, op=ALU.bitwise_and), r=PSK, w=[PC])
                                    PRF = posrf[0]
                                    PCF = poscf[0]
                                    dve(lambda v: v.tensor_copy(out=PRF.t[:], in_=PR.t[:]), r=[PR], w=[PRF])
                                    dve(lambda v: v.tensor_copy(out=PCF.t[:], in_=PC.t[:]), r=[PC], w=[PCF])
                                    OH = Tile(SS_.t[:].rearrange("p a c -> p (a c)").rearrange("p (k r) -> p k r", r=16), SS_.k)
                                    IJ = ijg[0]
                                    for (PF_, slot, half) in ((PRF, 0, 0), (PCF, 1, 1)):
                                        dve(lambda v: v.tensor_tensor(out=OH.t[:], in0=fdims(PF_.t[:], [[1, 128], [0, 16]]),
                                                                      in1=fdims(iota16.t[:], [[0, 128], [1, 16]]), op=ALU.is_equal),
                                            r=[PF_, iota16], w=[OH] + ["OHm.%d" % hd for hd in range(8)])
                                        for hd in range(8):
                                            dve(lambda v: v.tensor_tensor(out=OH.t[:, hd * 16:(hd + 1) * 16, :], in0=OH.t[:, hd * 16:(hd + 1) * 16, :],
                                                                          in1=fdims(I12F.t[:, half * 8 + hd, :], [[0, 16], [1, 16]]), op=ALU.mult),
                                                r=[OH, I12F], w=["OHm.%d" % hd])
                                        dve(lambda v: v.tensor_reduce(out=IJ.t[:, slot, :], in_=OH.t[:], axis=AX.X, op=ALU.add),
                                            r=["OHm.%d" % hd for hd in range(8)], w=["IJ.%d" % slot])
                                    GS = gsm[0]
                                    gwv = IJ.t[:, 2, :].rearrange("p (h k) -> p h k", k=16)
                                    dve(lambda v: v.tensor_tensor(out=gwv, in0=TP.t[:], in1=fdims(TP.t[:, :, 0], [[16, 8], [0, 16]]), op=ALU.subtract),
                                        r=TPK, w=[IJ])
                                    act(lambda a: a.activation(out=IJ.t[:, 2, :], in_=IJ.t[:, 2, :], func=AF.Exp), r=[IJ], w=[IJ])
                                    dve(lambda v: v.tensor_reduce(out=GS.t[:, 0:8], in_=gwv, axis=AX.X, op=ALU.add), r=[IJ], w=[GS])
                                    dve(lambda v: v.reciprocal(out=GS.t[:, 8:16], in_=GS.t[:, 0:8]), r=[GS], w=[GS])
                                    dve(lambda v: v.tensor_tensor(out=gwv, in0=gwv, in1=fdims(GS.t[:, 8:16], [[1, 8], [0, 16]]), op=ALU.mult),
                                        r=[IJ, GS], w=[IJ])
                                    for c in range(3):
                                        pe(lambda t: t.transpose(out=psF[:, 4, c * 128:(c + 1) * 128], in_=IJ.t[:, c, :], identity=identF.t[:]),
                                           r=[IJ, "IJ.0", "IJ.1", identF], w=[PF[4]], inc=(c == 2))
                                    act(lambda a: a.copy(out=IJG.t[:, :, tl * 128:(tl + 1) * 128], in_=psF[:, 4, 0:384].rearrange("p (a c) -> p a c", a=3)),
                                        r=[PF[4]], w=["IJG.%d" % tl])
                            s1.close()
                            fw.barrier()
                            s2 = contextlib.ExitStack()
                            cur[0] = s2
                            GT = sb(s2, "GT", [128, TG, 128], BF16)
                            wtb = rot("wtb", [128, TG], BF16, n=3)
                            ub = rot("ub", [128, 8, NI * 128], BF16, n=2)
                            vb = rot("vb", [128, NI, D], BF16, n=2)
                            NS = 16
                            r1s = rot("r1s", [128, NS, 128], BF16, n=2)
                            r2s = rot("r2s", [128, NS, 128], BF16, n=2)
                            atb = rot("atb", [128, TG], BF16, n=2)

                            first_touch = not cast_done[l]
                            cast_done[l] = True

                            def load_uv(ig):
                                U = nxt(ub)
                                V = nxt(vb)
                                if first_touch:
                                    ld(U.k, U.t[:], puT[:, ig * NI * 128:(ig + 1) * NI * 128].rearrange("(c p) e -> p c e", p=128), w=[U], q='pool')
                                    ld(V.k, V.t[:], pvv[ig * NI * 128:(ig + 1) * NI * 128, :].rearrange("(a p) d -> p a d", p=128), w=[V], q='pool')
                                    ld(U.k + "s", ubf[l, ig], U.t[:].rearrange("p c e -> p (c e)"), r=[U])
                                    ld(V.k + "s", vbf[l, ig], V.t[:].rearrange("p a d -> p (a d)"), r=[V])
                                else:
                                    ld(U.k, U.t[:].rearrange("p c e -> p (c e)"), ubf[l, ig], w=[U])
                                    ld(V.k, V.t[:].rearrange("p a d -> p (a d)"), vbf[l, ig], w=[V])
                                return U, V
                            pre = [load_uv(0), load_uv(1)]
                            gb = 0
                            for sbk in range(TG // NS):
                                t0 = sbk * NS
                                R1 = nxt(r1s)
                                R2 = nxt(r2s)
                                for t in range(NS):
                                    dve(lambda v: v.tensor_scalar(out=R2.t[:, t, :], in0=iota128b.t[:], scalar1=IJG.t[:, 1, t0 + t:t0 + t + 1], scalar2=None,
                                                                  op0=ALU.is_equal), r=[iota128b], w=["%s.%d" % (R2.k, t)])
                                for t in range(NS):
                                    dve(lambda v: v.tensor_scalar(out=R1.t[:, t, :], in0=iota128b.t[:], scalar1=IJG.t[:, 0, t0 + t:t0 + t + 1],
                                                                  scalar2=IJG.t[:, 2, t0 + t:t0 + t + 1], op0=ALU.is_equal, op1=ALU.mult),
                                        r=[iota128b], w=["%s.%d" % (R1.k, t)])
                                for q4 in range(NS // 4):
                                    bk = gb % 4
                                    gb += 1
                                    for tt in range(4):
                                        t = q4 * 4 + tt
                                        pe(lambda te: te.matmul(psF[:, bk, tt * 128:(tt + 1) * 128], lhsT=R2.t[:, t, :], rhs=R1.t[:, t, :],
                                                                start=True, stop=True), r=["%s.%d" % (R1.k, t), "%s.%d" % (R2.k, t)], w=[PF[bk]], inc=(tt == 3))
                                    tb = t0 + q4 * 4
                                    act(lambda a: a.copy(out=GT.t[:, tb:tb + 4, :], in_=psF[:, bk, :].rearrange("p (t i) -> p t i", t=4)),
                                        r=[PF[bk]], w=[GT])
                            def vside(i, V, ii, WT):
                                for tt in range(TGT):
                                    for hf in range(2):
                                        pe(lambda te: te.matmul(psF[:, tt * 2 + hf, :], lhsT=WT.t[:, tt * 128:(tt + 1) * 128],
                                                                rhs=V.t[:, ii, hf * 512:(hf + 1) * 512], start=(i == 0), stop=(i == 127)),
                                           r=[WT, V], w=[PF[tt * 2 + hf]], inc=(i == 127 or (tt == TGT - 1 and hf == 1)))
                            prev = None
                            for ig in range(128 // NI):
                                U, V = pre[ig] if ig < 2 else load_uv(ig)
                                for ii in range(NI):
                                    i = ig * NI + ii
                                    ab = 4 + (i % 2)
                                    for kc in range(8):
                                        pe(lambda te: te.matmul(psF[:, ab, 0:TG], lhsT=U.t[:, kc, ii * 128:(ii + 1) * 128], rhs=h2Tg.t[:, kc, :],
                                                                start=(kc == 0), stop=(kc == 7)),
                                           r=[U, "h2T.0", "h2T.1"], w=[PF[ab]], inc=(kc == 7))
                                    AT = nxt(atb)
                                    act(lambda a: a.activation(out=AT.t[:], in_=psF[:, ab, 0:TG], func=AF.Gelu_apprx_tanh), r=[PF[ab]], w=[AT])
                                    WT = nxt(wtb)
                                    dve(lambda v: v.tensor_tensor(out=WT.t[:], in0=GT.t[:, :, i], in1=AT.t[:], op=ALU.mult),
                                        r=[AT, GT], w=[WT])
                                    if prev is not None:
                                        vside(*prev)
                                    prev = (i, V, ii, WT)
                            vside(*prev)
                            for tl in range(TGT):
                                ti = g0 + tl
                                gt = tile0 + ti
                                X = nxt(xts)
                                ld(X.k, X.t[:], xsc[l, 0, gt * 128:(gt + 1) * 128, :], w=[X])
                                Y = nxt(ybuf)
                                dve(lambda v: v.tensor_tensor(out=Y.t[:].rearrange("p (a c) -> p a c", a=2), in0=psF[:, tl * 2:tl * 2 + 2, :],
                                                              in1=G2.t[:].rearrange("p (a c) -> p a c", a=2), op=ALU.mult),
                                    r=[PF[tl * 2], PF[tl * 2 + 1], G2], w=[Y])
                                dve(lambda v: v.scalar_tensor_tensor(out=Y.t[:], in0=X.t[:], scalar=ALPHA, in1=Y.t[:], op0=ALU.mult, op1=ALU.add),
                                    r=[X, Y], w=[Y])
                                S = ln_stats(Y.t[:], [Y])
                                X2 = nxt(x2b)
                                act(lambda a: a.activation(out=X2.t[:], in_=Y.t[:], func=AF.Identity, scale=S.t[:, 4:5], bias=S.t[:, 5:6]),
                                    r=[Y, S], w=[X2])
                                dve(lambda v: v.tensor_tensor(out=X2.t[:], in0=X2.t[:], in1=lnb.t[:, 0, :], op=ALU.mult), r=[X2, lnb], w=[X2])
                                dve(lambda v: v.tensor_tensor(out=X2.t[:], in0=X2.t[:], in1=lnb.t[:, 1, :], op=ALU.add), r=[X2, lnb], w=[X2])
                                if l == NL - 1:
                                    if gt < 8:
                                        dst = yp[gt * 128:(gt + 1) * 128, :]
                                    else:
                                        dst = ys[(gt - 8) * 128:(gt - 7) * 128, :]
                                    ld(X2.k + "st", dst, X2.t[:], r=[X2])
                                    if dbg:
                                        ld(X2.k + "st", xsc[l, 1, gt * 128:(gt + 1) * 128, :], X2.t[:], r=[X2], w=["xsc1.%d" % gt])
                                else:
                                    ld(X2.k + "st", xsc[l, 1, gt * 128:(gt + 1) * 128, :], X2.t[:], r=[X2], w=["xsc1.%d" % gt])
                            s2.close()
                            fw.barrier()
                    fw.barrier()
        except _Stop:
            pass
        fw.finish()
        build_nc.ninst = fw.ninst
        build_nc.marks = fw.marks
    return nc


def _rope_tables():
    t = np.arange(SS)
    row = (t // 64).astype(np.float32)
    col = (t % 64).astype(np.float32)

    def tab(m, nh):
        freqs = (10000.0 ** (-np.arange(m, dtype=np.float32) / m)).astype(np.float32)
        ar = row[:, None] * freqs[None, :]
        ac = col[:, None] * freqs[None, :]
        cr, sr, cc, sn = np.cos(ar), np.sin(ar), np.cos(ac), np.sin(ac)
        C = np.concatenate([cr, cr, cc, cc], axis=1)
        S = np.concatenate([-sr, sr, -sn, sn], axis=1)
        return (np.tile(C, (1, nh)).astype(np.float32), np.tile(S, (1, nh)).astype(np.float32))
    C16, S16 = tab(16, 10)
    C8, S8 = tab(8, 5)
    return C16, S16, C8, S8


def make_in_maps(inp):
    f = lambda a: np.ascontiguousarray(np.asarray(a, dtype=np.float32))
    C16, S16, C8, S8 = _rope_tables()
    w_mod = f(inp["w_mod"])
    b_mod = f(inp["b_mod"])
    b_modT = f(b_mod.reshape(NL, 48, 128).transpose(0, 2, 1))
    gqk = f(np.concatenate([np.tile(inp["attn_q_norm"], (1, 8)), np.tile(inp["attn_k_norm"], (1, 2))], axis=1))
    wsT = f(np.asarray(inp["gmlp_ws"]).transpose(0, 3, 1, 2))
    bsT = f(np.asarray(inp["gmlp_b"]).transpose(0, 2, 1))
    lnp = f(np.stack([inp["ln1_g"], inp["ln1_b"], inp["ln2_g"], inp["ln2_b"]], axis=1))
    pwq = f(np.asarray(inp["peer_wq"]).reshape(NL, D, 8, 2, 128).transpose(0, 1, 3, 2, 4).reshape(NL, D, 2048))
    k12T = f(np.stack([np.asarray(inp["peer_k1"]).transpose(0, 2, 1), np.asarray(inp["peer_k2"]).transpose(0, 2, 1)], axis=2))
    shared = {
        "w_mod": w_mod, "b_modT": b_modT, "b_mod": b_mod, "w_in": f(inp["w_in"]), "gqk": gqk,
        "mqn": f(inp["mla_q_norm"]), "mkvn": f(inp["mla_kv_norm"]), "w_uq": f(inp["w_uq"]), "w_ukv": f(inp["w_ukv"]),
        "wsT": wsT, "bsT": bsT, "w_o": f(inp["w_o"]), "lnp": lnp, "peer_wq": pwq, "k12T": k12T,
        "puT0": f(np.asarray(inp["peer_u"][0]).T), "puT1": f(np.asarray(inp["peer_u"][1]).T), "pv0": f(inp["peer_v"][0]), "pv1": f(inp["peer_v"][1]),
        "ropeC": C16, "ropeS": S16, "ropeMC": C8, "ropeMS": S8,
    }
    maps = []
    xpr = np.asarray(inp["x_prompt"], dtype=np.float32)
    xsa = np.asarray(inp["x_sample"], dtype=np.float32)
    cctx = np.asarray(inp["c_ctx"], dtype=np.float32)
    for c in range(NCORES):
        m = dict(shared)
        m["xp"] = f(xpr[PB * c:PB * (c + 1)].reshape(PB * PS, D))
        m["xs"] = f(xsa[c])
        m["ck"] = f(np.asarray(inp["cache_attn_k"])[c].reshape(NL, PAST, 128))
        m["cv"] = f(np.asarray(inp["cache_attn_v"])[c].reshape(NL, PAST, 128))
        m["cckv"] = f(np.asarray(inp["cache_mla_ckv"])[c])
        m["ckr"] = f(np.asarray(inp["cache_mla_krope"])[c])
        cc = np.stack([cctx, np.asarray(inp["c"], dtype=np.float32)[c]], axis=-1)
        m["cT"] = f(cc.reshape(8, 128, 2).transpose(1, 0, 2))
        maps.append(m)
    return maps


_NC_CACHE = {}


def kernel(**inputs):
    if "nc" not in _NC_CACHE:
        _NC_CACHE["nc"] = build_nc(False)
    nc = _NC_CACHE["nc"]
    maps = make_in_maps(inputs)
    res = run_bass_kernel_spmd(nc, maps, core_ids=list(range(NCORES)))
    R = res.results
    y_prompt = np.concatenate([R[c]["yp"].reshape(PB, PS, D) for c in range(NCORES)], axis=0)
    y_sample = np.stack([R[c]["ys"] for c in range(NCORES)], axis=0)
    nk = np.concatenate([R[c]["nk"].reshape(PB, NL, PS, 2, 64) for c in range(NCORES)], axis=0)
    nv = np.concatenate([R[c]["nv"].reshape(PB, NL, PS, 2, 64) for c in range(NCORES)], axis=0)
    nckv = np.concatenate([R[c]["nckv"] for c in range(NCORES)], axis=0)
    nkr = np.concatenate([R[c]["nkr"] for c in range(NCORES)], axis=0)
    return (y_prompt.astype(np.float32), y_sample.astype(np.float32), nk.astype(np.float32), nv.astype(np.float32),
            nckv.astype(np.float32), nkr.astype(np.float32))
```

```python
import contextlib
import math
import os

import numpy as np
import concourse.bass as bass
import concourse.mybir as mybir
from concourse.bass_utils import run_bass_kernel_spmd

F32 = mybir.dt.float32
BF16 = mybir.dt.bfloat16
I32 = mybir.dt.int32
U32 = mybir.dt.uint32
AF = mybir.ActivationFunctionType
ALU = mybir.AluOpType
AX = mybir.AxisListType

D = 1024
NL = 2
IN_W = 1696
EPS = 1e-6
ALPHA = (2.0 * NL) ** 0.25
NCORES = 8
PB = 4
PS = 256
SS = 2048
PAST = 256
NTILES = (PB * PS + SS) // 128


class _Stop(Exception):
    pass


class Tile:
    def __init__(self, t, k):
        self.t = t
        self.k = k


def _keys(lst):
    out = []
    for x in lst:
        out.append(x.k if isinstance(x, Tile) else x)
    return out


class FW:
    def __init__(self, nc, es):
        self.nc = nc
        self.es = es
        self.eng = {'pe': nc.tensor, 'act': nc.scalar, 'dve': nc.vector, 'pool': nc.gpsimd, 'sp': nc.sync}
        self.sem = {}
        self.cnt = {}
        self.semobj = {}
        for e in self.eng:
            self.sem[e] = es.enter_context(nc.semaphore("sem_" + e))
            self.cnt[e] = 0
            self.semobj["sem_" + e] = self.sem[e]
        self.dsem = {}
        self.dall = []
        self.dfree = {'hw': [], 'sw': []}
        self.waited = {e: {} for e in self.eng}
        self.lastw = {}
        self.readers = {}
        self.ninst = 0
        self.nops = 0
        self.limit = int(os.environ["MK_LIMIT"]) if os.environ.get("MK_LIMIT") else None
        self.marks = []

    def mark(self, name):
        self.marks.append((name, self.nops))

    def _wait(self, e, ev):
        if ev is None:
            return
        name, val = ev
        if e == 'pe' and name == 'sem_pe':
            return
        w = self.waited[e]
        if w.get(name, 0) >= val:
            return
        w[name] = val
        self.eng[e].wait_ge(self.semobj[name], val)
        self.ninst += 1

    def _deps(self, e, reads, writes):
        for k in reads:
            self._wait(e, self.lastw.get(k))
        for k in writes:
            self._wait(e, self.lastw.get(k))
            for ev in self.readers.get(k, {}).values():
                self._wait(e, ev)

    def _record(self, ev, reads, writes):
        for k in reads:
            self.readers.setdefault(k, {})[ev[0]] = ev
        for k in writes:
            self.lastw[k] = ev
            self.readers[k] = {}

    def op(self, e, fn, r=(), w=(), inc=True):
        self.nops += 1
        if self.limit is not None and self.nops > self.limit:
            return None
        reads = _keys(r)
        writes = _keys(w)
        if e != 'pe':
            ps = [k for k in reads if k.startswith("ps")]
            if ps:
                reads = [k for k in reads if not k.startswith("ps")]
                writes = list(writes) + ps
        self._deps(e, reads, writes)
        inst = fn(self.eng[e])
        self.ninst += 1
        name = "sem_" + e
        if inc:
            self.cnt[e] += 1
            inst.then_inc(self.sem[e], 1)
            ev = (name, self.cnt[e])
        else:
            ev = (name, self.cnt[e] + 1)
        self._record(ev, reads, writes)
        return inst

    def dma(self, q, key, fn, r=(), w=()):
        self.nops += 1
        if self.limit is not None and self.nops > self.limit:
            return None
        reads = _keys(r)
        writes = _keys(w)
        self._deps(q, reads, writes)
        cls = 'sw' if q == 'pool' else 'hw'
        key = cls + ":" + key
        if key not in self.dsem:
            if self.dfree[cls]:
                d = self.dfree[cls].pop()
            else:
                nm = "dsem_%d" % len(self.dall)
                sm = self.es.enter_context(self.nc.semaphore(nm))
                self.semobj[nm] = sm
                d = [sm, 0, nm, cls]
                self.dall.append(d)
            self.dsem[key] = d
        d = self.dsem[key]
        inst = fn(self.eng[q])
        self.ninst += 1
        d[1] += 16
        inst.then_inc(d[0], 16)
        ev = (d[2], d[1])
        self._record(ev, reads, writes)
        return inst

    def _all_events(self):
        evs = [("sem_" + e, self.cnt[e]) for e in self.eng if self.cnt[e] > 0]
        evs += [(d[2], d[1]) for d in self.dall if d[1] > 0]
        return evs

    def barrier(self):
        evs = self._all_events()
        for e in self.eng:
            for ev in evs:
                self._wait(e, ev)
        self.lastw = {}
        self.readers = {}
        for d in self.dsem.values():
            self.dfree[d[3]].append(d)
        self.dsem = {}

    def finish(self):
        for ev in self._all_events():
            self._wait('sp', ev)


def mkap(ap, dims, off=0):
    return bass.AP(ap.tensor, ap.offset + off, [list(x) for x in dims])


def fdims(ap, dims, off=0):
    base = list(ap.ap)
    return bass.AP(ap.tensor, ap.offset + off, [list(base[0])] + [list(x) for x in dims])


def build_nc(dbg=False):
    nc = bass.Bass("TRN2", target_bir_lowering=False)

    def din(name, shape, dt=F32):
        return nc.dram_tensor(name, shape, dt, kind="ExternalInput").ap()

    def dout(name, shape, dt=F32):
        return nc.dram_tensor(name, shape, dt, kind="ExternalOutput").ap()

    xp = din("xp", [PB * PS, D])
    xs = din("xs", [SS, D])
    ck = din("ck", [NL, PAST, 128])
    cv = din("cv", [NL, PAST, 128])
    cckv = din("cckv", [NL, PAST, 128])
    ckr = din("ckr", [NL, PAST, 32])
    cT = din("cT", [128, 8, 2])
    w_mod = din("w_mod", [NL, D, 6 * D])
    b_modT = din("b_modT", [NL, 128, 48])
    b_mod = din("b_mod", [NL, 6 * D])
    w_in = din("w_in", [NL, D, IN_W])
    gqk = din("gqk", [NL, 640])
    mqn = din("mqn", [NL, 256])
    mkvn = din("mkvn", [NL, 128])
    w_uq = din("w_uq", [NL, 256, 384])
    w_ukv = din("w_ukv", [NL, 128, 512])
    wsT = din("wsT", [NL, 128, 4, 128])
    bsT = din("bsT", [NL, 128, 4])
    w_o = din("w_o", [NL, D, D])
    lnp = din("lnp", [NL, 4, D])
    peer_wq = din("peer_wq", [NL, D, 2048])
    k12T = din("k12T", [NL, 128, 2, 128])
    puTs = [din("puT%d" % l, [D, 16384]) for l in range(NL)]
    pv = [din("pv%d" % l, [16384, D]) for l in range(NL)]
    ropeC = din("ropeC", [SS, 640])
    ropeS = din("ropeS", [SS, 640])
    ropeMC = din("ropeMC", [SS, 160])
    ropeMS = din("ropeMS", [SS, 160])

    yp = dout("yp", [PB * PS, D])
    ys = dout("ys", [SS, D])
    nk = dout("nk", [PB, NL, PS, 128])
    nv = dout("nv", [PB, NL, PS, 128])
    nckv = dout("nckv", [PB, NL, PS, 128])
    nkr = dout("nkr", [PB, NL, PS, 32])
    if dbg:
        xsc = dout("xsc", [NL, 2, NTILES * 128, D])
    else:
        xsc = nc.dram_tensor("xsc", [NL, 2, NTILES * 128, D], F32, kind="Internal").ap()
    gsc = nc.dram_tensor("gsc", [NL, 2, 4, 128, D], F32, kind="Internal").ap()
    NI = 2
    NIG = 128 // NI
    ubf = nc.dram_tensor("ubf", [NL, NIG, 128, 8 * NI * 128], BF16, kind="Internal").ap()
    vbf = nc.dram_tensor("vbf", [NL, NIG, 128, NI * D], BF16, kind="Internal").ap()
    cast_done = [False] * NL
    wscr = {"win": nc.dram_tensor("winb", [NL, 128, 8 * IN_W], BF16, kind="Internal").ap(),
            "wo": nc.dram_tensor("wob", [NL, 128, 8 * D], BF16, kind="Internal").ap(),
            "wq": nc.dram_tensor("wqb", [NL, 128, 8 * 2048], BF16, kind="Internal").ap()}
    wdone = {}

    uid = [0]

    with contextlib.ExitStack() as es:
        fw = FW(nc, es)

        def sb(stack, name, shape, dt=F32):
            uid[0] += 1
            nm = "%s_%d" % (name, uid[0])
            return Tile(stack.enter_context(nc.sbuf_tensor(nm, shape, dt)), nm)

        def dve(fn, r=(), w=(), inc=True):
            return fw.op('dve', fn, r, w, inc)

        def act(fn, r=(), w=(), inc=True):
            return fw.op('act', fn, r, w, inc)

        def pe(fn, r=(), w=(), inc=True):
            return fw.op('pe', fn, r, w, inc)

        def pool(fn, r=(), w=(), inc=True):
            return fw.op('pool', fn, r, w, inc)

        def ld(key, out_ap, in_ap, r=(), w=(), q='sp'):
            return fw.dma(q, key, lambda e: e.dma_start(out=out_ap, in_=in_ap), r, w)

        psF = es.enter_context(nc.psum_tensor("psF", [128, 6, 512], F32))
        psB = es.enter_context(nc.psum_tensor("psB", [128, 2, 1024], BF16))
        PF = ["psF%d" % i for i in range(6)]
        PBK = ["psB0", "psB1"]

        ident = sb(es, "ident", [128, 128], BF16)
        pool(lambda p: p.memset(ident.t[:], 0.0), w=[ident])
        pool(lambda p: p.affine_select(out=ident.t[:], in_=ident.t[:], pattern=[[-1, 128]],
                                       compare_op=ALU.not_equal, fill=1.0, base=0, channel_multiplier=1),
             r=[ident], w=[ident])
        epsb = sb(es, "epsb", [128, 1])
        dve(lambda v: v.memset(epsb.t[:], EPS), w=[epsb])
        iota16 = sb(es, "iota16", [128, 16])
        iota16i = sb(es, "iota16i", [128, 16], I32)
        pool(lambda p: p.iota(iota16i.t[:], pattern=[[1, 16]], base=0, channel_multiplier=0), w=[iota16i])
        dve(lambda v: v.tensor_copy(out=iota16.t[:], in_=iota16i.t[:]), r=[iota16i], w=[iota16])
        identF = sb(es, "identF", [128, 128])
        pool(lambda p: p.memset(identF.t[:], 0.0), w=[identF])
        pool(lambda p: p.affine_select(out=identF.t[:], in_=identF.t[:], pattern=[[-1, 128]],
                                       compare_op=ALU.not_equal, fill=1.0, base=0, channel_multiplier=1),
             r=[identF], w=[identF])
        iota128 = sb(es, "iota128", [128, 128])
        iota128i = sb(es, "iota128i", [128, 128], I32)
        pool(lambda p: p.iota(iota128i.t[:], pattern=[[1, 128]], base=0, channel_multiplier=0), w=[iota128i])
        dve(lambda v: v.tensor_copy(out=iota128.t[:], in_=iota128i.t[:]), r=[iota128i], w=[iota128])
        iota128b = sb(es, "iota128b", [128, 128], BF16)
        dve(lambda v: v.tensor_copy(out=iota128b.t[:], in_=iota128i.t[:]), r=[iota128i], w=[iota128b])
        modcol = sb(es, "modcol", [128, NL, 48, 2])

        with contextlib.ExitStack() as ps0:
            cTs = sb(ps0, "cTs", [128, 8, 2])
            scT = sb(ps0, "scT", [128, 8, 2], BF16)
            scR = sb(ps0, "scR", [128, 8, 2, 128], BF16)
            ld("cTs", cTs.t[:], cT, w=[cTs])
            act(lambda a: a.activation(out=scT.t[:], in_=cTs.t[:], func=AF.Silu), r=[cTs], w=[scT])
            dve(lambda v: v.tensor_copy(out=scR.t[:].rearrange("p a g m -> p (a g) m"),
                                        in_=fdims(scT.t[:], [[1, 16], [0, 128]])), r=[scT], w=[scR])
            bmT = sb(ps0, "bmT", [128, NL, 48])
            for l in range(NL):
                ld("bmT", bmT.t[:, l, :], b_modT[l], w=[bmT])
            wm = [sb(ps0, "wm%d" % i, [128, 8, 512], BF16) for i in range(2)]
            bmb = [sb(ps0, "bmb%d" % i, [128, 512]) for i in range(2)]
            gst = [sb(ps0, "gst%d" % i, [128, 512]) for i in range(2)]
            it = 0
            for l in range(NL):
                for ci in range(12):
                    role = ci // 2
                    W = wm[it % 2]
                    ld(W.k, W.t[:], w_mod[l, :, ci * 512:(ci + 1) * 512].rearrange("(c p) n -> p c n", p=128),
                       w=[W], q='pool')
                    if role in (0, 1, 3, 4):
                        for b4 in range(4):
                            blk = ci * 4 + b4
                            for kc in range(8):
                                pe(lambda t: t.matmul(psF[:, 4, b4 * 2:b4 * 2 + 2], lhsT=W.t[:, kc, b4 * 128:(b4 + 1) * 128],
                                                      rhs=scT.t[:, kc, :], start=(kc == 0), stop=(kc == 7)),
                                   r=[W, scT], w=[PF[4]], inc=(kc == 7))
                            addc = 1.0 if role in (1, 4) else 0.0
                            dve(lambda v: v.scalar_tensor_tensor(out=modcol.t[:, l, blk, :], in0=psF[:, 4, b4 * 2:b4 * 2 + 2],
                                                                 scalar=addc, in1=fdims(bmT.t[:, l, blk:blk + 1], [[0, 2]]),
                                                                 op0=ALU.add, op1=ALU.add),
                                r=[PF[4], bmT], w=[modcol])
                    if role in (2, 3, 4, 5):
                        slot = {2: 0, 4: 1, 3: 2, 5: 3}[role]
                        Bb = bmb[it % 2]
                        ld(Bb.k, Bb.t[:], mkap(b_mod[l, ci * 512:(ci + 1) * 512], [[0, 128], [1, 512]]), w=[Bb])
                        for g in range(2):
                            pb = PF[2 + g]
                            for kc in range(8):
                                pe(lambda t: t.matmul(psF[:, 2 + g, :], lhsT=scR.t[:, kc, g, :], rhs=W.t[:, kc, :],
                                                      start=(kc == 0), stop=(kc == 7)),
                                   r=[W, scR], w=[pb], inc=(kc == 7))
                            G = gst[g]
                            addc = 1.0 if role == 4 else 0.0
                            dve(lambda v: v.scalar_tensor_tensor(out=G.t[:], in0=psF[:, 2 + g, :], scalar=addc, in1=Bb.t[:],
                                                                 op0=ALU.add, op1=ALU.add),
                                r=[pb, Bb], w=[G])
                            half = ci % 2
                            ld("gst%d" % g, gsc[l, g, slot, :, half * 512:(half + 1) * 512], G.t[:], r=[G], w=["gsc"])
                    it += 1
        fw.barrier()

        def chk(tag):
            if dbg and os.environ.get("MK_STOP") == tag:
                raise _Stop()

        seqs = [(2 * b, 2, 0, False, b) for b in range(PB)] + [(8, 16, 1, True, -1)]
        if dbg and os.environ.get("MK_SEQS"):
            seqs = [seqs[int(i)] for i in os.environ["MK_SEQS"].split(",")]
        nlayers = int(os.environ.get("MK_LAYERS", NL)) if dbg else NL
        skip_peer = bool(dbg and os.environ.get("MK_SKIP_PEER"))
        stopA = bool(dbg and os.environ.get('MK_STOP') == 'A')
        stopB = bool(dbg and os.environ.get('MK_STOP') == 'B')

        def x_src(l, ph, gt):
            if l < 0:
                if gt < 8:
                    return xp[gt * 128:(gt + 1) * 128, :]
                return xs[(gt - 8) * 128:(gt - 7) * 128, :]
            return xsc[l, ph, gt * 128:(gt + 1) * 128, :]

        try:
            chk('pro')
            for (tile0, nt, grp, is_s, pbi) in seqs:
                nkc = nt + (2 if is_s else 0)
                koff = 2 if is_s else 0
                for l in range(nlayers):
                    with contextlib.ExitStack() as sa:
                        NTOK = nt * 128
                        NKEY = nkc * 128
                        QT = sb(sa, "QT", [128, 4, NTOK], BF16)
                        KT = sb(sa, "KT", [128, NKEY], BF16)
                        VA = sb(sa, "VA", [128, nkc, 2, 65], BF16)
                        QMT = sb(sa, "QMT", [128, 4, NTOK], BF16)
                        KMT = sb(sa, "KMT", [128, 4, NKEY], BF16)
                        VM = sb(sa, "VM", [128, nkc, 4, 65], BF16)
                        OC = sb(sa, "OC", [128, nt, 256], BF16)
                        pool(lambda p: p.memset(VA.t[:], 1.0), w=[VA])
                        pool(lambda p: p.memset(VM.t[:], 1.0), w=[VM])

                        sA = contextlib.ExitStack()
                        cur = [sA]
                        gq = sb(sA, "gq", [128, 640])
                        ld(gq.k, gq.t[:], mkap(gqk[l], [[0, 128], [1, 640]]), w=[gq])
                        gmq = sb(sA, "gmq", [128, 256])
                        ld(gmq.k, gmq.t[:], mkap(mqn[l], [[0, 128], [1, 256]]), w=[gmq])
                        gmk = sb(sA, "gmk", [128, 128])
                        ld(gmk.k, gmk.t[:], mkap(mkvn[l], [[0, 128], [1, 128]]), w=[gmk])
                        bst = sb(sA, "bst", [128, 4])
                        ld(bst.k, bst.t[:], bsT[l], w=[bst])
                        win = sb(sA, "win", [128, 8, IN_W], BF16)
                        if ("win", l) not in wdone:
                            wdone[("win", l)] = True
                            for kc in range(8):
                                ld(win.k, win.t[:, kc, :], w_in[l, kc * 128:(kc + 1) * 128, :], w=[win], q='pool')
                            ld(win.k + "s", wscr["win"][l], win.t[:].rearrange("p a c -> p (a c)"), r=[win])
                        else:
                            ld(win.k, win.t[:].rearrange("p a c -> p (a c)"), wscr["win"][l], w=[win])
                        wuq = sb(sA, "wuq", [128, 2, 384], BF16)
                        ld(wuq.k, wuq.t[:], w_uq[l].rearrange("(c p) n -> p c n", p=128), w=[wuq], q='pool')
                        wukv = sb(sA, "wukv", [128, 512], BF16)
                        ld(wukv.k, wukv.t[:], w_ukv[l], w=[wukv], q='pool')
                        wst = sb(sA, "wst", [128, 4, 128], BF16)
                        ld(wst.k, wst.t[:], wsT[l], w=[wst], q='pool')

                        def rot(name, shape, dt=F32, n=2):
                            return [sb(cur[0], name + str(i), shape, dt) for i in range(n)]
                        xts = rot("xt", [128, D])
                        xnb = rot("xnb", [128, D], BF16, n=1)
                        hT = rot("hT", [128, 8, 128], BF16, n=1)
                        pj = rot("pj", [128, IN_W], n=2)
                        sq = rot("sq", [128, 1152], n=1)
                        small = rot("small", [128, 64], n=3)
                        qkn = rot("qkn", [128, 640], n=1)
                        rc_t = rot("ropeC", [128, 640], n=1)
                        rs_t = rot("ropeS", [128, 640], n=1)
                        rmc_t = rot("ropeMC", [128, 160], n=1)
                        rms_t = rot("ropeMS", [128, 160], n=1)
                        tmp640 = rot("tmp640", [128, 640], n=1)
                        tmp640b = rot("tmp640b", [128, 640], n=1)
                        qkb = rot("qkb", [128, 640], BF16)
                        cqb = rot("cqb", [128, 256], BF16)
                        cqT = rot("cqT", [128, 2, 128], BF16)
                        ckvn = rot("ckvn", [128, 128])
                        ckb = rot("ckb", [128, 128], BF16)
                        ckT = rot("ckT", [128, 128], BF16)
                        qm = rot("qm", [128, 4, 96])
                        mr = rot("mr", [128, 160], n=1)
                        mr3 = rot("mr3", [128, 160], n=1)
                        qmb = rot("qmb", [128, 4, 96], BF16)
                        kmb = rot("kmb", [128, 4, 96], BF16)
                        vg = rot("vg", [128, 256], BF16)
                        vtmp = rot("vtmp", [128, 256])
                        cst = rot("cst", [128, 128])
                        cstb = rot("cstb", [128, 128], BF16)
                        krs = rot("krs", [128, 32])
                        rr = {}

                        def nxt(lst):
                            i = rr.get(id(lst), 0)
                            rr[id(lst)] = i + 1
                            return lst[i % len(lst)]

                        def ln_stats(xin, xkeys):
                            S = nxt(small)
                            dve(lambda v: v.bn_stats(out=S.t[:, 8:14], in_=xin[:, 0:512]), r=xkeys, w=[S])
                            dve(lambda v: v.bn_stats(out=S.t[:, 14:20], in_=xin[:, 512:1024]), r=xkeys, w=[S])
                            dve(lambda v: v.bn_aggr(out=S.t[:, 0:2], in_=S.t[:, 8:20]), r=[S], w=[S])
                            act(lambda a: a.activation(out=S.t[:, 2:3], in_=S.t[:, 1:2], func=AF.Sqrt, bias=epsb.t[:], scale=1.0),
                                r=[S, epsb], w=[S])
                            dve(lambda v: v.reciprocal(out=S.t[:, 4:5], in_=S.t[:, 2:3]), r=[S], w=[S])
                            dve(lambda v: v.scalar_tensor_tensor(out=S.t[:, 5:6], in0=S.t[:, 0:1], scalar=-1.0, in1=S.t[:, 4:5],
                                                                 op0=ALU.mult, op1=ALU.mult), r=[S], w=[S])
                            return S

                        tcount = [0]

                        def transposes(srcs, dst_fn, rkeys, wkeys, evac='act'):
                            b = tcount[0] % 2
                            tcount[0] += 1
                            for i, (src, n) in enumerate(srcs):
                                pe(lambda t: t.transpose(out=psB[0:n, b, i * 128:(i + 1) * 128], in_=src, identity=ident.t[:]),
                                   r=list(rkeys) + [ident], w=[PBK[b]], inc=(i == len(srcs) - 1))
                            dst_fn(b)

                        def rope(src, nh, hd, ctab, stab, coff, dsts, rkeys, wkeys):
                            n = nh * hd
                            q4 = hd // 4
                            T1 = nxt(tmp640)
                            T2 = nxt(tmp640b)
                            dve(lambda v: v.tensor_tensor(out=T1.t[:, 0:n], in0=src, in1=ctab.t[:, coff:coff + n], op=ALU.mult),
                                r=list(rkeys) + [ctab], w=[T1])
                            nb = n // (2 * q4)
                            def hv(ap, off, half):
                                return fdims(ap, [[2 * q4, nb], [1, q4]], off + half * q4)
                            dve(lambda v: v.tensor_tensor(out=hv(T2.t[:, 0:n], 0, 0), in0=hv(src, 0, 1), in1=hv(stab.t[:, 0:n], coff, 0),
                                                          op=ALU.mult), r=list(rkeys) + [stab], w=[T2])
                            dve(lambda v: v.tensor_tensor(out=hv(T2.t[:, 0:n], 0, 1), in0=hv(src, 0, 0), in1=hv(stab.t[:, 0:n], coff, 1),
                                                          op=ALU.mult), r=list(rkeys) + [stab], w=[T2])
                            for (oap, i0, i1, j0, j1) in dsts:
                                dve(lambda v: v.tensor_tensor(out=oap, in0=i0(T1.t), in1=i0(T2.t), op=ALU.add), r=[T1, T2], w=wkeys)

                        def mla_keys(ckb_ap, ckb_keys, kr_ap, kr_keys, kc):
                            CT = nxt(ckT)
                            def ev(b):
                                act(lambda a: a.copy(out=CT.t[:], in_=psB[:, b, 0:128]), r=[PBK[b]], w=[CT])
                            transposes([(ckb_ap, 128)], ev, ckb_keys, None)
                            pe(lambda t: t.matmul(psF[:, 5, :], lhsT=CT.t[:], rhs=wukv.t[:], start=True, stop=True),
                               r=[CT, wukv], w=[PF[5]])
                            KB = nxt(kmb)
                            kvv = psF[:, 5, :].rearrange("p (h c) -> p h c", h=4)
                            act(lambda a: a.copy(out=KB.t[:, :, 0:64], in_=kvv[:, :, 0:64]), r=[PF[5]], w=[KB])
                            dve(lambda v: v.tensor_copy(out=VM.t[:, kc, :, 0:64], in_=kvv[:, :, 64:128]), r=[PF[5], VM], w=["VM.%d" % kc])
                            dve(lambda v: v.tensor_copy(out=KB.t[:, :, 64:96], in_=fdims(kr_ap, [[0, 4], [1, 32]])),
                                r=list(kr_keys), w=[KB])
                            def ev2(b):
                                act(lambda a: a.copy(out=KMT.t[0:96, :, kc * 128:(kc + 1) * 128],
                                                     in_=psB[0:96, b, 0:512].rearrange("p (h c) -> p h c", h=4)),
                                    r=[PBK[b]], w=["KMT.%d" % kc])
                            transposes([(KB.t[:, h, :], 96) for h in range(4)], ev2, [KB], None)

                        fw.mark('A-start')
                        if is_s:
                            for j in range(2):
                                C1 = nxt(cst)
                                ld(C1.k, C1.t[:], ck[l, j * 128:(j + 1) * 128, :], w=[C1])
                                CB = nxt(cstb)
                                dve(lambda v: v.tensor_copy(out=CB.t[:], in_=C1.t[:]), r=[C1], w=[CB])
                                def evk(b):
                                    act(lambda a: a.copy(out=KT.t[:, j * 128:(j + 1) * 128], in_=psB[:, b, 0:128]),
                                        r=[PBK[b]], w=["KT.%d" % j])
                                transposes([(CB.t[:], 128)], evk, [CB], None)
                                C2 = nxt(cst)
                                ld(C2.k, C2.t[:], cv[l, j * 128:(j + 1) * 128, :], w=[C2])
                                dve(lambda v: v.tensor_copy(out=VA.t[:, j, :, 0:64], in_=C2.t[:].rearrange("p (h c) -> p h c", h=2)),
                                    r=[C2, VA], w=["VA.%d" % j])
                                C3 = nxt(cst)
                                ld(C3.k, C3.t[:], cckv[l, j * 128:(j + 1) * 128, :], w=[C3])
                                CB3 = nxt(cstb)
                                dve(lambda v: v.tensor_copy(out=CB3.t[:], in_=C3.t[:]), r=[C3], w=[CB3])
                                K4 = nxt(krs)
                                ld(K4.k, K4.t[:], ckr[l, j * 128:(j + 1) * 128, :], w=[K4])
                                mla_keys(CB3.t[:], [CB3], K4.t[:], [K4], j)

                        for pr in range(0, nt, 2):
                          for stage in range(2):
                            for ti in range(pr, min(pr + 2, nt)):
                                if stage == 0:
                                    gt = tile0 + ti
                                    kc_new = koff + ti
                                    X = nxt(xts)
                                    ld(X.k, X.t[:], x_src(l - 1, 1, gt), w=[X])
                                    fw.mark('tile%d-ln' % ti)
                                    S = ln_stats(X.t[:], [X])
                                    XN = nxt(xnb)
                                    act(lambda a: a.activation(out=XN.t[:], in_=X.t[:], func=AF.Identity, scale=S.t[:, 4:5], bias=S.t[:, 5:6]),
                                        r=[X, S], w=[XN])
                                    H = nxt(hT)
                                    def evh(b):
                                        for c in range(8):
                                            act(lambda a: a.activation(out=H.t[:, c, :], in_=psB[:, b, c * 128:(c + 1) * 128], func=AF.Identity,
                                                                       scale=modcol.t[:, l, 8 + c, grp:grp + 1], bias=modcol.t[:, l, c, grp:grp + 1]),
                                                r=[PBK[b], modcol], w=[H])
                                    transposes([(XN.t[:, c * 128:(c + 1) * 128], 128) for c in range(8)], evh, [XN], None)
                                    fw.mark('proj')
                                    segs = [(0, 512), (512, 512), (1024, 512), (1536, 160)]
                                    for bi, (c0, cn) in enumerate(segs):
                                        for kc in range(8):
                                            pe(lambda t: t.matmul(psF[:, bi, 0:cn], lhsT=H.t[:, kc, :], rhs=win.t[:, kc, c0:c0 + cn],
                                                                  start=(kc == 0), stop=(kc == 7)), r=[H, win], w=[PF[bi]], inc=(kc == 7))
                                    P = pj[ti % 2]
                                    for bi, (c0, cn) in enumerate(segs):
                                        act(lambda a: a.copy(out=P.t[:, c0:c0 + cn], in_=psF[:, bi, 0:cn]), r=[PF[bi]], w=[P])
                                else:
                                    gt = tile0 + ti
                                    kc_new = koff + ti
                                    P = pj[ti % 2]
                                    fw.mark('rms')
                                    SQ = nxt(sq)
                                    dve(lambda v: v.tensor_tensor(out=SQ.t[:], in0=P.t[:, 0:1152], in1=P.t[:, 0:1152], op=ALU.mult), r=[P], w=[SQ])
                                    S2 = nxt(small)
                                    dve(lambda v: v.tensor_reduce(out=S2.t[:, 0:10], in_=SQ.t[:, 0:640].rearrange("p (h c) -> p h c", c=64),
                                                                  axis=AX.X, op=ALU.add), r=[SQ], w=[S2])
                                    dve(lambda v: v.tensor_reduce(out=S2.t[:, 10:11], in_=SQ.t[:, 768:1024], axis=AX.X, op=ALU.add), r=[SQ], w=[S2])
                                    dve(lambda v: v.tensor_reduce(out=S2.t[:, 11:12], in_=SQ.t[:, 1024:1152], axis=AX.X, op=ALU.add), r=[SQ], w=[S2])
                                    dve(lambda v: v.tensor_scalar(out=S2.t[:, 16:26], in0=S2.t[:, 0:10], scalar1=1.0 / 64, scalar2=EPS,
                                                                  op0=ALU.mult, op1=ALU.add), r=[S2], w=[S2])
                                    dve(lambda v: v.tensor_scalar(out=S2.t[:, 26:27], in0=S2.t[:, 10:11], scalar1=1.0 / 256, scalar2=EPS,
                                                                  op0=ALU.mult, op1=ALU.add), r=[S2], w=[S2])
                                    dve(lambda v: v.tensor_scalar(out=S2.t[:, 27:28], in0=S2.t[:, 11:12], scalar1=1.0 / 128, scalar2=EPS,
                                                                  op0=ALU.mult, op1=ALU.add), r=[S2], w=[S2])
                                    act(lambda a: a.activation(out=S2.t[:, 32:44], in_=S2.t[:, 16:28], func=AF.Sqrt), r=[S2], w=[S2])
                                    dve(lambda v: v.reciprocal(out=S2.t[:, 48:60], in_=S2.t[:, 32:44]), r=[S2], w=[S2])
                                    fw.mark('qk')
                                    QK = nxt(qkn)
                                    dve(lambda v: v.tensor_tensor(out=QK.t[:].rearrange("p (h c) -> p h c", c=64),
                                                                  in0=P.t[:, 0:640].rearrange("p (h c) -> p h c", c=64),
                                                                  in1=fdims(S2.t[:, 48:58], [[1, 10], [0, 64]]), op=ALU.mult), r=[P, S2], w=[QK])
                                    dve(lambda v: v.tensor_tensor(out=QK.t[:], in0=QK.t[:], in1=gq.t[:], op=ALU.mult), r=[QK, gq], w=[QK])
                                    QB = nxt(qkb)
                                    def qdst(T, j):
                                        return T[:, j * 256:(j + 1) * 256].rearrange("p (s c) -> p s c", c=64)
                                    def qout(j):
                                        return fdims(QB.t[:], [[128, 4], [1, 64]], j * 64)
                                    if is_s:
                                        RC = rc_t[0]
                                        RS = rs_t[0]
                                        ld(RC.k, RC.t[:], ropeC[ti * 128:(ti + 1) * 128, :], w=[RC])
                                        ld(RS.k, RS.t[:], ropeS[ti * 128:(ti + 1) * 128, :], w=[RS])
                                        dsts = [(qout(0), lambda T: qdst(T, 0), None, 0, 0), (qout(1), lambda T: qdst(T, 1), None, 0, 0),
                                                (QB.t[:, 512:640], lambda T: T[:, 512:640], None, 0, 0)]
                                        rope(QK.t[:], 10, 64, RC, RS, 0, dsts, [QK], [QB])
                                    else:
                                        for j in range(2):
                                            dve(lambda v: v.tensor_copy(out=qout(j), in_=qdst(QK.t, j)), r=[QK], w=[QB])
                                        dve(lambda v: v.tensor_copy(out=QB.t[:, 512:640], in_=QK.t[:, 512:640]), r=[QK], w=[QB])
                                    def evq(b):
                                        act(lambda a: a.copy(out=QT.t[:, :, ti * 128:(ti + 1) * 128],
                                                             in_=psB[:, b, 0:512].rearrange("p (s c) -> p s c", s=4)),
                                            r=[PBK[b]], w=["QT.%d" % ti])
                                        act(lambda a: a.copy(out=KT.t[:, kc_new * 128:(kc_new + 1) * 128], in_=psB[:, b, 512:640]),
                                            r=[PBK[b]], w=["KT.%d" % kc_new])
                                    transposes([(QB.t[:, i * 128:(i + 1) * 128], 128) for i in range(5)], evq, [QB], None)
                                    fw.mark('va')
                                    dve(lambda v: v.tensor_copy(out=VA.t[:, kc_new, :, 0:64], in_=P.t[:, 640:768].rearrange("p (h c) -> p h c", h=2)),
                                        r=[P, VA], w=["VA.%d" % kc_new])
                                    fw.mark('mlaq')
                                    CQ = nxt(cqb)
                                    dve(lambda v: v.scalar_tensor_tensor(out=CQ.t[:], in0=P.t[:, 768:1024], scalar=S2.t[:, 58:59], in1=gmq.t[:],
                                                                         op0=ALU.mult, op1=ALU.mult), r=[P, S2, gmq], w=[CQ])
                                    CQT = nxt(cqT)
                                    def evc(b):
                                        act(lambda a: a.copy(out=CQT.t[:], in_=psB[:, b, 0:256].rearrange("p (s c) -> p s c", s=2)),
                                            r=[PBK[b]], w=[CQT])
                                    transposes([(CQ.t[:, i * 128:(i + 1) * 128], 128) for i in range(2)], evc, [CQ], None)
                                    for kc in range(2):
                                        pe(lambda t: t.matmul(psF[:, 4, 0:384], lhsT=CQT.t[:, kc, :], rhs=wuq.t[:, kc, :],
                                                              start=(kc == 0), stop=(kc == 1)), r=[CQT, wuq], w=[PF[4]], inc=(kc == 1))
                                    QM = nxt(qm)
                                    act(lambda a: a.copy(out=QM.t[:].rearrange("p h c -> p (h c)"), in_=psF[:, 4, 0:384]), r=[PF[4]], w=[QM])
                                    QMB = nxt(qmb)
                                    dve(lambda v: v.tensor_copy(out=QMB.t[:, :, 0:64], in_=QM.t[:, :, 0:64]), r=[QM], w=[QMB])
                                    MR = nxt(mr)
                                    dve(lambda v: v.tensor_copy(out=MR.t[:, 0:128].rearrange("p (h c) -> p h c", h=4), in_=QM.t[:, :, 64:96]),
                                        r=[QM], w=[MR])
                                    dve(lambda v: v.tensor_copy(out=MR.t[:, 128:160], in_=P.t[:, 1152:1184]), r=[P], w=[MR])
                                    MR3 = nxt(mr3)
                                    if is_s:
                                        RMC = rmc_t[0]
                                        RMS = rms_t[0]
                                        ld(RMC.k, RMC.t[:], ropeMC[ti * 128:(ti + 1) * 128, :], w=[RMC])
                                        ld(RMS.k, RMS.t[:], ropeMS[ti * 128:(ti + 1) * 128, :], w=[RMS])
                                        rope(MR.t[:], 5, 32, RMC, RMS, 0, [(MR3.t[:], lambda T: T[:, 0:160], None, 0, 0)], [MR], [MR3])
                                    else:
                                        dve(lambda v: v.tensor_copy(out=MR3.t[:], in_=MR.t[:]), r=[MR], w=[MR3])
                                    dve(lambda v: v.tensor_copy(out=QMB.t[:, :, 64:96], in_=MR3.t[:, 0:128].rearrange("p (h c) -> p h c", h=4)),
                                        r=[MR3], w=[QMB])
                                    def evqm(b):
                                        act(lambda a: a.copy(out=QMT.t[0:96, :, ti * 128:(ti + 1) * 128],
                                                             in_=psB[0:96, b, 0:512].rearrange("p (h c) -> p h c", h=4)),
                                            r=[PBK[b]], w=["QMT.%d" % ti])
                                    transposes([(QMB.t[:, h, :], 96) for h in range(4)], evqm, [QMB], None)
                                    fw.mark('mlakv')
                                    CK = nxt(ckvn)
                                    dve(lambda v: v.scalar_tensor_tensor(out=CK.t[:], in0=P.t[:, 1024:1152], scalar=S2.t[:, 59:60], in1=gmk.t[:],
                                                                         op0=ALU.mult, op1=ALU.mult), r=[P, S2, gmk], w=[CK])
                                    CKB = nxt(ckb)
                                    dve(lambda v: v.tensor_copy(out=CKB.t[:], in_=CK.t[:]), r=[CK], w=[CKB])
                                    mla_keys(CKB.t[:], [CKB], MR3.t[:, 128:160], [MR3], kc_new)
                                    fw.mark('gmlp')
                                    VT = nxt(vtmp)
                                    S3 = nxt(small)
                                    vcv = P.t[:, 1440:1696].rearrange("p (g c) -> p g c", g=4)
                                    dve(lambda v: v.tensor_reduce(out=S3.t[:, 0:4], in_=vcv, axis=AX.X, op=ALU.add), r=[P], w=[S3])
                                    dve(lambda v: v.tensor_tensor(out=VT.t[:], in0=P.t[:, 1440:1696], in1=P.t[:, 1440:1696], op=ALU.mult), r=[P], w=[VT])
                                    dve(lambda v: v.tensor_reduce(out=S3.t[:, 4:8], in_=VT.t[:].rearrange("p (g c) -> p g c", g=4), axis=AX.X, op=ALU.add),
                                        r=[VT], w=[S3])
                                    dve(lambda v: v.tensor_scalar(out=S3.t[:, 8:12], in0=S3.t[:, 0:4], scalar1=1.0 / 64, scalar2=None, op0=ALU.mult),
                                        r=[S3], w=[S3])
                                    dve(lambda v: v.tensor_tensor(out=S3.t[:, 12:16], in0=S3.t[:, 8:12], in1=S3.t[:, 8:12], op=ALU.mult), r=[S3], w=[S3])
                                    dve(lambda v: v.scalar_tensor_tensor(out=S3.t[:, 16:20], in0=S3.t[:, 4:8], scalar=1.0 / 64, in1=S3.t[:, 12:16],
                                                                         op0=ALU.mult, op1=ALU.subtract), r=[S3], w=[S3])
                                    dve(lambda v: v.tensor_scalar(out=S3.t[:, 20:24], in0=S3.t[:, 16:20], scalar1=EPS, scalar2=None, op0=ALU.add),
                                        r=[S3], w=[S3])
                                    act(lambda a: a.activation(out=S3.t[:, 24:28], in_=S3.t[:, 20:24], func=AF.Sqrt), r=[S3], w=[S3])
                                    dve(lambda v: v.reciprocal(out=S3.t[:, 28:32], in_=S3.t[:, 24:28]), r=[S3], w=[S3])
                                    dve(lambda v: v.tensor_tensor(out=VT.t[:].rearrange("p (g c) -> p g c", g=4), in0=vcv,
                                                                  in1=fdims(S3.t[:, 8:12], [[1, 4], [0, 64]]), op=ALU.subtract), r=[P, S3], w=[VT])
                                    VG = nxt(vg)
                                    dve(lambda v: v.tensor_tensor(out=VG.t[:].rearrange("p (g c) -> p g c", g=4),
                                                                  in0=VT.t[:].rearrange("p (g c) -> p g c", g=4),
                                                                  in1=fdims(S3.t[:, 28:32], [[1, 4], [0, 64]]), op=ALU.mult), r=[VT, S3], w=[VG])
                                    for g in range(4):
                                        pe(lambda t: t.matmul(psF[:, 4, g * 64:(g + 1) * 64], lhsT=wst.t[:, g, :], rhs=VG.t[:, g * 64:(g + 1) * 64],
                                                              start=True, stop=True), r=[VG, wst], w=[PF[4]], inc=(g == 3))
                                    dve(lambda v: v.tensor_tensor(out=VT.t[:].rearrange("p (g c) -> p g c", g=4),
                                                                  in0=psF[:, 4, 0:256].rearrange("p (g c) -> p g c", g=4),
                                                                  in1=fdims(bst.t[:], [[1, 4], [0, 64]]), op=ALU.add), r=[PF[4], bst], w=[VT])
                                    dve(lambda v: v.tensor_tensor(out=OC.t[:, ti, :], in0=VT.t[:], in1=P.t[:, 1184:1440], op=ALU.mult),
                                        r=[VT, P], w=["OC.%d" % ti])
                                    fw.mark('outs')
                                    if not is_s:
                                        r0 = ti * 128
                                        ld("o_nk%d" % (ti % 2), nk[pbi, l, r0:r0 + 128, :], QK.t[:, 512:640], r=[QK])
                                        ld("o_nv%d" % (ti % 2), nv[pbi, l, r0:r0 + 128, :], P.t[:, 640:768], r=[P])
                                        ld("o_nc%d" % (ti % 2), nckv[pbi, l, r0:r0 + 128, :], CK.t[:], r=[CK])
                                        ld("o_nr%d" % (ti % 2), nkr[pbi, l, r0:r0 + 128, :], P.t[:, 1152:1184], r=[P])
                        sA.close()
                        fw.barrier()
                        sB = contextlib.ExitStack()
                        cur[0] = sB
                        G1 = sb(sB, "G1", [128, D])
                        ld(G1.k, G1.t[:], gsc[l, grp, 0], w=[G1])
                        lnb = sb(sB, "lnb", [128, 2, D])
                        for j in range(2):
                            ld(lnb.k, lnb.t[:, j, :], mkap(lnp[l, j], [[0, 128], [1, D]]), w=[lnb])
                        wo = sb(sB, "wo", [128, 8, D], BF16)
                        if ("wo", l) not in wdone:
                            wdone[("wo", l)] = True
                            for kc in range(8):
                                ld(wo.k, wo.t[:, kc, :], w_o[l, kc * 128:(kc + 1) * 128, :], w=[wo], q='pool')
                            ld(wo.k + "s", wscr["wo"][l], wo.t[:].rearrange("p a c -> p (a c)"), r=[wo])
                        else:
                            ld(wo.k, wo.t[:].rearrange("p a c -> p (a c)"), wscr["wo"][l], w=[wo])
                        xts = rot("xtB", [128, D])
                        small = rot("smallB", [128, 64], n=3)
                        ntq = 4 if is_s else 2
                        NQ = ntq * 128
                        PT = rot("PT", [128, NQ], BF16, n=3)
                        mix = sb(sB, "mix", [128, ntq, D], BF16)
                        mixT = rot("mixT", [128, 8, 128], BF16)
                        ybuf = rot("ybuf", [128, D], n=1)
                        x1b = rot("x1b", [128, D])
                        rec = rot("rec", [128, 8])
                        sc_ = [0]
                        oc_ = [0]
                        for qg in range(0 if stopA else nt // ntq):
                            q0 = qg * NQ
                            allkeys_q = ["QT.%d" % (qg * ntq + i) for i in range(ntq)]
                            allkeys_qm = ["QMT.%d" % (qg * ntq + i) for i in range(ntq)]
                            def emit_S(hh, kc):
                                isA = hh < 8
                                h = hh if isA else hh - 8
                                sbk = sc_[0] % 3
                                sc_[0] += 1
                                if isA:
                                    base = (h // 4) * 64
                                    pe(lambda t: t.matmul(psF[:, sbk, 0:NQ], lhsT=KT.t[base:base + 64, kc * 128:(kc + 1) * 128],
                                                          rhs=QT.t[base:base + 64, h % 4, q0:q0 + NQ], start=True, stop=True),
                                       r=["KT.%d" % kc] + allkeys_q, w=[PF[sbk]])
                                    return sbk, 0.125
                                pe(lambda t: t.matmul(psF[:, sbk, 0:NQ], lhsT=KMT.t[0:96, h, kc * 128:(kc + 1) * 128],
                                                      rhs=QMT.t[0:96, h, q0:q0 + NQ], start=True, stop=True),
                                   r=["KMT.%d" % kc] + allkeys_qm, w=[PF[sbk]])
                                return sbk, 1.0 / math.sqrt(96.0)
                            its = [(hh, kc) for hh in range(12) for kc in range(nkc)]
                            pend = emit_S(*its[0])
                            for n, (hh, kc) in enumerate(its):
                                isA = hh < 8
                                h = hh if isA else hh - 8
                                if kc == 0:
                                    ob = 3 + (oc_[0] % 2)
                                    oc_[0] += 1
                                    oview = psF[:, ob, 0:ntq * 65].rearrange("p (q c) -> p q c", c=65)
                                sbk, scl = pend
                                if n + 1 < len(its):
                                    pend = emit_S(*its[n + 1])
                                Pt = nxt(PT)
                                act(lambda a: a.activation(out=Pt.t[:], in_=psF[:, sbk, 0:NQ], func=AF.Exp, scale=scl), r=[PF[sbk]], w=[Pt])
                                for qt in range(ntq):
                                    if isA:
                                        rhs = VA.t[:, kc, h // 4, :]
                                        vk = "VA.%d" % kc
                                    else:
                                        rhs = VM.t[:, kc, h, :]
                                        vk = "VM.%d" % kc
                                    pe(lambda t: t.matmul(oview[:, qt, :], lhsT=Pt.t[:, qt * 128:(qt + 1) * 128], rhs=rhs,
                                                          start=(kc == 0 and qt == 0), stop=(kc == nkc - 1 and qt == ntq - 1)),
                                       r=[Pt, vk], w=[PF[ob]], inc=(qt == ntq - 1))
                                if kc == nkc - 1:
                                    R = nxt(rec)
                                    dve(lambda v: v.reciprocal(out=R.t[:, 0:ntq], in_=oview[:, :, 64]), r=[PF[ob]], w=[R])
                                    col = h * 64 if isA else 512 + h * 64
                                    dve(lambda v: v.tensor_tensor(out=mix.t[:, :, col:col + 64], in0=oview[:, :, 0:64],
                                                                  in1=fdims(R.t[:, 0:ntq], [[1, ntq], [0, 64]]), op=ALU.mult),
                                        r=[PF[ob], R], w=[mix])
                            dve(lambda v: v.tensor_copy(out=mix.t[:, :, 768:1024], in_=OC.t[:, qg * ntq:(qg + 1) * ntq, :]),
                                r=["OC.%d" % (qg * ntq + i) for i in range(ntq)], w=[mix])
                            for qt in range(ntq):
                                ti = qg * ntq + qt
                                gt = tile0 + ti
                                MT = nxt(mixT)
                                def evm(b):
                                    act(lambda a: a.copy(out=MT.t[:].rearrange("p a c -> p (a c)"), in_=psB[:, b, :]), r=[PBK[b]], w=[MT])
                                transposes([(mix.t[:, qt, c * 128:(c + 1) * 128], 128) for c in range(8)], evm, [mix], None)
                                for hf in range(2):
                                    for kc in range(8):
                                        pe(lambda t: t.matmul(psF[:, hf, :], lhsT=MT.t[:, kc, :], rhs=wo.t[:, kc, hf * 512:(hf + 1) * 512],
                                                              start=(kc == 0), stop=(kc == 7)), r=[MT, wo], w=[PF[hf]], inc=(kc == 7))
                                X = nxt(xts)
                                ld(X.k, X.t[:], x_src(l - 1, 1, gt), w=[X])
                                Y = nxt(ybuf)
                                dve(lambda v: v.tensor_tensor(out=Y.t[:].rearrange("p (a c) -> p a c", a=2), in0=psF[:, 0:2, :],
                                                              in1=G1.t[:].rearrange("p (a c) -> p a c", a=2), op=ALU.mult),
                                    r=[PF[0], PF[1], G1], w=[Y])
                                dve(lambda v: v.scalar_tensor_tensor(out=Y.t[:], in0=X.t[:], scalar=ALPHA, in1=Y.t[:],
                                                                     op0=ALU.mult, op1=ALU.add), r=[X, Y], w=[Y])
                                S = ln_stats(Y.t[:], [Y])
                                X1 = nxt(x1b)
                                act(lambda a: a.activation(out=X1.t[:], in_=Y.t[:], func=AF.Identity, scale=S.t[:, 4:5], bias=S.t[:, 5:6]),
                                    r=[Y, S], w=[X1])
                                dve(lambda v: v.tensor_tensor(out=X1.t[:], in0=X1.t[:], in1=lnb.t[:, 0, :], op=ALU.mult), r=[X1, lnb], w=[X1])
                                dve(lambda v: v.tensor_tensor(out=X1.t[:], in0=X1.t[:], in1=lnb.t[:, 1, :], op=ALU.add), r=[X1, lnb], w=[X1])
                                ld(X1.k + "st", xsc[l, 0, gt * 128:(gt + 1) * 128, :], X1.t[:], r=[X1], w=["xsc.%d" % gt])
                        sB.close()
                    fw.barrier()

                    with contextlib.ExitStack() as sc:
                        cur[0] = sc
                        G2 = sb(sc, "G2", [128, D])
                        ld(G2.k, G2.t[:], gsc[l, grp, 3], w=[G2])
                        A2 = sb(sc, "A2", [128, D])
                        ld(A2.k, A2.t[:], gsc[l, grp, 1], w=[A2])
                        B2 = sb(sc, "B2", [128, D])
                        ld(B2.k, B2.t[:], gsc[l, grp, 2], w=[B2])
                        lnb = sb(sc, "lnb2", [128, 2, D])
                        for j in range(2):
                            ld(lnb.k, lnb.t[:, j, :], mkap(lnp[l, 2 + j], [[0, 128], [1, D]]), w=[lnb])
                        wq = sb(sc, "wq", [128, 8, 2048], BF16)
                        if ("wq", l) not in wdone:
                            wdone[("wq", l)] = True
                            for kc in range(8):
                                ld(wq.k, wq.t[:, kc, :], peer_wq[l, kc * 128:(kc + 1) * 128, :], w=[wq], q='pool')
                            ld(wq.k + "s", wscr["wq"][l], wq.t[:].rearrange("p a c -> p (a c)"), r=[wq])
                        else:
                            ld(wq.k, wq.t[:].rearrange("p a c -> p (a c)"), wscr["wq"][l], w=[wq])
                        kT = sb(sc, "kT", [128, 2, 128], BF16)
                        ld(kT.k, kT.t[:], k12T[l], w=[kT], q='pool')
                        TGT = 2
                        TG = TGT * 128
                        IJG = sb(sc, "IJG", [128, 3, TG])
                        h2Tg = sb(sc, "h2Tg", [128, 8, TG], BF16)
                        rr = {}

                        def nxt(lst):
                            i = rr.get(id(lst), 0)
                            rr[id(lst)] = i + 1
                            return lst[i % len(lst)]
                        xts = rot("cxt", [128, D])
                        small = rot("csmall", [128, 64], n=3)
                        ybuf = rot("cy", [128, D], n=1)
                        x2b = rot("x2b", [128, D])
                        tcount = [0]
                        puT = puTs[l]
                        pvv = pv[l]
                        for g0 in range(0, 0 if (stopA or stopB) else nt, TGT):
                            s1 = contextlib.ExitStack()
                            cur[0] = s1
                            h2 = rot("h2", [128, D], n=1)
                            xnf = rot("xnf", [128, D], n=1)
                            h2b = rot("h2b", [128, D], BF16, n=1)
                            qTs = rot("qTs", [128, 16, 128], BF16, n=1)
                            sc_s = rot("scs", [128, 16, 128], n=2)
                            sc_r = rot("scr", [128, 16, 128], n=1)
                            v12 = rot("v12", [128, 16, 16], n=1)
                            i12 = rot("i12", [128, 16, 16], U32, n=1)
                            i12f = rot("i12f", [128, 16, 16], n=1)
                            cand = rot("cand", [128, 8, 256], n=1)
                            top = rot("top", [128, 8, 16], n=1)
                            pos = rot("pos", [128, 8, 16], U32, n=1)
                            posr = rot("posr", [128, 128], U32, n=1)
                            posc = rot("posc", [128, 128], U32, n=1)
                            posrf = rot("posrf", [128, 128], n=1)
                            poscf = rot("poscf", [128, 128], n=1)
                            ijg = rot("ijg", [128, 3, 128], n=1)
                            gsm = rot("gsm", [128, 16], n=1)
                            for stage in range(2):
                              for tl in range(TGT):
                                if stage == 0:
                                    ti = g0 + tl
                                    gt = tile0 + ti
                                    X = nxt(xts)
                                    ld(X.k, X.t[:], xsc[l, 0, gt * 128:(gt + 1) * 128, :], w=[X])
                                    S = ln_stats(X.t[:], [X])
                                    XN = xnf[0]
                                    act(lambda a: a.activation(out=XN.t[:], in_=X.t[:], func=AF.Identity, scale=S.t[:, 4:5], bias=S.t[:, 5:6]),
                                        r=[X, S], w=[XN])
                                    H2 = h2[0]
                                    dve(lambda v: v.tensor_tensor(out=H2.t[:], in0=XN.t[:], in1=A2.t[:], op=ALU.mult), r=[XN, A2], w=[H2])
                                    H2B = h2b[0]
                                    dve(lambda v: v.tensor_tensor(out=H2B.t[:], in0=H2.t[:], in1=B2.t[:], op=ALU.add), r=[H2, B2], w=[H2B])
                                    b = tcount[0] % 2
                                    tcount[0] += 1
                                    for c in range(8):
                                        pe(lambda t: t.transpose(out=psB[:, b, c * 128:(c + 1) * 128], in_=H2B.t[:, c * 128:(c + 1) * 128],
                                                                 identity=ident.t[:]), r=[H2B, ident], w=[PBK[b]], inc=(c == 7))
                                    act(lambda a: a.copy(out=h2Tg.t[:, :, tl * 128:(tl + 1) * 128], in_=psB[:, b, :].rearrange("p (a c) -> p a c", a=8)),
                                        r=[PBK[b]], w=["h2T.%d" % tl])
                                    QS = qTs[0]
                                    for c in range(16):
                                        bk = c // 4
                                        for kc in range(8):
                                            pe(lambda t: t.matmul(psF[:, bk, (c % 4) * 128:(c % 4 + 1) * 128], lhsT=wq.t[:, kc, c * 128:(c + 1) * 128],
                                                                  rhs=h2Tg.t[:, kc, tl * 128:(tl + 1) * 128], start=(kc == 0), stop=(kc == 7)),
                                               r=["h2T.%d" % tl, wq], w=[PF[bk]], inc=(kc == 7))
                                    for bk in range(4):
                                        act(lambda a: a.copy(out=QS.t[:, bk * 4:(bk + 1) * 4, :].rearrange("p a c -> p (a c)"), in_=psF[:, bk, :]),
                                            r=[PF[bk]], w=[QS])
                                    SS_ = sc_s[tl]
                                    for c in range(16):
                                        bk = c // 4
                                        pe(lambda t: t.matmul(psF[:, bk, (c % 4) * 128:(c % 4 + 1) * 128], lhsT=QS.t[:, c, :], rhs=kT.t[:, c // 8, :],
                                                              start=True, stop=True), r=[QS, kT], w=[PF[bk]], inc=(c % 4 == 3))
                                    for bk in range(4):
                                        act(lambda a: a.copy(out=SS_.t[:, bk * 4:(bk + 1) * 4, :].rearrange("p a c -> p (a c)"), in_=psF[:, bk, :]),
                                            r=[PF[bk]], w=[SS_])
                                else:
                                    SS_ = sc_s[tl]
                                    SR = sc_r[0]
                                    V12 = v12[0]
                                    I12 = i12[0]
                                    for c in range(16):
                                        dve(lambda v: v.max(out=V12.t[:, c, 0:8], in_=SS_.t[:, c, :]), r=[SS_], w=["V12a.%d" % c])
                                    for c in range(16):
                                        dve(lambda v: v.max_index(out=I12.t[:, c, 0:8], in_max=V12.t[:, c, 0:8], in_values=SS_.t[:, c, :]),
                                            r=[SS_, "V12a.%d" % c], w=["I12a.%d" % c])
                                    for c in range(16):
                                        dve(lambda v: v.match_replace(out=SR.t[:, c, :], in_to_replace=V12.t[:, c, 0:8], in_values=SS_.t[:, c, :],
                                                                      imm_value=-1e30), r=[SS_, "V12a.%d" % c], w=["SR.%d" % c, "CD2.%d" % (c // 2)])
                                    for c in range(16):
                                        dve(lambda v: v.max(out=V12.t[:, c, 8:16], in_=SR.t[:, c, :]), r=["SR.%d" % c], w=["V12b.%d" % c])
                                    for c in range(16):
                                        dve(lambda v: v.max_index(out=I12.t[:, c, 8:16], in_max=V12.t[:, c, 8:16], in_values=SR.t[:, c, :]),
                                            r=["SR.%d" % c, "V12b.%d" % c], w=["I12b.%d" % c])
                                    V12K = ["V12a.%d" % c for c in range(16)] + ["V12b.%d" % c for c in range(16)]
                                    I12K = ["I12a.%d" % c for c in range(16)] + ["I12b.%d" % c for c in range(16)]
                                    SRK = ["SR.%d" % c for c in range(16)]
                                    I12F = i12f[0]
                                    dve(lambda v: v.tensor_copy(out=I12F.t[:], in_=I12.t[:]), r=I12K, w=[I12F])
                                    CD = cand[0]
                                    for hd in range(8):
                                        dve(lambda v: v.tensor_tensor(out=CD.t[:, hd, :].rearrange("p (r c) -> p r c", c=16),
                                                                      in0=fdims(V12.t[:, hd, :], [[1, 16], [0, 16]]),
                                                                      in1=fdims(V12.t[:, 8 + hd, :], [[0, 16], [1, 16]]), op=ALU.add),
                                            r=V12K, w=["CD.%d" % hd])
                                    CD2 = Tile(SR.t[:].rearrange("p a c -> p (a c)").rearrange("p (h c) -> p h c", h=8), SR.k)
                                    TP = top[0]
                                    PS_ = pos[0]
                                    for hd in range(8):
                                        dve(lambda v: v.max(out=TP.t[:, hd, 0:8], in_=CD.t[:, hd, :]), r=["CD.%d" % hd], w=["TPa.%d" % hd])
                                    for hd in range(8):
                                        dve(lambda v: v.max_index(out=PS_.t[:, hd, 0:8], in_max=TP.t[:, hd, 0:8], in_values=CD.t[:, hd, :]),
                                            r=["CD.%d" % hd, "TPa.%d" % hd], w=["PSa.%d" % hd])
                                    for hd in range(8):
                                        dve(lambda v: v.match_replace(out=CD2.t[:, hd, :], in_to_replace=TP.t[:, hd, 0:8], in_values=CD.t[:, hd, :],
                                                                      imm_value=-1e30), r=["CD.%d" % hd, "TPa.%d" % hd], w=["CD2.%d" % hd, "SR.%d" % (2 * hd), "SR.%d" % (2 * hd + 1)])
                                    for hd in range(8):
                                        dve(lambda v: v.max(out=TP.t[:, hd, 8:16], in_=CD2.t[:, hd, :]), r=["CD2.%d" % hd], w=["TPb.%d" % hd])
                                    for hd in range(8):
                                        dve(lambda v: v.max_index(out=PS_.t[:, hd, 8:16], in_max=TP.t[:, hd, 8:16], in_values=CD2.t[:, hd, :]),
                                            r=["CD2.%d" % hd, "TPb.%d" % hd], w=["PSb.%d" % hd])
                                    TPK = ["TPa.%d" % hd for hd in range(8)] + ["TPb.%d" % hd for hd in range(8)]
                                    PSK = ["PSa.%d" % hd for hd in range(8)] + ["PSb.%d" % hd for hd in range(8)]
                                    PR = posr[0]
                                    PC = posc[0]
                                    pflat = PS_.t[:].rearrange("p h k -> p (h k)")
                                    dve(lambda v: v.tensor_single_scalar(out=PR.t[:], in_=pflat, scalar=4, op=ALU.logical_shift_right), r=PSK, w=[PR])
                                    dve(lambda v: v.tensor_single_scalar(out=PC.t[:], in_=pflat, scalar=15, op=ALU.bitwise_and), r=PSK, w=[PC])
                                    PRF = posrf[0]
                                    PCF = poscf[0]
                                    dve(lambda v: v.tensor_copy(out=PRF.t[:], in_=PR.t[:]), r=[PR], w=[PRF])
                                    dve(lambda v: v.tensor_copy(out=PCF.t[:], in_=PC.t[:]), r=[PC], w=[PCF])
                                    OH = Tile(SS_.t[:].rearrange("p a c -> p (a c)").rearrange("p (k r) -> p k r", r=16), SS_.k)
                                    IJ = ijg[0]
                                    for (PF_, slot, half) in ((PRF, 0, 0), (PCF, 1, 1)):
                                        dve(lambda v: v.tensor_tensor(out=OH.t[:], in0=fdims(PF_.t[:], [[1, 128], [0, 16]]),
                                                                      in1=fdims(iota16.t[:], [[0, 128], [1, 16]]), op=ALU.is_equal),
                                            r=[PF_, iota16], w=[OH] + ["OHm.%d" % hd for hd in range(8)])
                                        for hd in range(8):
                                            dve(lambda v: v.tensor_tensor(out=OH.t[:, hd * 16:(hd + 1) * 16, :], in0=OH.t[:, hd * 16:(hd + 1) * 16, :],
                                                                          in1=fdims(I12F.t[:, half * 8 + hd, :], [[0, 16], [1, 16]]), op=ALU.mult),
                                                r=[OH, I12F], w=["OHm.%d" % hd])
                                        dve(lambda v: v.tensor_reduce(out=IJ.t[:, slot, :], in_=OH.t[:], axis=AX.X, op=ALU.add),
                                            r=["OHm.%d" % hd for hd in range(8)], w=["IJ.%d" % slot])
                                    GS = gsm[0]
                                    gwv = IJ.t[:, 2, :].rearrange("p (h k) -> p h k", k=16)
                                    dve(lambda v: v.tensor_tensor(out=gwv, in0=TP.t[:], in1=fdims(TP.t[:, :, 0], [[16, 8], [0, 16]]), op=ALU.subtract),
                                        r=TPK, w=[IJ])
                                    act(lambda a: a.activation(out=IJ.t[:, 2, :], in_=IJ.t[:, 2, :], func=AF.Exp), r=[IJ], w=[IJ])
                                    dve(lambda v: v.tensor_reduce(out=GS.t[:, 0:8], in_=gwv, axis=AX.X, op=ALU.add), r=[IJ], w=[GS])
                                    dve(lambda v: v.reciprocal(out=GS.t[:, 8:16], in_=GS.t[:, 0:8]), r=[GS], w=[GS])
                                    dve(lambda v: v.tensor_tensor(out=gwv, in0=gwv, in1=fdims(GS.t[:, 8:16], [[1, 8], [0, 16]]), op=ALU.mult),
                                        r=[IJ, GS], w=[IJ])
                                    for c in range(3):
                                        pe(lambda t: t.transpose(out=psF[:, 4, c * 128:(c + 1) * 128], in_=IJ.t[:, c, :], identity=identF.t[:]),
                                           r=[IJ, "IJ.0", "IJ.1", identF], w=[PF[4]], inc=(c == 2))
                                    act(lambda a: a.copy(out=IJG.t[:, :, tl * 128:(tl + 1) * 128], in_=psF[:, 4, 0:384].rearrange("p (a c) -> p a c", a=3)),
                                        r=[PF[4]], w=["IJG.%d" % tl])
                            s1.close()
                            fw.barrier()
                            s2 = contextlib.ExitStack()
                            cur[0] = s2
                            GT = sb(s2, "GT", [128, TG, 128], BF16)
                            wtb = rot("wtb", [128, TG], BF16, n=3)
                            ub = rot("ub", [128, 8, NI * 128], BF16, n=2)
                            vb = rot("vb", [128, NI, D], BF16, n=2)
                            NS = 16
                            r1s = rot("r1s", [128, NS, 128], BF16, n=2)
                            r2s = rot("r2s", [128, NS, 128], BF16, n=2)
                            atb = rot("atb", [128, TG], BF16, n=2)

                            first_touch = not cast_done[l]
                            cast_done[l] = True

                            def load_uv(ig):
                                U = nxt(ub)
                                V = nxt(vb)
                                if first_touch:
                                    ld(U.k, U.t[:], puT[:, ig * NI * 128:(ig + 1) * NI * 128].rearrange("(c p) e -> p c e", p=128), w=[U], q='pool')
                                    ld(V.k, V.t[:], pvv[ig * NI * 128:(ig + 1) * NI * 128, :].rearrange("(a p) d -> p a d", p=128), w=[V], q='pool')
                                    ld(U.k + "s", ubf[l, ig], U.t[:].rearrange("p c e -> p (c e)"), r=[U])
                                    ld(V.k + "s", vbf[l, ig], V.t[:].rearrange("p a d -> p (a d)"), r=[V])
                                else:
                                    ld(U.k, U.t[:].rearrange("p c e -> p (c e)"), ubf[l, ig], w=[U])
                                    ld(V.k, V.t[:].rearrange("p a d -> p (a d)"), vbf[l, ig], w=[V])
                                return U, V
                            pre = [load_uv(0), load_uv(1)]
                            gb = 0
                            for sbk in range(TG // NS):
                                t0 = sbk * NS
                                R1 = nxt(r1s)
                                R2 = nxt(r2s)
                                for t in range(NS):
                                    dve(lambda v: v.tensor_scalar(out=R2.t[:, t, :], in0=iota128b.t[:], scalar1=IJG.t[:, 1, t0 + t:t0 + t + 1], scalar2=None,
                                                                  op0=ALU.is_equal), r=[iota128b], w=["%s.%d" % (R2.k, t)])
                                for t in range(NS):
                                    dve(lambda v: v.tensor_scalar(out=R1.t[:, t, :], in0=iota128b.t[:], scalar1=IJG.t[:, 0, t0 + t:t0 + t + 1],
                                                                  scalar2=IJG.t[:, 2, t0 + t:t0 + t + 1], op0=ALU.is_equal, op1=ALU.mult),
                                        r=[iota128b], w=["%s.%d" % (R1.k, t)])
                                for q4 in range(NS // 4):
                                    bk = gb % 4
                                    gb += 1
                                    for tt in range(4):
                                        t = q4 * 4 + tt
                                        pe(lambda te: te.matmul(psF[:, bk, tt * 128:(tt + 1) * 128], lhsT=R2.t[:, t, :], rhs=R1.t[:, t, :],
                                                                start=True, stop=True), r=["%s.%d" % (R1.k, t), "%s.%d" % (R2.k, t)], w=[PF[bk]], inc=(tt == 3))
                                    tb = t0 + q4 * 4
                                    act(lambda a: a.copy(out=GT.t[:, tb:tb + 4, :], in_=psF[:, bk, :].rearrange("p (t i) -> p t i", t=4)),
                                        r=[PF[bk]], w=[GT])
                            def vside(i, V, ii, WT):
                                for tt in range(TGT):
                                    for hf in range(2):
                                        pe(lambda te: te.matmul(psF[:, tt * 2 + hf, :], lhsT=WT.t[:, tt * 128:(tt + 1) * 128],
                                                                rhs=V.t[:, ii, hf * 512:(hf + 1) * 512], start=(i == 0), stop=(i == 127)),
                                           r=[WT, V], w=[PF[tt * 2 + hf]], inc=(i == 127 or (tt == TGT - 1 and hf == 1)))
                            prev = None
                            for ig in range(128 // NI):
                                U, V = pre[ig] if ig < 2 else load_uv(ig)
                                for ii in range(NI):
                                    i = ig * NI + ii
                                    ab = 4 + (i % 2)
                                    for kc in range(8):
                                        pe(lambda te: te.matmul(psF[:, ab, 0:TG], lhsT=U.t[:, kc, ii * 128:(ii + 1) * 128], rhs=h2Tg.t[:, kc, :],
                                                                start=(kc == 0), stop=(kc == 7)),
                                           r=[U, "h2T.0", "h2T.1"], w=[PF[ab]], inc=(kc == 7))
                                    AT = nxt(atb)
                                    act(lambda a: a.activation(out=AT.t[:], in_=psF[:, ab, 0:TG], func=AF.Gelu_apprx_tanh), r=[PF[ab]], w=[AT])
                                    WT = nxt(wtb)
                                    dve(lambda v: v.tensor_tensor(out=WT.t[:], in0=GT.t[:, :, i], in1=AT.t[:], op=ALU.mult),
                                        r=[AT, GT], w=[WT])
                                    if prev is not None:
                                        vside(*prev)
                                    prev = (i, V, ii, WT)
                            vside(*prev)
                            for tl in range(TGT):
                                ti = g0 + tl
                                gt = tile0 + ti
                                X = nxt(xts)
                                ld(X.k, X.t[:], xsc[l, 0, gt * 128:(gt + 1) * 128, :], w=[X])
                                Y = nxt(ybuf)
                                dve(lambda v: v.tensor_tensor(out=Y.t[:].rearrange("p (a c) -> p a c", a=2), in0=psF[:, tl * 2:tl * 2 + 2, :],
                                                              in1=G2.t[:].rearrange("p (a c) -> p a c", a=2), op=ALU.mult),
                                    r=[PF[tl * 2], PF[tl * 2 + 1], G2], w=[Y])
                                dve(lambda v: v.scalar_tensor_tensor(out=Y.t[:], in0=X.t[:], scalar=ALPHA, in1=Y.t[:], op0=ALU.mult, op1=ALU.add),
                                    r=[X, Y], w=[Y])
                                S = ln_stats(Y.t[:], [Y])
                                X2 = nxt(x2b)
                                act(lambda a: a.activation(out=X2.t[:], in_=Y.t[:], func=AF.Identity, scale=S.t[:, 4:5], bias=S.t[:, 5:6]),
                                    r=[Y, S], w=[X2])
                                dve(lambda v: v.tensor_tensor(out=X2.t[:], in0=X2.t[:], in1=lnb.t[:, 0, :], op=ALU.mult), r=[X2, lnb], w=[X2])
                                dve(lambda v: v.tensor_tensor(out=X2.t[:], in0=X2.t[:], in1=lnb.t[:, 1, :], op=ALU.add), r=[X2, lnb], w=[X2])
                                if l == NL - 1:
                                    if gt < 8:
                                        dst = yp[gt * 128:(gt + 1) * 128, :]
                                    else:
                                        dst = ys[(gt - 8) * 128:(gt - 7) * 128, :]
                                    ld(X2.k + "st", dst, X2.t[:], r=[X2])
                                    if dbg:
                                        ld(X2.k + "st", xsc[l, 1, gt * 128:(gt + 1) * 128, :], X2.t[:], r=[X2], w=["xsc1.%d" % gt])
                                else:
                                    ld(X2.k + "st", xsc[l, 1, gt * 128:(gt + 1) * 128, :], X2.t[:], r=[X2], w=["xsc1.%d" % gt])
                            s2.close()
                            fw.barrier()
                    fw.barrier()
        except _Stop:
            pass
        fw.finish()
        build_nc.ninst = fw.ninst
        build_nc.marks = fw.marks
    return nc


def _rope_tables():
    t = np.arange(SS)
    row = (t // 64).astype(np.float32)
    col = (t % 64).astype(np.float32)

    def tab(m, nh):
        freqs = (10000.0 ** (-np.arange(m, dtype=np.float32) / m)).astype(np.float32)
        ar = row[:, None] * freqs[None, :]
        ac = col[:, None] * freqs[None, :]
        cr, sr, cc, sn = np.cos(ar), np.sin(ar), np.cos(ac), np.sin(ac)
        C = np.concatenate([cr, cr, cc, cc], axis=1)
        S = np.concatenate([-sr, sr, -sn, sn], axis=1)
        return (np.tile(C, (1, nh)).astype(np.float32), np.tile(S, (1, nh)).astype(np.float32))
    C16, S16 = tab(16, 10)
    C8, S8 = tab(8, 5)
    return C16, S16, C8, S8


def make_in_maps(inp):
    f = lambda a: np.ascontiguousarray(np.asarray(a, dtype=np.float32))
    C16, S16, C8, S8 = _rope_tables()
    w_mod = f(inp["w_mod"])
    b_mod = f(inp["b_mod"])
    b_modT = f(b_mod.reshape(NL, 48, 128).transpose(0, 2, 1))
    gqk = f(np.concatenate([np.tile(inp["attn_q_norm"], (1, 8)), np.tile(inp["attn_k_norm"], (1, 2))], axis=1))
    wsT = f(np.asarray(inp["gmlp_ws"]).transpose(0, 3, 1, 2))
    bsT = f(np.asarray(inp["gmlp_b"]).transpose(0, 2, 1))
    lnp = f(np.stack([inp["ln1_g"], inp["ln1_b"], inp["ln2_g"], inp["ln2_b"]], axis=1))
    pwq = f(np.asarray(inp["peer_wq"]).reshape(NL, D, 8, 2, 128).transpose(0, 1, 3, 2, 4).reshape(NL, D, 2048))
    k12T = f(np.stack([np.asarray(inp["peer_k1"]).transpose(0, 2, 1), np.asarray(inp["peer_k2"]).transpose(0, 2, 1)], axis=2))
    shared = {
        "w_mod": w_mod, "b_modT": b_modT, "b_mod": b_mod, "w_in": f(inp["w_in"]), "gqk": gqk,
        "mqn": f(inp["mla_q_norm"]), "mkvn": f(inp["mla_kv_norm"]), "w_uq": f(inp["w_uq"]), "w_ukv": f(inp["w_ukv"]),
        "wsT": wsT, "bsT": bsT, "w_o": f(inp["w_o"]), "lnp": lnp, "peer_wq": pwq, "k12T": k12T,
        "puT0": f(np.asarray(inp["peer_u"][0]).T), "puT1": f(np.asarray(inp["peer_u"][1]).T), "pv0": f(inp["peer_v"][0]), "pv1": f(inp["peer_v"][1]),
        "ropeC": C16, "ropeS": S16, "ropeMC": C8, "ropeMS": S8,
    }
    maps = []
    xpr = np.asarray(inp["x_prompt"], dtype=np.float32)
    xsa = np.asarray(inp["x_sample"], dtype=np.float32)
    cctx = np.asarray(inp["c_ctx"], dtype=np.float32)
    for c in range(NCORES):
        m = dict(shared)
        m["xp"] = f(xpr[PB * c:PB * (c + 1)].reshape(PB * PS, D))
        m["xs"] = f(xsa[c])
        m["ck"] = f(np.asarray(inp["cache_attn_k"])[c].reshape(NL, PAST, 128))
        m["cv"] = f(np.asarray(inp["cache_attn_v"])[c].reshape(NL, PAST, 128))
        m["cckv"] = f(np.asarray(inp["cache_mla_ckv"])[c])
        m["ckr"] = f(np.asarray(inp["cache_mla_krope"])[c])
        cc = np.stack([cctx, np.asarray(inp["c"], dtype=np.float32)[c]], axis=-1)
        m["cT"] = f(cc.reshape(8, 128, 2).transpose(1, 0, 2))
        maps.append(m)
    return maps


_NC_CACHE = {}


def kernel(**inputs):
    if "nc" not in _NC_CACHE:
        _NC_CACHE["nc"] = build_nc(False)
    nc = _NC_CACHE["nc"]
    maps = make_in_maps(inputs)
    res = run_bass_kernel_spmd(nc, maps, core_ids=list(range(NCORES)))
    R = res.results
    y_prompt = np.concatenate([R[c]["yp"].reshape(PB, PS, D) for c in range(NCORES)], axis=0)
    y_sample = np.stack([R[c]["ys"] for c in range(NCORES)], axis=0)
    nk = np.concatenate([R[c]["nk"].reshape(PB, NL, PS, 2, 64) for c in range(NCORES)], axis=0)
    nv = np.concatenate([R[c]["nv"].reshape(PB, NL, PS, 2, 64) for c in range(NCORES)], axis=0)
    nckv = np.concatenate([R[c]["nckv"] for c in range(NCORES)], axis=0)
    nkr = np.concatenate([R[c]["nkr"] for c in range(NCORES)], axis=0)
    return (y_prompt.astype(np.float32), y_sample.astype(np.float32), nk.astype(np.float32), nv.astype(np.float32),
            nckv.astype(np.float32), nkr.astype(np.float32))
```
